# Optimizing a Trainium2 kernel written in Bass

```python
import math
import jax, jax.numpy as jnp
from jax import lax
import numpy as np

D_MODEL = 1024
BATCH = 8
SEQ = 2048
DEPTH = 2

GRID_W = 64
CTX_LEN = 256
MIX = D_MODEL
GROUP = MIX // 4
HEAD_DIM = 64
N_HEADS = GROUP // HEAD_DIM
SHORT_W = 3
GDN_CHUNK = 64
GDN_COLS = 4 * GROUP + 4 * N_HEADS
SC_COLS = 3 * GROUP
HY_COLS = 3 * GROUP
NA_PROJ = 3 * GROUP
OFF_SC = GDN_COLS
OFF_HY = OFF_SC + SC_COLS
OFF_NA = OFF_HY + HY_COLS
IN_COLS = OFF_NA + NA_PROJ
HY_EMB = 33
HY_BANDS = (HY_EMB - 1) // 2
HY_HIDDEN = 64
HY_TARGET = 1e-2
HY_FAST = 0.3
HY_SLOW = 1.5
HY_MAX_DECAY = math.log(HY_TARGET) / HY_FAST
HY_MIN_DECAY = math.log(HY_TARGET) / HY_SLOW
NA_ROWS = 8
NA_COLS = 16
NA_QCOLS = 16
NA_KCOLS = 32
ROPE_BASE = 10000.0
D_FF = 4 * D_MODEL
EPS = 1e-6
NEG = -1e30

kernel_name = "hybrid_parallel_heads_diffusion_block"


def rmsnorm(t, g):
    tf = t.astype(jnp.float32)
    tf = tf * lax.rsqrt(jnp.mean(tf * tf, axis=-1, keepdims=True) + EPS)
    return (tf * g.astype(jnp.float32)).astype(t.dtype)


def l2norm(t):
    return t * lax.rsqrt(jnp.sum(t * t, axis=-1, keepdims=True) + EPS)


def modulation(cond, w, b):
    m = jax.nn.silu(cond) @ w + b
    m = m.reshape(cond.shape[:-1] + (6, D_MODEL))
    return [m[..., i, None, :] for i in range(6)]


def dwconv(t, w):
    k = w.shape[0]
    return lax.conv_general_dilated(t, w[:, None, :].astype(t.dtype), window_strides=(1,),
                                    padding=[(k // 2, k // 2)],
                                    dimension_numbers=("NWC", "WIO", "NWC"),
                                    feature_group_count=t.shape[-1])


def axial_rope(t):
    L = t.shape[2]
    pos = jnp.arange(L)
    row = (pos // GRID_W).astype(jnp.float32)
    col = (pos % GRID_W).astype(jnp.float32)
    nf = HEAD_DIM // 4
    inv = ROPE_BASE ** (-jnp.arange(nf, dtype=jnp.float32) / nf)
    ang = jnp.concatenate([row[:, None] * inv, col[:, None] * inv], axis=-1)
    cos, sin = jnp.cos(ang), jnp.sin(ang)
    t1, t2 = t[..., :HEAD_DIM // 2], t[..., HEAD_DIM // 2:]
    return jnp.concatenate([t1 * cos - t2 * sin, t1 * sin + t2 * cos], axis=-1)


def split_heads(t):
    bsz, L, _ = t.shape
    return t.reshape(bsz, L, N_HEADS, HEAD_DIM).transpose(0, 2, 1, 3)


def merge_heads(t):
    bsz, _, L, _ = t.shape
    return t.transpose(0, 2, 1, 3).reshape(bsz, L, GROUP)


def gdn_chunked(q, k, v, log_a, beta, s0):
    bsz, nh, L, dh = q.shape
    n = L // GDN_CHUNK
    def chunks(t):
        return t.reshape((bsz, nh, n, GDN_CHUNK) + t.shape[3:])
    q, k, v, log_a, beta = (chunks(t) for t in (q, k, v, log_a, beta))
    g = jnp.cumsum(log_a, axis=-1)
    incl = jnp.tril(jnp.ones((GDN_CHUNK, GDN_CHUNK), dtype=bool))
    strict = jnp.tril(jnp.ones((GDN_CHUNK, GDN_CHUNK), dtype=bool), -1)
    decay = jnp.exp(jnp.where(incl, g[..., :, None] - g[..., None, :], -jnp.inf))
    kb = k * beta[..., None]
    a_mat = jnp.where(strict, jnp.einsum("bhnid,bhnjd->bhnij", kb, k) * decay, 0.0)
    sys = a_mat + jnp.eye(GDN_CHUNK, dtype=q.dtype)
    rhs = jnp.concatenate([v * beta[..., None], kb * jnp.exp(g)[..., None]], axis=-1)
    sol = lax.linalg.triangular_solve(sys, rhs, left_side=True, lower=True, unit_diagonal=True)
    u, w = sol[..., :dh], sol[..., dh:]
    qk = jnp.einsum("bhnid,bhnjd->bhnij", q, k) * decay
    q_dec = q * jnp.exp(g)[..., None]
    k_dec = k * jnp.exp(g[..., -1:] - g)[..., None]
    g_last = jnp.exp(g[..., -1])

    def step(s, inp):
        qd, kd, u_i, w_i, qk_i, gl = inp
        v_new = u_i - jnp.einsum("bhck,bhkv->bhcv", w_i, s)
        o = jnp.einsum("bhck,bhkv->bhcv", qd, s) + jnp.einsum("bhij,bhjv->bhiv", qk_i, v_new)
        s = s * gl[..., None, None] + jnp.einsum("bhck,bhcv->bhkv", kd, v_new)
        return s, o

    xs = tuple(jnp.moveaxis(t, 2, 0) for t in (q_dec, k_dec, u, w, qk, g_last))
    s_fin, o = lax.scan(step, s0, xs)
    o = jnp.moveaxis(o, 0, 2).reshape(bsz, nh, L, dh)
    return o, s_fin


def gdn_prepare(p, conv_w, a_log, dt_bias, latent):
    bsz, L, _ = p.shape
    qkv = jax.nn.silu(dwconv(p[..., :3 * GROUP], conv_w)).astype(jnp.float32)
    gate = p[..., 3 * GROUP:4 * GROUP]
    a = p[..., 4 * GROUP:4 * GROUP + 2 * N_HEADS].astype(jnp.float32).reshape(bsz, L, 2, N_HEADS)
    b = p[..., 4 * GROUP + 2 * N_HEADS:].astype(jnp.float32).reshape(bsz, L, 2, N_HEADS)
    q, k, v = (split_heads(t) for t in jnp.split(qkv, 3, axis=-1))
    q, k = l2norm(q), l2norm(k)
    if latent:
        q, k = axial_rope(q), axial_rope(k)
    q = q * HEAD_DIM ** -0.5
    log_a = -jnp.exp(a_log.astype(jnp.float32)) * jax.nn.softplus(a + dt_bias.astype(jnp.float32))
    log_a = log_a.transpose(2, 0, 3, 1)
    beta = jax.nn.sigmoid(b).transpose(2, 0, 3, 1)
    return q, k, v, log_a, beta, gate


def gdn_output(o, gate, norm_g):
    bsz, _, L, _ = o.shape
    o = o.transpose(0, 2, 1, 3)
    g = gate.reshape(bsz, L, N_HEADS, HEAD_DIM).astype(jnp.float32)
    y = rmsnorm(o, norm_g) * jax.nn.silu(g)
    return y.reshape(bsz, L, GROUP).astype(gate.dtype)


def gdn_mixer(pc, px, conv_w, a_log, dt_bias, norm_g, with_ctx):
    qc, kc, vc, lac, bc, gc = gdn_prepare(pc, conv_w, a_log, dt_bias, latent=False)
    qx, kx, vx, lx, bx, gx = gdn_prepare(px, conv_w, a_log, dt_bias, latent=True)
    bsz = px.shape[0]
    s0 = jnp.zeros((bsz, N_HEADS, HEAD_DIM, HEAD_DIM), jnp.float32)
    flip = lambda t: jnp.flip(t, axis=2)
    oc_f, sc_f = gdn_chunked(qc, kc, vc, lac[0], bc[0], s0)
    ox_f, _ = gdn_chunked(qx, kx, vx, lx[0], bx[0], sc_f)
    oc_b, sc_b = gdn_chunked(flip(qc), flip(kc), flip(vc), flip(lac[1]), flip(bc[1]), s0)
    ox_b, _ = gdn_chunked(flip(qx), flip(kx), flip(vx), flip(lx[1]), flip(bx[1]), sc_b)
    yx = gdn_output(ox_f + flip(ox_b), gx, norm_g)
    yc = gdn_output(oc_f + flip(oc_b), gc, norm_g) if with_ctx else None
    return yc, yx


def short_conv_mixer(p, conv_w):
    bg, cg, xt = jnp.split(p, 3, axis=-1)
    return bg * dwconv(cg * xt, conv_w)


def hyena_filters(L, w1, b1, w2, b2, w3, b3, freq, w4):
    f32 = jnp.float32
    t = jnp.linspace(0.0, 1.0, L, dtype=f32)[:, None]
    bands = jnp.linspace(1e-4, HY_BANDS - 1, HY_BANDS, dtype=f32)
    ang = (2.0 * math.pi / L) * jnp.arange(L, dtype=f32)[:, None] * bands
    z = jnp.concatenate([t, jnp.cos(ang), -jnp.sin(ang)], axis=-1)
    fr = freq.astype(f32)
    h = jnp.sin(fr[0] * (z @ w1.astype(f32) + b1.astype(f32)))
    h = jnp.sin(fr[1] * (h @ w2.astype(f32) + b2.astype(f32)))
    h = jnp.sin(fr[2] * (h @ w3.astype(f32) + b3.astype(f32)))
    h = (h @ w4.astype(f32)).reshape(L, 2, GROUP)
    deltas = jnp.abs(jnp.linspace(HY_MIN_DECAY, HY_MAX_DECAY, GROUP, dtype=f32))
    h = h * jnp.exp(-t * deltas)[:, None, :]
    h = h / jnp.sum(jnp.abs(h), axis=(0, 1), keepdims=True)
    return h[:, 0], h[:, 1]


def long_conv(u, h_fwd, h_bwd, bias):
    L = u.shape[1]
    kern = jnp.concatenate([h_fwd, jnp.zeros((1, GROUP), jnp.float32), h_bwd[:0:-1]], axis=0)
    uf = u.astype(jnp.float32)
    y = jnp.fft.irfft(jnp.fft.rfft(uf, n=2 * L, axis=1) * jnp.fft.rfft(kern, axis=0)[None],
                      n=2 * L, axis=1)[:, :L]
    return (y + uf * bias.astype(jnp.float32)).astype(u.dtype)


def hyena_mixer(p, conv_w, w1, b1, w2, b2, w3, b3, freq, w4, bias):
    u = dwconv(p, conv_w)
    x0, x1, v = jnp.split(u, 3, axis=-1)
    h_fwd, h_bwd = hyena_filters(p.shape[1], w1, b1, w2, b2, w3, b3, freq, w4)
    return x0 * long_conv(x1 * v, h_fwd, h_bwd, bias)


def na_mixer(pc, px, rpb, with_ctx):
    f32 = jnp.float32
    bsz, L, _ = px.shape
    rows = L // GRID_W
    wr = min(NA_ROWS, rows)
    scale = HEAD_DIM ** -0.5
    qc, kc, vc = (split_heads(t.astype(f32)) for t in jnp.split(pc, 3, axis=-1))
    qx, kx, vx = (split_heads(t.astype(f32)) for t in jnp.split(px, 3, axis=-1))
    yc = None
    if with_ctx:
        p_cc = jax.nn.softmax(jnp.einsum("bhqd,bhkd->bhqk", qc * scale, kc), axis=-1)
        yc = merge_heads(jnp.einsum("bhqk,bhkd->bhqd", p_cc, vc)).astype(pc.dtype)
    ncb = GRID_W // NA_QCOLS
    r = jnp.arange(rows)
    row_idx = jnp.clip(r - wr // 2, 0, rows - wr)[:, None] + jnp.arange(wr)[None]
    qcol = jnp.arange(GRID_W).reshape(ncb, NA_QCOLS)
    col_idx = jnp.clip(qcol[:, 0] - NA_COLS // 2, 0, GRID_W - NA_KCOLS)[:, None] + jnp.arange(NA_KCOLS)[None]
    def gather_blocks(t):
        g = t.reshape(bsz, N_HEADS, rows, GRID_W, HEAD_DIM)[:, :, row_idx]
        return g[:, :, :, :, col_idx]
    kb, vb = gather_blocks(kx), gather_blocks(vx)
    qb = qx.reshape(bsz, N_HEADS, rows, ncb, NA_QCOLS, HEAD_DIM) * scale
    s_win = jnp.einsum("bhrnqd,bhrwnkd->bhrnqwk", qb, kb)
    col_start = jnp.clip(qcol - NA_COLS // 2, 0, GRID_W - NA_COLS)
    kcol = col_idx[:, None, :]
    valid = (kcol >= col_start[..., None]) & (kcol < col_start[..., None] + NA_COLS)
    dr = row_idx - r[:, None]
    dc = jnp.clip(kcol - qcol[..., None], -(NA_COLS - 1), NA_COLS - 1)
    bias = rpb.astype(f32)[:, (dr + NA_ROWS - 1)[:, None, None, :, None],
                           (dc + NA_COLS - 1)[None, :, :, None, :]]
    s_win = jnp.where(valid[:, :, None, :], s_win + bias, NEG)
    s_ctx = jnp.einsum("bhrnqd,bhcd->bhrnqc", qb, kc)
    nwin = wr * NA_KCOLS
    s_all = jnp.concatenate([s_win.reshape(s_win.shape[:5] + (nwin,)), s_ctx], axis=-1)
    p_all = jax.nn.softmax(s_all, axis=-1)
    p_win = p_all[..., :nwin].reshape(s_win.shape)
    p_ctx = p_all[..., nwin:]
    o = (jnp.einsum("bhrnqwk,bhrwnkd->bhrnqd", p_win, vb)
         + jnp.einsum("bhrnqc,bhcd->bhrnqd", p_ctx, vc))
    yx = merge_heads(o.reshape(bsz, N_HEADS, L, HEAD_DIM)).astype(px.dtype)
    return yc, yx


def sq_relu_mlp(h, w1, w2):
    return jnp.square(jax.nn.relu(h @ w1)) @ w2


def setup_inputs(seed: int = 0) -> dict:
    key = jax.random.key(seed)
    ks = iter(jax.random.split(key, 48))
    D = D_MODEL
    def nrm(shape, s):
        return jax.random.normal(next(ks), shape, jnp.float32) * s
    def gain(shape):
        return 1.0 + nrm(shape, 0.02)
    dt = jnp.exp(jax.random.uniform(next(ks), (DEPTH, 2, N_HEADS), jnp.float32,
                                    math.log(1e-3), math.log(1e-1)))
    return {
        "x": nrm((BATCH, SEQ, D), 1.0),
        "c": nrm((BATCH, D), 1.0),
        "ctx": nrm((BATCH, CTX_LEN, D), 1.0),
        "c_ctx": nrm((D,), 1.0),
        "ada_w": nrm((DEPTH, D, 6 * D), 0.5 * D ** -0.5),
        "ada_b": nrm((DEPTH, 6 * D), 0.02),
        "norm_pre_mix": gain((DEPTH, D)),
        "norm_post_mix": gain((DEPTH, D)),
        "norm_pre_mlp": gain((DEPTH, D)),
        "norm_post_mlp": gain((DEPTH, D)),
        "w_in": nrm((DEPTH, D, IN_COLS), D ** -0.5),
        "w_out": nrm((DEPTH, MIX, D), MIX ** -0.5),
        "gdn_conv": nrm((DEPTH, SHORT_W, 3 * GROUP), SHORT_W ** -0.5),
        "gdn_a_log": jnp.log(jax.random.uniform(next(ks), (DEPTH, 2, N_HEADS), jnp.float32, 1.0, 16.0)),
        "gdn_dt_bias": dt + jnp.log(-jnp.expm1(-dt)),
        "gdn_norm": gain((DEPTH, HEAD_DIM)),
        "sc_conv": nrm((DEPTH, SHORT_W, GROUP), SHORT_W ** -0.5),
        "hy_conv": nrm((DEPTH, SHORT_W, 3 * GROUP), SHORT_W ** -0.5),
        "hy_w1": nrm((DEPTH, HY_EMB, HY_HIDDEN), HY_EMB ** -0.5),
        "hy_b1": nrm((DEPTH, HY_HIDDEN), 0.02),
        "hy_w2": nrm((DEPTH, HY_HIDDEN, HY_HIDDEN), HY_HIDDEN ** -0.5),
        "hy_b2": nrm((DEPTH, HY_HIDDEN), 0.02),
        "hy_w3": nrm((DEPTH, HY_HIDDEN, HY_HIDDEN), HY_HIDDEN ** -0.5),
        "hy_b3": nrm((DEPTH, HY_HIDDEN), 0.02),
        "hy_freq": gain((DEPTH, 3, HY_HIDDEN)),
        "hy_w4": nrm((DEPTH, HY_HIDDEN, 2 * GROUP), HY_HIDDEN ** -0.5),
        "hy_bias": nrm((DEPTH, GROUP), 1.0),
        "na_rpb": nrm((DEPTH, N_HEADS, 2 * NA_ROWS - 1, 2 * NA_COLS - 1), 0.02),
        "mlp_w1": nrm((DEPTH, D, D_FF), D ** -0.5),
        "mlp_w2": nrm((DEPTH, D_FF, D), D_FF ** -0.5),
    }


def reference(x, c, ctx, c_ctx, ada_w, ada_b, norm_pre_mix, norm_post_mix, norm_pre_mlp, norm_post_mlp,
              w_in, w_out, gdn_conv, gdn_a_log, gdn_dt_bias, gdn_norm, sc_conv, hy_conv,
              hy_w1, hy_b1, hy_w2, hy_b2, hy_w3, hy_b3, hy_freq, hy_w4, hy_bias, na_rpb, mlp_w1, mlp_w2):
    xc = ctx
    for l in range(DEPTH):
        with_ctx = l < DEPTH - 1
        mx = modulation(c, ada_w[l], ada_b[l])
        mc = modulation(c_ctx, ada_w[l], ada_b[l])
        hx = rmsnorm(x, norm_pre_mix[l]) * (1.0 + mx[1]) + mx[0]
        hc = rmsnorm(xc, norm_pre_mix[l]) * (1.0 + mc[1]) + mc[0]
        px = hx @ w_in[l]
        pc = hc @ w_in[l]
        hy_params = (hy_w1[l], hy_b1[l], hy_w2[l], hy_b2[l], hy_w3[l], hy_b3[l], hy_freq[l], hy_w4[l], hy_bias[l])
        gdn_c, gdn_x = gdn_mixer(pc[..., :OFF_SC], px[..., :OFF_SC], gdn_conv[l], gdn_a_log[l],
                                 gdn_dt_bias[l], gdn_norm[l], with_ctx)
        sc_x = short_conv_mixer(px[..., OFF_SC:OFF_HY], sc_conv[l])
        hy_x = hyena_mixer(px[..., OFF_HY:OFF_NA], hy_conv[l], *hy_params)
        na_c, na_x = na_mixer(pc[..., OFF_NA:], px[..., OFF_NA:], na_rpb[l], with_ctx)
        yx = jnp.concatenate([gdn_x, sc_x, hy_x, na_x], axis=-1) @ w_out[l]
        x = x + mx[2] * rmsnorm(yx, norm_post_mix[l])
        hx = rmsnorm(x, norm_pre_mlp[l]) * (1.0 + mx[4]) + mx[3]
        x = x + mx[5] * rmsnorm(sq_relu_mlp(hx, mlp_w1[l], mlp_w2[l]), norm_post_mlp[l])
        if with_ctx:
            sc_c = short_conv_mixer(pc[..., OFF_SC:OFF_HY], sc_conv[l])
            hy_c = hyena_mixer(pc[..., OFF_HY:OFF_NA], hy_conv[l], *hy_params)
            yc = jnp.concatenate([gdn_c, sc_c, hy_c, na_c], axis=-1) @ w_out[l]
            xc = xc + mc[2] * rmsnorm(yc, norm_post_mix[l])
            hc = rmsnorm(xc, norm_pre_mlp[l]) * (1.0 + mc[4]) + mc[3]
            xc = xc + mc[5] * rmsnorm(sq_relu_mlp(hc, mlp_w1[l], mlp_w2[l]), norm_post_mlp[l])
    return x
```

```python
import numpy as np
import ml_dtypes
from contextlib import ExitStack
import concourse.bass as bass
import concourse.mybir as mybir

F32 = mybir.dt.float32
BF16 = mybir.dt.bfloat16
I32 = mybir.dt.int32
AF = mybir.ActivationFunctionType
ALU = mybir.AluOpType

D = 1024
L = 2048
LC = 256
T = L + LC
NT = T // 128
DEPTH = 2
EPS = 1e-6
IN_COLS = 3344
OFF_SC, OFF_HY, OFF_NA = 1040, 1808, 2576


class Dep:
    __slots__ = ("lw", "rd")

    def __init__(self):
        self.lw = None
        self.rd = []


class Chan:
    __slots__ = ("sem", "cnt", "key", "q")

    def __init__(self, sem, key):
        self.sem = sem
        self.cnt = 0
        self.key = key


class K:
    def __init__(self, nc, stack):
        self.nc = nc
        self.stack = stack
        self.eng = {"pe": nc.tensor, "act": nc.scalar, "dve": nc.vector, "pool": nc.gpsimd, "sp": nc.sync}
        self.sems = {}
        self.ecnt = {}
        self.waited = {e: {} for e in self.eng}
        for e in self.eng:
            self.sems["e_" + e] = stack.enter_context(nc.semaphore("e_" + e))
            self.ecnt[e] = 0
        self.chans = []
        self.bar_sem = stack.enter_context(nc.semaphore("bar"))
        self.bar_cnt = 0
        self.nops = 0

    def chan(self):
        key = "c%d" % len(self.chans)
        s = self.stack.enter_context(self.nc.semaphore(key))
        self.sems[key] = s
        c = Chan(s, key)
        c.q = None
        self.chans.append(c)
        return c

    def _wait(self, e, ev):
        key, val, src = ev
        if src == e and e == "pe":
            return
        w = self.waited[e]
        if w.get(key, 0) >= val:
            return
        w[key] = val
        self.eng[e].wait_ge(self.sems[key], val)

    def _deps(self, e, r, w):
        for d in r:
            if d.lw is not None:
                self._wait(e, d.lw)
        for d in w:
            if d.lw is not None and d.lw[2] != e:
                self._wait(e, d.lw)
            for ev in d.rd:
                if ev[2] != e:
                    self._wait(e, ev)

    def _commit(self, ev, r, w):
        for d in w:
            d.lw = ev
            d.rd = []
        for d in r:
            d.rd.append(ev)
            if len(d.rd) > 48:
                best = {}
                for x in d.rd:
                    if x[0] not in best or best[x[0]][1] < x[1]:
                        best[x[0]] = x
                d.rd = list(best.values())

    def op(self, e, fn, r=(), w=()):
        self._deps(e, r, w)
        ins = fn(self.eng[e])
        self.ecnt[e] += 1
        ins.then_inc(self.sems["e_" + e], 1)
        self._commit(("e_" + e, self.ecnt[e], e), r, w)
        self.nops += 1
        return ins

    def pe(self, fn, r=(), w=()):
        return self.op("pe", fn, r, w)

    def act(self, fn, r=(), w=()):
        return self.op("act", fn, r, w)

    def dve(self, fn, r=(), w=()):
        return self.op("dve", fn, r, w)

    def pool(self, fn, r=(), w=()):
        return self.op("pool", fn, r, w)

    def dma(self, q, out, in_, ch, r=(), w=(), **kw):
        self._deps(q, r, w)
        ins = self.eng[q].dma_start(out=out, in_=in_, **kw)
        ch.cnt += 16
        ch.q = q
        ins.then_inc(ch.sem, 16)
        self._commit((ch.key, ch.cnt, "dma"), r, w)
        self.nops += 1
        return ins

    def barrier(self):
        for e in self.eng:
            if self.ecnt[e] > 0:
                self._wait(e, ("e_" + e, self.ecnt[e], "self"))
        for c in self.chans:
            if c.cnt > 0:
                self._wait(c.q or "sp", (c.key, c.cnt, "dma"))
        for e in self.eng:
            self.eng[e].sem_inc(self.bar_sem, 1)
        self.bar_cnt += len(self.eng)
        for e in self.eng:
            self.eng[e].wait_ge(self.bar_sem, self.bar_cnt)
            for e2 in self.eng:
                self.waited[e]["e_" + e2] = self.ecnt[e2]
            for c in self.chans:
                self.waited[e][c.key] = c.cnt


class MK:
    def __init__(self, dbg=None, layers=(0, 1), inject_cat=False, mixers=("gdn", "sc", "hy", "na"), do_mlp=True,
                 phases=("mod", "p1", "mix", "p3", "p4"), inject_h=False):
        self.phases = phases
        self.inject_h = inject_h
        self.dbg = dbg or {}
        self.layers = layers
        self.inject_cat = inject_cat
        self.mixers = mixers
        self.do_mlp = do_mlp
        self.inputs = {}
        self.outputs = {}

    def din(self, name, shape, dtype=F32):
        t = self.nc.dram_tensor(name, list(shape), dtype, kind="ExternalInput").ap()
        self.inputs[name] = (tuple(shape), dtype)
        return t

    def dout(self, name, shape, dtype=F32):
        t = self.nc.dram_tensor(name, list(shape), dtype, kind="ExternalOutput").ap()
        self.outputs[name] = (tuple(shape), dtype)
        return t

    def W(self, name):
        if name not in self._w:
            self._w[name] = self.din(name, self._wshape[name])
        return self._w[name]

    def sb(self, st, name, shape, dtype=F32):
        self._n = getattr(self, "_n", 0) + 1
        return st.enter_context(self.nc.sbuf_tensor("%s_%d" % (name, self._n), list(shape), dtype))

    def build(self):
        nc = bass.Bass("TRN2", target_bir_lowering=False)
        self.nc = nc
        with ExitStack() as st:
            self.st = st
            self.k = K(nc, st)
            self._declare()
            self._consts()
            for l in self.layers:
                self._layer(l)
            self._finish()
        return nc

    def _declare(self):
        nc = self.nc
        self._wshape = {}
        self._w = {}
        self.x_in = self.din("x", [L, D])
        self.ctx_in = self.din("ctx", [LC, D])
        self.cvec_in = self.din("cvec", [128, 16])
        self._wshape["ada_w"] = [DEPTH, D, 6 * D]
        self.ada_bT = self.din("ada_bT", [128, DEPTH * 48])
        self.gains_in = self.din("gains", [128, 4 * DEPTH * 8])
        self._wshape["w_in"] = [DEPTH, D, IN_COLS]
        self._wshape["w_out"] = [DEPTH, D, D]
        self._wshape["mlp_w1"] = [DEPTH, D, 4 * D]
        self._wshape["mlp_w2"] = [DEPTH, 4 * D, D]
        self.ident_f_in = self.din("ident_f", [128, 128])
        self.ident_b_in = self.din("ident_b", [128, 128], BF16)
        self.out = self.dout("out", [L, D])
        self.xres = nc.dram_tensor("xres", [T, D], F32).ap()
        self.d_xres = [Dep() for _ in range(NT)]
        self.d_out = [Dep() for _ in range(NT)]
        if self.inject_cat:
            self.cat_in = self.din("cat_in", [D, T], BF16)
        self.ps = [self.st.enter_context(nc.psum_tensor("ps%d" % i, [128, 1024], F32)) for i in range(4)]
        self.d_ps = [[Dep(), Dep()] for _ in range(4)]

    def _consts(self):
        k, st = self.k, self.st
        sb = lambda n, s, d=F32: self.sb(st, n, s, d)
        self.ident_f = sb("ident_f", [128, 128])
        self.ident_b = sb("ident_b", [128, 128], BF16)
        self.ones_f = sb("ones_f", [128, 128])
        self.cvec = sb("cvec", [128, 16])
        self.adab = sb("adab", [128, DEPTH * 48])
        self.gains = sb("gains", [128, 4 * DEPTH * 8])
        self.d_const = Dep()
        ch = k.chan()
        k.dma("sp", self.ident_f[:], self.ident_f_in[:, :], ch, w=[self.d_const])
        k.dma("sp", self.ident_b[:], self.ident_b_in[:, :], ch, w=[self.d_const])
        k.dma("sp", self.cvec[:], self.cvec_in[:, :], ch, w=[self.d_const])
        k.dma("sp", self.adab[:], self.ada_bT[:, :], ch, w=[self.d_const])
        k.dma("sp", self.gains[:], self.gains_in[:, :], ch, w=[self.d_const])
        k.dve(lambda e: e.memset(self.ones_f[:], 1.0), w=[self.d_const])
        self.H = sb("H", [128, 8, T], BF16)
        self.C = sb("C", [128, 8, T], BF16)
        self.d_H = [Dep() for _ in range(NT)]
        self.d_C = [Dep() for _ in range(NT)]
        self.modT = sb("modT", [128, 48, 2])
        self.A1 = sb("A1", [128, 8, 2])
        self.A2 = sb("A2", [128, 8, 2])
        self.G1f = sb("G1f", [128, 8, 2])
        self.G2f = sb("G2f", [128, 8, 2])
        self.Gbc = sb("Gbc", [128, 2, 2, D])
        self.d_mod = Dep()
        self.d_gbc = Dep()
        self.NAR = 28416
        self.AR = sb("arena", [128, self.NAR])
        self.ar_off = 0
        self.NXT = 3
        self.d_xt = [Dep() for _ in range(self.NXT)]
        self.ch_xt_ld = [k.chan() for _ in range(self.NXT)]
        self.ch_xt_st = [k.chan() for _ in range(self.NXT)]
        self.d_xn = [Dep(), Dep()]
        self.d_junk = Dep()
        self.d_stat = [Dep() for _ in range(4)]
        self.d_tmpb = [Dep(), Dep()]
        self.d_tmpf = [Dep(), Dep()]
        self.xt_i = self.xn_i = self.stat_i = self.tmp_i = 0

    def arena_reset(self):
        self.k.barrier()
        self.ar_off = 0

    def aa(self, shape, dtype=F32):
        esz = 4 if dtype in (F32, I32) else 2
        n = int(np.prod(shape[1:]))
        nbytes = (n * esz + 31) // 32 * 32
        o = self.ar_off
        assert o + nbytes <= self.NAR * 4, "arena overflow %d" % (o + nbytes)
        self.ar_off = o + nbytes
        v = self.AR[:, o // 4:(o + nbytes) // 4]
        if dtype != F32:
            v = v.bitcast(dtype)
        v = v[:, 0:n]
        if len(shape) == 3:
            v = v.rearrange("p (a b) -> p a b", b=shape[2])
        elif len(shape) == 4:
            v = v.rearrange("p (a b c) -> p a b c", b=shape[2], c=shape[3])
        if shape[0] != 128:
            v = v[0:shape[0]]
        return v

    def _staging(self, norm=True):
        self.xt = [self.aa([128, D]) for i in range(self.NXT)]
        self.junk = self.aa([128, D], BF16)
        self.stat = [self.aa([128, 8]) for i in range(4)]
        self.tmpf = [self.aa([128, D]) for i in range(2)]
        if norm:
            self.xn = [self.aa([128, D], BF16) for i in range(2)]
            self.tmpb = [self.aa([128, D], BF16) for i in range(2)]

    def gain(self, kind, l):
        o = (kind * DEPTH + l) * 8
        return self.gains[:, o:o + 8]

    def _layer(self, l):
        ph = self.phases
        if "mod" in ph:
            self._modulation(l)
        if "p1" in ph:
            self.arena_reset()
            self._staging()
            for tt in range(NT):
                s = 0 if tt < 16 else 1
                xi = self._load_x(l, tt, first=True)
                self._norm_tile(xi, s, self.A1, self.modT[:, 0:8, :], self.H, tt, self.d_H[tt])
        elif self.inject_h:
            ch = self.k.chan()
            hin = self.din("h_in", [D, T], BF16)
            for tt in range(NT):
                self.k.dma("sp", self.H[:, :, tt * 128:(tt + 1) * 128],
                           hin[:, tt * 128:(tt + 1) * 128].rearrange("(a p) t -> p a t", p=128), ch, w=[self.d_H[tt]])
        if "hx" in self.dbg and self.dbg["hx"] == l:
            self._dump_feat("dbg_hx", self.H, self.d_H)
        if "mix" in ph:
            self._mixers(l)
        if "cat" in self.dbg and self.dbg["cat"] == l:
            self._dump_feat("dbg_cat", self.C, self.d_C)
        if "p3" in ph:
            self._p3(l)
        if "hx2" in self.dbg and self.dbg["hx2"] == l:
            self._dump_feat("dbg_hx2", self.C, self.d_C)
        if "p4" in ph:
            self._p4(l)

    def _xsrc(self, l, tt, first):
        if l == self.layers[0] and l == 0 and first:
            if tt < 16:
                return self.x_in[tt * 128:(tt + 1) * 128, :], None
            return self.ctx_in[(tt - 16) * 128:(tt - 15) * 128, :], None
        return self.xres[tt * 128:(tt + 1) * 128, :], self.d_xres[tt]

    def _load_x(self, l, tt, first):
        k = self.k
        i = self.xt_i
        self.xt_i = (i + 1) % self.NXT
        src, dep = self._xsrc(l, tt, first)
        k.dma("sp", self.xt[i][:], src, self.ch_xt_ld[i], r=[dep] if dep else [], w=[self.d_xt[i]])
        return i

    def _store_x(self, i, dst_ap, dst_dep):
        self.k.dma("sp", dst_ap, self.xt[i][:], self.ch_xt_st[i], r=[self.d_xt[i]], w=[dst_dep])

    def _rstd(self, src_ap, src_deps):
        k = self.k
        j = self.stat_i
        self.stat_i = (j + 1) % 4
        stt, dst = self.stat[j], self.d_stat[j]
        k.act(lambda e: e.activation(out=self.junk[:], in_=src_ap, func=AF.Square, accum_out=stt[:, 0:1]),
              r=src_deps, w=[self.d_junk, dst])
        k.act(lambda e: e.activation(out=stt[:, 1:2], in_=stt[:, 0:1], func=AF.Sqrt, bias=EPS, scale=1.0 / D),
              r=[dst], w=[dst])
        k.dve(lambda e: e.reciprocal(out=stt[:, 2:3], in_=stt[:, 1:2]), r=[dst], w=[dst])
        return stt[:, 2:3], dst

    def _norm_tile(self, xi, s, A, B, Hbuf, tt, dH):
        k = self.k
        xt, dxt = self.xt[xi], self.d_xt[xi]
        rs, drs = self._rstd(xt[:], [dxt])
        j = self.xn_i
        self.xn_i = 1 - j
        xn, dxn = self.xn[j], self.d_xn[j]
        k.dve(lambda e: e.tensor_scalar(out=xn[:], in0=xt[:], scalar1=rs, scalar2=None, op0=ALU.mult),
              r=[dxt, drs], w=[dxn])
        pi = 3
        psb = self.ps[pi][:, 0:512].bitcast(BF16)
        dps = self.d_ps[pi][0]
        for dt in range(8):
            k.pe(lambda e, dt=dt: e.transpose(out=psb[:, dt * 128:(dt + 1) * 128], in_=xn[:, dt * 128:(dt + 1) * 128],
                                              identity=self.ident_b[:]),
                 r=[dxn, self.d_const], w=[dps])
        ti = self.tmp_i
        self.tmp_i = 1 - ti
        tb, dtb = self.tmpb[ti], self.d_tmpb[ti]
        k.dve(lambda e: e.tensor_tensor(out=tb[:].rearrange("p (a b) -> p a b", a=8),
                                        in0=psb.rearrange("p (a b) -> p a b", a=8),
                                        in1=A[:, :, s:s + 1].to_broadcast([128, 8, 128]), op=ALU.mult),
              r=[dps, self.d_mod], w=[dtb])
        k.pool(lambda e: e.tensor_tensor(out=Hbuf[:, :, tt * 128:(tt + 1) * 128],
                                         in0=tb[:].rearrange("p (a b) -> p a b", a=8),
                                         in1=B[:, :, s:s + 1].to_broadcast([128, 8, 128]), op=ALU.add),
               r=[dtb, self.d_mod], w=[dH])

    def _modulation(self, l):
        k = self.k
        self.arena_reset()
        if True:
            sT = self.aa([128, 16], BF16)
            d_sT = Dep()
            k.act(lambda e: e.activation(out=sT[:], in_=self.cvec[:], func=AF.Silu), r=[self.d_const], w=[d_sT])
            wb = [self.aa([128, 8, 512], BF16) for i in range(2)]
            dwb = [Dep(), Dep()]
            chw = [k.chan(), k.chan()]
            mod_ps = self.ps[0][:, 0:96]
            dps = self.d_ps[0][0]
            for g in range(12):
                i = g % 2
                src = self.W("ada_w")[l, :, g * 512:(g + 1) * 512].rearrange("(kt p) c -> p kt c", p=128)
                k.dma("pool", wb[i][:], src, chw[i], w=[dwb[i]])
                for jj in range(4):
                    jt = g * 4 + jj
                    for kt in range(8):
                        k.pe(lambda e, i=i, jj=jj, jt=jt, kt=kt: e.matmul(
                            mod_ps[:, jt * 2:jt * 2 + 2], wb[i][:, kt, jj * 128:(jj + 1) * 128],
                            sT[:, kt * 2:kt * 2 + 2], start=(kt == 0), stop=(kt == 7)),
                            r=[dwb[i], d_sT], w=[dps])
            dm = self.d_mod
            k.dve(lambda e: e.tensor_tensor(out=self.modT[:], in0=mod_ps.rearrange("p (a b) -> p a b", b=2),
                                            in1=self.adab[:, l * 48:(l + 1) * 48].unsqueeze(2).to_broadcast([128, 48, 2]),
                                            op=ALU.add), r=[dps, self.d_const], w=[dm])
            g = lambda kind: self.gain(kind, l).unsqueeze(2).to_broadcast([128, 8, 2])
            k.dve(lambda e: e.scalar_tensor_tensor(out=self.A1[:], in0=self.modT[:, 8:16, :], scalar=1.0, in1=g(0),
                                                   op0=ALU.add, op1=ALU.mult), r=[dm, self.d_const], w=[dm])
            k.dve(lambda e: e.scalar_tensor_tensor(out=self.A2[:], in0=self.modT[:, 32:40, :], scalar=1.0, in1=g(2),
                                                   op0=ALU.add, op1=ALU.mult), r=[dm, self.d_const], w=[dm])
            k.dve(lambda e: e.tensor_tensor(out=self.G1f[:], in0=self.modT[:, 16:24, :], in1=g(1), op=ALU.mult),
                  r=[dm, self.d_const], w=[dm])
            k.dve(lambda e: e.tensor_tensor(out=self.G2f[:], in0=self.modT[:, 40:48, :], in1=g(3), op=ALU.mult),
                  r=[dm, self.d_const], w=[dm])
            diag = [self.aa([128, 128]) for i in range(2)]
            ddiag = [Dep(), Dep()]
            n = 0
            for kind, Gf in enumerate((self.G1f, self.G2f)):
                for s in range(2):
                    for dt in range(8):
                        i = n % 2
                        n += 1
                        k.dve(lambda e, i=i, Gf=Gf, dt=dt, s=s: e.tensor_scalar(
                            out=diag[i][:], in0=self.ident_f[:], scalar1=Gf[:, dt, s:s + 1], scalar2=None, op0=ALU.mult),
                            r=[dm, self.d_const], w=[ddiag[i]])
                        pi = 1 + (n % 2)
                        pst = self.ps[pi][:, 0:128]
                        k.pe(lambda e, i=i, pst=pst: e.matmul(pst, self.ones_f[:], diag[i][:], start=True, stop=True),
                             r=[ddiag[i], self.d_const], w=[self.d_ps[pi][0]])
                        k.act(lambda e, pst=pst, kind=kind, s=s, dt=dt: e.activation(
                            out=self.Gbc[:, kind, s, dt * 128:(dt + 1) * 128], in_=pst, func=AF.Copy),
                            r=[self.d_ps[pi][0]], w=[self.d_gbc])
        if "mod" in self.dbg and self.dbg["mod"] == l:
            o = self.dout("dbg_mod", [128, 96])
            ch = k.chan()
            k.dma("sp", o[:, :], self.modT[:].rearrange("p a b -> p (a b)"), ch, r=[self.d_mod])
            o2 = self.dout("dbg_gbc", [128, 4 * D])
            k.dma("sp", o2[:, :], self.Gbc[:].rearrange("p a b c -> p (a b c)"), ch, r=[self.d_gbc])

    def _mixers(self, l):
        k = self.k
        if self.inject_cat:
            ch = k.chan()
            for tt in range(NT):
                k.dma("sp", self.C[:, :, tt * 128:(tt + 1) * 128],
                      self.cat_in[:, tt * 128:(tt + 1) * 128].rearrange("(a p) t -> p a t", p=128), ch, w=[self.d_C[tt]])
            return
        raise NotImplementedError

    def _p3(self, l):
        k = self.k
        last = (l == DEPTH - 1)
        ntt = 16 if last else NT
        self.arena_reset()
        self._staging()
        if True:
            wo = self.aa([128, 8, D], BF16)
            dwo = Dep()
            ch = k.chan()
            k.dma("pool", wo[:], self.W("w_out")[l].rearrange("(kt p) c -> p kt c", p=128), ch, w=[dwo])
            for tt in range(ntt):
                s = 0 if tt < 16 else 1
                pi = tt % 3
                yps = self.ps[pi]
                for half in range(2):
                    for mt in range(8):
                        k.pe(lambda e, half=half, mt=mt, yps=yps, tt=tt: e.matmul(
                            yps[:, half * 512:(half + 1) * 512], self.C[:, mt, tt * 128:(tt + 1) * 128],
                            wo[:, mt, half * 512:(half + 1) * 512], start=(mt == 0), stop=(mt == 7)),
                            r=[self.d_C[tt], dwo], w=[self.d_ps[pi][half]])
                xi = self._load_x(l, tt, first=True)
                self._resid_update(xi, yps[:], self.d_ps[pi], 0, s)
                self._store_x(xi, self.xres[tt * 128:(tt + 1) * 128, :], self.d_xres[tt])
                self._norm_tile(xi, s, self.A2, self.modT[:, 24:32, :], self.C, tt, self.d_C[tt])

    def _resid_update(self, xi, y_ap, y_deps, kind, s):
        k = self.k
        rs, drs = self._rstd(y_ap, list(y_deps))
        ti = self.tmp_i
        self.tmp_i = 1 - ti
        tf, dtf = self.tmpf[ti], self.d_tmpf[ti]
        k.dve(lambda e: e.scalar_tensor_tensor(out=tf[:], in0=y_ap, scalar=rs, in1=self.Gbc[:, kind, s, :],
                                               op0=ALU.mult, op1=ALU.mult),
              r=list(y_deps) + [drs, self.d_gbc], w=[dtf])
        xt, dxt = self.xt[xi], self.d_xt[xi]
        k.pool(lambda e: e.tensor_tensor(out=xt[:], in0=xt[:], in1=tf[:], op=ALU.add), r=[dtf, dxt], w=[dxt])

    def _p4(self, l):
        k = self.k
        last = (l == DEPTH - 1)
        if last:
            sblocks = [(0, 768), (768, 768), (1536, 512)]
        else:
            sblocks = [(0, 768), (768, 768), (1536, 768)]
        self.arena_reset()
        self._staging(norm=False)
        if True:
            hT = self.aa([128, 32, 768], BF16)
            d_hT = [[Dep() for _ in range(2)] for _ in range(32)]
            HF = self.H[:].rearrange("p a t -> p (a t)")
            w1c = [HF[:, i * 4096:(i + 1) * 4096].rearrange("p (k c) -> p k c", c=512) for i in range(2)]
            d_w1c = [Dep(), Dep()]
            ch_w1 = [k.chan(), k.chan()]
            w2c = [HF[:, 8192 + i * 2048: 8192 + (i + 1) * 2048].rearrange("p (k c) -> p k c", c=512) for i in range(3)]
            d_w2c = [Dep() for _ in range(3)]
            ch_w2 = [k.chan() for _ in range(3)]
            rl = [self.aa([128, 384], BF16) for i in range(2)]
            d_rl = [Dep(), Dep()]
            ytok = self.aa([128, 6, D])
            d_ytok = [Dep() for _ in range(6)]
            n_w1 = 0
            n_w2 = 0
            n_rl = 0
            for (t0, n) in sblocks:
                n2 = n // 2
                ntl = n // 128
                for ffc in range(8):
                    i = n_w1 % 2
                    n_w1 += 1
                    k.dma("pool", w1c[i], self.W("mlp_w1")[l, :, ffc * 512:(ffc + 1) * 512].rearrange("(kt p) c -> p kt c", p=128),
                          ch_w1[i], w=[d_w1c[i]])
                    for f in range(4):
                        fft = ffc * 4 + f
                        for sbk in range(2):
                            hps = self.ps[3][:, sbk * 512: sbk * 512 + n2]
                            dhps = self.d_ps[3][sbk]
                            tts = range((t0 + sbk * n2) // 128, (t0 + (sbk + 1) * n2 + 127) // 128)
                            rdeps = [self.d_C[t] for t in tts]
                            for dt in range(8):
                                k.pe(lambda e, i=i, f=f, dt=dt, hps=hps, sbk=sbk: e.matmul(
                                    hps, w1c[i][:, dt, f * 128:(f + 1) * 128],
                                    self.C[:, dt, t0 + sbk * n2: t0 + (sbk + 1) * n2], start=(dt == 0), stop=(dt == 7)),
                                    r=[d_w1c[i]] + rdeps, w=[dhps])
                            j = n_rl % 2
                            n_rl += 1
                            k.act(lambda e, j=j, hps=hps: e.activation(out=rl[j][:, 0:n2], in_=hps, func=AF.Relu),
                                  r=[dhps], w=[d_rl[j]])
                            k.pool(lambda e, j=j, fft=fft, sbk=sbk: e.tensor_tensor(
                                out=hT[:, fft, sbk * n2:(sbk + 1) * n2], in0=rl[j][:, 0:n2], in1=rl[j][:, 0:n2], op=ALU.mult),
                                r=[d_rl[j]], w=[d_hT[fft][sbk]])
                for dh in range(2):
                    for ffc in range(8):
                        i = n_w2 % 3
                        n_w2 += 1
                        k.dma("pool", w2c[i],
                              self.W("mlp_w2")[l, ffc * 512:(ffc + 1) * 512, dh * 512:(dh + 1) * 512].rearrange("(f p) c -> p f c", p=128),
                              ch_w2[i], w=[d_w2c[i]])
                        for f in range(4):
                            fft = ffc * 4 + f
                            for tl in range(ntl):
                                pi, hb = tl // 2, tl % 2
                                sbk = (tl * 128) // n2
                                k.pe(lambda e, i=i, f=f, fft=fft, tl=tl, pi=pi, hb=hb: e.matmul(
                                    self.ps[pi][:, hb * 512:(hb + 1) * 512], hT[:, fft, tl * 128:(tl + 1) * 128],
                                    w2c[i][:, f, :], start=(fft == 0), stop=(fft == 31)),
                                    r=[d_w2c[i], d_hT[fft][sbk]], w=[self.d_ps[pi][hb]])
                    for tl in range(ntl):
                        pi, hb = tl // 2, tl % 2
                        k.act(lambda e, tl=tl, pi=pi, hb=hb, dh=dh: e.activation(
                            out=ytok[:, tl, dh * 512:(dh + 1) * 512], in_=self.ps[pi][:, hb * 512:(hb + 1) * 512], func=AF.Copy),
                            r=[self.d_ps[pi][hb]], w=[d_ytok[tl]])
                for tl in range(ntl):
                    tt = t0 // 128 + tl
                    s = 0 if tt < 16 else 1
                    xi = self._load_x(l, tt, first=False)
                    self._resid_update(xi, ytok[:, tl, :], [d_ytok[tl]], 1, s)
                    if last:
                        self._store_x(xi, self.out[tt * 128:(tt + 1) * 128, :], self.d_out[tt])
                    else:
                        self._store_x(xi, self.xres[tt * 128:(tt + 1) * 128, :], self.d_xres[tt])

    def _dump_feat(self, name, buf, deps):
        k = self.k
        o = self.dout(name, [D, T])
        self.arena_reset()
        if True:
            stg = self.aa([128, 8, 128])
            dst = Dep()
            ch = k.chan()
            for tt in range(NT):
                k.dve(lambda e, tt=tt: e.tensor_copy(out=stg[:], in_=buf[:, :, tt * 128:(tt + 1) * 128]), r=[deps[tt]], w=[dst])
                k.dma("sp", o[:, tt * 128:(tt + 1) * 128].rearrange("(a p) t -> p a t", p=128), stg[:], ch, r=[dst])

    def _finish(self):
        k = self.k
        if "xres" in self.dbg:
            self.arena_reset()
            self._staging()
            o = self.dout("dbg_xres", [T, D])
            ch = k.chan()
            for tt in range(NT):
                xi = self._load_x(1, tt, first=False)
                self._store_x(xi, o[tt * 128:(tt + 1) * 128, :], Dep())
        k.barrier()


def host_inputs(mk, inputs, extra=None):
    bf = ml_dtypes.bfloat16
    f32 = np.float32
    shared = {}
    shared["ada_w"] = np.ascontiguousarray(inputs["ada_w"], dtype=f32)
    shared["ada_bT"] = np.ascontiguousarray(
        inputs["ada_b"].reshape(DEPTH, 48, 128).transpose(2, 0, 1).reshape(128, DEPTH * 48), dtype=f32)
    g = np.stack([inputs["norm_pre_mix"], inputs["norm_post_mix"], inputs["norm_pre_mlp"], inputs["norm_post_mlp"]])
    shared["gains"] = np.ascontiguousarray(g.reshape(4, DEPTH, 8, 128).transpose(3, 0, 1, 2).reshape(128, -1), dtype=f32)
    for n in ("w_in", "w_out", "mlp_w1", "mlp_w2"):
        shared[n] = np.ascontiguousarray(inputs[n], dtype=f32)
    shared["ident_f"] = np.eye(128, dtype=f32)
    shared["ident_b"] = np.eye(128, dtype=f32).astype(bf)
    maps = []
    for b in range(inputs["x"].shape[0]):
        m = dict(shared)
        m["x"] = np.ascontiguousarray(inputs["x"][b], dtype=f32)
        m["ctx"] = np.ascontiguousarray(inputs["ctx"][b], dtype=f32)
        cv = np.stack([inputs["c"][b].reshape(8, 128), inputs["c_ctx"].reshape(8, 128)], axis=-1)
        m["cvec"] = np.ascontiguousarray(cv.transpose(1, 0, 2).reshape(128, 16), dtype=f32)
        if extra:
            m.update(extra(b))
        maps.append({kk: v for kk, v in m.items() if kk in mk.inputs})
    return maps


PP_ENTRIES = [("scw", 6), ("hyw", 18), ("gdw", 18), ("hybias", 2), ("gnorm", 1), ("hy_w1", 64), ("hy_w2", 64),
              ("hy_w3", 64), ("hy_w4", 512), ("hy_b", 3), ("hy_f", 3), ("alog", 8), ("dtb", 8)]
PP_OFF = {}
_o = 0
for _n, _w in PP_ENTRIES:
    PP_OFF[_n] = (_o, _w)
    _o += _w
PP_W = _o


def pp_host(inputs):
    pp = np.zeros((128, DEPTH * PP_W), np.float32)
    for l in range(DEPTH):
        def put(name, arr):
            o, w = PP_OFF[name]
            arr = np.asarray(arr, np.float32)
            assert arr.shape[1] == w, (name, arr.shape)
            pp[:arr.shape[0], l * PP_W + o: l * PP_W + o + w] = arr
        put("scw", inputs["sc_conv"][l].reshape(3, 2, 128).transpose(2, 1, 0).reshape(128, 6))
        put("hyw", inputs["hy_conv"][l].reshape(3, 6, 128).transpose(2, 1, 0).reshape(128, 18))
        put("gdw", inputs["gdn_conv"][l].reshape(3, 6, 128).transpose(2, 1, 0).reshape(128, 18))
        put("hybias", inputs["hy_bias"][l].reshape(2, 128).T)
        put("gnorm", np.tile(inputs["gdn_norm"][l], 2).reshape(128, 1))
        put("hy_w1", inputs["hy_w1"][l])
        put("hy_w2", inputs["hy_w2"][l])
        put("hy_w3", inputs["hy_w3"][l])
        put("hy_w4", inputs["hy_w4"][l])
        put("hy_b", np.stack([inputs["hy_b1"][l], inputs["hy_b2"][l], inputs["hy_b3"][l]], axis=1))
        put("hy_f", inputs["hy_freq"][l].T)
        put("alog", np.tile(inputs["gdn_a_log"][l].reshape(1, 8), (128, 1)))
        put("dtb", np.tile(inputs["gdn_dt_bias"][l].reshape(1, 8), (128, 1)))
    return pp


def na_consts(inputs):
    rpb = np.asarray(inputs["na_rpb"], np.float32)
    par = np.arange(2)[:, None, None, None]
    kc = np.arange(64)[None, :, None, None]
    i = np.arange(14)[None, None, :, None]
    qc = np.arange(64)[None, None, None, :]
    dc = np.clip(kc - qc, -15, 15) + 15
    di = np.broadcast_to(i + par, (2, 64, 14, 64))
    dcb = np.broadcast_to(dc, (2, 64, 14, 64))
    g = rpb[:, :, di, dcb]
    g = g.transpose(0, 2, 3, 1, 4, 5).reshape(DEPTH, 128, 4 * 14 * 64)
    cs = np.clip(np.arange(64) - 8, 0, 48)
    kcv = np.arange(64)[:, None]
    valid = (kcv >= cs[None, :]) & (kcv < cs[None, :] + 16)
    m = np.where(valid, 0.0, -1e30).astype(np.float32)
    mask = np.concatenate([m, m], axis=0)
    return np.ascontiguousarray(g), np.ascontiguousarray(mask)


def _mix_common_init(self):
    if getattr(self, "pp", None) is not None:
        return
    k = self.k
    self.pp_in = self.din("pp", [128, DEPTH * PP_W])
    self.pp = self.sb(self.st, "pp", [128, DEPTH * PP_W])
    self.d_pp = Dep()
    ch = k.chan()
    k.dma("sp", self.pp[:], self.pp_in[:, :], ch, w=[self.d_pp])


def _ppv(self, l, name, rows=128):
    o, w = PP_OFF[name]
    return self.pp[0:rows, l * PP_W + o: l * PP_W + o + w]


def _blocks(self, with_ctx):
    b = [(i * 512, 512) for i in range(4)]
    if with_ctx:
        b.append((L, LC))
    return b


def _wring_init(self, n=2):
    self.wring = [(self.aa([128, 8, 512], BF16), Dep(), self.wring_ch[i]) for i in range(n)]
    self.wring_i = 0
    self.bank_i = 0


def _proj(self, l, chunks, blocks, evac, banks):
    k = self.k
    for (c0, ncol) in chunks:
        i = self.wring_i
        self.wring_i = (i + 1) % len(self.wring)
        wap, dw, chw = self.wring[i]
        k.dma("pool", wap[:, :, 0:ncol], self.W("w_in")[l, :, c0:c0 + ncol].rearrange("(kt p) c -> p kt c", p=128),
              chw, w=[dw])
        for cc in range(0, ncol, 128):
            m = min(128, ncol - cc)
            for (t0, n) in blocks:
                pi, hb = banks[self.bank_i % len(banks)]
                self.bank_i += 1
                pst = self.ps[pi][0:m, hb * 512: hb * 512 + n]
                dps = self.d_ps[pi][hb]
                hd = [self.d_H[t] for t in range(t0 // 128, (t0 + n + 127) // 128)]
                for dt in range(8):
                    k.pe(lambda e, dt=dt, pst=pst, wap=wap, cc=cc, m=m, t0=t0, n=n: e.matmul(
                        pst, wap[:, dt, cc:cc + m], self.H[:, dt, t0:t0 + n], start=(dt == 0), stop=(dt == 7)),
                        r=[dw] + hd, w=[dps])
                evac(c0 + cc, m, t0, n, pst, dps)


def _dwconv(self, eng, out_ap, in_ap, w3, ranges, r, w):
    k = self.k
    for (a, b) in ranges:
        k.op(eng, lambda e, a=a, b=b: e.tensor_scalar(out=out_ap[:, a:b], in0=in_ap[:, a:b], scalar1=w3[:, 1:2],
                                                      scalar2=None, op0=ALU.mult), r=r, w=w)
        k.op(eng, lambda e, a=a, b=b: e.scalar_tensor_tensor(out=out_ap[:, a + 1:b], in0=in_ap[:, a:b - 1], scalar=w3[:, 0:1],
                                                             in1=out_ap[:, a + 1:b], op0=ALU.mult, op1=ALU.add),
             r=list(r) + list(w), w=w)
        k.op(eng, lambda e, a=a, b=b: e.scalar_tensor_tensor(out=out_ap[:, a:b - 1], in0=in_ap[:, a + 1:b], scalar=w3[:, 2:3],
                                                             in1=out_ap[:, a:b - 1], op0=ALU.mult, op1=ALU.add),
             r=list(r) + list(w), w=w)


PROJ_BANKS = [(0, 0), (0, 1), (1, 0), (1, 1)]


def _mix_sc(self, l, with_ctx):
    k = self.k
    self.arena_reset()
    _wring_init(self)
    ranges = [(0, L)] + ([(L, T)] if with_ctx else [])
    blocks = _blocks(self, with_ctx)
    ntok = T if with_ctx else L
    ntt = ntok // 128
    pxb = self.aa([128, 6, T], BF16)
    d_px = [Dep() for _ in range(6)]

    def evac(c, m, t0, n, pst, dps):
        ct = (c - OFF_SC) // 128
        k.act(lambda e: e.activation(out=pxb[:, ct, t0:t0 + n], in_=pst, func=AF.Copy), r=[dps], w=[d_px[ct]])
    _proj(self, l, [(OFF_SC, 512), (OFF_SC + 512, 256)], blocks, evac, PROJ_BANKS)
    z = [self.aa([128, T]) for _ in range(2)]
    acc = [self.aa([128, T]) for _ in range(2)]
    scw = _ppv(self, l, "scw")
    for j in range(2):
        dz, dacc = Dep(), Dep()
        eng = "dve" if j == 0 else "pool"
        k.op(eng, lambda e, j=j: e.tensor_tensor(out=z[j][:, 0:ntok], in0=pxb[:, 2 + j, 0:ntok], in1=pxb[:, 4 + j, 0:ntok],
                                                 op=ALU.mult), r=[d_px[2 + j], d_px[4 + j]], w=[dz])
        _dwconv(self, "dve", acc[j], z[j], scw[:, j * 3:(j + 1) * 3], ranges, [dz, self.d_pp], [dacc])
        k.op(eng, lambda e, j=j: e.tensor_tensor(out=self.C[:, 2 + j, 0:ntok], in0=pxb[:, j, 0:ntok], in1=acc[j][:, 0:ntok],
                                                 op=ALU.mult), r=[d_px[j], dacc], w=[self.d_C[t] for t in range(ntt)])


def _mixers(self, l):
    k = self.k
    if self.inject_cat:
        ch = k.chan()
        for tt in range(NT):
            k.dma("sp", self.C[:, :, tt * 128:(tt + 1) * 128],
                  self.cat_in[:, tt * 128:(tt + 1) * 128].rearrange("(a p) t -> p a t", p=128), ch, w=[self.d_C[tt]])
        return
    _mix_common_init(self)
    if not hasattr(self, "wring_ch"):
        self.wring_ch = [k.chan() for _ in range(3)]
    with_ctx = l < DEPTH - 1
    if "sc" in self.mixers:
        _mix_sc(self, l, with_ctx)
    if "na" in self.mixers:
        _mix_na(self, l, with_ctx)
    if "hy" in self.mixers:
        _mix_hy(self, l, with_ctx)
    if "gdn" in self.mixers:
        _mix_gdn(self, l, with_ctx)


MK._mixers = _mixers


def const_inputs(inputs):
    out = {}
    g, mask = na_consts(inputs)
    out["na_rpbg"] = g
    out["na_mask"] = mask
    out.update(hy_consts())
    out.update(gdn_consts())
    return out


def _mix_na(self, l, with_ctx):
    k = self.k
    self.arena_reset()
    _wring_init(self)
    blocks = _blocks(self, True)
    qT = self.aa([128, 2, T], BF16)
    kT = self.aa([128, 2, T], BF16)
    d_q = [Dep() for _ in range(NT)]
    d_k = [Dep() for _ in range(NT)]

    def evac(c, m, t0, n, pst, dps):
        ct = (c - OFF_NA) // 128
        tts = range(t0 // 128, (t0 + n) // 128)
        if ct < 2:
            k.act(lambda e: e.activation(out=qT[:, ct, t0:t0 + n], in_=pst, func=AF.Copy, scale=0.125),
                  r=[dps], w=[d_q[t] for t in tts])
        else:
            k.dve(lambda e: e.tensor_copy(out=kT[:, ct - 2, t0:t0 + n], in_=pst), r=[dps], w=[d_k[t] for t in tts])
    _proj(self, l, [(OFF_NA, 512)], blocks, evac, PROJ_BANKS)
    Ve = self.aa([128, NT, 4, 65], BF16)
    Vo = self.aa([128, 15, 4, 65], BF16)
    d_Ve, d_Vo = Dep(), Dep()
    k.pool(lambda e: e.memset(Ve, 1.0), w=[d_Ve])
    k.pool(lambda e: e.memset(Vo, 1.0), w=[d_Vo])
    i = self.wring_i
    self.wring_i = (i + 1) % len(self.wring)
    wv, dwv, chv = self.wring[i]
    k.dma("pool", wv[:, :, 0:256], self.W("w_in")[l, :, OFF_NA + 512:OFF_NA + 768].rearrange("(kt p) c -> p kt c", p=128),
          chv, w=[dwv])
    nb = 0
    for (Vx, dV, ntl, off) in ((Ve, d_Ve, NT, 0), (Vo, d_Vo, 15, 64)):
        for j in range(ntl):
            pi, hb = PROJ_BANKS[nb % 4]
            nb += 1
            pst = self.ps[pi][:, hb * 512: hb * 512 + 256]
            dps = self.d_ps[pi][hb]
            a = off + j * 128
            hd = [self.d_H[t] for t in range(a // 128, (a + 255) // 128)]
            for dt in range(8):
                k.pe(lambda e, dt=dt, pst=pst, a=a: e.matmul(pst, self.H[:, dt, a:a + 128], wv[:, dt, 0:256],
                                                             start=(dt == 0), stop=(dt == 7)), r=[dwv] + hd, w=[dps])
            k.act(lambda e, Vx=Vx, j=j, pst=pst: e.activation(out=Vx[:, j, :, 0:64], in_=pst.rearrange("p (h d) -> p h d", h=4),
                                                              func=AF.Copy), r=[dps], w=[dV])
    T2 = self.aa([128, 4, 14, 64])
    msk = self.aa([128, 64])
    d_T2, d_msk = Dep(), Dep()
    if not hasattr(self, "na_rpbg_in"):
        self.na_rpbg_in = self.din("na_rpbg", [DEPTH, 128, 4 * 14 * 64])
        self.na_mask_in = self.din("na_mask", [128, 64])
        self.ch_na = self.k.chan()
    k.dma("sp", T2.rearrange("p a b c -> p (a b c)"), self.na_rpbg_in[l], self.ch_na, w=[d_T2])
    k.dma("sp", msk, self.na_mask_in[:, :], self.ch_na, w=[d_msk])
    k.dve(lambda e: e.tensor_tensor(out=T2.rearrange("p a b c -> p (a b) c"), in0=T2.rearrange("p a b c -> p (a b) c"),
                                    in1=msk.unsqueeze(1).to_broadcast([128, 56, 64]), op=ALU.add),
          r=[d_T2, d_msk], w=[d_T2])
    Sb = [self.aa([128, 4, 64]) for _ in range(2)]
    d_Sb = [Dep(), Dep()]
    E = [self.aa([128, 6, 64], BF16) for _ in range(3)]
    d_E = [Dep() for _ in range(3)]
    rs = [self.aa([64, 4]) for _ in range(2)]
    d_rs = [Dep(), Dep()]
    On = [self.aa([64, 256], BF16) for _ in range(2)]
    d_On = [Dep(), Dep()]
    SB = [(2, 0), (2, 1), (3, 0), (3, 1)]
    OB = [(0, 0), (0, 1)]
    TB = (1, 0)
    psT = self.ps[TB[0]][:, TB[1] * 512: TB[1] * 512 + 512].bitcast(BF16)
    d_psT = self.d_ps[TB[0]][TB[1]]
    n = 0
    for r in range(32):
        s = min(max(r - 4, 0), 24)
        base = s - r + 7
        opi, ohb = OB[r % 2]
        O_ps = self.ps[opi][0:64, ohb * 512: ohb * 512 + 260]
        d_O = self.d_ps[opi][ohb]
        tq = (64 * r) // 128
        for h in range(4):
            hp = slice(64 * (h % 2), 64 * (h % 2) + 64)
            hc = h // 2
            spi, shb = SB[n % 4]
            S_ps = self.ps[spi][:, shb * 512: shb * 512 + 384]
            d_S = self.d_ps[spi][shb]
            for kt in range(6):
                ks = 64 * s + 128 * kt if kt < 4 else L + 128 * (kt - 4)
                kd = [d_k[t] for t in range(ks // 128, (ks + 255) // 128)]
                k.pe(lambda e, kt=kt, ks=ks, S_ps=S_ps, hp=hp, hc=hc, r=r: e.matmul(
                    S_ps[:, kt * 64:(kt + 1) * 64], kT[hp, hc, ks:ks + 128], qT[hp, hc, 64 * r:64 * r + 64],
                    start=True, stop=True), r=kd + [d_q[tq]], w=[d_S])
            sb_i = n % 2
            e_i = n % 3
            k.dve(lambda e, sb_i=sb_i, S_ps=S_ps, h=h, base=base: e.tensor_tensor(
                out=Sb[sb_i], in0=S_ps[:, 0:256].rearrange("p (a b) -> p a b", a=4),
                in1=T2[:, h, base:base + 7:2, :], op=ALU.add), r=[d_S, d_T2], w=[d_Sb[sb_i]])
            k.act(lambda e, sb_i=sb_i, e_i=e_i: e.activation(out=E[e_i][:, 0:4, :], in_=Sb[sb_i], func=AF.Exp),
                  r=[d_Sb[sb_i]], w=[d_E[e_i]])
            k.act(lambda e, e_i=e_i, S_ps=S_ps: e.activation(out=E[e_i][:, 4:6, :],
                                                             in_=S_ps[:, 256:384].rearrange("p (a b) -> p a b", a=2),
                                                             func=AF.Exp), r=[d_S], w=[d_E[e_i]])
            for kt in range(6):
                if kt < 4:
                    if s % 2 == 0:
                        vt, dv = Ve[:, s // 2 + kt, h, :], d_Ve
                    else:
                        vt, dv = Vo[:, (s - 1) // 2 + kt, h, :], d_Vo
                else:
                    vt, dv = Ve[:, 16 + kt - 4, h, :], d_Ve
                k.pe(lambda e, kt=kt, vt=vt, e_i=e_i, O_ps=O_ps, h=h: e.matmul(
                    O_ps[:, h * 65:(h + 1) * 65], E[e_i][:, kt, :], vt, start=(kt == 0), stop=(kt == 5)),
                    r=[d_E[e_i], dv], w=[d_O])
            n += 1
        j = r % 2
        O3 = O_ps.rearrange("p (h d) -> p h d", h=4)
        k.dve(lambda e, j=j, O3=O3: e.reciprocal(out=rs[j], in_=O3[:, :, 64]), r=[d_O], w=[d_rs[j]])
        k.dve(lambda e, j=j, O3=O3: e.tensor_tensor(out=On[j].rearrange("p (h d) -> p h d", h=4), in0=O3[:, :, 0:64],
                                                    in1=rs[j].unsqueeze(2).to_broadcast([64, 4, 64]), op=ALU.mult),
              r=[d_O, d_rs[j]], w=[d_On[j]])
        for hc in range(2):
            k.pe(lambda e, j=j, hc=hc: e.transpose(out=psT[:, hc * 64:(hc + 1) * 64], in_=On[j][:, hc * 128:(hc + 1) * 128],
                                                   identity=self.ident_b[0:64, 0:64]), r=[d_On[j], self.d_const], w=[d_psT])
        k.act(lambda e, r=r: e.activation(out=self.C[:, 6:8, 64 * r:64 * r + 64],
                                          in_=psT[:, 0:128].rearrange("p (a b) -> p a b", a=2), func=AF.Copy),
              r=[d_psT], w=[self.d_C[tq]])
    if with_ctx:
        Ec = self.aa([128, 2, 256], BF16)
        d_Ec = Dep()
        Onc = self.aa([128, 256], BF16)
        d_Onc = Dep()
        rsc = self.aa([128, 4])
        d_rsc = Dep()
        for qt in range(2):
            opi, ohb = OB[qt % 2]
            O_ps = self.ps[opi][:, ohb * 512: ohb * 512 + 260]
            d_O = self.d_ps[opi][ohb]
            for h in range(4):
                hp = slice(64 * (h % 2), 64 * (h % 2) + 64)
                hc = h // 2
                spi, shb = SB[n % 4]
                n += 1
                S_ps = self.ps[spi][:, shb * 512: shb * 512 + 256]
                d_S = self.d_ps[spi][shb]
                for c in range(2):
                    k.pe(lambda e, c=c, S_ps=S_ps, hp=hp, hc=hc, qt=qt: e.matmul(
                        S_ps[:, c * 128:(c + 1) * 128], kT[hp, hc, L + 128 * c:L + 128 * c + 128],
                        qT[hp, hc, L + 128 * qt:L + 128 * qt + 128], start=True, stop=True),
                        r=[d_k[16 + c], d_q[16 + qt]], w=[d_S])
                k.act(lambda e, S_ps=S_ps: e.activation(out=Ec[:, :, 0:128], in_=S_ps.rearrange("p (a b) -> p a b", a=2),
                                                        func=AF.Exp), r=[d_S], w=[d_Ec])
                for c in range(2):
                    k.pe(lambda e, c=c, O_ps=O_ps, h=h: e.matmul(O_ps[:, h * 65:(h + 1) * 65], Ec[:, c, 0:128],
                                                                 Ve[:, 16 + c, h, :], start=(c == 0), stop=(c == 1)),
                         r=[d_Ec, d_Ve], w=[d_O])
            O3 = O_ps.rearrange("p (h d) -> p h d", h=4)
            k.dve(lambda e, O3=O3: e.reciprocal(out=rsc, in_=O3[:, :, 64]), r=[d_O], w=[d_rsc])
            k.dve(lambda e, O3=O3: e.tensor_tensor(out=Onc.rearrange("p (h d) -> p h d", h=4), in0=O3[:, :, 0:64],
                                                   in1=rsc.unsqueeze(2).to_broadcast([128, 4, 64]), op=ALU.mult),
                  r=[d_O, d_rsc], w=[d_Onc])
            for hc in range(2):
                k.pe(lambda e, hc=hc: e.transpose(out=psT[:, hc * 128:(hc + 1) * 128], in_=Onc[:, hc * 128:(hc + 1) * 128],
                                                  identity=self.ident_b[:]), r=[d_Onc, self.d_const], w=[d_psT])
            k.act(lambda e, qt=qt: e.activation(out=self.C[:, 6:8, L + 128 * qt:L + 128 * qt + 128],
                                                in_=psT[:, 0:256].rearrange("p (a b) -> p a b", a=2), func=AF.Copy),
                  r=[d_psT], w=[self.d_C[16 + qt]])


import math
HY_EMB = 33
HY_BANDS = 16


def hy_consts():
    bf = ml_dtypes.bfloat16
    out = {}
    max_decay = math.log(1e-2) / 0.3
    min_decay = math.log(1e-2) / 1.5
    deltas = np.abs(np.linspace(min_decay, max_decay, 256, dtype=np.float32))
    for tag, Ls in (("lat", L), ("ctx", LC)):
        nt = Ls // 128
        t = np.linspace(0.0, 1.0, Ls, dtype=np.float32)[:, None]
        bands = np.linspace(1e-4, HY_BANDS - 1, HY_BANDS, dtype=np.float32)
        ang = (np.float32(2.0 * math.pi / Ls)) * np.arange(Ls, dtype=np.float32)[:, None] * bands
        z = np.concatenate([t, np.cos(ang), -np.sin(ang)], axis=-1).astype(np.float32)
        out["hy_zT_" + tag] = np.ascontiguousarray(z.T)
        dec = np.exp(-t * deltas[None, :]).astype(np.float32)
        out["hy_dec_" + tag] = np.ascontiguousarray(dec.reshape(nt, 128, 256).transpose(1, 0, 2).reshape(128, nt * 256))
        N = 2 * Ls
        tt_ = np.arange(Ls, dtype=np.int64)
        ff = np.arange(Ls, dtype=np.int64)
        m = ((2 * ff[None, :] + 1) * tt_[:, None]) % (2 * N)
        th = m.astype(np.float64) * (math.pi / N)
        Cm = np.cos(th)
        Sm = -np.sin(th)
        for nm, M_ in (("C", Cm), ("S", Sm)):
            f4 = M_.reshape(nt, 128, nt, 128).transpose(2, 1, 0, 3).reshape(nt, 128, nt * 128)
            out["hy_%sf_%s" % (nm, tag)] = np.ascontiguousarray(f4).astype(bf)
            Wd = min(512, Ls)
            ntb = Ls // Wd
            i4 = M_.reshape(ntb, Wd, nt, 128).transpose(0, 3, 2, 1).reshape(ntb, 128, nt * Wd)
            out["hy_%si_%s" % (nm, tag)] = np.ascontiguousarray(i4).astype(bf)
    return out


def _arena_rewind(self, mark):
    self.k.barrier()
    self.ar_off = mark


def _hy_filters(self, l, Ls, tag, WHx, d_WHx):
    k = self.k
    nt = Ls // 128
    BW = min(512, Ls)
    nb = Ls // BW
    zin = self.din("hy_zT_" + tag, [HY_EMB, Ls]) if ("hy_zT_" + tag) not in self.inputs else self._hyin["hy_zT_" + tag]
    din_dec = self.din("hy_dec_" + tag, [128, nt * 256]) if ("hy_dec_" + tag) not in self.inputs else self._hyin["hy_dec_" + tag]
    self._hyin["hy_zT_" + tag] = zin
    self._hyin["hy_dec_" + tag] = din_dec
    zT = self.aa([HY_EMB, Ls])
    dec = self.aa([128, nt, 256])
    d_z, d_dec = Dep(), Dep()
    k.dma("sp", zT, zin[:, :], self.ch_hy, w=[d_z])
    k.dma("sp", dec.rearrange("p a b -> p (a b)"), din_dec[:, :], self.ch_hy, w=[d_dec])
    hb = [self.aa([64, Ls]) for _ in range(2)]
    d_hb = [Dep(), Dep()]
    v = self.aa([64, 512])
    ki = self.aa([64, 512], I32)
    kf = self.aa([64, 512])
    d_v, d_ki, d_kf = Dep(), Dep(), Dep()
    fb = self.aa([64, 3])
    d_fb = Dep()
    fr = _ppv(self, l, "hy_f", 64)
    bb = _ppv(self, l, "hy_b", 64)
    k.dve(lambda e: e.tensor_tensor(out=fb, in0=fr, in1=bb, op=ALU.mult), r=[self.d_pp], w=[d_fb])
    ws = [_ppv(self, l, "hy_w1", HY_EMB), _ppv(self, l, "hy_w2", 64), _ppv(self, l, "hy_w3", 64)]
    PB = [(2, 0), (2, 1)]
    nps = 0
    src, d_src = zT, d_z
    for li in range(3):
        dst, d_dst = hb[li % 2], d_hb[li % 2]
        for b in range(nb):
            pi, hbk = PB[nps % 2]
            nps += 1
            pst = self.ps[pi][0:64, hbk * 512: hbk * 512 + BW]
            dps = self.d_ps[pi][hbk]
            k.pe(lambda e, li=li, b=b, pst=pst, src=src: e.matmul(pst, ws[li], src[:, b * BW:(b + 1) * BW], start=True, stop=True),
                 r=[self.d_pp, d_src], w=[dps])
            k.dve(lambda e, li=li, pst=pst: e.tensor_scalar(out=v[:, 0:BW], in0=pst, scalar1=fr[:, li:li + 1], scalar2=fb[:, li:li + 1],
                                                            op0=ALU.mult, op1=ALU.add), r=[dps, d_fb, self.d_pp], w=[d_v])
            k.dve(lambda e: e.tensor_scalar(out=ki[:, 0:BW], in0=v[:, 0:BW], scalar1=1.0 / (2.0 * math.pi), scalar2=None, op0=ALU.mult),
                  r=[d_v], w=[d_ki])
            k.dve(lambda e: e.tensor_copy(out=kf[:, 0:BW], in_=ki[:, 0:BW]), r=[d_ki], w=[d_kf])
            k.dve(lambda e: e.scalar_tensor_tensor(out=v[:, 0:BW], in0=kf[:, 0:BW], scalar=-2.0 * math.pi, in1=v[:, 0:BW],
                                                   op0=ALU.mult, op1=ALU.add), r=[d_kf, d_v], w=[d_v])
            k.dve(lambda e: e.tensor_scalar(out=v[:, 0:BW], in0=v[:, 0:BW], scalar1=3.1415925, scalar2=-3.1415925,
                                            op0=ALU.min, op1=ALU.max), r=[d_v], w=[d_v])
            k.act(lambda e, dst=dst, b=b: e.activation(out=dst[:, b * BW:(b + 1) * BW], in_=v[:, 0:BW], func=AF.Sin),
                  r=[d_v], w=[d_dst])
        src, d_src = dst, d_dst
    h3, d_h3 = src, d_src
    w4 = _ppv(self, l, "hy_w4", 64)
    hd = [self.aa([128, 2, 256]) for _ in range(2)]
    d_hd = [Dep(), Dep()]
    ab = [self.aa([128, 512]) for _ in range(2)]
    d_ab = [Dep(), Dep()]
    nrm_ps = self.ps[3][:, 0:512]
    d_nrm = self.d_ps[3][0]
    rn = self.aa([128, 256])
    d_rn = Dep()
    for pss in range(2):
        for tt in range(nt):
            pi, hbk = PB[nps % 2]
            nps += 1
            pst = self.ps[pi][:, hbk * 512: hbk * 512 + 512]
            dps = self.d_ps[pi][hbk]
            k.pe(lambda e, tt=tt, pst=pst: e.matmul(pst, h3[:, tt * 128:(tt + 1) * 128], w4, start=True, stop=True),
                 r=[d_h3, self.d_pp], w=[dps])
            j = tt % 2
            k.dve(lambda e, j=j, tt=tt, pst=pst: e.tensor_tensor(out=hd[j], in0=pst.rearrange("p (a b) -> p a b", a=2),
                                                                 in1=dec[:, tt, :].unsqueeze(1).to_broadcast([128, 2, 256]),
                                                                 op=ALU.mult), r=[dps, d_dec], w=[d_hd[j]])
            if pss == 0:
                k.act(lambda e, j=j: e.activation(out=ab[j], in_=hd[j].rearrange("p a b -> p (a b)"), func=AF.Abs),
                      r=[d_hd[j]], w=[d_ab[j]])
                k.pe(lambda e, j=j, tt=tt: e.matmul(nrm_ps, self.ones_f[:], ab[j], start=(tt == 0), stop=(tt == nt - 1)),
                     r=[d_ab[j], self.d_const], w=[d_nrm])
            else:
                if tt == 0:
                    k.dve(lambda e, j=j: e.memset(hd[j][0:1, 1, :], 0.0), r=[d_hd[j]], w=[d_hd[j]])
                k.dve(lambda e, j=j: e.tensor_tensor(out=hd[j], in0=hd[j], in1=rn.unsqueeze(1).to_broadcast([128, 2, 256]),
                                                     op=ALU.mult), r=[d_hd[j], d_rn], w=[d_hd[j]])
                k.dve(lambda e, j=j, tt=tt: e.tensor_tensor(out=WHx[:, tt, 1, :], in0=hd[j][:, 0, :], in1=hd[j][:, 1, :], op=ALU.add),
                      r=[d_hd[j]], w=[d_WHx])
                k.pool(lambda e, j=j, tt=tt: e.tensor_tensor(out=WHx[:, tt, 2, :], in0=hd[j][:, 0, :], in1=hd[j][:, 1, :],
                                                             op=ALU.subtract), r=[d_hd[j]], w=[d_WHx])
        if pss == 0:
            k.dve(lambda e: e.tensor_copy(out=rn, in_=nrm_ps[:, 0:256]), r=[d_nrm], w=[d_rn])
            k.dve(lambda e: e.tensor_tensor(out=rn, in0=rn, in1=nrm_ps[:, 256:512], op=ALU.add), r=[d_nrm, d_rn], w=[d_rn])
            k.dve(lambda e: e.reciprocal(out=rn, in_=rn), r=[d_rn], w=[d_rn])
            k.dve(lambda e: e.tensor_scalar(out=rn, in0=rn, scalar1=2.0 / (2 * Ls), scalar2=None, op0=ALU.mult), r=[d_rn], w=[d_rn])


def _hy_dft(self, l, Ls, tag, toff, WHx, d_WHx, x0T, wTm, d_x0w, Yh, ring):
    k = self.k
    nt = Ls // 128
    Wd = min(512, Ls)
    ntb = Ls // Wd
    names = ["hy_Cf_", "hy_Sf_", "hy_Ci_", "hy_Si_"]
    tabs = []
    for nm in names:
        key = nm + tag
        if key not in self._hyin:
            shp = [nt, 128, nt * 128] if nm[4] == "f" else [ntb, 128, nt * Wd]
            self._hyin[key] = self.din(key, shp, BF16)
        tabs.append(self._hyin[key])
    Cf, Sf, Ci, Si = tabs
    d_Yh = Dep()
    Kr = self.aa([128, 4, 256])
    d_K = [Dep(), Dep()]
    tm = [self.aa([128, 256]) for _ in range(4)]
    d_tm = [Dep() for _ in range(4)]
    hybias = _ppv(self, l, "hybias")
    for j in range(nt):
        slot, dsl, chs = ring[self.hyring_i % len(ring)]
        self.hyring_i += 1
        cst = slot[:, 0:nt * 128].rearrange("p (a b) -> p a b", b=128)
        sst = slot[:, 2048:2048 + nt * 128].rearrange("p (a b) -> p a b", b=128)
        k.dma("sp", slot[:, 0:nt * 128], Cf[j], chs, w=[dsl])
        k.dma("sp", slot[:, 2048:2048 + nt * 128], Sf[j], chs, w=[dsl])
        ps_r = self.ps[0][:, 0:512]
        ps_i = self.ps[0][:, 512:1024]
        for tt in range(nt):
            k.pe(lambda e, tt=tt, cst=cst: e.matmul(ps_r, cst[:, tt, :], WHx[:, tt, 0:2, :], start=(tt == 0), stop=(tt == nt - 1)),
                 r=[dsl, d_WHx], w=[self.d_ps[0][0]])
        for tt in range(nt):
            k.pe(lambda e, tt=tt, sst=sst: e.matmul(ps_i, sst[:, tt, :], WHx[:, tt, 0:3:2, :], start=(tt == 0), stop=(tt == nt - 1)),
                 r=[dsl, d_WHx], w=[self.d_ps[0][1]])
        k.act(lambda e: e.activation(out=Kr[:, 0:2, :], in_=ps_r.rearrange("p (a b) -> p a b", a=2), func=AF.Copy),
              r=[self.d_ps[0][0]], w=[d_K[0]])
        k.act(lambda e: e.activation(out=Kr[:, 2:4, :], in_=ps_i.rearrange("p (a b) -> p a b", a=2), func=AF.Copy),
              r=[self.d_ps[0][1]], w=[d_K[1]])
        Ur, Kre, Ui, Kie = Kr[:, 0, :], Kr[:, 1, :], Kr[:, 2, :], Kr[:, 3, :]
        k.dve(lambda e: e.tensor_tensor(out=tm[0], in0=Ur, in1=Kre, op=ALU.mult), r=[d_K[0]], w=[d_tm[0]])
        k.pool(lambda e: e.tensor_tensor(out=tm[1], in0=Ui, in1=Kie, op=ALU.mult), r=[d_K[1]], w=[d_tm[1]])
        k.dve(lambda e, j=j: e.tensor_tensor(out=Yh[:, j, 0, :], in0=tm[0], in1=tm[1], op=ALU.subtract),
              r=[d_tm[0], d_tm[1]], w=[d_Yh])
        k.pool(lambda e: e.tensor_tensor(out=tm[2], in0=Ur, in1=Kie, op=ALU.mult), r=d_K, w=[d_tm[2]])
        k.dve(lambda e: e.tensor_tensor(out=tm[3], in0=Ui, in1=Kre, op=ALU.mult), r=d_K, w=[d_tm[3]])
        k.pool(lambda e, j=j: e.tensor_tensor(out=Yh[:, j, 1, :], in0=tm[2], in1=tm[3], op=ALU.add),
               r=[d_tm[2], d_tm[3]], w=[d_Yh])
    G = 4 if nt >= 4 else nt
    yt = [self.aa([128, 512]) for _ in range(2)]
    d_yt = [Dep(), Dep()]
    for tb in range(ntb):
        for g in range(nt // G):
            slot, dsl, chs = ring[self.hyring_i % len(ring)]
            self.hyring_i += 1
            k.dma("sp", slot[:, 0:G * Wd], Ci[tb, :, g * G * Wd:(g + 1) * G * Wd], chs, w=[dsl])
            k.dma("sp", slot[:, 2048:2048 + G * Wd], Si[tb, :, g * G * Wd:(g + 1) * G * Wd], chs, w=[dsl])
            cst = slot[:, 0:G * Wd].rearrange("p (a b) -> p a b", b=Wd)
            sst = slot[:, 2048:2048 + G * Wd].rearrange("p (a b) -> p a b", b=Wd)
            for fi in range(G):
                ft = g * G + fi
                for ct in range(2):
                    py = self.ps[1][:, ct * 512: ct * 512 + Wd]
                    k.pe(lambda e, fi=fi, ft=ft, ct=ct, py=py, cst=cst: e.matmul(py, Yh[:, ft, 0, ct * 128:(ct + 1) * 128], cst[:, fi, :],
                                                                               start=(ft == 0), stop=False),
                         r=[dsl, d_Yh], w=[self.d_ps[1][ct]])
                    k.pe(lambda e, fi=fi, ft=ft, ct=ct, py=py, sst=sst: e.matmul(py, Yh[:, ft, 1, ct * 128:(ct + 1) * 128], sst[:, fi, :],
                                                                               start=False, stop=(ft == nt - 1)),
                         r=[dsl, d_Yh], w=[self.d_ps[1][ct]])
        a = toff + tb * Wd
        tts = [self.d_C[t] for t in range(a // 128, (a + Wd) // 128)]
        for ct in range(2):
            py = self.ps[1][:, ct * 512: ct * 512 + Wd]
            k.dve(lambda e, ct=ct, py=py, a=a: e.scalar_tensor_tensor(out=yt[ct][:, 0:Wd], in0=wTm[:, ct, a:a + Wd], scalar=hybias[:, ct:ct + 1],
                                                                      in1=py, op0=ALU.mult, op1=ALU.add),
                  r=[self.d_ps[1][ct], d_x0w, self.d_pp], w=[d_yt[ct]])
            k.pool(lambda e, ct=ct, a=a: e.tensor_tensor(out=self.C[:, 4 + ct, a:a + Wd], in0=yt[ct][:, 0:Wd], in1=x0T[:, ct, a:a + Wd],
                                                         op=ALU.mult), r=[d_yt[ct], d_x0w], w=tts)


def _mix_hy(self, l, with_ctx):
    k = self.k
    self.arena_reset()
    if not hasattr(self, "_hyin"):
        self._hyin = {}
        self.ch_hy = k.chan()
        self.ch_hyring = [k.chan() for _ in range(3)]
    ntok = T if with_ctx else L
    WH = self.aa([128, 16, 3, 256], BF16)
    d_WH = Dep()
    if with_ctx:
        WHc = self.aa([128, 2, 3, 256], BF16)
        d_WHc = Dep()
    x0T = self.aa([128, 2, T], BF16)
    wTm = self.aa([128, 2, T], BF16)
    d_x0w = Dep()
    markA = self.ar_off
    _hy_filters(self, l, L, "lat", WH, d_WH)
    if with_ctx:
        _arena_rewind(self, markA)
        _hy_filters(self, l, LC, "ctx", WHc, d_WHc)
    _arena_rewind(self, markA)
    _wring_init(self)
    ranges = [(0, L)] + ([(L, T)] if with_ctx else [])
    blocks = _blocks(self, with_ctx)
    pin = [self.aa([128, T]) for _ in range(1)]
    d_pin = [Dep()]
    x1v = self.aa([128, 4, T], BF16)
    d_x1v = [Dep() for _ in range(4)]
    cacc = [self.aa([128, T]) for _ in range(1)]
    d_cacc = Dep()
    hyw = _ppv(self, l, "hyw")

    def evac(c, m, t0, n, pst, dps):
        ct = (c - OFF_HY) // 128
        j = 0
        k.act(lambda e: e.activation(out=pin[j][:, t0:t0 + n], in_=pst, func=AF.Copy), r=[dps], w=[d_pin[j]])
        if (t0, n) == blocks[-1]:
            _dwconv(self, "dve", cacc[0], pin[j], hyw[:, ct * 3:(ct + 1) * 3], ranges, [d_pin[j], self.d_pp], [d_cacc])
            if ct < 2:
                k.pool(lambda e: e.tensor_copy(out=x0T[:, ct, 0:ntok], in_=cacc[0][:, 0:ntok]), r=[d_cacc], w=[d_x0w])
            else:
                k.pool(lambda e: e.tensor_copy(out=x1v[:, ct - 2, 0:ntok], in_=cacc[0][:, 0:ntok]), r=[d_cacc], w=[d_x1v[ct - 2]])
    _proj(self, l, [(OFF_HY, 512), (OFF_HY + 512, 256)], blocks, evac, PROJ_BANKS)
    for ct in range(2):
        k.pool(lambda e, ct=ct: e.tensor_tensor(out=wTm[:, ct, 0:ntok], in0=x1v[:, ct, 0:ntok], in1=x1v[:, 2 + ct, 0:ntok], op=ALU.mult),
               r=[d_x1v[ct], d_x1v[2 + ct]], w=[d_x0w])
    psT = self.ps[2][:, 0:512].bitcast(BF16)
    d_psT = self.d_ps[2][0]
    for tt in range(ntok // 128):
        for ct in range(2):
            k.pe(lambda e, tt=tt, ct=ct: e.transpose(out=psT[:, ct * 128:(ct + 1) * 128], in_=wTm[:, ct, tt * 128:(tt + 1) * 128],
                                                     identity=self.ident_b[:]), r=[d_x0w, self.d_const], w=[d_psT])
        if tt < 16:
            k.act(lambda e, tt=tt: e.activation(out=WH[:, tt, 0, :], in_=psT[:, 0:256], func=AF.Copy), r=[d_psT], w=[d_WH])
        else:
            k.act(lambda e, tt=tt: e.activation(out=WHc[:, tt - 16, 0, :], in_=psT[:, 0:256], func=AF.Copy), r=[d_psT], w=[d_WHc])
    _arena_rewind(self, markA)
    Yh = self.aa([128, 16, 2, 256], BF16)
    ring = [(self.aa([128, 4096], BF16), Dep(), self.ch_hyring[i]) for i in range(3)]
    self.hyring_i = 0
    markD = self.ar_off
    _hy_dft(self, l, L, "lat", 0, WH, d_WH, x0T, wTm, d_x0w, Yh, ring)
    if with_ctx:
        _arena_rewind(self, markD)
        _hy_dft(self, l, LC, "ctx", L, WHc, d_WHc, x0T, wTm, d_x0w, Yh, ring)


def gdn_consts():
    out = {}
    m = np.arange(128)[:, None]
    i = np.arange(128)[None, :]
    LE = (m <= i).astype(np.float32)
    GE = (m >= i).astype(np.float32)
    GT = (m > i).astype(np.float32)
    LT = (m < i).astype(np.float32)
    NEGF = np.tile(np.where(i > m, -1e9, 0.0).astype(np.float32), (1, 4))
    NEGB = np.tile(np.where(i < m, -1e9, 0.0).astype(np.float32), (1, 4))
    out["gdn_gm"] = np.ascontiguousarray(np.concatenate([LE, GE, GT, LT, NEGF, NEGB], axis=1))
    pm = np.zeros((128, 128), np.float32)
    for mm in range(128):
        if (mm % 64) < 32:
            pm[mm + 32, mm] = -1.0
        else:
            pm[mm - 32, mm] = 1.0
    out["gdn_pm"] = pm
    ii = np.arange(128)[:, None]
    jj = np.arange(128)[None, :]
    lms, ums = [], []
    for s_ in range(7):
        b_ = 1 << s_
        same2 = (ii // (2 * b_)) == (jj // (2 * b_))
        diff1 = (ii // b_) != (jj // b_)
        lms.append(np.where(same2 & diff1 & (jj < ii), -1.0, 0.0))
        ums.append(np.where(same2 & diff1 & (jj > ii), -1.0, 0.0))
    out["gdn_lm"] = np.ascontiguousarray(np.concatenate(lms + ums, axis=1)).astype(ml_dtypes.bfloat16)
    pos = np.arange(L)
    row = (pos // 64).astype(np.float32)
    col = (pos % 64).astype(np.float32)
    inv = (10000.0 ** (-np.arange(16, dtype=np.float32) / 16)).astype(np.float32)
    ang = np.concatenate([row[:, None] * inv, col[:, None] * inv], axis=-1)
    idx = (np.arange(128) % 64) % 32
    cs = np.stack([np.cos(ang).T[idx], np.sin(ang).T[idx]], axis=1)
    out["gdn_rope"] = np.ascontiguousarray(cs.reshape(128, 2 * L)).astype(ml_dtypes.bfloat16)
    return out


def _mix_gdn(self, l, with_ctx):
    k = self.k
    self.arena_reset()
    if not hasattr(self, "gdn_gm_in"):
        self.gdn_gm_in = self.din("gdn_gm", [128, 1536])
        self.gdn_pm_in = self.din("gdn_pm", [128, 128])
        self.gdn_rope_in = self.din("gdn_rope", [128, 2 * L], BF16)
        self.ch_gdn = k.chan()
    otok = self.aa([128, NT, 256])
    markO = self.ar_off
    qT = self.aa([128, 2, T], BF16)
    kT = self.aa([128, 2, T], BF16)
    d_qk = Dep()
    vtok = self.aa([128, NT, 256], BF16)
    ktok = self.aa([128, NT, 256], BF16)
    d_vtok, d_ktok = Dep(), Dep()
    la = self.aa([128, NT, 8])
    beta = self.aa([128, NT, 8])
    d_la, d_beta = Dep(), Dep()
    GM = self.aa([128, 1536])
    d_GM = Dep()
    k.dma("sp", GM, self.gdn_gm_in[:, :], self.ch_gdn, w=[d_GM])
    LE, GE, GT, LT = (GM[:, i * 128:(i + 1) * 128] for i in range(4))
    NEGF, NEGB = GM[:, 512:1024], GM[:, 1024:1536]
    LMK = self.aa([128, 14 * 128], BF16)
    d_LMK = Dep()
    if not hasattr(self, "gdn_lm_in"):
        self.gdn_lm_in = self.din("gdn_lm", [128, 14 * 128], BF16)
    k.dma("sp", LMK, self.gdn_lm_in[:, :], self.ch_gdn, w=[d_LMK])
    markP = self.ar_off
    _wring_init(self, 1)
    pm = self.aa([128, 128])
    rope = self.aa([128, 2, L], BF16)
    blk64 = self.aa([128, 128])
    d_c1 = Dep()
    k.dma("sp", pm, self.gdn_pm_in[:, :], self.ch_gdn, w=[d_c1])
    k.dma("sp", rope.rearrange("p a b -> p (a b)"), self.gdn_rope_in[:, :], self.ch_gdn, w=[d_c1])
    k.pool(lambda e: e.memset(blk64, 0.0), w=[d_c1])
    k.pool(lambda e: e.memset(blk64[0:64, 0:64], 1.0), w=[d_c1])
    k.pool(lambda e: e.memset(blk64[64:128, 64:128], 1.0), w=[d_c1])
    wab, dwab, chab = self.wring[0]
    k.dma("pool", wab[:, :, 0:16], self.W("w_in")[l, :, 1024:1040].rearrange("(kt p) c -> p kt c", p=128), chab, w=[dwab])
    ab_ps = self.ps[3][:, 0:NT * 16]
    d_ab = self.d_ps[3][0]
    for tt in range(NT):
        for dt in range(8):
            k.pe(lambda e, tt=tt, dt=dt: e.matmul(ab_ps[:, tt * 16:(tt + 1) * 16], self.H[:, dt, tt * 128:(tt + 1) * 128],
                                                  wab[:, dt, 0:16], start=(dt == 0), stop=(dt == 7)),
                 r=[dwab, self.d_H[tt]], w=[d_ab])
    ab3 = ab_ps.rearrange("p (t c) -> p t c", c=16)
    xa = self.aa([128, NT, 8])
    ea = self.aa([128, 8])
    d_xa, d_ea = Dep(), Dep()
    k.dve(lambda e: e.tensor_tensor(out=xa, in0=ab3[:, :, 0:8], in1=_ppv(self, l, "dtb").unsqueeze(1).to_broadcast([128, NT, 8]),
                                    op=ALU.add), r=[d_ab, self.d_pp], w=[d_xa])
    k.act(lambda e: e.activation(out=xa, in_=xa, func=AF.Exp), r=[d_xa], w=[d_xa])
    k.act(lambda e: e.activation(out=xa, in_=xa, func=AF.Ln, bias=1.0, scale=1.0), r=[d_xa], w=[d_xa])
    k.act(lambda e: e.activation(out=ea, in_=_ppv(self, l, "alog"), func=AF.Exp), r=[self.d_pp], w=[d_ea])
    k.dve(lambda e: e.scalar_tensor_tensor(out=la, in0=xa, scalar=-1.0, in1=ea.unsqueeze(1).to_broadcast([128, NT, 8]),
                                           op0=ALU.mult, op1=ALU.mult), r=[d_xa, d_ea], w=[d_la])
    k.act(lambda e: e.activation(out=beta, in_=ab3[:, :, 8:16], func=AF.Sigmoid), r=[d_ab], w=[d_beta])
    blocks = _blocks(self, True)
    ranges = [(0, L), (L, T)]
    ofl = otok.rearrange("p a b -> p (a b)")
    pin = ofl[:, 0:T]
    cacc = ofl[:, T:2 * T]
    d_pin, d_cacc = Dep(), Dep()
    vTt = self.aa([128, T], BF16)
    d_vTt = Dep()
    rv = self.aa([128, 512])
    t1 = self.aa([128, 512])
    t2 = self.aa([128, 512])
    d_rv, d_t1, d_t2 = Dep(), Dep(), Dep()
    gdw = _ppv(self, l, "gdw")
    psT = self.ps[2][:, 0:512].bitcast(BF16)
    d_psT = self.d_ps[2][0]

    def finish_tile(ct):
        _dwconv(self, "dve", cacc, pin, gdw[:, ct * 3:(ct + 1) * 3], ranges, [d_pin, self.d_pp], [d_cacc])
        if ct >= 4:
            k.act(lambda e: e.activation(out=vTt, in_=cacc, func=AF.Silu), r=[d_cacc], w=[d_vTt])
            for tt in range(NT):
                k.pe(lambda e, tt=tt: e.transpose(out=psT[:, 0:128], in_=vTt[:, tt * 128:(tt + 1) * 128], identity=self.ident_b[:]),
                     r=[d_vTt, self.d_const], w=[d_psT])
                k.dve(lambda e, tt=tt: e.tensor_copy(out=vtok[:, tt, (ct - 4) * 128:(ct - 3) * 128], in_=psT[:, 0:128]),
                      r=[d_psT], w=[d_vtok])
            return
        isq = ct < 2
        dst = qT if isq else kT
        cc = ct % 2
        k.act(lambda e: e.activation(out=pin, in_=cacc, func=AF.Silu), r=[d_cacc, d_pin], w=[d_pin])
        k.act(lambda e: e.activation(out=cacc, in_=pin, func=AF.Square), r=[d_pin, d_cacc], w=[d_cacc])
        for (t0, n) in blocks:
            ss = self.ps[3][:, 512:512 + n]
            dss = self.d_ps[3][1]
            k.pe(lambda e, t0=t0, n=n, ss=ss: e.matmul(ss, blk64, cacc[:, t0:t0 + n], start=True, stop=True), r=[d_cacc, d_c1], w=[dss])
            sc_, bi_ = (64.0, 64.0 * EPS) if isq else (1.0, EPS)
            k.act(lambda e, n=n, ss=ss: e.activation(out=rv[:, 0:n], in_=ss, func=AF.Sqrt, bias=bi_, scale=sc_), r=[dss], w=[d_rv])
            k.dve(lambda e, n=n: e.reciprocal(out=rv[:, 0:n], in_=rv[:, 0:n]), r=[d_rv], w=[d_rv])
            k.dve(lambda e, t0=t0, n=n: e.tensor_tensor(out=pin[:, t0:t0 + n], in0=pin[:, t0:t0 + n], in1=rv[:, 0:n], op=ALU.mult),
                  r=[d_rv, d_pin], w=[d_pin])
            if t0 < L:
                pv = self.ps[2][:, 512:512 + n]
                dpv = self.d_ps[2][1]
                k.pe(lambda e, t0=t0, n=n, pv=pv: e.matmul(pv, pm, pin[:, t0:t0 + n], start=True, stop=True), r=[d_pin, d_c1], w=[dpv])
                k.dve(lambda e, t0=t0, n=n: e.tensor_tensor(out=t1[:, 0:n], in0=pin[:, t0:t0 + n], in1=rope[:, 0, t0:t0 + n], op=ALU.mult),
                      r=[d_pin, d_c1], w=[d_t1])
                k.dve(lambda e, t0=t0, n=n, pv=pv: e.tensor_tensor(out=t2[:, 0:n], in0=pv, in1=rope[:, 1, t0:t0 + n], op=ALU.mult),
                      r=[dpv, d_c1], w=[d_t2])
                k.pool(lambda e, t0=t0, n=n: e.tensor_tensor(out=dst[:, cc, t0:t0 + n], in0=t1[:, 0:n], in1=t2[:, 0:n], op=ALU.add),
                       r=[d_t1, d_t2], w=[d_qk])
            else:
                k.pool(lambda e, t0=t0, n=n: e.tensor_copy(out=dst[:, cc, t0:t0 + n], in_=pin[:, t0:t0 + n]), r=[d_pin], w=[d_qk])
        if not isq:
            for tt in range(NT):
                k.pe(lambda e, tt=tt: e.transpose(out=psT[:, 0:128], in_=kT[:, cc, tt * 128:(tt + 1) * 128], identity=self.ident_b[:]),
                     r=[d_qk, self.d_const], w=[d_psT])
                k.dve(lambda e, tt=tt: e.tensor_copy(out=ktok[:, tt, cc * 128:(cc + 1) * 128], in_=psT[:, 0:128]),
                      r=[d_psT], w=[d_ktok])

    def evac(c, m, t0, n, pst, dps):
        ct = c // 128
        k.act(lambda e: e.activation(out=pin[:, t0:t0 + n], in_=pst, func=AF.Copy), r=[dps], w=[d_pin])
        if (t0, n) == blocks[-1]:
            finish_tile(ct)
    _proj(self, l, [(512, 256), (256, 256), (0, 256)], blocks, evac, PROJ_BANKS)
    import os
    _stop = os.environ.get("GDN_STOP", "")
    if "gdn_g1" in self.dbg:
        self.k.barrier()
        chd = k.chan()
        for nm, ap_, n_ in (("qT", qT.rearrange("p a b -> p (a b)"), 2 * T), ("kT", kT.rearrange("p a b -> p (a b)"), 2 * T),
                            ("vtok", vtok.rearrange("p a b -> p (a b)"), NT * 256), ("ktok", ktok.rearrange("p a b -> p (a b)"), NT * 256)):
            o_ = self.dout("dbg_" + nm, [128, n_], BF16)
            k.dma("sp", o_[:, :], ap_, chd)
        for nm, ap_ in (("la", la), ("beta", beta)):
            o_ = self.dout("dbg_" + nm, [128, NT * 8])
            k.dma("sp", o_[:, :], ap_.rearrange("p a b -> p (a b)"), chd)
        self.k.barrier()
    if _stop == "g1":
        return
    _arena_rewind(self, markP)
    d_otok = [Dep() for _ in range(NT)]
    hm = self.aa([128, 4])
    d_hm = Dep()
    k.pool(lambda e: e.memset(hm, 0.0), w=[d_hm])
    k.pool(lambda e: e.memset(hm[0:64, 0:4:2], 1.0), w=[d_hm])
    k.pool(lambda e: e.memset(hm[64:128, 1:4:2], 1.0), w=[d_hm])
    first_visit = [True] * NT
    orderF = [16, 17] + list(range(16))
    orderB = [17, 16] + list(range(15, -1, -1))
    W_ = {}
    for dd in range(2):
        w = {}
        w["Z"] = self.aa([128, 4, 128]); w["D"] = self.aa([128, 4, 128]); w["Ds"] = w["Z"]
        w["A"] = [self.aa([128, 4, 128], BF16) for _ in range(2)]
        w["AT"] = [self.aa([128, 4, 128], BF16) for _ in range(2)]
        w["X"] = [self.aa([128, 4, 128], BF16) for _ in range(2)]
        w["QKm"] = self.aa([128, 4, 128], BF16)
        w["tmpb"] = self.aa([128, 4, 128], BF16)
        w["R"] = self.aa([128, 4, 128], BF16)
        w["u"] = self.aa([128, 4, 64], BF16)
        w["d_tmpb"], w["d_R"], w["d_u"] = Dep(), Dep(), Dep()
        w["QKT"] = self.aa([128, 4, 128], BF16)
        w["wT"] = self.aa([64, 4, 128], BF16)
        w["kd"] = self.aa([128, 4, 128], BF16)
        w["kc"] = self.aa([128, 2, 2, 128], BF16)
        w["d_kc"] = Dep()
        k.pool(lambda e, w=w: e.memset(w["kc"], 0.0), w=[w["d_kc"]])
        w["SbQ"] = self.aa([128, 4, 64], BF16)
        w["d_SbQ"] = Dep()
        k.pool(lambda e, w=w: e.memset(w["SbQ"], 0.0), w=[w["d_SbQ"]])
        w["ev"] = self.aa([128, 12])
        w["beg"] = self.aa([128, 4])
        w["vn"] = self.aa([128, 4, 64], BF16)
        w["o2"] = self.aa([128, 4, 64])
        w["ot"] = self.aa([128, 4, 64])
        w["S"] = self.aa([128, 4, 64])
        w["Sb"] = self.aa([128, 4, 64], BF16)
        for nm in ("Z", "D", "Ds", "QKm", "QKT", "wT", "kd", "ev", "beg", "vn", "o2", "ot", "S", "Sb"):
            w["d_" + nm] = Dep()
        w["d_Ds"] = w["d_Z"]
        w["d_A"] = [Dep(), Dep()]; w["d_AT"] = [Dep(), Dep()]; w["d_X"] = [Dep(), Dep()]
        k.dve(lambda e, w=w: e.memset(w["S"], 0.0), w=[w["d_S"]])
        k.pool(lambda e, w=w: e.memset(w["Sb"], 0.0), w=[w["d_Sb"]])
        W_[dd] = w

    def bank(dd, i, half=None):
        t = self.ps[2 * dd + i // 2]
        hb = i % 2
        return t[:, hb * 512:(hb + 1) * 512], self.d_ps[2 * dd + i // 2][hb]

    def unit_pre(dd, n):
        w = W_[dd]
        c0 = n * 128
        lacol = la[:, n, dd * 4:dd * 4 + 4]
        becol = beta[:, n, dd * 4:dd * 4 + 4]
        Mz = GT if dd == 0 else LT
        Um = LE if dd == 0 else GE
        Ugt = GT if dd == 0 else LT
        NEG = NEGF if dd == 0 else NEGB
        strict = GT if dd == 0 else LT
        B0, dB0 = bank(dd, 0)
        B1, dB1 = bank(dd, 1)
        B2, dB2 = bank(dd, 2)
        B3, dB3 = bank(dd, 3)
        k.dve(lambda e: e.tensor_tensor(out=w["Z"], in0=Mz.unsqueeze(1).to_broadcast([128, 4, 128]),
                                        in1=lacol.unsqueeze(2).to_broadcast([128, 4, 128]), op=ALU.mult),
              r=[d_GM, d_la], w=[w["d_Z"]])
        k.pe(lambda e: e.matmul(B0, Um, w["Z"].rearrange("p a b -> p (a b)"), start=True, stop=False), r=[d_GM, w["d_Z"]], w=[dB0])
        k.pe(lambda e: e.matmul(B0, self.ident_f[:], NEG, start=False, stop=True), r=[d_GM, self.d_const], w=[dB0])
        k.act(lambda e: e.activation(out=w["D"].rearrange("p a b -> p (a b)"), in_=B0, func=AF.Exp), r=[dB0], w=[w["d_D"]])
        if _stop == "pre1":
            return None, None
        k.pe(lambda e: e.matmul(B3[:, 0:4], Um, lacol, start=True, stop=True), r=[d_GM, d_la], w=[dB3])
        k.pe(lambda e: e.matmul(B3[:, 4:8], Ugt, lacol, start=True, stop=True), r=[d_GM, d_la], w=[dB3])
        k.pe(lambda e: e.matmul(B3[:, 8:12], self.ones_f[:], lacol, start=True, stop=True), r=[self.d_const, d_la], w=[dB3])
        k.act(lambda e: e.activation(out=w["ev"], in_=B3[:, 0:12], func=AF.Exp), r=[dB3], w=[w["d_ev"]])
        if _stop == "pre2":
            return None, None
        k.dve(lambda e: e.tensor_tensor(out=w["Ds"], in0=w["D"], in1=strict.unsqueeze(1).to_broadcast([128, 4, 128]), op=ALU.mult),
              r=[w["d_D"], d_GM], w=[w["d_Ds"]])
        if _stop == "pre2a":
            return None, None
        for h in range(4 if _stop != "pre2h" else 1):
            hp = slice(64 * (h % 2), 64 * (h % 2) + 64)
            if h == 0:
                k.pool(lambda e: e.tensor_copy(out=w["kc"][0:64, :, 0, :], in_=kT[0:64, :, c0:c0 + 128]), r=[d_qk], w=[w["d_kc"]])
                k.pool(lambda e: e.tensor_copy(out=w["kc"][64:128, :, 1, :], in_=kT[64:128, :, c0:c0 + 128]), r=[d_qk], w=[w["d_kc"]])
            k.pe(lambda e, h=h, hp=hp: e.matmul(B1[:, h * 128:(h + 1) * 128], kT[:, h // 2, c0:c0 + 128], w["kc"][:, h // 2, h % 2, :],
                                                start=True, stop=True), r=[d_qk, w["d_kc"]], w=[dB1])
        if _stop in ("pre2b", "pre2h"):
            return None, None
        k.dve(lambda e: e.tensor_tensor(out=w["Ds"].rearrange("p a b -> p (a b)"), in0=B1, in1=w["Ds"].rearrange("p a b -> p (a b)"),
                                        op=ALU.mult), r=[dB1, w["d_Ds"]], w=[w["d_Ds"]])
        k.dve(lambda e: e.tensor_tensor(out=w["A"][0], in0=w["Ds"], in1=becol.unsqueeze(2).to_broadcast([128, 4, 128]), op=ALU.mult),
              r=[w["d_Ds"], d_beta], w=[w["d_A"][0]])
        if _stop == "pre3":
            return None, None
        for h in range(4):
            hp = slice(64 * (h % 2), 64 * (h % 2) + 64)
            k.pe(lambda e, h=h, hp=hp: e.matmul(B2[:, h * 128:(h + 1) * 128], qT[:, h // 2, c0:c0 + 128], w["kc"][:, h // 2, h % 2, :],
                                                start=True, stop=True), r=[d_qk, w["d_kc"]], w=[dB2])
        k.dve(lambda e: e.tensor_tensor(out=w["QKm"].rearrange("p a b -> p (a b)"), in0=B2, in1=w["D"].rearrange("p a b -> p (a b)"),
                                        op=ALU.mult), r=[dB2, w["d_D"]], w=[w["d_QKm"]])
        if _stop == "pre4":
            return None, None
        B1b = B1.bitcast(BF16)
        B2b = B2.bitcast(BF16)
        for h in range(4):
            k.pe(lambda e, h=h: e.transpose(out=B1b[:, h * 128:(h + 1) * 128], in_=w["A"][0][:, h, :], identity=self.ident_b[:]),
                 r=[w["d_A"][0], self.d_const], w=[dB1])
        k.act(lambda e: e.activation(out=w["AT"][0].rearrange("p a b -> p (a b)"), in_=B1b[:, 0:512], func=AF.Copy),
              r=[dB1], w=[w["d_AT"][0]])
        for h in range(4):
            k.pe(lambda e, h=h: e.transpose(out=B2b[:, h * 128:(h + 1) * 128], in_=w["QKm"][:, h, :], identity=self.ident_b[:]),
                 r=[w["d_QKm"], self.d_const], w=[dB2])
        k.act(lambda e: e.activation(out=w["QKT"].rearrange("p a b -> p (a b)"), in_=B2b[:, 0:512], func=AF.Copy),
              r=[dB2], w=[w["d_QKT"]])
        if _stop == "pre5":
            return None, None
        k.dve(lambda e: e.tensor_tensor(out=w["beg"], in0=becol, in1=w["ev"][:, 0:4], op=ALU.mult), r=[d_beta, w["d_ev"]], w=[w["d_beg"]])
        X0 = w["R"].rearrange("p h (two d) -> p h two d", two=2)
        k.dve(lambda e: e.tensor_tensor(out=X0[:, :, 0, :], in0=vtok[:, n, :].rearrange("p (h d) -> p h d", h=4),
                                        in1=becol.unsqueeze(2).to_broadcast([128, 4, 64]), op=ALU.mult),
              r=[d_vtok, d_beta], w=[w["d_R"]])
        k.dve(lambda e: e.tensor_tensor(out=X0[:, :, 1, :], in0=ktok[:, n, :].rearrange("p (h d) -> p h d", h=4),
                                        in1=w["beg"].unsqueeze(2).to_broadcast([128, 4, 64]), op=ALU.mult),
              r=[d_ktok, w["d_beg"]], w=[w["d_R"]])
        for half in range(2):
            k.pool(lambda e, half=half: e.tensor_tensor(out=w["kd"][:, :, half * 64:(half + 1) * 64],
                                                        in0=ktok[:, n, :].rearrange("p (h d) -> p h d", h=4),
                                                        in1=w["ev"][:, 4:8].unsqueeze(2).to_broadcast([128, 4, 64]), op=ALU.mult),
                   r=[d_ktok, w["d_ev"]], w=[w["d_kd"]])
        if _stop == "pre6":
            return None, None
        A, AT = w["A"][0], w["AT"][0]
        dA, dAT = w["d_A"][0], w["d_AT"][0]
        Tm, TT = w["A"][1], w["AT"][1]
        dT, dTT = w["d_A"][1], w["d_AT"][1]
        M1, M1t = w["X"][0], w["X"][1]
        dM1, dM1t = w["d_X"][0], w["d_X"][1]
        tmpa, tmpb_ = w["QKm"], w["tmpb"]
        dtmpa, dtmpb = w["d_QKm"], w["d_tmpb"]
        mo = 0 if dd == 0 else 7
        mt = 7 if dd == 0 else 0
        I4 = self.ident_b[:].unsqueeze(1).to_broadcast([128, 4, 128])
        msk = lambda s_: LMK[:, (mo + s_) * 128:(mo + s_ + 1) * 128].unsqueeze(1).to_broadcast([128, 4, 128])
        mskT = lambda s_: LMK[:, (mt + s_) * 128:(mt + s_ + 1) * 128].unsqueeze(1).to_broadcast([128, 4, 128])
        k.pool(lambda e: e.tensor_tensor(out=Tm, in0=A, in1=msk(0), op=ALU.mult), r=[dA, d_LMK], w=[dT])
        k.pool(lambda e: e.tensor_tensor(out=Tm, in0=Tm, in1=I4, op=ALU.add), r=[dT, self.d_const], w=[dT])
        k.pool(lambda e: e.tensor_tensor(out=TT, in0=AT, in1=mskT(0), op=ALU.mult), r=[dAT, d_LMK], w=[dTT])
        k.pool(lambda e: e.tensor_tensor(out=TT, in0=TT, in1=I4, op=ALU.add), r=[dTT, self.d_const], w=[dTT])
        fl = lambda x: x.rearrange("p a b -> p (a b)")
        for lev in range(1, 7):
            for h in range(4):
                k.pe(lambda e, h=h: e.matmul(B0[:, h * 128:(h + 1) * 128], AT[:, h, :], Tm[:, h, :], start=True, stop=True),
                     r=[dAT, dT], w=[dB0])
            for h in range(4):
                k.pe(lambda e, h=h: e.matmul(B1[:, h * 128:(h + 1) * 128], A[:, h, :], TT[:, h, :], start=True, stop=True),
                     r=[dA, dTT], w=[dB1])
            k.act(lambda e: e.activation(out=fl(M1), in_=B0, func=AF.Copy), r=[dB0], w=[dM1])
            k.act(lambda e: e.activation(out=fl(M1t), in_=B1, func=AF.Copy), r=[dB1], w=[dM1t])
            for h in range(4):
                k.pe(lambda e, h=h: e.matmul(B2[:, h * 128:(h + 1) * 128], TT[:, h, :], M1[:, h, :], start=True, stop=True),
                     r=[dTT, dM1], w=[dB2])
            for h in range(4):
                k.pe(lambda e, h=h: e.matmul(B0[:, h * 128:(h + 1) * 128], Tm[:, h, :], M1t[:, h, :], start=True, stop=True),
                     r=[dT, dM1t], w=[dB0])
            k.dve(lambda e, lev=lev: e.tensor_tensor(out=tmpa, in0=B2.rearrange("p (a b) -> p a b", a=4), in1=msk(lev), op=ALU.mult),
                  r=[dB2, d_LMK], w=[dtmpa])
            k.dve(lambda e, lev=lev: e.tensor_tensor(out=tmpb_, in0=B0.rearrange("p (a b) -> p a b", a=4), in1=mskT(lev), op=ALU.mult),
                  r=[dB0, d_LMK], w=[dtmpb])
            k.pool(lambda e: e.tensor_tensor(out=Tm, in0=Tm, in1=tmpa, op=ALU.add), r=[dT, dtmpa], w=[dT])
            k.pool(lambda e: e.tensor_tensor(out=TT, in0=TT, in1=tmpb_, op=ALU.add), r=[dTT, dtmpb], w=[dTT])
        Rm = w["R"].rearrange("p h (two d) -> p h two d", two=2)
        for h in range(4):
            k.pe(lambda e, h=h: e.matmul(B1[:, h * 64:(h + 1) * 64], TT[:, h, :], Rm[:, h, 0, :], start=True, stop=True),
                 r=[dTT, w["d_R"]], w=[dB1])
        k.act(lambda e: e.activation(out=w["u"].rearrange("p a b -> p (a b)"), in_=B1[:, 0:256], func=AF.Copy), r=[dB1], w=[w["d_u"]])
        for h in range(4):
            k.pe(lambda e, h=h: e.matmul(B2[0:64, h * 128:(h + 1) * 128], Rm[:, h, 1, :], TT[:, h, :], start=True, stop=True),
                 r=[dTT, w["d_R"]], w=[dB2])
        k.act(lambda e: e.activation(out=w["wT"].rearrange("p a b -> p (a b)"), in_=B2[0:64, :], func=AF.Copy),
              r=[dB2], w=[w["d_wT"]])
        return w["u"], w["d_u"]

    def unit_scan(dd, n, Xf, dXf, need_out):
        w = W_[dd]
        c0 = n * 128
        B0, dB0 = bank(dd, 0)
        B1, dB1 = bank(dd, 1)
        for h in range(4):
            k.pe(lambda e, h=h: e.matmul(B0[:, h * 64:(h + 1) * 64], w["wT"][:, h, :], w["Sb"][0:64, h, :], start=True, stop=True),
                 r=[w["d_wT"], w["d_Sb"]], w=[dB0])
        if need_out:
            for h in range(4):
                hp = slice(64 * (h % 2), 64 * (h % 2) + 64)
                k.pe(lambda e, h=h, hp=hp: e.matmul(B0[:, 256 + h * 64:256 + (h + 1) * 64], qT[:, h // 2, c0:c0 + 128],
                                                    w["SbQ"][:, h, :], start=True, stop=True),
                     r=[d_qk, w["d_SbQ"]], w=[dB0])
        k.dve(lambda e: e.tensor_tensor(out=w["vn"], in0=Xf, in1=B0[:, 0:256].rearrange("p (h d) -> p h d", h=4), op=ALU.subtract),
              r=[dXf, dB0], w=[w["d_vn"]])
        if need_out:
            for h in range(4):
                k.pe(lambda e, h=h: e.matmul(B1[:, h * 64:(h + 1) * 64], w["QKT"][:, h, :], w["vn"][:, h, :], start=True, stop=True),
                     r=[w["d_QKT"], w["d_vn"]], w=[dB1])
            k.act(lambda e: e.activation(out=w["o2"].rearrange("p a b -> p (a b)"), in_=B1[:, 0:256], func=AF.Copy), r=[dB1], w=[w["d_o2"]])
            k.dve(lambda e: e.tensor_tensor(out=w["ot"], in0=B0[:, 256:512].rearrange("p (h d) -> p h d", h=4),
                                            in1=w["ev"][:, 0:4].unsqueeze(2).to_broadcast([128, 4, 64]), op=ALU.mult),
                  r=[dB0, w["d_ev"]], w=[w["d_ot"]])
            k.pool(lambda e: e.tensor_tensor(out=w["ot"], in0=w["ot"], in1=w["o2"], op=ALU.add), r=[w["d_ot"], w["d_o2"]], w=[w["d_ot"]])
            if first_visit[n]:
                first_visit[n] = False
                k.pool(lambda e: e.tensor_copy(out=otok[:, n, :].rearrange("p (h d) -> p h d", h=4), in_=w["ot"]), r=[w["d_ot"]], w=[d_otok[n]])
            else:
                k.pool(lambda e: e.tensor_tensor(out=otok[:, n, :].rearrange("p (h d) -> p h d", h=4),
                                                 in0=otok[:, n, :].rearrange("p (h d) -> p h d", h=4), in1=w["ot"], op=ALU.add),
                       r=[w["d_ot"], d_otok[n]], w=[d_otok[n]])
        for h in range(4):
            k.pe(lambda e, h=h: e.matmul(B1[:, 256 + h * 64:256 + (h + 1) * 64], w["kd"][:, h, :], w["vn"][:, h, :], start=True, stop=True),
                 r=[w["d_kd"], w["d_vn"]], w=[dB1])
        k.dve(lambda e: e.tensor_tensor(out=w["S"], in0=w["S"], in1=w["ev"][:, 8:12].unsqueeze(2).to_broadcast([128, 4, 64]), op=ALU.mult),
              r=[w["d_S"], w["d_ev"]], w=[w["d_S"]])
        k.dve(lambda e: e.tensor_tensor(out=w["S"], in0=w["S"], in1=B1[:, 256:512].rearrange("p (h d) -> p h d", h=4), op=ALU.add),
              r=[w["d_S"], dB1], w=[w["d_S"]])
        k.act(lambda e: e.activation(out=w["Sb"], in_=w["S"], func=AF.Copy), r=[w["d_S"]], w=[w["d_Sb"]])
        k.pool(lambda e: e.tensor_tensor(out=w["SbQ"], in0=w["S"], in1=hm.unsqueeze(2).to_broadcast([128, 4, 64]), op=ALU.mult),
               r=[w["d_S"], d_hm], w=[w["d_SbQ"]])

    for step in range(1 if _stop.startswith("pre") else (int(_stop[4:].rstrip("p")) if _stop.startswith("step") else 18)):
        pend = []
        for dd in range(2):
            n = (orderF if dd == 0 else orderB)[step]
            need_out = (n < 16) or with_ctx
            Xf, dXf = unit_pre(dd, n)
            pend.append((dd, n, Xf, dXf, need_out))
        if _stop.startswith("pre") or (_stop.startswith("step") and _stop.endswith("p")):
            continue
        if "gdn_u0" in self.dbg and step == 0:
            self.k.barrier()
            chd = k.chan()
            w0 = W_[0]
            Xf0 = pend[0][2]
            for nm, ap_, n_, dt_ in (("D", w0["D"], 512, F32), ("X", Xf0, 256, BF16), ("QKT", w0["QKT"], 512, BF16),
                                     ("A0", w0["A"][1], 512, BF16), ("AT0", w0["AT"][1], 512, BF16), ("kd", w0["kd"], 512, BF16)):
                o_ = self.dout("dbg_" + nm, [128, n_], dt_)
                k.dma("sp", o_[:, :], ap_.rearrange("p a b -> p (a b)"), chd)
            o_ = self.dout("dbg_ev", [128, 12])
            k.dma("sp", o_[:, :], w0["ev"], chd)
            o_ = self.dout("dbg_wT", [64, 512], BF16)
            k.dma("sp", o_[:, :], w0["wT"].rearrange("p a b -> p (a b)"), chd)
            self.k.barrier()
        for (dd, n, Xf, dXf, need_out) in pend:
            unit_scan(dd, n, Xf, dXf, need_out)
    if "gdn_otok" in self.dbg:
        self.k.barrier()
        chd = k.chan()
        o_ = self.dout("dbg_otok", [128, NT * 256])
        k.dma("sp", o_[:, :], otok.rearrange("p a b -> p (a b)"), chd)
        self.k.barrier()
    if _stop:
        return
    _arena_rewind(self, markO)
    _gdn_out(self, l, with_ctx, otok, d_otok)


def _gdn_out(self, l, with_ctx, otok, d_otok):
    k = self.k
    ntt = NT if with_ctx else 16
    _wring_init(self, 1)
    gT = self.aa([128, 2, T], BF16)
    d_g = Dep()
    blocks = _blocks(self, with_ctx)

    def evac(c, m, t0, n, pst, dps):
        ct = (c - 768) // 128
        k.act(lambda e: e.activation(out=gT[:, ct, t0:t0 + n], in_=pst, func=AF.Silu), r=[dps], w=[d_g])
    _proj(self, l, [(768, 256)], blocks, evac, PROJ_BANKS)
    gn = _ppv(self, l, "gnorm")
    sq = self.aa([128, 4, 64])
    ss = self.aa([128, 4])
    on = [self.aa([128, 256], BF16) for _ in range(2)]
    d_sq, d_ss = Dep(), Dep()
    d_on = [Dep(), Dep()]
    psT = self.ps[2][:, 0:512].bitcast(BF16)
    d_psT = self.d_ps[2][0]
    for tt in range(ntt):
        o3 = otok[:, tt, :].rearrange("p (h d) -> p h d", h=4)
        j = tt % 2
        k.dve(lambda e, o3=o3: e.tensor_tensor(out=sq, in0=o3, in1=o3, op=ALU.mult), r=[d_otok[tt]], w=[d_sq])
        k.dve(lambda e: e.reduce_sum(out=ss, in_=sq, axis=mybir.AxisListType.X), r=[d_sq], w=[d_ss])
        k.act(lambda e: e.activation(out=ss, in_=ss, func=AF.Sqrt, bias=EPS, scale=1.0 / 64), r=[d_ss], w=[d_ss])
        k.dve(lambda e: e.reciprocal(out=ss, in_=ss), r=[d_ss], w=[d_ss])
        k.dve(lambda e, o3=o3, j=j: e.tensor_tensor(out=on[j].rearrange("p (h d) -> p h d", h=4), in0=o3,
                                                    in1=ss.unsqueeze(2).to_broadcast([128, 4, 64]), op=ALU.mult),
              r=[d_otok[tt], d_ss], w=[d_on[j]])
        for ct in range(2):
            k.pe(lambda e, j=j, ct=ct: e.transpose(out=psT[:, ct * 128:(ct + 1) * 128], in_=on[j][:, ct * 128:(ct + 1) * 128],
                                                   identity=self.ident_b[:]), r=[d_on[j], self.d_const], w=[d_psT])
        for ct in range(2):
            k.dve(lambda e, ct=ct, tt=tt: e.scalar_tensor_tensor(out=self.C[:, ct, tt * 128:(tt + 1) * 128], in0=psT[:, ct * 128:(ct + 1) * 128],
                                                                 scalar=gn[:, 0:1], in1=gT[:, ct, tt * 128:(tt + 1) * 128],
                                                                 op0=ALU.mult, op1=ALU.mult),
                  r=[d_psT, d_g, self.d_pp], w=[self.d_C[tt]])


from concourse.bass_utils import run_bass_kernel_spmd


def kernel(**inputs):
    inputs = {k_: np.asarray(v) for k_, v in inputs.items()}
    mk = MK()
    nc = mk.build()
    consts = const_inputs(inputs)
    pp = pp_host(inputs)

    def extra(b):
        e = {"pp": pp}
        e.update(consts)
        return e
    maps = host_inputs(mk, inputs, extra=extra)
    n = len(maps)
    res = run_bass_kernel_spmd(nc, maps, core_ids=list(range(n)))
    out = np.stack([np.asarray(res.results[b]["out"]) for b in range(n)], axis=0)
    return out.astype(np.float32)
```

```python
import numpy as np
import ml_dtypes
from contextlib import ExitStack
import concourse.bass as bass
import concourse.mybir as mybir

F32 = mybir.dt.float32
BF16 = mybir.dt.bfloat16
I32 = mybir.dt.int32
AF = mybir.ActivationFunctionType
ALU = mybir.AluOpType

D = 1024
L = 2048
LC = 256
T = L + LC
NT = T // 128
DEPTH = 2
EPS = 1e-6
IN_COLS = 3344
OFF_SC, OFF_HY, OFF_NA = 1040, 1808, 2576


class Dep:
    __slots__ = ("lw", "rd")

    def __init__(self):
        self.lw = None
        self.rd = []


class Chan:
    __slots__ = ("sem", "cnt", "key", "q")

    def __init__(self, sem, key):
        self.sem = sem
        self.cnt = 0
        self.key = key


class K:
    def __init__(self, nc, stack):
        self.nc = nc
        self.stack = stack
        self.eng = {"pe": nc.tensor, "act": nc.scalar, "dve": nc.vector, "pool": nc.gpsimd, "sp": nc.sync}
        self.sems = {}
        self.ecnt = {}
        self.waited = {e: {} for e in self.eng}
        for e in self.eng:
            self.sems["e_" + e] = stack.enter_context(nc.semaphore("e_" + e))
            self.ecnt[e] = 0
        self.chans = []
        self.bar_sem = stack.enter_context(nc.semaphore("bar"))
        self.bar_cnt = 0
        self.nops = 0

    def chan(self):
        key = "c%d" % len(self.chans)
        s = self.stack.enter_context(self.nc.semaphore(key))
        self.sems[key] = s
        c = Chan(s, key)
        c.q = None
        self.chans.append(c)
        return c

    def _wait(self, e, ev):
        key, val, src = ev
        if src == e and e == "pe":
            return
        w = self.waited[e]
        if w.get(key, 0) >= val:
            return
        w[key] = val
        self.eng[e].wait_ge(self.sems[key], val)

    def _deps(self, e, r, w):
        for d in r:
            if d.lw is not None:
                self._wait(e, d.lw)
        for d in w:
            if d.lw is not None and d.lw[2] != e:
                self._wait(e, d.lw)
            for ev in d.rd:
                if ev[2] != e:
                    self._wait(e, ev)

    def _commit(self, ev, r, w):
        for d in w:
            d.lw = ev
            d.rd = []
        for d in r:
            d.rd.append(ev)
            if len(d.rd) > 48:
                best = {}
                for x in d.rd:
                    if x[0] not in best or best[x[0]][1] < x[1]:
                        best[x[0]] = x
                d.rd = list(best.values())

    def op(self, e, fn, r=(), w=()):
        self._deps(e, r, w)
        ins = fn(self.eng[e])
        self.ecnt[e] += 1
        ins.then_inc(self.sems["e_" + e], 1)
        self._commit(("e_" + e, self.ecnt[e], e), r, w)
        self.nops += 1
        return ins

    def pe(self, fn, r=(), w=()):
        return self.op("pe", fn, r, w)

    def act(self, fn, r=(), w=()):
        return self.op("act", fn, r, w)

    def dve(self, fn, r=(), w=()):
        return self.op("dve", fn, r, w)

    def pool(self, fn, r=(), w=()):
        return self.op("pool", fn, r, w)

    def dma(self, q, out, in_, ch, r=(), w=(), **kw):
        self._deps(q, r, w)
        ins = self.eng[q].dma_start(out=out, in_=in_, **kw)
        ch.cnt += 16
        ch.q = q
        ins.then_inc(ch.sem, 16)
        self._commit((ch.key, ch.cnt, "dma"), r, w)
        self.nops += 1
        return ins

    def barrier(self):
        for e in self.eng:
            if self.ecnt[e] > 0:
                self._wait(e, ("e_" + e, self.ecnt[e], "self"))
        for c in self.chans:
            if c.cnt > 0:
                self._wait(c.q or "sp", (c.key, c.cnt, "dma"))
        for e in self.eng:
            self.eng[e].sem_inc(self.bar_sem, 1)
        self.bar_cnt += len(self.eng)
        for e in self.eng:
            self.eng[e].wait_ge(self.bar_sem, self.bar_cnt)
            for e2 in self.eng:
                self.waited[e]["e_" + e2] = self.ecnt[e2]
            for c in self.chans:
                self.waited[e][c.key] = c.cnt


class MK:
    def __init__(self, dbg=None, layers=(0, 1), inject_cat=False, mixers=("gdn", "sc", "hy", "na"), do_mlp=True,
                 phases=("mod", "p1", "mix", "p3", "p4"), inject_h=False):
        self.phases = phases
        self.inject_h = inject_h
        self.dbg = dbg or {}
        self.layers = layers
        self.inject_cat = inject_cat
        self.mixers = mixers
        self.do_mlp = do_mlp
        self.inputs = {}
        self.outputs = {}

    def din(self, name, shape, dtype=F32):
        t = self.nc.dram_tensor(name, list(shape), dtype, kind="ExternalInput").ap()
        self.inputs[name] = (tuple(shape), dtype)
        return t

    def dout(self, name, shape, dtype=F32):
        t = self.nc.dram_tensor(name, list(shape), dtype, kind="ExternalOutput").ap()
        self.outputs[name] = (tuple(shape), dtype)
        return t

    def W(self, name):
        if name not in self._w:
            self._w[name] = self.din(name, self._wshape[name])
        return self._w[name]

    def sb(self, st, name, shape, dtype=F32):
        self._n = getattr(self, "_n", 0) + 1
        return st.enter_context(self.nc.sbuf_tensor("%s_%d" % (name, self._n), list(shape), dtype))

    def build(self):
        nc = bass.Bass("TRN2", target_bir_lowering=False)
        self.nc = nc
        with ExitStack() as st:
            self.st = st
            self.k = K(nc, st)
            self._declare()
            self._consts()
            for l in self.layers:
                self._layer(l)
            self._finish()
        return nc

    def _declare(self):
        nc = self.nc
        self._wshape = {}
        self._w = {}
        self.x_in = self.din("x", [L, D])
        self.ctx_in = self.din("ctx", [LC, D])
        self.cvec_in = self.din("cvec", [128, 16])
        self._wshape["ada_w"] = [DEPTH, D, 6 * D]
        self.ada_bT = self.din("ada_bT", [128, DEPTH * 48])
        self.gains_in = self.din("gains", [128, 4 * DEPTH * 8])
        self._wshape["w_in"] = [DEPTH, D, IN_COLS]
        self._wshape["w_out"] = [DEPTH, D, D]
        self._wshape["mlp_w1"] = [DEPTH, D, 4 * D]
        self._wshape["mlp_w2"] = [DEPTH, 4 * D, D]
        self.ident_f_in = self.din("ident_f", [128, 128])
        self.ident_b_in = self.din("ident_b", [128, 128], BF16)
        self.out = self.dout("out", [L, D])
        self.xres = nc.dram_tensor("xres", [T, D], F32).ap()
        self.d_xres = [Dep() for _ in range(NT)]
        self.d_out = [Dep() for _ in range(NT)]
        if self.inject_cat:
            self.cat_in = self.din("cat_in", [D, T], BF16)
        self.ps = [self.st.enter_context(nc.psum_tensor("ps%d" % i, [128, 1024], F32)) for i in range(4)]
        self.d_ps = [[Dep(), Dep()] for _ in range(4)]

    def _consts(self):
        k, st = self.k, self.st
        sb = lambda n, s, d=F32: self.sb(st, n, s, d)
        self.ident_f = sb("ident_f", [128, 128])
        self.ident_b = sb("ident_b", [128, 128], BF16)
        self.ones_f = sb("ones_f", [128, 128])
        self.cvec = sb("cvec", [128, 16])
        self.adab = sb("adab", [128, DEPTH * 48])
        self.gains = sb("gains", [128, 4 * DEPTH * 8])
        self.d_const = Dep()
        ch = k.chan()
        k.dma("sp", self.ident_f[:], self.ident_f_in[:, :], ch, w=[self.d_const])
        k.dma("sp", self.ident_b[:], self.ident_b_in[:, :], ch, w=[self.d_const])
        k.dma("sp", self.cvec[:], self.cvec_in[:, :], ch, w=[self.d_const])
        k.dma("sp", self.adab[:], self.ada_bT[:, :], ch, w=[self.d_const])
        k.dma("sp", self.gains[:], self.gains_in[:, :], ch, w=[self.d_const])
        k.dve(lambda e: e.memset(self.ones_f[:], 1.0), w=[self.d_const])
        self.H = sb("H", [128, 8, T], BF16)
        self.C = sb("C", [128, 8, T], BF16)
        self.d_H = [Dep() for _ in range(NT)]
        self.d_C = [Dep() for _ in range(NT)]
        self.modT = sb("modT", [128, 48, 2])
        self.A1 = sb("A1", [128, 8, 2])
        self.A2 = sb("A2", [128, 8, 2])
        self.G1f = sb("G1f", [128, 8, 2])
        self.G2f = sb("G2f", [128, 8, 2])
        self.Gbc = sb("Gbc", [128, 2, 2, D])
        self.d_mod = Dep()
        self.d_gbc = Dep()
        self.NAR = 28416
        self.AR = sb("arena", [128, self.NAR])
        self.ar_off = 0
        self.NXT = 3
        self.d_xt = [Dep() for _ in range(self.NXT)]
        self.ch_xt_ld = [k.chan() for _ in range(self.NXT)]
        self.ch_xt_st = [k.chan() for _ in range(self.NXT)]
        self.d_xn = [Dep(), Dep()]
        self.d_junk = Dep()
        self.d_stat = [Dep() for _ in range(4)]
        self.d_tmpb = [Dep(), Dep()]
        self.d_tmpf = [Dep(), Dep()]
        self.xt_i = self.xn_i = self.stat_i = self.tmp_i = 0

    def arena_reset(self):
        self.k.barrier()
        self.ar_off = 0

    def aa(self, shape, dtype=F32):
        esz = 4 if dtype in (F32, I32) else 2
        n = int(np.prod(shape[1:]))
        nbytes = (n * esz + 31) // 32 * 32
        o = self.ar_off
        assert o + nbytes <= self.NAR * 4, "arena overflow %d" % (o + nbytes)
        self.ar_off = o + nbytes
        v = self.AR[:, o // 4:(o + nbytes) // 4]
        if dtype != F32:
            v = v.bitcast(dtype)
        v = v[:, 0:n]
        if len(shape) == 3:
            v = v.rearrange("p (a b) -> p a b", b=shape[2])
        elif len(shape) == 4:
            v = v.rearrange("p (a b c) -> p a b c", b=shape[2], c=shape[3])
        if shape[0] != 128:
            v = v[0:shape[0]]
        return v

    def _staging(self, norm=True):
        self.xt = [self.aa([128, D]) for i in range(self.NXT)]
        self.junk = self.aa([128, D], BF16)
        self.stat = [self.aa([128, 8]) for i in range(4)]
        self.tmpf = [self.aa([128, D]) for i in range(2)]
        if norm:
            self.xn = [self.aa([128, D], BF16) for i in range(2)]
            self.tmpb = [self.aa([128, D], BF16) for i in range(2)]

    def gain(self, kind, l):
        o = (kind * DEPTH + l) * 8
        return self.gains[:, o:o + 8]

    def _layer(self, l):
        ph = self.phases
        if "mod" in ph:
            self._modulation(l)
        if "p1" in ph:
            self.arena_reset()
            self._staging()
            for tt in range(NT):
                s = 0 if tt < 16 else 1
                xi = self._load_x(l, tt, first=True)
                self._norm_tile(xi, s, self.A1, self.modT[:, 0:8, :], self.H, tt, self.d_H[tt])
        elif self.inject_h:
            ch = self.k.chan()
            hin = self.din("h_in", [D, T], BF16)
            for tt in range(NT):
                self.k.dma("sp", self.H[:, :, tt * 128:(tt + 1) * 128],
                           hin[:, tt * 128:(tt + 1) * 128].rearrange("(a p) t -> p a t", p=128), ch, w=[self.d_H[tt]])
        if "hx" in self.dbg and self.dbg["hx"] == l:
            self._dump_feat("dbg_hx", self.H, self.d_H)
        if "mix" in ph:
            self._mixers(l)
        if "cat" in self.dbg and self.dbg["cat"] == l:
            self._dump_feat("dbg_cat", self.C, self.d_C)
        if "p3" in ph:
            self._p3(l)
        if "hx2" in self.dbg and self.dbg["hx2"] == l:
            self._dump_feat("dbg_hx2", self.C, self.d_C)
        if "p4" in ph:
            self._p4(l)

    def _xsrc(self, l, tt, first):
        if l == self.layers[0] and l == 0 and first:
            if tt < 16:
                return self.x_in[tt * 128:(tt + 1) * 128, :], None
            return self.ctx_in[(tt - 16) * 128:(tt - 15) * 128, :], None
        return self.xres[tt * 128:(tt + 1) * 128, :], self.d_xres[tt]

    def _load_x(self, l, tt, first):
        k = self.k
        i = self.xt_i
        self.xt_i = (i + 1) % self.NXT
        src, dep = self._xsrc(l, tt, first)
        k.dma("sp", self.xt[i][:], src, self.ch_xt_ld[i], r=[dep] if dep else [], w=[self.d_xt[i]])
        return i

    def _store_x(self, i, dst_ap, dst_dep):
        self.k.dma("sp", dst_ap, self.xt[i][:], self.ch_xt_st[i], r=[self.d_xt[i]], w=[dst_dep])

    def _rstd(self, src_ap, src_deps):
        k = self.k
        j = self.stat_i
        self.stat_i = (j + 1) % 4
        stt, dst = self.stat[j], self.d_stat[j]
        k.act(lambda e: e.activation(out=self.junk[:], in_=src_ap, func=AF.Square, accum_out=stt[:, 0:1]),
              r=src_deps, w=[self.d_junk, dst])
        k.act(lambda e: e.activation(out=stt[:, 1:2], in_=stt[:, 0:1], func=AF.Sqrt, bias=EPS, scale=1.0 / D),
              r=[dst], w=[dst])
        k.dve(lambda e: e.reciprocal(out=stt[:, 2:3], in_=stt[:, 1:2]), r=[dst], w=[dst])
        return stt[:, 2:3], dst

    def _norm_tile(self, xi, s, A, B, Hbuf, tt, dH):
        k = self.k
        xt, dxt = self.xt[xi], self.d_xt[xi]
        rs, drs = self._rstd(xt[:], [dxt])
        j = self.xn_i
        self.xn_i = 1 - j
        xn, dxn = self.xn[j], self.d_xn[j]
        k.dve(lambda e: e.tensor_scalar(out=xn[:], in0=xt[:], scalar1=rs, scalar2=None, op0=ALU.mult),
              r=[dxt, drs], w=[dxn])
        pi = 3
        psb = self.ps[pi][:, 0:512].bitcast(BF16)
        dps = self.d_ps[pi][0]
        for dt in range(8):
            k.pe(lambda e, dt=dt: e.transpose(out=psb[:, dt * 128:(dt + 1) * 128], in_=xn[:, dt * 128:(dt + 1) * 128],
                                              identity=self.ident_b[:]),
                 r=[dxn, self.d_const], w=[dps])
        ti = self.tmp_i
        self.tmp_i = 1 - ti
        tb, dtb = self.tmpb[ti], self.d_tmpb[ti]
        k.dve(lambda e: e.tensor_tensor(out=tb[:].rearrange("p (a b) -> p a b", a=8),
                                        in0=psb.rearrange("p (a b) -> p a b", a=8),
                                        in1=A[:, :, s:s + 1].to_broadcast([128, 8, 128]), op=ALU.mult),
              r=[dps, self.d_mod], w=[dtb])
        k.pool(lambda e: e.tensor_tensor(out=Hbuf[:, :, tt * 128:(tt + 1) * 128],
                                         in0=tb[:].rearrange("p (a b) -> p a b", a=8),
                                         in1=B[:, :, s:s + 1].to_broadcast([128, 8, 128]), op=ALU.add),
               r=[dtb, self.d_mod], w=[dH])

    def _modulation(self, l):
        k = self.k
        self.arena_reset()
        if True:
            sT = self.aa([128, 16], BF16)
            d_sT = Dep()
            k.act(lambda e: e.activation(out=sT[:], in_=self.cvec[:], func=AF.Silu), r=[self.d_const], w=[d_sT])
            wb = [self.aa([128, 8, 512], BF16) for i in range(2)]
            dwb = [Dep(), Dep()]
            chw = [k.chan(), k.chan()]
            mod_ps = self.ps[0][:, 0:96]
            dps = self.d_ps[0][0]
            for g in range(12):
                i = g % 2
                src = self.W("ada_w")[l, :, g * 512:(g + 1) * 512].rearrange("(kt p) c -> p kt c", p=128)
                k.dma("pool", wb[i][:], src, chw[i], w=[dwb[i]])
                for jj in range(4):
                    jt = g * 4 + jj
                    for kt in range(8):
                        k.pe(lambda e, i=i, jj=jj, jt=jt, kt=kt: e.matmul(
                            mod_ps[:, jt * 2:jt * 2 + 2], wb[i][:, kt, jj * 128:(jj + 1) * 128],
                            sT[:, kt * 2:kt * 2 + 2], start=(kt == 0), stop=(kt == 7)),
                            r=[dwb[i], d_sT], w=[dps])
            dm = self.d_mod
            k.dve(lambda e: e.tensor_tensor(out=self.modT[:], in0=mod_ps.rearrange("p (a b) -> p a b", b=2),
                                            in1=self.adab[:, l * 48:(l + 1) * 48].unsqueeze(2).to_broadcast([128, 48, 2]),
                                            op=ALU.add), r=[dps, self.d_const], w=[dm])
            g = lambda kind: self.gain(kind, l).unsqueeze(2).to_broadcast([128, 8, 2])
            k.dve(lambda e: e.scalar_tensor_tensor(out=self.A1[:], in0=self.modT[:, 8:16, :], scalar=1.0, in1=g(0),
                                                   op0=ALU.add, op1=ALU.mult), r=[dm, self.d_const], w=[dm])
            k.dve(lambda e: e.scalar_tensor_tensor(out=self.A2[:], in0=self.modT[:, 32:40, :], scalar=1.0, in1=g(2),
                                                   op0=ALU.add, op1=ALU.mult), r=[dm, self.d_const], w=[dm])
            k.dve(lambda e: e.tensor_tensor(out=self.G1f[:], in0=self.modT[:, 16:24, :], in1=g(1), op=ALU.mult),
                  r=[dm, self.d_const], w=[dm])
            k.dve(lambda e: e.tensor_tensor(out=self.G2f[:], in0=self.modT[:, 40:48, :], in1=g(3), op=ALU.mult),
                  r=[dm, self.d_const], w=[dm])
            diag = [self.aa([128, 128]) for i in range(2)]
            ddiag = [Dep(), Dep()]
            n = 0
            for kind, Gf in enumerate((self.G1f, self.G2f)):
                for s in range(2):
                    for dt in range(8):
                        i = n % 2
                        n += 1
                        k.dve(lambda e, i=i, Gf=Gf, dt=dt, s=s: e.tensor_scalar(
                            out=diag[i][:], in0=self.ident_f[:], scalar1=Gf[:, dt, s:s + 1], scalar2=None, op0=ALU.mult),
                            r=[dm, self.d_const], w=[ddiag[i]])
                        pi = 1 + (n % 2)
                        pst = self.ps[pi][:, 0:128]
                        k.pe(lambda e, i=i, pst=pst: e.matmul(pst, self.ones_f[:], diag[i][:], start=True, stop=True),
                             r=[ddiag[i], self.d_const], w=[self.d_ps[pi][0]])
                        k.act(lambda e, pst=pst, kind=kind, s=s, dt=dt: e.activation(
                            out=self.Gbc[:, kind, s, dt * 128:(dt + 1) * 128], in_=pst, func=AF.Copy),
                            r=[self.d_ps[pi][0]], w=[self.d_gbc])
        if "mod" in self.dbg and self.dbg["mod"] == l:
            o = self.dout("dbg_mod", [128, 96])
            ch = k.chan()
            k.dma("sp", o[:, :], self.modT[:].rearrange("p a b -> p (a b)"), ch, r=[self.d_mod])
            o2 = self.dout("dbg_gbc", [128, 4 * D])
            k.dma("sp", o2[:, :], self.Gbc[:].rearrange("p a b c -> p (a b c)"), ch, r=[self.d_gbc])

    def _mixers(self, l):
        k = self.k
        if self.inject_cat:
            ch = k.chan()
            for tt in range(NT):
                k.dma("sp", self.C[:, :, tt * 128:(tt + 1) * 128],
                      self.cat_in[:, tt * 128:(tt + 1) * 128].rearrange("(a p) t -> p a t", p=128), ch, w=[self.d_C[tt]])
            return
        raise NotImplementedError

    def _p3(self, l):
        k = self.k
        last = (l == DEPTH - 1)
        ntt = 16 if last else NT
        self.arena_reset()
        self._staging()
        if True:
            wo = self.aa([128, 8, D], BF16)
            dwo = Dep()
            ch = k.chan()
            k.dma("pool", wo[:], self.W("w_out")[l].rearrange("(kt p) c -> p kt c", p=128), ch, w=[dwo])
            for tt in range(ntt):
                s = 0 if tt < 16 else 1
                pi = tt % 3
                yps = self.ps[pi]
                for half in range(2):
                    for mt in range(8):
                        k.pe(lambda e, half=half, mt=mt, yps=yps, tt=tt: e.matmul(
                            yps[:, half * 512:(half + 1) * 512], self.C[:, mt, tt * 128:(tt + 1) * 128],
                            wo[:, mt, half * 512:(half + 1) * 512], start=(mt == 0), stop=(mt == 7)),
                            r=[self.d_C[tt], dwo], w=[self.d_ps[pi][half]])
                xi = self._load_x(l, tt, first=True)
                self._resid_update(xi, yps[:], self.d_ps[pi], 0, s)
                self._store_x(xi, self.xres[tt * 128:(tt + 1) * 128, :], self.d_xres[tt])
                self._norm_tile(xi, s, self.A2, self.modT[:, 24:32, :], self.C, tt, self.d_C[tt])

    def _resid_update(self, xi, y_ap, y_deps, kind, s):
        k = self.k
        rs, drs = self._rstd(y_ap, list(y_deps))
        ti = self.tmp_i
        self.tmp_i = 1 - ti
        tf, dtf = self.tmpf[ti], self.d_tmpf[ti]
        k.dve(lambda e: e.scalar_tensor_tensor(out=tf[:], in0=y_ap, scalar=rs, in1=self.Gbc[:, kind, s, :],
                                               op0=ALU.mult, op1=ALU.mult),
              r=list(y_deps) + [drs, self.d_gbc], w=[dtf])
        xt, dxt = self.xt[xi], self.d_xt[xi]
        k.pool(lambda e: e.tensor_tensor(out=xt[:], in0=xt[:], in1=tf[:], op=ALU.add), r=[dtf, dxt], w=[dxt])

    def _p4(self, l):
        k = self.k
        last = (l == DEPTH - 1)
        if last:
            sblocks = [(0, 768), (768, 768), (1536, 512)]
        else:
            sblocks = [(0, 768), (768, 768), (1536, 768)]
        self.arena_reset()
        self._staging(norm=False)
        if True:
            hT = self.aa([128, 32, 768], BF16)
            d_hT = [[Dep() for _ in range(2)] for _ in range(32)]
            HF = self.H[:].rearrange("p a t -> p (a t)")
            w1c = [HF[:, i * 4096:(i + 1) * 4096].rearrange("p (k c) -> p k c", c=512) for i in range(3)]
            d_w1c = [Dep(), Dep(), Dep()]
            ch_w1 = [k.chan(), k.chan(), k.chan()]
            w2c = [HF[:, 12288 + i * 2048: 12288 + (i + 1) * 2048].rearrange("p (k c) -> p k c", c=512) for i in range(3)]
            d_w2c = [Dep() for _ in range(3)]
            ch_w2 = [k.chan() for _ in range(3)]
            rl = [self.aa([128, 384], BF16) for i in range(2)]
            d_rl = [Dep(), Dep()]
            ytok = self.aa([128, 6, D])
            d_ytok = [Dep() for _ in range(6)]
            n_w1 = 0
            n_w2 = 0
            n_rl = 0
            for (t0, n) in sblocks:
                n2 = n // 2
                ntl = n // 128
                for ffc in range(8):
                    i = n_w1 % 3
                    n_w1 += 1
                    k.dma("pool", w1c[i], self.W("mlp_w1")[l, :, ffc * 512:(ffc + 1) * 512].rearrange("(kt p) c -> p kt c", p=128),
                          ch_w1[i], w=[d_w1c[i]])
                    for f in range(4):
                        fft = ffc * 4 + f
                        for sbk in range(2):
                            hps = self.ps[3][:, sbk * 512: sbk * 512 + n2]
                            dhps = self.d_ps[3][sbk]
                            tts = range((t0 + sbk * n2) // 128, (t0 + (sbk + 1) * n2 + 127) // 128)
                            rdeps = [self.d_C[t] for t in tts]
                            for dt in range(8):
                                k.pe(lambda e, i=i, f=f, dt=dt, hps=hps, sbk=sbk: e.matmul(
                                    hps, w1c[i][:, dt, f * 128:(f + 1) * 128],
                                    self.C[:, dt, t0 + sbk * n2: t0 + (sbk + 1) * n2], start=(dt == 0), stop=(dt == 7)),
                                    r=[d_w1c[i]] + rdeps, w=[dhps])
                            j = n_rl % 2
                            n_rl += 1
                            k.act(lambda e, j=j, hps=hps: e.activation(out=rl[j][:, 0:n2], in_=hps, func=AF.Relu),
                                  r=[dhps], w=[d_rl[j]])
                            k.dve(lambda e, j=j, fft=fft, sbk=sbk: e.tensor_tensor(
                                out=hT[:, fft, sbk * n2:(sbk + 1) * n2], in0=rl[j][:, 0:n2], in1=rl[j][:, 0:n2], op=ALU.mult),
                                r=[d_rl[j]], w=[d_hT[fft][sbk]])
                for dh in range(2):
                    for ffc in range(8):
                        i = n_w2 % 3
                        n_w2 += 1
                        k.dma("pool", w2c[i],
                              self.W("mlp_w2")[l, ffc * 512:(ffc + 1) * 512, dh * 512:(dh + 1) * 512].rearrange("(f p) c -> p f c", p=128),
                              ch_w2[i], w=[d_w2c[i]])
                        for f in range(4):
                            fft = ffc * 4 + f
                            for tl in range(ntl):
                                pi, hb = tl // 2, tl % 2
                                sbk = (tl * 128) // n2
                                k.pe(lambda e, i=i, f=f, fft=fft, tl=tl, pi=pi, hb=hb: e.matmul(
                                    self.ps[pi][:, hb * 512:(hb + 1) * 512], hT[:, fft, tl * 128:(tl + 1) * 128],
                                    w2c[i][:, f, :], start=(fft == 0), stop=(fft == 31)),
                                    r=[d_w2c[i], d_hT[fft][sbk]], w=[self.d_ps[pi][hb]])
                    for tl in range(ntl):
                        pi, hb = tl // 2, tl % 2
                        k.act(lambda e, tl=tl, pi=pi, hb=hb, dh=dh: e.activation(
                            out=ytok[:, tl, dh * 512:(dh + 1) * 512], in_=self.ps[pi][:, hb * 512:(hb + 1) * 512], func=AF.Copy),
                            r=[self.d_ps[pi][hb]], w=[d_ytok[tl]])
                for tl in range(ntl):
                    tt = t0 // 128 + tl
                    s = 0 if tt < 16 else 1
                    xi = self._load_x(l, tt, first=False)
                    self._resid_update(xi, ytok[:, tl, :], [d_ytok[tl]], 1, s)
                    if last:
                        self._store_x(xi, self.out[tt * 128:(tt + 1) * 128, :], self.d_out[tt])
                    else:
                        self._store_x(xi, self.xres[tt * 128:(tt + 1) * 128, :], self.d_xres[tt])

    def _dump_feat(self, name, buf, deps):
        k = self.k
        o = self.dout(name, [D, T])
        self.arena_reset()
        if True:
            stg = self.aa([128, 8, 128])
            dst = Dep()
            ch = k.chan()
            for tt in range(NT):
                k.dve(lambda e, tt=tt: e.tensor_copy(out=stg[:], in_=buf[:, :, tt * 128:(tt + 1) * 128]), r=[deps[tt]], w=[dst])
                k.dma("sp", o[:, tt * 128:(tt + 1) * 128].rearrange("(a p) t -> p a t", p=128), stg[:], ch, r=[dst])

    def _finish(self):
        k = self.k
        if "xres" in self.dbg:
            self.arena_reset()
            self._staging()
            o = self.dout("dbg_xres", [T, D])
            ch = k.chan()
            for tt in range(NT):
                xi = self._load_x(1, tt, first=False)
                self._store_x(xi, o[tt * 128:(tt + 1) * 128, :], Dep())
        k.barrier()


def host_inputs(mk, inputs, extra=None):
    bf = ml_dtypes.bfloat16
    f32 = np.float32
    shared = {}
    shared["ada_w"] = np.ascontiguousarray(inputs["ada_w"], dtype=f32)
    shared["ada_bT"] = np.ascontiguousarray(
        inputs["ada_b"].reshape(DEPTH, 48, 128).transpose(2, 0, 1).reshape(128, DEPTH * 48), dtype=f32)
    g = np.stack([inputs["norm_pre_mix"], inputs["norm_post_mix"], inputs["norm_pre_mlp"], inputs["norm_post_mlp"]])
    shared["gains"] = np.ascontiguousarray(g.reshape(4, DEPTH, 8, 128).transpose(3, 0, 1, 2).reshape(128, -1), dtype=f32)
    for n in ("w_in", "w_out", "mlp_w1", "mlp_w2"):
        shared[n] = np.ascontiguousarray(inputs[n], dtype=f32)
    shared["ident_f"] = np.eye(128, dtype=f32)
    shared["ident_b"] = np.eye(128, dtype=f32).astype(bf)
    maps = []
    for b in range(inputs["x"].shape[0]):
        m = dict(shared)
        m["x"] = np.ascontiguousarray(inputs["x"][b], dtype=f32)
        m["ctx"] = np.ascontiguousarray(inputs["ctx"][b], dtype=f32)
        cv = np.stack([inputs["c"][b].reshape(8, 128), inputs["c_ctx"].reshape(8, 128)], axis=-1)
        m["cvec"] = np.ascontiguousarray(cv.transpose(1, 0, 2).reshape(128, 16), dtype=f32)
        if extra:
            m.update(extra(b))
        maps.append({kk: v for kk, v in m.items() if kk in mk.inputs})
    return maps


PP_ENTRIES = [("scw", 6), ("hyw", 18), ("gdw", 18), ("hybias", 2), ("gnorm", 1), ("hy_w1", 64), ("hy_w2", 64),
              ("hy_w3", 64), ("hy_w4", 512), ("hy_b", 3), ("hy_f", 3), ("alog", 8), ("dtb", 8)]
PP_OFF = {}
_o = 0
for _n, _w in PP_ENTRIES:
    PP_OFF[_n] = (_o, _w)
    _o += _w
PP_W = _o


def pp_host(inputs):
    pp = np.zeros((128, DEPTH * PP_W), np.float32)
    for l in range(DEPTH):
        def put(name, arr):
            o, w = PP_OFF[name]
            arr = np.asarray(arr, np.float32)
            assert arr.shape[1] == w, (name, arr.shape)
            pp[:arr.shape[0], l * PP_W + o: l * PP_W + o + w] = arr
        put("scw", inputs["sc_conv"][l].reshape(3, 2, 128).transpose(2, 1, 0).reshape(128, 6))
        put("hyw", inputs["hy_conv"][l].reshape(3, 6, 128).transpose(2, 1, 0).reshape(128, 18))
        put("gdw", inputs["gdn_conv"][l].reshape(3, 6, 128).transpose(2, 1, 0).reshape(128, 18))
        put("hybias", inputs["hy_bias"][l].reshape(2, 128).T)
        put("gnorm", np.tile(inputs["gdn_norm"][l], 2).reshape(128, 1))
        put("hy_w1", inputs["hy_w1"][l])
        put("hy_w2", inputs["hy_w2"][l])
        put("hy_w3", inputs["hy_w3"][l])
        put("hy_w4", inputs["hy_w4"][l])
        put("hy_b", np.stack([inputs["hy_b1"][l], inputs["hy_b2"][l], inputs["hy_b3"][l]], axis=1))
        put("hy_f", inputs["hy_freq"][l].T)
        put("alog", np.tile(inputs["gdn_a_log"][l].reshape(1, 8), (128, 1)))
        put("dtb", np.tile(inputs["gdn_dt_bias"][l].reshape(1, 8), (128, 1)))
    return pp


def na_consts(inputs):
    rpb = np.asarray(inputs["na_rpb"], np.float32)
    par = np.arange(2)[:, None, None, None]
    kc = np.arange(64)[None, :, None, None]
    i = np.arange(14)[None, None, :, None]
    qc = np.arange(64)[None, None, None, :]
    dc = np.clip(kc - qc, -15, 15) + 15
    di = np.broadcast_to(i + par, (2, 64, 14, 64))
    dcb = np.broadcast_to(dc, (2, 64, 14, 64))
    g = rpb[:, :, di, dcb]
    g = g.transpose(0, 2, 3, 1, 4, 5).reshape(DEPTH, 128, 4 * 14 * 64)
    cs = np.clip(np.arange(64) - 8, 0, 48)
    kcv = np.arange(64)[:, None]
    valid = (kcv >= cs[None, :]) & (kcv < cs[None, :] + 16)
    m = np.where(valid, 0.0, -1e30).astype(np.float32)
    mask = np.concatenate([m, m], axis=0)
    return np.ascontiguousarray(g), np.ascontiguousarray(mask)


def _mix_common_init(self):
    if getattr(self, "pp", None) is not None:
        return
    k = self.k
    self.pp_in = self.din("pp", [128, DEPTH * PP_W])
    self.pp = self.sb(self.st, "pp", [128, DEPTH * PP_W])
    self.d_pp = Dep()
    ch = k.chan()
    k.dma("sp", self.pp[:], self.pp_in[:, :], ch, w=[self.d_pp])


def _ppv(self, l, name, rows=128):
    o, w = PP_OFF[name]
    return self.pp[0:rows, l * PP_W + o: l * PP_W + o + w]


def _blocks(self, with_ctx):
    b = [(i * 512, 512) for i in range(4)]
    if with_ctx:
        b.append((L, LC))
    return b


def _wring_init(self, n=2):
    self.wring = [(self.aa([128, 8, 512], BF16), Dep(), self.wring_ch[i]) for i in range(n)]
    self.wring_i = 0
    self.bank_i = 0


def _proj(self, l, chunks, blocks, evac, banks):
    k = self.k
    for (c0, ncol) in chunks:
        i = self.wring_i
        self.wring_i = (i + 1) % len(self.wring)
        wap, dw, chw = self.wring[i]
        k.dma("pool", wap[:, :, 0:ncol], self.W("w_in")[l, :, c0:c0 + ncol].rearrange("(kt p) c -> p kt c", p=128),
              chw, w=[dw])
        for cc in range(0, ncol, 128):
            m = min(128, ncol - cc)
            for (t0, n) in blocks:
                pi, hb = banks[self.bank_i % len(banks)]
                self.bank_i += 1
                pst = self.ps[pi][0:m, hb * 512: hb * 512 + n]
                dps = self.d_ps[pi][hb]
                hd = [self.d_H[t] for t in range(t0 // 128, (t0 + n + 127) // 128)]
                for dt in range(8):
                    k.pe(lambda e, dt=dt, pst=pst, wap=wap, cc=cc, m=m, t0=t0, n=n: e.matmul(
                        pst, wap[:, dt, cc:cc + m], self.H[:, dt, t0:t0 + n], start=(dt == 0), stop=(dt == 7)),
                        r=[dw] + hd, w=[dps])
                evac(c0 + cc, m, t0, n, pst, dps)


def _dwconv(self, eng, out_ap, in_ap, w3, ranges, r, w):
    k = self.k
    for (a, b) in ranges:
        k.op(eng, lambda e, a=a, b=b: e.tensor_scalar(out=out_ap[:, a:b], in0=in_ap[:, a:b], scalar1=w3[:, 1:2],
                                                      scalar2=None, op0=ALU.mult), r=r, w=w)
        k.op(eng, lambda e, a=a, b=b: e.scalar_tensor_tensor(out=out_ap[:, a + 1:b], in0=in_ap[:, a:b - 1], scalar=w3[:, 0:1],
                                                             in1=out_ap[:, a + 1:b], op0=ALU.mult, op1=ALU.add),
             r=list(r) + list(w), w=w)
        k.op(eng, lambda e, a=a, b=b: e.scalar_tensor_tensor(out=out_ap[:, a:b - 1], in0=in_ap[:, a + 1:b], scalar=w3[:, 2:3],
                                                             in1=out_ap[:, a:b - 1], op0=ALU.mult, op1=ALU.add),
             r=list(r) + list(w), w=w)


PROJ_BANKS = [(0, 0), (0, 1), (1, 0), (1, 1)]


def _mix_sc(self, l, with_ctx):
    k = self.k
    self.arena_reset()
    _wring_init(self)
    ranges = [(0, L)] + ([(L, T)] if with_ctx else [])
    blocks = _blocks(self, with_ctx)
    ntok = T if with_ctx else L
    ntt = ntok // 128
    pxb = self.aa([128, 6, T], BF16)
    d_px = [Dep() for _ in range(6)]

    def evac(c, m, t0, n, pst, dps):
        ct = (c - OFF_SC) // 128
        k.act(lambda e: e.activation(out=pxb[:, ct, t0:t0 + n], in_=pst, func=AF.Copy), r=[dps], w=[d_px[ct]])
    _proj(self, l, [(OFF_SC, 512), (OFF_SC + 512, 256)], blocks, evac, PROJ_BANKS)
    z = [self.aa([128, T]) for _ in range(2)]
    acc = [self.aa([128, T]) for _ in range(2)]
    scw = _ppv(self, l, "scw")
    for j in range(2):
        dz, dacc = Dep(), Dep()
        eng = "dve" if j == 0 else "pool"
        k.op(eng, lambda e, j=j: e.tensor_tensor(out=z[j][:, 0:ntok], in0=pxb[:, 2 + j, 0:ntok], in1=pxb[:, 4 + j, 0:ntok],
                                                 op=ALU.mult), r=[d_px[2 + j], d_px[4 + j]], w=[dz])
        _dwconv(self, "dve", acc[j], z[j], scw[:, j * 3:(j + 1) * 3], ranges, [dz, self.d_pp], [dacc])
        k.op(eng, lambda e, j=j: e.tensor_tensor(out=self.C[:, 2 + j, 0:ntok], in0=pxb[:, j, 0:ntok], in1=acc[j][:, 0:ntok],
                                                 op=ALU.mult), r=[d_px[j], dacc], w=[self.d_C[t] for t in range(ntt)])


def _mixers(self, l):
    k = self.k
    if self.inject_cat:
        ch = k.chan()
        for tt in range(NT):
            k.dma("sp", self.C[:, :, tt * 128:(tt + 1) * 128],
                  self.cat_in[:, tt * 128:(tt + 1) * 128].rearrange("(a p) t -> p a t", p=128), ch, w=[self.d_C[tt]])
        return
    _mix_common_init(self)
    if not hasattr(self, "wring_ch"):
        self.wring_ch = [k.chan() for _ in range(3)]
    with_ctx = l < DEPTH - 1
    if "sc" in self.mixers:
        _mix_sc(self, l, with_ctx)
    if "na" in self.mixers:
        _mix_na(self, l, with_ctx)
    if "hy" in self.mixers:
        _mix_hy(self, l, with_ctx)
    if "gdn" in self.mixers:
        _mix_gdn(self, l, with_ctx)


MK._mixers = _mixers


def const_inputs(inputs):
    out = {}
    g, mask = na_consts(inputs)
    out["na_rpbg"] = g
    out["na_mask"] = mask
    out.update(hy_consts())
    out.update(gdn_consts())
    return out


def _mix_na(self, l, with_ctx):
    k = self.k
    self.arena_reset()
    _wring_init(self)
    blocks = _blocks(self, True)
    qT = self.aa([128, 2, T], BF16)
    kT = self.aa([128, 2, T], BF16)
    d_q = [Dep() for _ in range(NT)]
    d_k = [Dep() for _ in range(NT)]

    def evac(c, m, t0, n, pst, dps):
        ct = (c - OFF_NA) // 128
        tts = range(t0 // 128, (t0 + n) // 128)
        if ct < 2:
            k.act(lambda e: e.activation(out=qT[:, ct, t0:t0 + n], in_=pst, func=AF.Copy, scale=0.125),
                  r=[dps], w=[d_q[t] for t in tts])
        else:
            k.dve(lambda e: e.tensor_copy(out=kT[:, ct - 2, t0:t0 + n], in_=pst), r=[dps], w=[d_k[t] for t in tts])
    _proj(self, l, [(OFF_NA, 512)], blocks, evac, PROJ_BANKS)
    Ve = self.aa([128, NT, 4, 65], BF16)
    Vo = self.aa([128, 15, 4, 65], BF16)
    d_Ve, d_Vo = Dep(), Dep()
    k.pool(lambda e: e.memset(Ve, 1.0), w=[d_Ve])
    k.pool(lambda e: e.memset(Vo, 1.0), w=[d_Vo])
    i = self.wring_i
    self.wring_i = (i + 1) % len(self.wring)
    wv, dwv, chv = self.wring[i]
    k.dma("pool", wv[:, :, 0:256], self.W("w_in")[l, :, OFF_NA + 512:OFF_NA + 768].rearrange("(kt p) c -> p kt c", p=128),
          chv, w=[dwv])
    nb = 0
    for (Vx, dV, ntl, off) in ((Ve, d_Ve, NT, 0), (Vo, d_Vo, 15, 64)):
        for j in range(ntl):
            pi, hb = PROJ_BANKS[nb % 4]
            nb += 1
            pst = self.ps[pi][:, hb * 512: hb * 512 + 256]
            dps = self.d_ps[pi][hb]
            a = off + j * 128
            hd = [self.d_H[t] for t in range(a // 128, (a + 255) // 128)]
            for dt in range(8):
                k.pe(lambda e, dt=dt, pst=pst, a=a: e.matmul(pst, self.H[:, dt, a:a + 128], wv[:, dt, 0:256],
                                                             start=(dt == 0), stop=(dt == 7)), r=[dwv] + hd, w=[dps])
            k.act(lambda e, Vx=Vx, j=j, pst=pst: e.activation(out=Vx[:, j, :, 0:64], in_=pst.rearrange("p (h d) -> p h d", h=4),
                                                              func=AF.Copy), r=[dps], w=[dV])
    T2 = self.aa([128, 4, 14, 64])
    msk = self.aa([128, 64])
    d_T2 = Dep()
    d_msk = d_T2
    if not hasattr(self, "na_rpbg_in"):
        self.na_rpbg_in = self.din("na_rpbg", [DEPTH, 128, 4 * 14 * 64])
        self.na_mask_in = self.din("na_mask", [128, 64])
        self.ch_na = self.k.chan()
    k.dma("sp", T2.rearrange("p a b c -> p (a b c)"), self.na_rpbg_in[l], self.ch_na, w=[d_T2])
    k.dma("sp", msk, self.na_mask_in[:, :], self.ch_na, w=[d_msk])
    k.dve(lambda e: e.tensor_tensor(out=T2.rearrange("p a b c -> p (a b) c"), in0=T2.rearrange("p a b c -> p (a b) c"),
                                    in1=msk.unsqueeze(1).to_broadcast([128, 56, 64]), op=ALU.add),
          r=[d_T2, d_msk], w=[d_T2])
    Sb = [self.aa([128, 4, 64]) for _ in range(2)]
    d_Sb = [Dep(), Dep()]
    E = [self.aa([128, 6, 64], BF16) for _ in range(3)]
    d_E = [Dep() for _ in range(3)]
    rs = [self.aa([64, 4]) for _ in range(2)]
    d_rs = [Dep(), Dep()]
    On = [self.aa([64, 256], BF16) for _ in range(2)]
    d_On = [Dep(), Dep()]
    SB = [(2, 0), (2, 1), (3, 0), (3, 1)]
    OB = [(0, 0), (0, 1)]
    TB = (1, 0)
    psT = self.ps[TB[0]][:, TB[1] * 512: TB[1] * 512 + 512].bitcast(BF16)
    d_psT = self.d_ps[TB[0]][TB[1]]
    n = 0
    for r in range(32):
        s = min(max(r - 4, 0), 24)
        base = s - r + 7
        opi, ohb = OB[r % 2]
        O_ps = self.ps[opi][0:64, ohb * 512: ohb * 512 + 260]
        d_O = self.d_ps[opi][ohb]
        tq = (64 * r) // 128
        for h in range(4):
            hp = slice(64 * (h % 2), 64 * (h % 2) + 64)
            hc = h // 2
            spi, shb = SB[n % 4]
            S_ps = self.ps[spi][:, shb * 512: shb * 512 + 384]
            d_S = self.d_ps[spi][shb]
            for kt in range(6):
                ks = 64 * s + 128 * kt if kt < 4 else L + 128 * (kt - 4)
                kd = [d_k[t] for t in range(ks // 128, (ks + 255) // 128)]
                k.pe(lambda e, kt=kt, ks=ks, S_ps=S_ps, hp=hp, hc=hc, r=r: e.matmul(
                    S_ps[:, kt * 64:(kt + 1) * 64], kT[hp, hc, ks:ks + 128], qT[hp, hc, 64 * r:64 * r + 64],
                    start=True, stop=True), r=kd + [d_q[tq]], w=[d_S])
            sb_i = n % 2
            e_i = n % 3
            k.dve(lambda e, sb_i=sb_i, S_ps=S_ps, h=h, base=base: e.tensor_tensor(
                out=Sb[sb_i], in0=S_ps[:, 0:256].rearrange("p (a b) -> p a b", a=4),
                in1=T2[:, h, base:base + 7:2, :], op=ALU.add), r=[d_S, d_T2], w=[d_Sb[sb_i]])
            k.act(lambda e, sb_i=sb_i, e_i=e_i: e.activation(out=E[e_i][:, 0:4, :], in_=Sb[sb_i], func=AF.Exp),
                  r=[d_Sb[sb_i]], w=[d_E[e_i]])
            k.act(lambda e, e_i=e_i, S_ps=S_ps: e.activation(out=E[e_i][:, 4:6, :],
                                                             in_=S_ps[:, 256:384].rearrange("p (a b) -> p a b", a=2),
                                                             func=AF.Exp), r=[d_S], w=[d_E[e_i]])
            for kt in range(6):
                if kt < 4:
                    if s % 2 == 0:
                        vt, dv = Ve[:, s // 2 + kt, h, :], d_Ve
                    else:
                        vt, dv = Vo[:, (s - 1) // 2 + kt, h, :], d_Vo
                else:
                    vt, dv = Ve[:, 16 + kt - 4, h, :], d_Ve
                k.pe(lambda e, kt=kt, vt=vt, e_i=e_i, O_ps=O_ps, h=h: e.matmul(
                    O_ps[:, h * 65:(h + 1) * 65], E[e_i][:, kt, :], vt, start=(kt == 0), stop=(kt == 5)),
                    r=[d_E[e_i], dv], w=[d_O])
            n += 1
        j = r % 2
        O3 = O_ps.rearrange("p (h d) -> p h d", h=4)
        k.dve(lambda e, j=j, O3=O3: e.reciprocal(out=rs[j], in_=O3[:, :, 64]), r=[d_O], w=[d_rs[j]])
        k.dve(lambda e, j=j, O3=O3: e.tensor_tensor(out=On[j].rearrange("p (h d) -> p h d", h=4), in0=O3[:, :, 0:64],
                                                    in1=rs[j].unsqueeze(2).to_broadcast([64, 4, 64]), op=ALU.mult),
              r=[d_O, d_rs[j]], w=[d_On[j]])
        for hc in range(2):
            k.pe(lambda e, j=j, hc=hc: e.transpose(out=psT[:, hc * 64:(hc + 1) * 64], in_=On[j][:, hc * 128:(hc + 1) * 128],
                                                   identity=self.ident_b[0:64, 0:64]), r=[d_On[j], self.d_const], w=[d_psT])
        k.act(lambda e, r=r: e.activation(out=self.C[:, 6:8, 64 * r:64 * r + 64],
                                          in_=psT[:, 0:128].rearrange("p (a b) -> p a b", a=2), func=AF.Copy),
              r=[d_psT], w=[self.d_C[tq]])
    if with_ctx:
        Ec = self.aa([128, 2, 256], BF16)
        d_Ec = Dep()
        Onc = self.aa([128, 256], BF16)
        d_Onc = Dep()
        rsc = self.aa([128, 4])
        d_rsc = Dep()
        for qt in range(2):
            opi, ohb = OB[qt % 2]
            O_ps = self.ps[opi][:, ohb * 512: ohb * 512 + 260]
            d_O = self.d_ps[opi][ohb]
            for h in range(4):
                hp = slice(64 * (h % 2), 64 * (h % 2) + 64)
                hc = h // 2
                spi, shb = SB[n % 4]
                n += 1
                S_ps = self.ps[spi][:, shb * 512: shb * 512 + 256]
                d_S = self.d_ps[spi][shb]
                for c in range(2):
                    k.pe(lambda e, c=c, S_ps=S_ps, hp=hp, hc=hc, qt=qt: e.matmul(
                        S_ps[:, c * 128:(c + 1) * 128], kT[hp, hc, L + 128 * c:L + 128 * c + 128],
                        qT[hp, hc, L + 128 * qt:L + 128 * qt + 128], start=True, stop=True),
                        r=[d_k[16 + c], d_q[16 + qt]], w=[d_S])
                k.act(lambda e, S_ps=S_ps: e.activation(out=Ec[:, :, 0:128], in_=S_ps.rearrange("p (a b) -> p a b", a=2),
                                                        func=AF.Exp), r=[d_S], w=[d_Ec])
                for c in range(2):
                    k.pe(lambda e, c=c, O_ps=O_ps, h=h: e.matmul(O_ps[:, h * 65:(h + 1) * 65], Ec[:, c, 0:128],
                                                                 Ve[:, 16 + c, h, :], start=(c == 0), stop=(c == 1)),
                         r=[d_Ec, d_Ve], w=[d_O])
            O3 = O_ps.rearrange("p (h d) -> p h d", h=4)
            k.dve(lambda e, O3=O3: e.reciprocal(out=rsc, in_=O3[:, :, 64]), r=[d_O], w=[d_rsc])
            k.dve(lambda e, O3=O3: e.tensor_tensor(out=Onc.rearrange("p (h d) -> p h d", h=4), in0=O3[:, :, 0:64],
                                                   in1=rsc.unsqueeze(2).to_broadcast([128, 4, 64]), op=ALU.mult),
                  r=[d_O, d_rsc], w=[d_Onc])
            for hc in range(2):
                k.pe(lambda e, hc=hc: e.transpose(out=psT[:, hc * 128:(hc + 1) * 128], in_=Onc[:, hc * 128:(hc + 1) * 128],
                                                  identity=self.ident_b[:]), r=[d_Onc, self.d_const], w=[d_psT])
            k.act(lambda e, qt=qt: e.activation(out=self.C[:, 6:8, L + 128 * qt:L + 128 * qt + 128],
                                                in_=psT[:, 0:256].rearrange("p (a b) -> p a b", a=2), func=AF.Copy),
                  r=[d_psT], w=[self.d_C[16 + qt]])


import math
HY_EMB = 33
HY_BANDS = 16


def hy_consts():
    bf = ml_dtypes.bfloat16
    out = {}
    max_decay = math.log(1e-2) / 0.3
    min_decay = math.log(1e-2) / 1.5
    deltas = np.abs(np.linspace(min_decay, max_decay, 256, dtype=np.float32))
    for tag, Ls in (("lat", L), ("ctx", LC)):
        nt = Ls // 128
        t = np.linspace(0.0, 1.0, Ls, dtype=np.float32)[:, None]
        bands = np.linspace(1e-4, HY_BANDS - 1, HY_BANDS, dtype=np.float32)
        ang = (np.float32(2.0 * math.pi / Ls)) * np.arange(Ls, dtype=np.float32)[:, None] * bands
        z = np.concatenate([t, np.cos(ang), -np.sin(ang)], axis=-1).astype(np.float32)
        out["hy_zT_" + tag] = np.ascontiguousarray(z.T)
        dec = np.exp(-t * deltas[None, :]).astype(np.float32)
        out["hy_dec_" + tag] = np.ascontiguousarray(dec.reshape(nt, 128, 256).transpose(1, 0, 2).reshape(128, nt * 256))
        N = 2 * Ls
        tt_ = np.arange(Ls, dtype=np.int64)
        ff = np.arange(Ls, dtype=np.int64)
        m = ((2 * ff[None, :] + 1) * tt_[:, None]) % (2 * N)
        th = m.astype(np.float64) * (math.pi / N)
        Cm = np.cos(th)
        Sm = -np.sin(th)
        for nm, M_ in (("C", Cm), ("S", Sm)):
            f4 = M_.reshape(nt, 128, nt, 128).transpose(2, 1, 0, 3).reshape(nt, 128, nt * 128)
            out["hy_%sf_%s" % (nm, tag)] = np.ascontiguousarray(f4).astype(bf)
            Wd = min(512, Ls)
            ntb = Ls // Wd
            i4 = M_.reshape(ntb, Wd, nt, 128).transpose(0, 3, 2, 1).reshape(ntb, 128, nt * Wd)
            out["hy_%si_%s" % (nm, tag)] = np.ascontiguousarray(i4).astype(bf)
    return out


def _arena_rewind(self, mark):
    self.k.barrier()
    self.ar_off = mark


def _hy_filters(self, l, Ls, tag, WHx, d_WHx):
    k = self.k
    nt = Ls // 128
    BW = min(512, Ls)
    nb = Ls // BW
    zin = self.din("hy_zT_" + tag, [HY_EMB, Ls]) if ("hy_zT_" + tag) not in self.inputs else self._hyin["hy_zT_" + tag]
    din_dec = self.din("hy_dec_" + tag, [128, nt * 256]) if ("hy_dec_" + tag) not in self.inputs else self._hyin["hy_dec_" + tag]
    self._hyin["hy_zT_" + tag] = zin
    self._hyin["hy_dec_" + tag] = din_dec
    zT = self.aa([HY_EMB, Ls])
    dec = self.aa([128, nt, 256])
    d_z = Dep()
    d_dec = d_z
    k.dma("sp", zT, zin[:, :], self.ch_hy, w=[d_z])
    k.dma("sp", dec.rearrange("p a b -> p (a b)"), din_dec[:, :], self.ch_hy, w=[d_dec])
    hb = [self.aa([64, Ls]) for _ in range(2)]
    d_hb = [Dep(), Dep()]
    v = self.aa([64, 512])
    ki = self.aa([64, 512], I32)
    kf = self.aa([64, 512])
    d_v, d_ki, d_kf = Dep(), Dep(), Dep()
    fb = self.aa([64, 3])
    d_fb = Dep()
    fr = _ppv(self, l, "hy_f", 64)
    bb = _ppv(self, l, "hy_b", 64)
    k.dve(lambda e: e.tensor_tensor(out=fb, in0=fr, in1=bb, op=ALU.mult), r=[self.d_pp], w=[d_fb])
    ws = [_ppv(self, l, "hy_w1", HY_EMB), _ppv(self, l, "hy_w2", 64), _ppv(self, l, "hy_w3", 64)]
    PB = [(2, 0), (2, 1)]
    nps = 0
    src, d_src = zT, d_z
    for li in range(3):
        dst, d_dst = hb[li % 2], d_hb[li % 2]
        for b in range(nb):
            pi, hbk = PB[nps % 2]
            nps += 1
            pst = self.ps[pi][0:64, hbk * 512: hbk * 512 + BW]
            dps = self.d_ps[pi][hbk]
            k.pe(lambda e, li=li, b=b, pst=pst, src=src: e.matmul(pst, ws[li], src[:, b * BW:(b + 1) * BW], start=True, stop=True),
                 r=[self.d_pp, d_src], w=[dps])
            k.dve(lambda e, li=li, pst=pst: e.tensor_scalar(out=v[:, 0:BW], in0=pst, scalar1=fr[:, li:li + 1], scalar2=fb[:, li:li + 1],
                                                            op0=ALU.mult, op1=ALU.add), r=[dps, d_fb, self.d_pp], w=[d_v])
            k.dve(lambda e: e.tensor_scalar(out=ki[:, 0:BW], in0=v[:, 0:BW], scalar1=1.0 / (2.0 * math.pi), scalar2=None, op0=ALU.mult),
                  r=[d_v], w=[d_ki])
            k.dve(lambda e: e.tensor_copy(out=kf[:, 0:BW], in_=ki[:, 0:BW]), r=[d_ki], w=[d_kf])
            k.dve(lambda e: e.scalar_tensor_tensor(out=v[:, 0:BW], in0=kf[:, 0:BW], scalar=-2.0 * math.pi, in1=v[:, 0:BW],
                                                   op0=ALU.mult, op1=ALU.add), r=[d_kf, d_v], w=[d_v])
            k.dve(lambda e: e.tensor_scalar(out=v[:, 0:BW], in0=v[:, 0:BW], scalar1=3.1415925, scalar2=-3.1415925,
                                            op0=ALU.min, op1=ALU.max), r=[d_v], w=[d_v])
            k.act(lambda e, dst=dst, b=b: e.activation(out=dst[:, b * BW:(b + 1) * BW], in_=v[:, 0:BW], func=AF.Sin),
                  r=[d_v], w=[d_dst])
        src, d_src = dst, d_dst
    h3, d_h3 = src, d_src
    w4 = _ppv(self, l, "hy_w4", 64)
    hd = [self.aa([128, 2, 256]) for _ in range(2)]
    d_hd = [Dep(), Dep()]
    ab = [self.aa([128, 512]) for _ in range(2)]
    d_ab = [Dep(), Dep()]
    nrm_ps = self.ps[3][:, 0:512]
    d_nrm = self.d_ps[3][0]
    rn = self.aa([128, 256])
    d_rn = Dep()
    for pss in range(2):
        for tt in range(nt):
            pi, hbk = PB[nps % 2]
            nps += 1
            pst = self.ps[pi][:, hbk * 512: hbk * 512 + 512]
            dps = self.d_ps[pi][hbk]
            k.pe(lambda e, tt=tt, pst=pst: e.matmul(pst, h3[:, tt * 128:(tt + 1) * 128], w4, start=True, stop=True),
                 r=[d_h3, self.d_pp], w=[dps])
            j = tt % 2
            k.dve(lambda e, j=j, tt=tt, pst=pst: e.tensor_tensor(out=hd[j], in0=pst.rearrange("p (a b) -> p a b", a=2),
                                                                 in1=dec[:, tt, :].unsqueeze(1).to_broadcast([128, 2, 256]),
                                                                 op=ALU.mult), r=[dps, d_dec], w=[d_hd[j]])
            if pss == 0:
                k.act(lambda e, j=j: e.activation(out=ab[j], in_=hd[j].rearrange("p a b -> p (a b)"), func=AF.Abs),
                      r=[d_hd[j]], w=[d_ab[j]])
                k.pe(lambda e, j=j, tt=tt: e.matmul(nrm_ps, self.ones_f[:], ab[j], start=(tt == 0), stop=(tt == nt - 1)),
                     r=[d_ab[j], self.d_const], w=[d_nrm])
            else:
                if tt == 0:
                    k.dve(lambda e, j=j: e.memset(hd[j][0:1, 1, :], 0.0), r=[d_hd[j]], w=[d_hd[j]])
                k.dve(lambda e, j=j: e.tensor_tensor(out=hd[j], in0=hd[j], in1=rn.unsqueeze(1).to_broadcast([128, 2, 256]),
                                                     op=ALU.mult), r=[d_hd[j], d_rn], w=[d_hd[j]])
                k.dve(lambda e, j=j, tt=tt: e.tensor_tensor(out=WHx[:, tt, 1, :], in0=hd[j][:, 0, :], in1=hd[j][:, 1, :], op=ALU.add),
                      r=[d_hd[j]], w=[d_WHx])
                k.pool(lambda e, j=j, tt=tt: e.tensor_tensor(out=WHx[:, tt, 2, :], in0=hd[j][:, 0, :], in1=hd[j][:, 1, :],
                                                             op=ALU.subtract), r=[d_hd[j]], w=[d_WHx])
        if pss == 0:
            k.dve(lambda e: e.tensor_copy(out=rn, in_=nrm_ps[:, 0:256]), r=[d_nrm], w=[d_rn])
            k.dve(lambda e: e.tensor_tensor(out=rn, in0=rn, in1=nrm_ps[:, 256:512], op=ALU.add), r=[d_nrm, d_rn], w=[d_rn])
            k.dve(lambda e: e.reciprocal(out=rn, in_=rn), r=[d_rn], w=[d_rn])
            k.dve(lambda e: e.tensor_scalar(out=rn, in0=rn, scalar1=2.0 / (2 * Ls), scalar2=None, op0=ALU.mult), r=[d_rn], w=[d_rn])


def _hy_dft(self, l, Ls, tag, toff, WHx, d_WHx, x0T, wTm, d_x0w, Yh, ring):
    k = self.k
    nt = Ls // 128
    Wd = min(512, Ls)
    ntb = Ls // Wd
    names = ["hy_Cf_", "hy_Sf_", "hy_Ci_", "hy_Si_"]
    tabs = []
    for nm in names:
        key = nm + tag
        if key not in self._hyin:
            shp = [nt, 128, nt * 128] if nm[4] == "f" else [ntb, 128, nt * Wd]
            self._hyin[key] = self.din(key, shp, BF16)
        tabs.append(self._hyin[key])
    Cf, Sf, Ci, Si = tabs
    d_Yh = Dep()
    Kr = self.aa([128, 4, 256])
    d_K = [Dep(), Dep()]
    tm = [self.aa([128, 256]) for _ in range(4)]
    d_tm = [Dep() for _ in range(4)]
    hybias = _ppv(self, l, "hybias")
    for j in range(nt):
        slot, dsl, chs = ring[self.hyring_i % len(ring)]
        self.hyring_i += 1
        cst = slot[:, 0:nt * 128].rearrange("p (a b) -> p a b", b=128)
        sst = slot[:, 2048:2048 + nt * 128].rearrange("p (a b) -> p a b", b=128)
        k.dma("sp", slot[:, 0:nt * 128], Cf[j], chs, w=[dsl])
        k.dma("sp", slot[:, 2048:2048 + nt * 128], Sf[j], chs, w=[dsl])
        ps_r = self.ps[0][:, 0:512]
        ps_i = self.ps[0][:, 512:1024]
        for tt in range(nt):
            k.pe(lambda e, tt=tt, cst=cst: e.matmul(ps_r, cst[:, tt, :], WHx[:, tt, 0:2, :], start=(tt == 0), stop=(tt == nt - 1)),
                 r=[dsl, d_WHx], w=[self.d_ps[0][0]])
        for tt in range(nt):
            k.pe(lambda e, tt=tt, sst=sst: e.matmul(ps_i, sst[:, tt, :], WHx[:, tt, 0:3:2, :], start=(tt == 0), stop=(tt == nt - 1)),
                 r=[dsl, d_WHx], w=[self.d_ps[0][1]])
        k.act(lambda e: e.activation(out=Kr[:, 0:2, :], in_=ps_r.rearrange("p (a b) -> p a b", a=2), func=AF.Copy),
              r=[self.d_ps[0][0]], w=[d_K[0]])
        k.act(lambda e: e.activation(out=Kr[:, 2:4, :], in_=ps_i.rearrange("p (a b) -> p a b", a=2), func=AF.Copy),
              r=[self.d_ps[0][1]], w=[d_K[1]])
        Ur, Kre, Ui, Kie = Kr[:, 0, :], Kr[:, 1, :], Kr[:, 2, :], Kr[:, 3, :]
        k.dve(lambda e: e.tensor_tensor(out=tm[0], in0=Ur, in1=Kre, op=ALU.mult), r=[d_K[0]], w=[d_tm[0]])
        k.pool(lambda e: e.tensor_tensor(out=tm[1], in0=Ui, in1=Kie, op=ALU.mult), r=[d_K[1]], w=[d_tm[1]])
        k.dve(lambda e, j=j: e.tensor_tensor(out=Yh[:, j, 0, :], in0=tm[0], in1=tm[1], op=ALU.subtract),
              r=[d_tm[0], d_tm[1]], w=[d_Yh])
        k.pool(lambda e: e.tensor_tensor(out=tm[2], in0=Ur, in1=Kie, op=ALU.mult), r=d_K, w=[d_tm[2]])
        k.dve(lambda e: e.tensor_tensor(out=tm[3], in0=Ui, in1=Kre, op=ALU.mult), r=d_K, w=[d_tm[3]])
        k.pool(lambda e, j=j: e.tensor_tensor(out=Yh[:, j, 1, :], in0=tm[2], in1=tm[3], op=ALU.add),
               r=[d_tm[2], d_tm[3]], w=[d_Yh])
    G = 4 if nt >= 4 else nt
    yt = [self.aa([128, 512]) for _ in range(2)]
    d_yt = [Dep(), Dep()]
    for tb in range(ntb):
        for g in range(nt // G):
            slot, dsl, chs = ring[self.hyring_i % len(ring)]
            self.hyring_i += 1
            k.dma("sp", slot[:, 0:G * Wd], Ci[tb, :, g * G * Wd:(g + 1) * G * Wd], chs, w=[dsl])
            k.dma("sp", slot[:, 2048:2048 + G * Wd], Si[tb, :, g * G * Wd:(g + 1) * G * Wd], chs, w=[dsl])
            cst = slot[:, 0:G * Wd].rearrange("p (a b) -> p a b", b=Wd)
            sst = slot[:, 2048:2048 + G * Wd].rearrange("p (a b) -> p a b", b=Wd)
            for fi in range(G):
                ft = g * G + fi
                for ct in range(2):
                    py = self.ps[1][:, ct * 512: ct * 512 + Wd]
                    k.pe(lambda e, fi=fi, ft=ft, ct=ct, py=py, cst=cst: e.matmul(py, Yh[:, ft, 0, ct * 128:(ct + 1) * 128], cst[:, fi, :],
                                                                               start=(ft == 0), stop=False),
                         r=[dsl, d_Yh], w=[self.d_ps[1][ct]])
                    k.pe(lambda e, fi=fi, ft=ft, ct=ct, py=py, sst=sst: e.matmul(py, Yh[:, ft, 1, ct * 128:(ct + 1) * 128], sst[:, fi, :],
                                                                               start=False, stop=(ft == nt - 1)),
                         r=[dsl, d_Yh], w=[self.d_ps[1][ct]])
        a = toff + tb * Wd
        tts = [self.d_C[t] for t in range(a // 128, (a + Wd) // 128)]
        for ct in range(2):
            py = self.ps[1][:, ct * 512: ct * 512 + Wd]
            k.dve(lambda e, ct=ct, py=py, a=a: e.scalar_tensor_tensor(out=yt[ct][:, 0:Wd], in0=wTm[:, ct, a:a + Wd], scalar=hybias[:, ct:ct + 1],
                                                                      in1=py, op0=ALU.mult, op1=ALU.add),
                  r=[self.d_ps[1][ct], d_x0w, self.d_pp], w=[d_yt[ct]])
            k.pool(lambda e, ct=ct, a=a: e.tensor_tensor(out=self.C[:, 4 + ct, a:a + Wd], in0=yt[ct][:, 0:Wd], in1=x0T[:, ct, a:a + Wd],
                                                         op=ALU.mult), r=[d_yt[ct], d_x0w], w=tts)


def _mix_hy(self, l, with_ctx):
    k = self.k
    self.arena_reset()
    if not hasattr(self, "_hyin"):
        self._hyin = {}
        self.ch_hy = k.chan()
        self.ch_hyring = [k.chan() for _ in range(3)]
    ntok = T if with_ctx else L
    WH = self.aa([128, 16, 3, 256], BF16)
    d_WH = Dep()
    if with_ctx:
        WHc = self.aa([128, 2, 3, 256], BF16)
        d_WHc = Dep()
    x0T = self.aa([128, 2, T], BF16)
    wTm = self.aa([128, 2, T], BF16)
    d_x0w = Dep()
    markA = self.ar_off
    _hy_filters(self, l, L, "lat", WH, d_WH)
    if with_ctx:
        _arena_rewind(self, markA)
        _hy_filters(self, l, LC, "ctx", WHc, d_WHc)
    _arena_rewind(self, markA)
    _wring_init(self)
    ranges = [(0, L)] + ([(L, T)] if with_ctx else [])
    blocks = _blocks(self, with_ctx)
    pin = [self.aa([128, T]) for _ in range(1)]
    d_pin = [Dep()]
    x1v = self.aa([128, 4, T], BF16)
    d_x1v = [Dep() for _ in range(4)]
    cacc = [self.aa([128, T]) for _ in range(1)]
    d_cacc = Dep()
    hyw = _ppv(self, l, "hyw")

    def evac(c, m, t0, n, pst, dps):
        ct = (c - OFF_HY) // 128
        j = 0
        k.act(lambda e: e.activation(out=pin[j][:, t0:t0 + n], in_=pst, func=AF.Copy), r=[dps], w=[d_pin[j]])
        if (t0, n) == blocks[-1]:
            _dwconv(self, "dve", cacc[0], pin[j], hyw[:, ct * 3:(ct + 1) * 3], ranges, [d_pin[j], self.d_pp], [d_cacc])
            if ct < 2:
                k.pool(lambda e: e.tensor_copy(out=x0T[:, ct, 0:ntok], in_=cacc[0][:, 0:ntok]), r=[d_cacc], w=[d_x0w])
            else:
                k.pool(lambda e: e.tensor_copy(out=x1v[:, ct - 2, 0:ntok], in_=cacc[0][:, 0:ntok]), r=[d_cacc], w=[d_x1v[ct - 2]])
    _proj(self, l, [(OFF_HY, 512), (OFF_HY + 512, 256)], blocks, evac, PROJ_BANKS)
    for ct in range(2):
        k.pool(lambda e, ct=ct: e.tensor_tensor(out=wTm[:, ct, 0:ntok], in0=x1v[:, ct, 0:ntok], in1=x1v[:, 2 + ct, 0:ntok], op=ALU.mult),
               r=[d_x1v[ct], d_x1v[2 + ct]], w=[d_x0w])
    psT = self.ps[2][:, 0:512].bitcast(BF16)
    d_psT = self.d_ps[2][0]
    for tt in range(ntok // 128):
        for ct in range(2):
            k.pe(lambda e, tt=tt, ct=ct: e.transpose(out=psT[:, ct * 128:(ct + 1) * 128], in_=wTm[:, ct, tt * 128:(tt + 1) * 128],
                                                     identity=self.ident_b[:]), r=[d_x0w, self.d_const], w=[d_psT])
        if tt < 16:
            k.act(lambda e, tt=tt: e.activation(out=WH[:, tt, 0, :], in_=psT[:, 0:256], func=AF.Copy), r=[d_psT], w=[d_WH])
        else:
            k.act(lambda e, tt=tt: e.activation(out=WHc[:, tt - 16, 0, :], in_=psT[:, 0:256], func=AF.Copy), r=[d_psT], w=[d_WHc])
    _arena_rewind(self, markA)
    Yh = self.aa([128, 16, 2, 256], BF16)
    ring = [(self.aa([128, 4096], BF16), Dep(), self.ch_hyring[i]) for i in range(3)]
    self.hyring_i = 0
    markD = self.ar_off
    _hy_dft(self, l, L, "lat", 0, WH, d_WH, x0T, wTm, d_x0w, Yh, ring)
    if with_ctx:
        _arena_rewind(self, markD)
        _hy_dft(self, l, LC, "ctx", L, WHc, d_WHc, x0T, wTm, d_x0w, Yh, ring)


def gdn_consts():
    out = {}
    m = np.arange(128)[:, None]
    i = np.arange(128)[None, :]
    LE = (m <= i).astype(np.float32)
    GE = (m >= i).astype(np.float32)
    GT = (m > i).astype(np.float32)
    LT = (m < i).astype(np.float32)
    NEGF = np.tile(np.where(i > m, -1e9, 0.0).astype(np.float32), (1, 4))
    NEGB = np.tile(np.where(i < m, -1e9, 0.0).astype(np.float32), (1, 4))
    out["gdn_gm"] = np.ascontiguousarray(np.concatenate([LE, GE, GT, LT, NEGF, NEGB], axis=1))
    pm = np.zeros((128, 128), np.float32)
    for mm in range(128):
        if (mm % 64) < 32:
            pm[mm + 32, mm] = -1.0
        else:
            pm[mm - 32, mm] = 1.0
    out["gdn_pm"] = pm
    ii = np.arange(128)[:, None]
    jj = np.arange(128)[None, :]
    lms, ums = [], []
    for s_ in range(7):
        b_ = 1 << s_
        same2 = (ii // (2 * b_)) == (jj // (2 * b_))
        diff1 = (ii // b_) != (jj // b_)
        lms.append(np.where(same2 & diff1 & (jj < ii), -1.0, 0.0))
        ums.append(np.where(same2 & diff1 & (jj > ii), -1.0, 0.0))
    out["gdn_lm"] = np.ascontiguousarray(np.concatenate(lms + ums, axis=1)).astype(ml_dtypes.bfloat16)
    pos = np.arange(L)
    row = (pos // 64).astype(np.float32)
    col = (pos % 64).astype(np.float32)
    inv = (10000.0 ** (-np.arange(16, dtype=np.float32) / 16)).astype(np.float32)
    ang = np.concatenate([row[:, None] * inv, col[:, None] * inv], axis=-1)
    idx = (np.arange(128) % 64) % 32
    cs = np.stack([np.cos(ang).T[idx], np.sin(ang).T[idx]], axis=1)
    out["gdn_rope"] = np.ascontiguousarray(cs.reshape(128, 2 * L)).astype(ml_dtypes.bfloat16)
    return out


def _mix_gdn(self, l, with_ctx):
    k = self.k
    self.arena_reset()
    if not hasattr(self, "gdn_gm_in"):
        self.gdn_gm_in = self.din("gdn_gm", [128, 1536])
        self.gdn_pm_in = self.din("gdn_pm", [128, 128])
        self.gdn_rope_in = self.din("gdn_rope", [128, 2 * L], BF16)
        self.ch_gdn = k.chan()
    otok = self.aa([128, NT, 256], BF16)
    markO = self.ar_off
    qT = self.aa([128, 2, T], BF16)
    kT = self.aa([128, 2, T], BF16)
    d_qk = Dep()
    vtok = self.aa([128, NT, 256], BF16)
    ktok = self.aa([128, NT, 256], BF16)
    d_vtok, d_ktok = Dep(), Dep()
    la = self.aa([128, NT, 8])
    beta = self.aa([128, NT, 8])
    d_la, d_beta = Dep(), Dep()
    GM = self.aa([128, 1536])
    d_GM = Dep()
    d_LMK = d_GM
    d_c1 = d_GM
    k.dma("sp", GM, self.gdn_gm_in[:, :], self.ch_gdn, w=[d_GM])
    LE, GE, GT, LT = (GM[:, i * 128:(i + 1) * 128] for i in range(4))
    NEGF, NEGB = GM[:, 512:1024], GM[:, 1024:1536]
    LMK = self.aa([128, 14 * 128], BF16)
    if not hasattr(self, "gdn_lm_in"):
        self.gdn_lm_in = self.din("gdn_lm", [128, 14 * 128], BF16)
    k.dma("sp", LMK, self.gdn_lm_in[:, :], self.ch_gdn, w=[d_LMK])
    markP = self.ar_off
    _wring_init(self, 1)
    pm = self.aa([128, 128])
    rope = self.aa([128, 2, L], BF16)
    blk64 = self.aa([128, 128])
    d_blk = Dep()
    k.dma("sp", pm, self.gdn_pm_in[:, :], self.ch_gdn, w=[d_c1])
    k.dma("sp", rope.rearrange("p a b -> p (a b)"), self.gdn_rope_in[:, :], self.ch_gdn, w=[d_c1])
    k.pool(lambda e: e.memset(blk64, 0.0), w=[d_blk])
    k.pool(lambda e: e.memset(blk64[0:64, 0:64], 1.0), w=[d_blk])
    k.pool(lambda e: e.memset(blk64[64:128, 64:128], 1.0), w=[d_blk])
    wab, dwab, chab = self.wring[0]
    k.dma("pool", wab[:, :, 0:16], self.W("w_in")[l, :, 1024:1040].rearrange("(kt p) c -> p kt c", p=128), chab, w=[dwab])
    ab_ps = self.ps[3][:, 0:NT * 16]
    d_ab = self.d_ps[3][0]
    for tt in range(NT):
        for dt in range(8):
            k.pe(lambda e, tt=tt, dt=dt: e.matmul(ab_ps[:, tt * 16:(tt + 1) * 16], self.H[:, dt, tt * 128:(tt + 1) * 128],
                                                  wab[:, dt, 0:16], start=(dt == 0), stop=(dt == 7)),
                 r=[dwab, self.d_H[tt]], w=[d_ab])
    ab3 = ab_ps.rearrange("p (t c) -> p t c", c=16)
    xa = self.aa([128, NT, 8])
    ea = self.aa([128, 8])
    d_xa, d_ea = Dep(), Dep()
    k.dve(lambda e: e.tensor_tensor(out=xa, in0=ab3[:, :, 0:8], in1=_ppv(self, l, "dtb").unsqueeze(1).to_broadcast([128, NT, 8]),
                                    op=ALU.add), r=[d_ab, self.d_pp], w=[d_xa])
    k.act(lambda e: e.activation(out=xa, in_=xa, func=AF.Exp), r=[d_xa], w=[d_xa])
    k.act(lambda e: e.activation(out=xa, in_=xa, func=AF.Ln, bias=1.0, scale=1.0), r=[d_xa], w=[d_xa])
    k.act(lambda e: e.activation(out=ea, in_=_ppv(self, l, "alog"), func=AF.Exp), r=[self.d_pp], w=[d_ea])
    k.dve(lambda e: e.scalar_tensor_tensor(out=la, in0=xa, scalar=-1.0, in1=ea.unsqueeze(1).to_broadcast([128, NT, 8]),
                                           op0=ALU.mult, op1=ALU.mult), r=[d_xa, d_ea], w=[d_la])
    k.act(lambda e: e.activation(out=beta, in_=ab3[:, :, 8:16], func=AF.Sigmoid), r=[d_ab], w=[d_beta])
    blocks = _blocks(self, True)
    ranges = [(0, L), (L, T)]
    pin = self.aa([128, T])
    cacc = self.aa([128, T])
    d_pin, d_cacc = Dep(), Dep()
    vTt = self.aa([128, T], BF16)
    d_vTt = Dep()
    rv = self.aa([128, 512])
    t1 = self.aa([128, 512])
    t2 = self.aa([128, 512])
    d_rv, d_t1, d_t2 = Dep(), Dep(), Dep()
    gdw = _ppv(self, l, "gdw")
    psT = self.ps[2][:, 0:512].bitcast(BF16)
    d_psT = self.d_ps[2][0]

    def finish_tile(ct):
        _dwconv(self, "dve", cacc, pin, gdw[:, ct * 3:(ct + 1) * 3], ranges, [d_pin, self.d_pp], [d_cacc])
        if ct >= 4:
            k.act(lambda e: e.activation(out=vTt, in_=cacc, func=AF.Silu), r=[d_cacc], w=[d_vTt])
            for tt in range(NT):
                k.pe(lambda e, tt=tt: e.transpose(out=psT[:, 0:128], in_=vTt[:, tt * 128:(tt + 1) * 128], identity=self.ident_b[:]),
                     r=[d_vTt, self.d_const], w=[d_psT])
                k.dve(lambda e, tt=tt: e.tensor_copy(out=vtok[:, tt, (ct - 4) * 128:(ct - 3) * 128], in_=psT[:, 0:128]),
                      r=[d_psT], w=[d_vtok])
            return
        isq = ct < 2
        dst = qT if isq else kT
        cc = ct % 2
        k.act(lambda e: e.activation(out=pin, in_=cacc, func=AF.Silu), r=[d_cacc, d_pin], w=[d_pin])
        k.act(lambda e: e.activation(out=cacc, in_=pin, func=AF.Square), r=[d_pin, d_cacc], w=[d_cacc])
        for (t0, n) in blocks:
            ss = self.ps[3][:, 512:512 + n]
            dss = self.d_ps[3][1]
            k.pe(lambda e, t0=t0, n=n, ss=ss: e.matmul(ss, blk64, cacc[:, t0:t0 + n], start=True, stop=True), r=[d_cacc, d_blk], w=[dss])
            sc_, bi_ = (64.0, 64.0 * EPS) if isq else (1.0, EPS)
            k.act(lambda e, n=n, ss=ss: e.activation(out=rv[:, 0:n], in_=ss, func=AF.Sqrt, bias=bi_, scale=sc_), r=[dss], w=[d_rv])
            k.dve(lambda e, n=n: e.reciprocal(out=rv[:, 0:n], in_=rv[:, 0:n]), r=[d_rv], w=[d_rv])
            k.dve(lambda e, t0=t0, n=n: e.tensor_tensor(out=pin[:, t0:t0 + n], in0=pin[:, t0:t0 + n], in1=rv[:, 0:n], op=ALU.mult),
                  r=[d_rv, d_pin], w=[d_pin])
            if t0 < L:
                pv = self.ps[2][:, 512:512 + n]
                dpv = self.d_ps[2][1]
                k.pe(lambda e, t0=t0, n=n, pv=pv: e.matmul(pv, pm, pin[:, t0:t0 + n], start=True, stop=True), r=[d_pin, d_c1], w=[dpv])
                k.dve(lambda e, t0=t0, n=n: e.tensor_tensor(out=t1[:, 0:n], in0=pin[:, t0:t0 + n], in1=rope[:, 0, t0:t0 + n], op=ALU.mult),
                      r=[d_pin, d_c1], w=[d_t1])
                k.dve(lambda e, t0=t0, n=n, pv=pv: e.tensor_tensor(out=t2[:, 0:n], in0=pv, in1=rope[:, 1, t0:t0 + n], op=ALU.mult),
                      r=[dpv, d_c1], w=[d_t2])
                k.pool(lambda e, t0=t0, n=n: e.tensor_tensor(out=dst[:, cc, t0:t0 + n], in0=t1[:, 0:n], in1=t2[:, 0:n], op=ALU.add),
                       r=[d_t1, d_t2], w=[d_qk])
            else:
                k.pool(lambda e, t0=t0, n=n: e.tensor_copy(out=dst[:, cc, t0:t0 + n], in_=pin[:, t0:t0 + n]), r=[d_pin], w=[d_qk])
        if not isq:
            for tt in range(NT):
                k.pe(lambda e, tt=tt: e.transpose(out=psT[:, 0:128], in_=kT[:, cc, tt * 128:(tt + 1) * 128], identity=self.ident_b[:]),
                     r=[d_qk, self.d_const], w=[d_psT])
                k.dve(lambda e, tt=tt: e.tensor_copy(out=ktok[:, tt, cc * 128:(cc + 1) * 128], in_=psT[:, 0:128]),
                      r=[d_psT], w=[d_ktok])

    def evac(c, m, t0, n, pst, dps):
        ct = c // 128
        k.act(lambda e: e.activation(out=pin[:, t0:t0 + n], in_=pst, func=AF.Copy), r=[dps], w=[d_pin])
        if (t0, n) == blocks[-1]:
            finish_tile(ct)
    _proj(self, l, [(512, 256), (256, 256), (0, 256)], blocks, evac, PROJ_BANKS)
    import os
    _stop = os.environ.get("GDN_STOP", "")
    if "gdn_g1" in self.dbg:
        self.k.barrier()
        chd = k.chan()
        for nm, ap_, n_ in (("qT", qT.rearrange("p a b -> p (a b)"), 2 * T), ("kT", kT.rearrange("p a b -> p (a b)"), 2 * T),
                            ("vtok", vtok.rearrange("p a b -> p (a b)"), NT * 256), ("ktok", ktok.rearrange("p a b -> p (a b)"), NT * 256)):
            o_ = self.dout("dbg_" + nm, [128, n_], BF16)
            k.dma("sp", o_[:, :], ap_, chd)
        for nm, ap_ in (("la", la), ("beta", beta)):
            o_ = self.dout("dbg_" + nm, [128, NT * 8])
            k.dma("sp", o_[:, :], ap_.rearrange("p a b -> p (a b)"), chd)
        self.k.barrier()
    if _stop == "g1":
        return
    _arena_rewind(self, markP)
    d_otok = [Dep() for _ in range(NT)]
    hm = self.aa([128, 4])
    d_hm = Dep()
    k.pool(lambda e: e.memset(hm, 0.0), w=[d_hm])
    k.pool(lambda e: e.memset(hm[0:64, 0:4:2], 1.0), w=[d_hm])
    k.pool(lambda e: e.memset(hm[64:128, 1:4:2], 1.0), w=[d_hm])
    first_visit = [True] * NT
    orderF = [16, 17] + list(range(16))
    orderB = [17, 16] + list(range(15, -1, -1))
    W_ = {}
    for dd in range(2):
        w = {}
        w["Z"] = self.aa([128, 4, 128]); w["D"] = self.aa([128, 4, 128], BF16); w["Ds"] = w["Z"]
        w["E"] = [self.aa([128, 4, 128], BF16) for _ in range(2)]
        w["ET"] = [self.aa([128, 4, 128], BF16) for _ in range(2)]
        w["d_E"] = [Dep(), Dep()]; w["d_ET"] = [Dep(), Dep()]
        w["A"] = [self.aa([128, 4, 128], BF16) for _ in range(2)]
        w["AT"] = [self.aa([128, 4, 128], BF16) for _ in range(2)]
        w["X"] = [self.aa([128, 4, 128], BF16) for _ in range(2)]
        w["QKm"] = self.aa([128, 4, 128], BF16)
        w["R"] = self.aa([128, 4, 128], BF16)
        w["u"] = self.aa([128, 4, 64], BF16)
        w["d_R"], w["d_u"] = Dep(), Dep()
        w["QKT"] = self.aa([128, 4, 128], BF16)
        w["wT"] = self.aa([64, 4, 128], BF16)
        w["kd"] = self.aa([128, 4, 128], BF16)
        w["kc"] = self.aa([128, 2, 2, 128], BF16)
        w["d_kc"] = Dep()
        k.pool(lambda e, w=w: e.memset(w["kc"], 0.0), w=[w["d_kc"]])
        w["SbQ"] = self.aa([128, 4, 64], BF16)
        w["d_SbQ"] = Dep()
        k.pool(lambda e, w=w: e.memset(w["SbQ"], 0.0), w=[w["d_SbQ"]])
        w["ev"] = self.aa([128, 12])
        w["beg"] = self.aa([128, 4])
        w["vn"] = self.aa([128, 4, 64], BF16)
        w["o2"] = self.aa([128, 4, 64], BF16)
        w["ot"] = self.aa([128, 4, 64], BF16)
        w["S"] = self.aa([128, 4, 64])
        w["Sb"] = self.aa([128, 4, 64], BF16)
        for nm in ("Z", "D", "Ds", "QKm", "QKT", "wT", "kd", "ev", "beg", "vn", "o2", "ot", "S", "Sb"):
            w["d_" + nm] = Dep()
        w["d_Ds"] = w["d_Z"]
        w["d_A"] = [Dep(), Dep()]; w["d_AT"] = [Dep(), Dep()]; w["d_X"] = [Dep(), Dep()]
        k.dve(lambda e, w=w: e.memset(w["S"], 0.0), w=[w["d_S"]])
        k.pool(lambda e, w=w: e.memset(w["Sb"], 0.0), w=[w["d_Sb"]])
        W_[dd] = w

    def bank(dd, i, half=None):
        t = self.ps[2 * dd + i // 2]
        hb = i % 2
        return t[:, hb * 512:(hb + 1) * 512], self.d_ps[2 * dd + i // 2][hb]

    def unit_pre(dd, n):
        w = W_[dd]
        c0 = n * 128
        lacol = la[:, n, dd * 4:dd * 4 + 4]
        becol = beta[:, n, dd * 4:dd * 4 + 4]
        Mz = GT if dd == 0 else LT
        Um = LE if dd == 0 else GE
        Ugt = GT if dd == 0 else LT
        NEG = NEGF if dd == 0 else NEGB
        strict = GT if dd == 0 else LT
        B0, dB0 = bank(dd, 0)
        B1, dB1 = bank(dd, 1)
        B2, dB2 = bank(dd, 2)
        B3, dB3 = bank(dd, 3)
        k.dve(lambda e: e.tensor_tensor(out=w["Z"], in0=Mz.unsqueeze(1).to_broadcast([128, 4, 128]),
                                        in1=lacol.unsqueeze(2).to_broadcast([128, 4, 128]), op=ALU.mult),
              r=[d_GM, d_la], w=[w["d_Z"]])
        k.pe(lambda e: e.matmul(B0, Um, w["Z"].rearrange("p a b -> p (a b)"), start=True, stop=False), r=[d_GM, w["d_Z"]], w=[dB0])
        k.pe(lambda e: e.matmul(B0, self.ident_f[:], NEG, start=False, stop=True), r=[d_GM, self.d_const], w=[dB0])
        k.act(lambda e: e.activation(out=w["D"].rearrange("p a b -> p (a b)"), in_=B0, func=AF.Exp), r=[dB0], w=[w["d_D"]])
        if _stop == "pre1":
            return None, None
        k.pe(lambda e: e.matmul(B3[:, 0:4], Um, lacol, start=True, stop=True), r=[d_GM, d_la], w=[dB3])
        k.pe(lambda e: e.matmul(B3[:, 4:8], Ugt, lacol, start=True, stop=True), r=[d_GM, d_la], w=[dB3])
        k.pe(lambda e: e.matmul(B3[:, 8:12], self.ones_f[:], lacol, start=True, stop=True), r=[self.d_const, d_la], w=[dB3])
        k.act(lambda e: e.activation(out=w["ev"], in_=B3[:, 0:12], func=AF.Exp), r=[dB3], w=[w["d_ev"]])
        if _stop == "pre2":
            return None, None
        k.dve(lambda e: e.tensor_tensor(out=w["Ds"], in0=w["D"], in1=strict.unsqueeze(1).to_broadcast([128, 4, 128]), op=ALU.mult),
              r=[w["d_D"], d_GM], w=[w["d_Ds"]])
        if _stop == "pre2a":
            return None, None
        for h in range(4 if _stop != "pre2h" else 1):
            hp = slice(64 * (h % 2), 64 * (h % 2) + 64)
            if h == 0:
                k.pool(lambda e: e.tensor_copy(out=w["kc"][0:64, :, 0, :], in_=kT[0:64, :, c0:c0 + 128]), r=[d_qk], w=[w["d_kc"]])
                k.pool(lambda e: e.tensor_copy(out=w["kc"][64:128, :, 1, :], in_=kT[64:128, :, c0:c0 + 128]), r=[d_qk], w=[w["d_kc"]])
            k.pe(lambda e, h=h, hp=hp: e.matmul(B1[:, h * 128:(h + 1) * 128], kT[:, h // 2, c0:c0 + 128], w["kc"][:, h // 2, h % 2, :],
                                                start=True, stop=True), r=[d_qk, w["d_kc"]], w=[dB1])
        if _stop in ("pre2b", "pre2h"):
            return None, None
        k.dve(lambda e: e.tensor_tensor(out=w["Ds"].rearrange("p a b -> p (a b)"), in0=B1, in1=w["Ds"].rearrange("p a b -> p (a b)"),
                                        op=ALU.mult), r=[dB1, w["d_Ds"]], w=[w["d_Ds"]])
        k.dve(lambda e: e.tensor_tensor(out=w["A"][0], in0=w["Ds"], in1=becol.unsqueeze(2).to_broadcast([128, 4, 128]), op=ALU.mult),
              r=[w["d_Ds"], d_beta], w=[w["d_A"][0]])
        if _stop == "pre3":
            return None, None
        for h in range(4):
            hp = slice(64 * (h % 2), 64 * (h % 2) + 64)
            k.pe(lambda e, h=h, hp=hp: e.matmul(B2[:, h * 128:(h + 1) * 128], qT[:, h // 2, c0:c0 + 128], w["kc"][:, h // 2, h % 2, :],
                                                start=True, stop=True), r=[d_qk, w["d_kc"]], w=[dB2])
        k.dve(lambda e: e.tensor_tensor(out=w["QKm"].rearrange("p a b -> p (a b)"), in0=B2, in1=w["D"].rearrange("p a b -> p (a b)"),
                                        op=ALU.mult), r=[dB2, w["d_D"]], w=[w["d_QKm"]])
        if _stop == "pre4":
            return None, None
        B1b = B1.bitcast(BF16)
        B2b = B2.bitcast(BF16)
        for h in range(4):
            k.pe(lambda e, h=h: e.transpose(out=B1b[:, h * 128:(h + 1) * 128], in_=w["A"][0][:, h, :], identity=self.ident_b[:]),
                 r=[w["d_A"][0], self.d_const], w=[dB1])
        k.act(lambda e: e.activation(out=w["AT"][0].rearrange("p a b -> p (a b)"), in_=B1b[:, 0:512], func=AF.Copy),
              r=[dB1], w=[w["d_AT"][0]])
        for h in range(4):
            k.pe(lambda e, h=h: e.transpose(out=B2b[:, h * 128:(h + 1) * 128], in_=w["QKm"][:, h, :], identity=self.ident_b[:]),
                 r=[w["d_QKm"], self.d_const], w=[dB2])
        k.act(lambda e: e.activation(out=w["QKT"].rearrange("p a b -> p (a b)"), in_=B2b[:, 0:512], func=AF.Copy),
              r=[dB2], w=[w["d_QKT"]])
        if _stop == "pre5":
            return None, None
        k.dve(lambda e: e.tensor_tensor(out=w["beg"], in0=becol, in1=w["ev"][:, 0:4], op=ALU.mult), r=[d_beta, w["d_ev"]], w=[w["d_beg"]])
        X0 = w["R"].rearrange("p h (two d) -> p h two d", two=2)
        k.dve(lambda e: e.tensor_tensor(out=X0[:, :, 0, :], in0=vtok[:, n, :].rearrange("p (h d) -> p h d", h=4),
                                        in1=becol.unsqueeze(2).to_broadcast([128, 4, 64]), op=ALU.mult),
              r=[d_vtok, d_beta], w=[w["d_R"]])
        k.dve(lambda e: e.tensor_tensor(out=X0[:, :, 1, :], in0=ktok[:, n, :].rearrange("p (h d) -> p h d", h=4),
                                        in1=w["beg"].unsqueeze(2).to_broadcast([128, 4, 64]), op=ALU.mult),
              r=[d_ktok, w["d_beg"]], w=[w["d_R"]])
        for half in range(2):
            k.pool(lambda e, half=half: e.tensor_tensor(out=w["kd"][:, :, half * 64:(half + 1) * 64],
                                                        in0=ktok[:, n, :].rearrange("p (h d) -> p h d", h=4),
                                                        in1=w["ev"][:, 4:8].unsqueeze(2).to_broadcast([128, 4, 64]), op=ALU.mult),
                   r=[d_ktok, w["d_ev"]], w=[w["d_kd"]])
        if _stop == "pre6":
            return None, None
        A, AT = w["A"][0], w["AT"][0]
        dA, dAT = w["d_A"][0], w["d_AT"][0]
        Tm, TT = w["A"][1], w["AT"][1]
        dT, dTT = w["d_A"][1], w["d_AT"][1]
        M1, M1t = w["X"][0], w["X"][1]
        dM1, dM1t = w["d_X"][0], w["d_X"][1]
        mo = 0 if dd == 0 else 7
        mt = 7 if dd == 0 else 0
        I4 = self.ident_b[:].unsqueeze(1).to_broadcast([128, 4, 128])
        msk = lambda s_: LMK[:, (mo + s_) * 128:(mo + s_ + 1) * 128].unsqueeze(1).to_broadcast([128, 4, 128])
        mskT = lambda s_: LMK[:, (mt + s_) * 128:(mt + s_ + 1) * 128].unsqueeze(1).to_broadcast([128, 4, 128])
        k.pool(lambda e: e.tensor_tensor(out=Tm, in0=A, in1=msk(0), op=ALU.mult), r=[dA, d_LMK], w=[dT])
        k.pool(lambda e: e.tensor_tensor(out=Tm, in0=Tm, in1=I4, op=ALU.add), r=[dT, self.d_const], w=[dT])
        k.pool(lambda e: e.tensor_tensor(out=TT, in0=AT, in1=mskT(0), op=ALU.mult), r=[dAT, d_LMK], w=[dTT])
        k.pool(lambda e: e.tensor_tensor(out=TT, in0=TT, in1=I4, op=ALU.add), r=[dTT, self.d_const], w=[dTT])
        fl = lambda x: x.rearrange("p a b -> p (a b)")
        def mk_E(lev):
            j = lev % 2
            k.pool(lambda e, lev=lev, j=j: e.tensor_tensor(out=w["E"][j], in0=A, in1=msk(lev), op=ALU.mult), r=[dA, d_LMK], w=[w["d_E"][j]])
            k.pool(lambda e, lev=lev, j=j: e.tensor_tensor(out=w["ET"][j], in0=AT, in1=mskT(lev), op=ALU.mult), r=[dAT, d_LMK], w=[w["d_ET"][j]])
        mk_E(1)
        for lev in range(1, 7):
            j = lev % 2
            E, ET, dE, dET = w["E"][j], w["ET"][j], w["d_E"][j], w["d_ET"][j]
            for h in range(4):
                k.pe(lambda e, h=h, ET=ET: e.matmul(B0[:, h * 128:(h + 1) * 128], ET[:, h, :], Tm[:, h, :], start=True, stop=True),
                     r=[dET, dT], w=[dB0])
            for h in range(4):
                k.pe(lambda e, h=h, E=E: e.matmul(B1[:, h * 128:(h + 1) * 128], E[:, h, :], TT[:, h, :], start=True, stop=True),
                     r=[dE, dTT], w=[dB1])
            if lev < 6:
                mk_E(lev + 1)
            k.act(lambda e: e.activation(out=fl(M1), in_=B0, func=AF.Copy), r=[dB0], w=[dM1])
            k.act(lambda e: e.activation(out=fl(M1t), in_=B1, func=AF.Copy), r=[dB1], w=[dM1t])
            for h in range(4):
                k.pe(lambda e, h=h: e.matmul(B2[:, h * 128:(h + 1) * 128], TT[:, h, :], M1[:, h, :], start=True, stop=True),
                     r=[dTT, dM1], w=[dB2])
            for h in range(4):
                k.pe(lambda e, h=h: e.matmul(B3[:, h * 128:(h + 1) * 128], Tm[:, h, :], M1t[:, h, :], start=True, stop=True),
                     r=[dT, dM1t], w=[dB3])
            k.dve(lambda e: e.tensor_tensor(out=fl(Tm), in0=fl(Tm), in1=B2, op=ALU.add), r=[dT, dB2], w=[dT])
            k.dve(lambda e: e.tensor_tensor(out=fl(TT), in0=fl(TT), in1=B3, op=ALU.add), r=[dTT, dB3], w=[dTT])
        Rm = w["R"].rearrange("p h (two d) -> p h two d", two=2)
        for h in range(4):
            k.pe(lambda e, h=h: e.matmul(B1[:, h * 64:(h + 1) * 64], TT[:, h, :], Rm[:, h, 0, :], start=True, stop=True),
                 r=[dTT, w["d_R"]], w=[dB1])
        k.act(lambda e: e.activation(out=w["u"].rearrange("p a b -> p (a b)"), in_=B1[:, 0:256], func=AF.Copy), r=[dB1], w=[w["d_u"]])
        for h in range(4):
            k.pe(lambda e, h=h: e.matmul(B2[0:64, h * 128:(h + 1) * 128], Rm[:, h, 1, :], TT[:, h, :], start=True, stop=True),
                 r=[dTT, w["d_R"]], w=[dB2])
        k.act(lambda e: e.activation(out=w["wT"].rearrange("p a b -> p (a b)"), in_=B2[0:64, :], func=AF.Copy),
              r=[dB2], w=[w["d_wT"]])
        return w["u"], w["d_u"]

    def unit_scan(dd, n, Xf, dXf, need_out):
        w = W_[dd]
        c0 = n * 128
        B0, dB0 = bank(dd, 0)
        B1, dB1 = bank(dd, 1)
        for h in range(4):
            k.pe(lambda e, h=h: e.matmul(B0[:, h * 64:(h + 1) * 64], w["wT"][:, h, :], w["Sb"][0:64, h, :], start=True, stop=True),
                 r=[w["d_wT"], w["d_Sb"]], w=[dB0])
        if need_out:
            for h in range(4):
                hp = slice(64 * (h % 2), 64 * (h % 2) + 64)
                k.pe(lambda e, h=h, hp=hp: e.matmul(B0[:, 256 + h * 64:256 + (h + 1) * 64], qT[:, h // 2, c0:c0 + 128],
                                                    w["SbQ"][:, h, :], start=True, stop=True),
                     r=[d_qk, w["d_SbQ"]], w=[dB0])
        k.dve(lambda e: e.tensor_tensor(out=w["vn"], in0=Xf, in1=B0[:, 0:256].rearrange("p (h d) -> p h d", h=4), op=ALU.subtract),
              r=[dXf, dB0], w=[w["d_vn"]])
        if need_out:
            for h in range(4):
                k.pe(lambda e, h=h: e.matmul(B1[:, h * 64:(h + 1) * 64], w["QKT"][:, h, :], w["vn"][:, h, :], start=True, stop=True),
                     r=[w["d_QKT"], w["d_vn"]], w=[dB1])
            k.act(lambda e: e.activation(out=w["o2"].rearrange("p a b -> p (a b)"), in_=B1[:, 0:256], func=AF.Copy), r=[dB1], w=[w["d_o2"]])
            k.dve(lambda e: e.tensor_tensor(out=w["ot"], in0=B0[:, 256:512].rearrange("p (h d) -> p h d", h=4),
                                            in1=w["ev"][:, 0:4].unsqueeze(2).to_broadcast([128, 4, 64]), op=ALU.mult),
                  r=[dB0, w["d_ev"]], w=[w["d_ot"]])
            k.pool(lambda e: e.tensor_tensor(out=w["ot"], in0=w["ot"], in1=w["o2"], op=ALU.add), r=[w["d_ot"], w["d_o2"]], w=[w["d_ot"]])
            if first_visit[n]:
                first_visit[n] = False
                k.pool(lambda e: e.tensor_copy(out=otok[:, n, :].rearrange("p (h d) -> p h d", h=4), in_=w["ot"]), r=[w["d_ot"]], w=[d_otok[n]])
            else:
                k.pool(lambda e: e.tensor_tensor(out=otok[:, n, :].rearrange("p (h d) -> p h d", h=4),
                                                 in0=otok[:, n, :].rearrange("p (h d) -> p h d", h=4), in1=w["ot"], op=ALU.add),
                       r=[w["d_ot"], d_otok[n]], w=[d_otok[n]])
        for h in range(4):
            k.pe(lambda e, h=h: e.matmul(B1[:, 256 + h * 64:256 + (h + 1) * 64], w["kd"][:, h, :], w["vn"][:, h, :], start=True, stop=True),
                 r=[w["d_kd"], w["d_vn"]], w=[dB1])
        k.dve(lambda e: e.tensor_tensor(out=w["S"], in0=w["S"], in1=w["ev"][:, 8:12].unsqueeze(2).to_broadcast([128, 4, 64]), op=ALU.mult),
              r=[w["d_S"], w["d_ev"]], w=[w["d_S"]])
        k.dve(lambda e: e.tensor_tensor(out=w["S"], in0=w["S"], in1=B1[:, 256:512].rearrange("p (h d) -> p h d", h=4), op=ALU.add),
              r=[w["d_S"], dB1], w=[w["d_S"]])
        k.act(lambda e: e.activation(out=w["Sb"], in_=w["S"], func=AF.Copy), r=[w["d_S"]], w=[w["d_Sb"]])
        k.pool(lambda e: e.tensor_tensor(out=w["SbQ"], in0=w["S"], in1=hm.unsqueeze(2).to_broadcast([128, 4, 64]), op=ALU.mult),
               r=[w["d_S"], d_hm], w=[w["d_SbQ"]])

    for step in range(1 if _stop.startswith("pre") else (int(_stop[4:].rstrip("p")) if _stop.startswith("step") else 18)):
        pend = []
        for dd in range(2):
            n = (orderF if dd == 0 else orderB)[step]
            need_out = (n < 16) or with_ctx
            Xf, dXf = unit_pre(dd, n)
            pend.append((dd, n, Xf, dXf, need_out))
        if _stop.startswith("pre") or (_stop.startswith("step") and _stop.endswith("p")):
            continue
        if "gdn_u0" in self.dbg and step == 0:
            self.k.barrier()
            chd = k.chan()
            w0 = W_[0]
            Xf0 = pend[0][2]
            for nm, ap_, n_, dt_ in (("D", w0["D"], 512, F32), ("X", Xf0, 256, BF16), ("QKT", w0["QKT"], 512, BF16),
                                     ("A0", w0["A"][1], 512, BF16), ("AT0", w0["AT"][1], 512, BF16), ("kd", w0["kd"], 512, BF16)):
                o_ = self.dout("dbg_" + nm, [128, n_], dt_)
                k.dma("sp", o_[:, :], ap_.rearrange("p a b -> p (a b)"), chd)
            o_ = self.dout("dbg_ev", [128, 12])
            k.dma("sp", o_[:, :], w0["ev"], chd)
            o_ = self.dout("dbg_wT", [64, 512], BF16)
            k.dma("sp", o_[:, :], w0["wT"].rearrange("p a b -> p (a b)"), chd)
            self.k.barrier()
        for (dd, n, Xf, dXf, need_out) in pend:
            unit_scan(dd, n, Xf, dXf, need_out)
    if "gdn_otok" in self.dbg:
        self.k.barrier()
        chd = k.chan()
        o_ = self.dout("dbg_otok", [128, NT * 256])
        k.dma("sp", o_[:, :], otok.rearrange("p a b -> p (a b)"), chd)
        self.k.barrier()
    if _stop:
        return
    _arena_rewind(self, markO)
    _gdn_out(self, l, with_ctx, otok, d_otok)


def _gdn_out(self, l, with_ctx, otok, d_otok):
    k = self.k
    ntt = NT if with_ctx else 16
    _wring_init(self, 1)
    gT = self.aa([128, 2, T], BF16)
    d_g = Dep()
    blocks = _blocks(self, with_ctx)

    def evac(c, m, t0, n, pst, dps):
        ct = (c - 768) // 128
        k.act(lambda e: e.activation(out=gT[:, ct, t0:t0 + n], in_=pst, func=AF.Silu), r=[dps], w=[d_g])
    _proj(self, l, [(768, 256)], blocks, evac, PROJ_BANKS)
    gn = _ppv(self, l, "gnorm")
    sq = self.aa([128, 4, 64])
    ss = self.aa([128, 4])
    on = [self.aa([128, 256], BF16) for _ in range(2)]
    d_sq, d_ss = Dep(), Dep()
    d_on = [Dep(), Dep()]
    psT = self.ps[2][:, 0:512].bitcast(BF16)
    d_psT = self.d_ps[2][0]
    for tt in range(ntt):
        o3 = otok[:, tt, :].rearrange("p (h d) -> p h d", h=4)
        j = tt % 2
        k.dve(lambda e, o3=o3: e.tensor_tensor(out=sq, in0=o3, in1=o3, op=ALU.mult), r=[d_otok[tt]], w=[d_sq])
        k.dve(lambda e: e.reduce_sum(out=ss, in_=sq, axis=mybir.AxisListType.X), r=[d_sq], w=[d_ss])
        k.act(lambda e: e.activation(out=ss, in_=ss, func=AF.Sqrt, bias=EPS, scale=1.0 / 64), r=[d_ss], w=[d_ss])
        k.dve(lambda e: e.reciprocal(out=ss, in_=ss), r=[d_ss], w=[d_ss])
        k.dve(lambda e, o3=o3, j=j: e.tensor_tensor(out=on[j].rearrange("p (h d) -> p h d", h=4), in0=o3,
                                                    in1=ss.unsqueeze(2).to_broadcast([128, 4, 64]), op=ALU.mult),
              r=[d_otok[tt], d_ss], w=[d_on[j]])
        for ct in range(2):
            k.pe(lambda e, j=j, ct=ct: e.transpose(out=psT[:, ct * 128:(ct + 1) * 128], in_=on[j][:, ct * 128:(ct + 1) * 128],
                                                   identity=self.ident_b[:]), r=[d_on[j], self.d_const], w=[d_psT])
        for ct in range(2):
            k.dve(lambda e, ct=ct, tt=tt: e.scalar_tensor_tensor(out=self.C[:, ct, tt * 128:(tt + 1) * 128], in0=psT[:, ct * 128:(ct + 1) * 128],
                                                                 scalar=gn[:, 0:1], in1=gT[:, ct, tt * 128:(tt + 1) * 128],
                                                                 op0=ALU.mult, op1=ALU.mult),
                  r=[d_psT, d_g, self.d_pp], w=[self.d_C[tt]])


from concourse.bass_utils import run_bass_kernel_spmd


def kernel(**inputs):
    inputs = {k_: np.asarray(v) for k_, v in inputs.items()}
    mk = MK()
    nc = mk.build()
    consts = const_inputs(inputs)
    pp = pp_host(inputs)

    def extra(b):
        e = {"pp": pp}
        e.update(consts)
        return e
    maps = host_inputs(mk, inputs, extra=extra)
    n = len(maps)
    res = run_bass_kernel_spmd(nc, maps, core_ids=list(range(n)))
    out = np.stack([np.asarray(res.results[b]["out"]) for b in range(n)], axis=0)
    return out.astype(np.float32)
```

```python
import numpy as np
import ml_dtypes
from contextlib import ExitStack
import concourse.bass as bass
import concourse.mybir as mybir

F32 = mybir.dt.float32
BF16 = mybir.dt.bfloat16
I32 = mybir.dt.int32
AF = mybir.ActivationFunctionType
ALU = mybir.AluOpType

D = 1024
L = 2048
LC = 256
T = L + LC
NT = T // 128
DEPTH = 2
EPS = 1e-6
IN_COLS = 3344
OFF_SC, OFF_HY, OFF_NA = 1040, 1808, 2576


class Dep:
    __slots__ = ("lw", "rd")

    def __init__(self):
        self.lw = None
        self.rd = []


class Chan:
    __slots__ = ("sem", "cnt", "key", "q")

    def __init__(self, sem, key):
        self.sem = sem
        self.cnt = 0
        self.key = key


class K:
    def __init__(self, nc, stack):
        self.nc = nc
        self.stack = stack
        self.eng = {"pe": nc.tensor, "act": nc.scalar, "dve": nc.vector, "pool": nc.gpsimd, "sp": nc.sync}
        self.sems = {}
        self.ecnt = {}
        self.waited = {e: {} for e in self.eng}
        for e in self.eng:
            self.sems["e_" + e] = stack.enter_context(nc.semaphore("e_" + e))
            self.ecnt[e] = 0
        self.chans = []
        self.bar_sem = stack.enter_context(nc.semaphore("bar"))
        self.bar_cnt = 0
        self.nops = 0

    def chan(self):
        key = "c%d" % len(self.chans)
        s = self.stack.enter_context(self.nc.semaphore(key))
        self.sems[key] = s
        c = Chan(s, key)
        c.q = None
        self.chans.append(c)
        return c

    def _wait(self, e, ev):
        key, val, src = ev
        if src == e and e == "pe":
            return
        w = self.waited[e]
        if w.get(key, 0) >= val:
            return
        w[key] = val
        self.eng[e].wait_ge(self.sems[key], val)

    def _deps(self, e, r, w):
        for d in r:
            if d.lw is not None:
                self._wait(e, d.lw)
        for d in w:
            if d.lw is not None and d.lw[2] != e:
                self._wait(e, d.lw)
            for ev in d.rd:
                if ev[2] != e:
                    self._wait(e, ev)

    def _commit(self, ev, r, w):
        for d in w:
            d.lw = ev
            d.rd = []
        for d in r:
            d.rd.append(ev)
            if len(d.rd) > 48:
                best = {}
                for x in d.rd:
                    if x[0] not in best or best[x[0]][1] < x[1]:
                        best[x[0]] = x
                d.rd = list(best.values())

    def op(self, e, fn, r=(), w=()):
        self._deps(e, r, w)
        ins = fn(self.eng[e])
        self.ecnt[e] += 1
        ins.then_inc(self.sems["e_" + e], 1)
        self._commit(("e_" + e, self.ecnt[e], e), r, w)
        self.nops += 1
        return ins

    def pe(self, fn, r=(), w=()):
        return self.op("pe", fn, r, w)

    def act(self, fn, r=(), w=()):
        return self.op("act", fn, r, w)

    def dve(self, fn, r=(), w=()):
        return self.op("dve", fn, r, w)

    def pool(self, fn, r=(), w=()):
        return self.op("pool", fn, r, w)

    def dma(self, q, out, in_, ch, r=(), w=(), **kw):
        self._deps(q, r, w)
        ins = self.eng[q].dma_start(out=out, in_=in_, **kw)
        ch.cnt += 16
        ch.q = q
        ins.then_inc(ch.sem, 16)
        self._commit((ch.key, ch.cnt, "dma"), r, w)
        self.nops += 1
        return ins

    def barrier(self):
        for e in self.eng:
            if self.ecnt[e] > 0:
                self._wait(e, ("e_" + e, self.ecnt[e], "self"))
        for c in self.chans:
            if c.cnt > 0:
                self._wait(c.q or "sp", (c.key, c.cnt, "dma"))
        for e in self.eng:
            self.eng[e].sem_inc(self.bar_sem, 1)
        self.bar_cnt += len(self.eng)
        for e in self.eng:
            self.eng[e].wait_ge(self.bar_sem, self.bar_cnt)
            for e2 in self.eng:
                self.waited[e]["e_" + e2] = self.ecnt[e2]
            for c in self.chans:
                self.waited[e][c.key] = c.cnt


class MK:
    def __init__(self, dbg=None, layers=(0, 1), inject_cat=False, mixers=("gdn", "sc", "hy", "na"), do_mlp=True,
                 phases=("mod", "p1", "mix", "p3", "p4"), inject_h=False):
        self.phases = phases
        self.inject_h = inject_h
        self.dbg = dbg or {}
        self.layers = layers
        self.inject_cat = inject_cat
        self.mixers = mixers
        self.do_mlp = do_mlp
        self.inputs = {}
        self.outputs = {}

    def din(self, name, shape, dtype=F32):
        t = self.nc.dram_tensor(name, list(shape), dtype, kind="ExternalInput").ap()
        self.inputs[name] = (tuple(shape), dtype)
        return t

    def dout(self, name, shape, dtype=F32):
        t = self.nc.dram_tensor(name, list(shape), dtype, kind="ExternalOutput").ap()
        self.outputs[name] = (tuple(shape), dtype)
        return t

    def W(self, name):
        if name not in self._w:
            self._w[name] = self.din(name, self._wshape[name])
        return self._w[name]

    def sb(self, st, name, shape, dtype=F32):
        self._n = getattr(self, "_n", 0) + 1
        return st.enter_context(self.nc.sbuf_tensor("%s_%d" % (name, self._n), list(shape), dtype))

    def build(self):
        nc = bass.Bass("TRN2", target_bir_lowering=False)
        self.nc = nc
        with ExitStack() as st:
            self.st = st
            self.k = K(nc, st)
            self._declare()
            self._consts()
            for l in self.layers:
                self._layer(l)
            self._finish()
        return nc

    def _declare(self):
        nc = self.nc
        self._wshape = {}
        self._w = {}
        self.x_in = self.din("x", [L, D])
        self.ctx_in = self.din("ctx", [LC, D])
        self.cvec_in = self.din("cvec", [128, 16])
        self._wshape["ada_w"] = [DEPTH, D, 6 * D]
        self.ada_bT = self.din("ada_bT", [128, DEPTH * 48])
        self.gains_in = self.din("gains", [128, 4 * DEPTH * 8])
        self._wshape["w_in"] = [DEPTH, D, IN_COLS]
        self._wshape["w_out"] = [DEPTH, D, D]
        self._wshape["mlp_w1"] = [DEPTH, D, 4 * D]
        self._wshape["mlp_w2"] = [DEPTH, 4 * D, D]
        self.ident_f_in = self.din("ident_f", [128, 128])
        self.ident_b_in = self.din("ident_b", [128, 128], BF16)
        self.out = self.dout("out", [L, D])
        self.xres = nc.dram_tensor("xres", [T, D], F32).ap()
        self.d_xres = [Dep() for _ in range(NT)]
        self.d_out = [Dep() for _ in range(NT)]
        if self.inject_cat:
            self.cat_in = self.din("cat_in", [D, T], BF16)
        self.ps = [self.st.enter_context(nc.psum_tensor("ps%d" % i, [128, 1024], F32)) for i in range(4)]
        self.d_ps = [[Dep(), Dep()] for _ in range(4)]

    def _consts(self):
        k, st = self.k, self.st
        sb = lambda n, s, d=F32: self.sb(st, n, s, d)
        self.ident_f = sb("ident_f", [128, 128])
        self.ident_b = sb("ident_b", [128, 128], BF16)
        self.ones_f = sb("ones_f", [128, 128])
        self.cvec = sb("cvec", [128, 16])
        self.adab = sb("adab", [128, DEPTH * 48])
        self.gains = sb("gains", [128, 4 * DEPTH * 8])
        self.d_const = Dep()
        ch = k.chan()
        k.dma("sp", self.ident_f[:], self.ident_f_in[:, :], ch, w=[self.d_const])
        k.dma("sp", self.ident_b[:], self.ident_b_in[:, :], ch, w=[self.d_const])
        k.dma("sp", self.cvec[:], self.cvec_in[:, :], ch, w=[self.d_const])
        k.dma("sp", self.adab[:], self.ada_bT[:, :], ch, w=[self.d_const])
        k.dma("sp", self.gains[:], self.gains_in[:, :], ch, w=[self.d_const])
        k.dve(lambda e: e.memset(self.ones_f[:], 1.0), w=[self.d_const])
        self.H = sb("H", [128, 8, T], BF16)
        self.C = sb("C", [128, 8, T], BF16)
        self.d_H = [Dep() for _ in range(NT)]
        self.d_C = [Dep() for _ in range(NT)]
        self.modT = sb("modT", [128, 48, 2])
        self.A1 = sb("A1", [128, 8, 2])
        self.A2 = sb("A2", [128, 8, 2])
        self.G1f = sb("G1f", [128, 8, 2])
        self.G2f = sb("G2f", [128, 8, 2])
        self.Gbc = sb("Gbc", [128, 2, 2, D])
        self.d_mod = Dep()
        self.d_gbc = Dep()
        self.NAR = 28416
        self.AR = sb("arena", [128, self.NAR])
        self.ar_off = 0
        self.NXT = 3
        self.d_xt = [Dep() for _ in range(self.NXT)]
        self.ch_xt_ld = [k.chan() for _ in range(self.NXT)]
        self.ch_xt_st = [k.chan() for _ in range(self.NXT)]
        self.d_xn = [Dep(), Dep()]
        self.d_junk = Dep()
        self.d_stat = [Dep() for _ in range(4)]
        self.d_tmpb = [Dep(), Dep()]
        self.d_tmpf = [Dep(), Dep()]
        self.xt_i = self.xn_i = self.stat_i = self.tmp_i = 0

    def arena_reset(self):
        self.k.barrier()
        self.ar_off = 0

    def aa(self, shape, dtype=F32):
        esz = 4 if dtype in (F32, I32) else 2
        n = int(np.prod(shape[1:]))
        nbytes = (n * esz + 31) // 32 * 32
        o = self.ar_off
        assert o + nbytes <= self.NAR * 4, "arena overflow %d" % (o + nbytes)
        self.ar_off = o + nbytes
        v = self.AR[:, o // 4:(o + nbytes) // 4]
        if dtype != F32:
            v = v.bitcast(dtype)
        v = v[:, 0:n]
        if len(shape) == 3:
            v = v.rearrange("p (a b) -> p a b", b=shape[2])
        elif len(shape) == 4:
            v = v.rearrange("p (a b c) -> p a b c", b=shape[2], c=shape[3])
        if shape[0] != 128:
            v = v[0:shape[0]]
        return v

    def _staging(self, norm=True):
        self.xt = [self.aa([128, D]) for i in range(self.NXT)]
        self.junk = self.aa([128, D], BF16)
        self.stat = [self.aa([128, 8]) for i in range(4)]
        self.tmpf = [self.aa([128, D]) for i in range(2)]
        if norm:
            self.xn = [self.aa([128, D], BF16) for i in range(2)]
            self.tmpb = [self.aa([128, D], BF16) for i in range(2)]

    def gain(self, kind, l):
        o = (kind * DEPTH + l) * 8
        return self.gains[:, o:o + 8]

    def _layer(self, l):
        ph = self.phases
        if "mod" in ph:
            self._modulation(l)
        if "p1" in ph:
            self.arena_reset()
            self._staging()
            for tt in range(NT):
                s = 0 if tt < 16 else 1
                xi = self._load_x(l, tt, first=True)
                self._norm_tile(xi, s, self.A1, self.modT[:, 0:8, :], self.H, tt, self.d_H[tt])
        elif self.inject_h:
            ch = self.k.chan()
            hin = self.din("h_in", [D, T], BF16)
            for tt in range(NT):
                self.k.dma("sp", self.H[:, :, tt * 128:(tt + 1) * 128],
                           hin[:, tt * 128:(tt + 1) * 128].rearrange("(a p) t -> p a t", p=128), ch, w=[self.d_H[tt]])
        if "hx" in self.dbg and self.dbg["hx"] == l:
            self._dump_feat("dbg_hx", self.H, self.d_H)
        if "mix" in ph:
            self._mixers(l)
        if "cat" in self.dbg and self.dbg["cat"] == l:
            self._dump_feat("dbg_cat", self.C, self.d_C)
        if "p3" in ph:
            self._p3(l)
        if "hx2" in self.dbg and self.dbg["hx2"] == l:
            self._dump_feat("dbg_hx2", self.C, self.d_C)
        if "p4" in ph:
            self._p4(l)

    def _xsrc(self, l, tt, first):
        if l == self.layers[0] and l == 0 and first:
            if tt < 16:
                return self.x_in[tt * 128:(tt + 1) * 128, :], None
            return self.ctx_in[(tt - 16) * 128:(tt - 15) * 128, :], None
        return self.xres[tt * 128:(tt + 1) * 128, :], self.d_xres[tt]

    def _load_x(self, l, tt, first):
        k = self.k
        i = self.xt_i
        self.xt_i = (i + 1) % self.NXT
        src, dep = self._xsrc(l, tt, first)
        k.dma("sp", self.xt[i][:], src, self.ch_xt_ld[i], r=[dep] if dep else [], w=[self.d_xt[i]])
        return i

    def _store_x(self, i, dst_ap, dst_dep):
        self.k.dma("sp", dst_ap, self.xt[i][:], self.ch_xt_st[i], r=[self.d_xt[i]], w=[dst_dep])

    def _rstd(self, src_ap, src_deps):
        k = self.k
        j = self.stat_i
        self.stat_i = (j + 1) % 4
        stt, dst = self.stat[j], self.d_stat[j]
        k.act(lambda e: e.activation(out=self.junk[:], in_=src_ap, func=AF.Square, accum_out=stt[:, 0:1]),
              r=src_deps, w=[self.d_junk, dst])
        k.act(lambda e: e.activation(out=stt[:, 1:2], in_=stt[:, 0:1], func=AF.Sqrt, bias=EPS, scale=1.0 / D),
              r=[dst], w=[dst])
        k.dve(lambda e: e.reciprocal(out=stt[:, 2:3], in_=stt[:, 1:2]), r=[dst], w=[dst])
        return stt[:, 2:3], dst

    def _norm_tile(self, xi, s, A, B, Hbuf, tt, dH):
        k = self.k
        xt, dxt = self.xt[xi], self.d_xt[xi]
        rs, drs = self._rstd(xt[:], [dxt])
        j = self.xn_i
        self.xn_i = 1 - j
        xn, dxn = self.xn[j], self.d_xn[j]
        k.dve(lambda e: e.tensor_scalar(out=xn[:], in0=xt[:], scalar1=rs, scalar2=None, op0=ALU.mult),
              r=[dxt, drs], w=[dxn])
        pi = 3
        psb = self.ps[pi][:, 0:512].bitcast(BF16)
        dps = self.d_ps[pi][0]
        for dt in range(8):
            k.pe(lambda e, dt=dt: e.transpose(out=psb[:, dt * 128:(dt + 1) * 128], in_=xn[:, dt * 128:(dt + 1) * 128],
                                              identity=self.ident_b[:]),
                 r=[dxn, self.d_const], w=[dps])
        ti = self.tmp_i
        self.tmp_i = 1 - ti
        tb, dtb = self.tmpb[ti], self.d_tmpb[ti]
        k.dve(lambda e: e.tensor_tensor(out=tb[:].rearrange("p (a b) -> p a b", a=8),
                                        in0=psb.rearrange("p (a b) -> p a b", a=8),
                                        in1=A[:, :, s:s + 1].to_broadcast([128, 8, 128]), op=ALU.mult),
              r=[dps, self.d_mod], w=[dtb])
        k.pool(lambda e: e.tensor_tensor(out=Hbuf[:, :, tt * 128:(tt + 1) * 128],
                                         in0=tb[:].rearrange("p (a b) -> p a b", a=8),
                                         in1=B[:, :, s:s + 1].to_broadcast([128, 8, 128]), op=ALU.add),
               r=[dtb, self.d_mod], w=[dH])

    def _modulation(self, l):
        k = self.k
        self.arena_reset()
        if True:
            sT = self.aa([128, 16], BF16)
            d_sT = Dep()
            k.act(lambda e: e.activation(out=sT[:], in_=self.cvec[:], func=AF.Silu), r=[self.d_const], w=[d_sT])
            wb = [self.aa([128, 8, 512], BF16) for i in range(2)]
            dwb = [Dep(), Dep()]
            chw = [k.chan(), k.chan()]
            mod_ps = self.ps[0][:, 0:96]
            dps = self.d_ps[0][0]
            for g in range(12):
                i = g % 2
                src = self.W("ada_w")[l, :, g * 512:(g + 1) * 512].rearrange("(kt p) c -> p kt c", p=128)
                k.dma("pool", wb[i][:], src, chw[i], w=[dwb[i]])
                for jj in range(4):
                    jt = g * 4 + jj
                    for kt in range(8):
                        k.pe(lambda e, i=i, jj=jj, jt=jt, kt=kt: e.matmul(
                            mod_ps[:, jt * 2:jt * 2 + 2], wb[i][:, kt, jj * 128:(jj + 1) * 128],
                            sT[:, kt * 2:kt * 2 + 2], start=(kt == 0), stop=(kt == 7)),
                            r=[dwb[i], d_sT], w=[dps])
            dm = self.d_mod
            k.dve(lambda e: e.tensor_tensor(out=self.modT[:], in0=mod_ps.rearrange("p (a b) -> p a b", b=2),
                                            in1=self.adab[:, l * 48:(l + 1) * 48].unsqueeze(2).to_broadcast([128, 48, 2]),
                                            op=ALU.add), r=[dps, self.d_const], w=[dm])
            g = lambda kind: self.gain(kind, l).unsqueeze(2).to_broadcast([128, 8, 2])
            k.dve(lambda e: e.scalar_tensor_tensor(out=self.A1[:], in0=self.modT[:, 8:16, :], scalar=1.0, in1=g(0),
                                                   op0=ALU.add, op1=ALU.mult), r=[dm, self.d_const], w=[dm])
            k.dve(lambda e: e.scalar_tensor_tensor(out=self.A2[:], in0=self.modT[:, 32:40, :], scalar=1.0, in1=g(2),
                                                   op0=ALU.add, op1=ALU.mult), r=[dm, self.d_const], w=[dm])
            k.dve(lambda e: e.tensor_tensor(out=self.G1f[:], in0=self.modT[:, 16:24, :], in1=g(1), op=ALU.mult),
                  r=[dm, self.d_const], w=[dm])
            k.dve(lambda e: e.tensor_tensor(out=self.G2f[:], in0=self.modT[:, 40:48, :], in1=g(3), op=ALU.mult),
                  r=[dm, self.d_const], w=[dm])
            diag = [self.aa([128, 128]) for i in range(2)]
            ddiag = [Dep(), Dep()]
            n = 0
            for kind, Gf in enumerate((self.G1f, self.G2f)):
                for s in range(2):
                    for dt in range(8):
                        i = n % 2
                        n += 1
                        k.dve(lambda e, i=i, Gf=Gf, dt=dt, s=s: e.tensor_scalar(
                            out=diag[i][:], in0=self.ident_f[:], scalar1=Gf[:, dt, s:s + 1], scalar2=None, op0=ALU.mult),
                            r=[dm, self.d_const], w=[ddiag[i]])
                        pi = 1 + (n % 2)
                        pst = self.ps[pi][:, 0:128]
                        k.pe(lambda e, i=i, pst=pst: e.matmul(pst, self.ones_f[:], diag[i][:], start=True, stop=True),
                             r=[ddiag[i], self.d_const], w=[self.d_ps[pi][0]])
                        k.act(lambda e, pst=pst, kind=kind, s=s, dt=dt: e.activation(
                            out=self.Gbc[:, kind, s, dt * 128:(dt + 1) * 128], in_=pst, func=AF.Copy),
                            r=[self.d_ps[pi][0]], w=[self.d_gbc])
        if "mod" in self.dbg and self.dbg["mod"] == l:
            o = self.dout("dbg_mod", [128, 96])
            ch = k.chan()
            k.dma("sp", o[:, :], self.modT[:].rearrange("p a b -> p (a b)"), ch, r=[self.d_mod])
            o2 = self.dout("dbg_gbc", [128, 4 * D])
            k.dma("sp", o2[:, :], self.Gbc[:].rearrange("p a b c -> p (a b c)"), ch, r=[self.d_gbc])

    def _mixers(self, l):
        k = self.k
        if self.inject_cat:
            ch = k.chan()
            for tt in range(NT):
                k.dma("sp", self.C[:, :, tt * 128:(tt + 1) * 128],
                      self.cat_in[:, tt * 128:(tt + 1) * 128].rearrange("(a p) t -> p a t", p=128), ch, w=[self.d_C[tt]])
            return
        raise NotImplementedError

    def _p3(self, l):
        k = self.k
        last = (l == DEPTH - 1)
        ntt = 16 if last else NT
        self.arena_reset()
        self._staging()
        if True:
            wo = self.aa([128, 8, D], BF16)
            dwo = Dep()
            ch = k.chan()
            k.dma("pool", wo[:], self.W("w_out")[l].rearrange("(kt p) c -> p kt c", p=128), ch, w=[dwo])
            for tt in range(ntt):
                s = 0 if tt < 16 else 1
                pi = tt % 3
                yps = self.ps[pi]
                for half in range(2):
                    for mt in range(8):
                        k.pe(lambda e, half=half, mt=mt, yps=yps, tt=tt: e.matmul(
                            yps[:, half * 512:(half + 1) * 512], self.C[:, mt, tt * 128:(tt + 1) * 128],
                            wo[:, mt, half * 512:(half + 1) * 512], start=(mt == 0), stop=(mt == 7)),
                            r=[self.d_C[tt], dwo], w=[self.d_ps[pi][half]])
                xi = self._load_x(l, tt, first=True)
                self._resid_update(xi, yps[:], self.d_ps[pi], 0, s)
                self._store_x(xi, self.xres[tt * 128:(tt + 1) * 128, :], self.d_xres[tt])
                self._norm_tile(xi, s, self.A2, self.modT[:, 24:32, :], self.C, tt, self.d_C[tt])

    def _resid_update(self, xi, y_ap, y_deps, kind, s):
        k = self.k
        rs, drs = self._rstd(y_ap, list(y_deps))
        ti = self.tmp_i
        self.tmp_i = 1 - ti
        tf, dtf = self.tmpf[ti], self.d_tmpf[ti]
        k.dve(lambda e: e.scalar_tensor_tensor(out=tf[:], in0=y_ap, scalar=rs, in1=self.Gbc[:, kind, s, :],
                                               op0=ALU.mult, op1=ALU.mult),
              r=list(y_deps) + [drs, self.d_gbc], w=[dtf])
        xt, dxt = self.xt[xi], self.d_xt[xi]
        k.dve(lambda e: e.tensor_tensor(out=xt[:], in0=xt[:], in1=tf[:], op=ALU.add), r=[dtf, dxt], w=[dxt])

    def _p4(self, l):
        k = self.k
        last = (l == DEPTH - 1)
        if last:
            sblocks = [(0, 768), (768, 768), (1536, 512)]
        else:
            sblocks = [(0, 768), (768, 768), (1536, 768)]
        self.arena_reset()
        self._staging(norm=False)
        if True:
            hT = self.aa([128, 32, 768], BF16)
            d_hT = [[Dep() for _ in range(2)] for _ in range(32)]
            HF = self.H[:].rearrange("p a t -> p (a t)")
            w1c = [HF[:, i * 4096:(i + 1) * 4096].rearrange("p (k c) -> p k c", c=512) for i in range(3)]
            d_w1c = [Dep(), Dep(), Dep()]
            ch_w1 = [k.chan(), k.chan(), k.chan()]
            w2c = [HF[:, 12288 + i * 2048: 12288 + (i + 1) * 2048].rearrange("p (k c) -> p k c", c=512) for i in range(3)]
            d_w2c = [Dep() for _ in range(3)]
            ch_w2 = [k.chan() for _ in range(3)]
            rl = [self.aa([128, 384], BF16) for i in range(2)]
            d_rl = [Dep(), Dep()]
            ytok = self.aa([128, 6, D])
            d_ytok = [Dep() for _ in range(6)]
            n_w1 = 0
            n_w2 = 0
            n_rl = 0
            for (t0, n) in sblocks:
                n2 = n // 2
                ntl = n // 128
                for ffc in range(8):
                    i = n_w1 % 3
                    n_w1 += 1
                    k.dma("pool", w1c[i], self.W("mlp_w1")[l, :, ffc * 512:(ffc + 1) * 512].rearrange("(kt p) c -> p kt c", p=128),
                          ch_w1[i], w=[d_w1c[i]])
                    for f in range(4):
                        fft = ffc * 4 + f
                        for sbk in range(2):
                            hps = self.ps[3][:, sbk * 512: sbk * 512 + n2]
                            dhps = self.d_ps[3][sbk]
                            tts = range((t0 + sbk * n2) // 128, (t0 + (sbk + 1) * n2 + 127) // 128)
                            rdeps = [self.d_C[t] for t in tts]
                            for dt in range(8):
                                k.pe(lambda e, i=i, f=f, dt=dt, hps=hps, sbk=sbk: e.matmul(
                                    hps, w1c[i][:, dt, f * 128:(f + 1) * 128],
                                    self.C[:, dt, t0 + sbk * n2: t0 + (sbk + 1) * n2], start=(dt == 0), stop=(dt == 7)),
                                    r=[d_w1c[i]] + rdeps, w=[dhps])
                            j = n_rl % 2
                            n_rl += 1
                            k.act(lambda e, j=j, hps=hps: e.activation(out=rl[j][:, 0:n2], in_=hps, func=AF.Relu),
                                  r=[dhps], w=[d_rl[j]])
                            k.dve(lambda e, j=j, fft=fft, sbk=sbk: e.tensor_tensor(
                                out=hT[:, fft, sbk * n2:(sbk + 1) * n2], in0=rl[j][:, 0:n2], in1=rl[j][:, 0:n2], op=ALU.mult),
                                r=[d_rl[j]], w=[d_hT[fft][sbk]])
                for dh in range(2):
                    for ffc in range(8):
                        i = n_w2 % 3
                        n_w2 += 1
                        k.dma("pool", w2c[i],
                              self.W("mlp_w2")[l, ffc * 512:(ffc + 1) * 512, dh * 512:(dh + 1) * 512].rearrange("(f p) c -> p f c", p=128),
                              ch_w2[i], w=[d_w2c[i]])
                        for f in range(4):
                            fft = ffc * 4 + f
                            for tl in range(ntl):
                                pi, hb = tl // 2, tl % 2
                                sbk = (tl * 128) // n2
                                k.pe(lambda e, i=i, f=f, fft=fft, tl=tl, pi=pi, hb=hb: e.matmul(
                                    self.ps[pi][:, hb * 512:(hb + 1) * 512], hT[:, fft, tl * 128:(tl + 1) * 128],
                                    w2c[i][:, f, :], start=(fft == 0), stop=(fft == 31)),
                                    r=[d_w2c[i], d_hT[fft][sbk]], w=[self.d_ps[pi][hb]])
                    for tl in range(ntl):
                        pi, hb = tl // 2, tl % 2
                        k.act(lambda e, tl=tl, pi=pi, hb=hb, dh=dh: e.activation(
                            out=ytok[:, tl, dh * 512:(dh + 1) * 512], in_=self.ps[pi][:, hb * 512:(hb + 1) * 512], func=AF.Copy),
                            r=[self.d_ps[pi][hb]], w=[d_ytok[tl]])
                for tl in range(ntl):
                    tt = t0 // 128 + tl
                    s = 0 if tt < 16 else 1
                    xi = self._load_x(l, tt, first=False)
                    self._resid_update(xi, ytok[:, tl, :], [d_ytok[tl]], 1, s)
                    if last:
                        self._store_x(xi, self.out[tt * 128:(tt + 1) * 128, :], self.d_out[tt])
                    else:
                        self._store_x(xi, self.xres[tt * 128:(tt + 1) * 128, :], self.d_xres[tt])

    def _dump_feat(self, name, buf, deps):
        k = self.k
        o = self.dout(name, [D, T])
        self.arena_reset()
        if True:
            stg = self.aa([128, 8, 128])
            dst = Dep()
            ch = k.chan()
            for tt in range(NT):
                k.dve(lambda e, tt=tt: e.tensor_copy(out=stg[:], in_=buf[:, :, tt * 128:(tt + 1) * 128]), r=[deps[tt]], w=[dst])
                k.dma("sp", o[:, tt * 128:(tt + 1) * 128].rearrange("(a p) t -> p a t", p=128), stg[:], ch, r=[dst])

    def _finish(self):
        k = self.k
        if "xres" in self.dbg:
            self.arena_reset()
            self._staging()
            o = self.dout("dbg_xres", [T, D])
            ch = k.chan()
            for tt in range(NT):
                xi = self._load_x(1, tt, first=False)
                self._store_x(xi, o[tt * 128:(tt + 1) * 128, :], Dep())
        k.barrier()


def host_inputs(mk, inputs, extra=None):
    bf = ml_dtypes.bfloat16
    f32 = np.float32
    shared = {}
    shared["ada_w"] = np.ascontiguousarray(inputs["ada_w"], dtype=f32)
    shared["ada_bT"] = np.ascontiguousarray(
        inputs["ada_b"].reshape(DEPTH, 48, 128).transpose(2, 0, 1).reshape(128, DEPTH * 48), dtype=f32)
    g = np.stack([inputs["norm_pre_mix"], inputs["norm_post_mix"], inputs["norm_pre_mlp"], inputs["norm_post_mlp"]])
    shared["gains"] = np.ascontiguousarray(g.reshape(4, DEPTH, 8, 128).transpose(3, 0, 1, 2).reshape(128, -1), dtype=f32)
    for n in ("w_in", "w_out", "mlp_w1", "mlp_w2"):
        shared[n] = np.ascontiguousarray(inputs[n], dtype=f32)
    shared["ident_f"] = np.eye(128, dtype=f32)
    shared["ident_b"] = np.eye(128, dtype=f32).astype(bf)
    maps = []
    for b in range(inputs["x"].shape[0]):
        m = dict(shared)
        m["x"] = np.ascontiguousarray(inputs["x"][b], dtype=f32)
        m["ctx"] = np.ascontiguousarray(inputs["ctx"][b], dtype=f32)
        cv = np.stack([inputs["c"][b].reshape(8, 128), inputs["c_ctx"].reshape(8, 128)], axis=-1)
        m["cvec"] = np.ascontiguousarray(cv.transpose(1, 0, 2).reshape(128, 16), dtype=f32)
        if extra:
            m.update(extra(b))
        maps.append({kk: v for kk, v in m.items() if kk in mk.inputs})
    return maps


PP_ENTRIES = [("scw", 6), ("hyw", 18), ("gdw", 18), ("hybias", 2), ("gnorm", 1), ("hy_w1", 64), ("hy_w2", 64),
              ("hy_w3", 64), ("hy_w4", 512), ("hy_b", 3), ("hy_f", 3), ("alog", 8), ("dtb", 8)]
PP_OFF = {}
_o = 0
for _n, _w in PP_ENTRIES:
    PP_OFF[_n] = (_o, _w)
    _o += _w
PP_W = _o


def pp_host(inputs):
    pp = np.zeros((128, DEPTH * PP_W), np.float32)
    for l in range(DEPTH):
        def put(name, arr):
            o, w = PP_OFF[name]
            arr = np.asarray(arr, np.float32)
            assert arr.shape[1] == w, (name, arr.shape)
            pp[:arr.shape[0], l * PP_W + o: l * PP_W + o + w] = arr
        put("scw", inputs["sc_conv"][l].reshape(3, 2, 128).transpose(2, 1, 0).reshape(128, 6))
        put("hyw", inputs["hy_conv"][l].reshape(3, 6, 128).transpose(2, 1, 0).reshape(128, 18))
        put("gdw", inputs["gdn_conv"][l].reshape(3, 6, 128).transpose(2, 1, 0).reshape(128, 18))
        put("hybias", inputs["hy_bias"][l].reshape(2, 128).T)
        put("gnorm", np.tile(inputs["gdn_norm"][l], 2).reshape(128, 1))
        put("hy_w1", inputs["hy_w1"][l])
        put("hy_w2", inputs["hy_w2"][l])
        put("hy_w3", inputs["hy_w3"][l])
        put("hy_w4", inputs["hy_w4"][l])
        put("hy_b", np.stack([inputs["hy_b1"][l], inputs["hy_b2"][l], inputs["hy_b3"][l]], axis=1))
        put("hy_f", inputs["hy_freq"][l].T)
        put("alog", np.tile(inputs["gdn_a_log"][l].reshape(1, 8), (128, 1)))
        put("dtb", np.tile(inputs["gdn_dt_bias"][l].reshape(1, 8), (128, 1)))
    return pp


def na_consts(inputs):
    rpb = np.asarray(inputs["na_rpb"], np.float32)
    par = np.arange(2)[:, None, None, None]
    kc = np.arange(64)[None, :, None, None]
    i = np.arange(14)[None, None, :, None]
    qc = np.arange(64)[None, None, None, :]
    dc = np.clip(kc - qc, -15, 15) + 15
    di = np.broadcast_to(i + par, (2, 64, 14, 64))
    dcb = np.broadcast_to(dc, (2, 64, 14, 64))
    g = rpb[:, :, di, dcb]
    g = g.transpose(0, 2, 3, 1, 4, 5).reshape(DEPTH, 128, 4 * 14 * 64)
    cs = np.clip(np.arange(64) - 8, 0, 48)
    kcv = np.arange(64)[:, None]
    valid = (kcv >= cs[None, :]) & (kcv < cs[None, :] + 16)
    m = np.where(valid, 0.0, -1e30).astype(np.float32)
    mask = np.concatenate([m, m], axis=0)
    return np.ascontiguousarray(g), np.ascontiguousarray(mask)


def _mix_common_init(self):
    if getattr(self, "pp", None) is not None:
        return
    k = self.k
    self.pp_in = self.din("pp", [128, DEPTH * PP_W])
    self.pp = self.sb(self.st, "pp", [128, DEPTH * PP_W])
    self.d_pp = Dep()
    ch = k.chan()
    k.dma("sp", self.pp[:], self.pp_in[:, :], ch, w=[self.d_pp])


def _ppv(self, l, name, rows=128):
    o, w = PP_OFF[name]
    return self.pp[0:rows, l * PP_W + o: l * PP_W + o + w]


def _blocks(self, with_ctx):
    b = [(i * 512, 512) for i in range(4)]
    if with_ctx:
        b.append((L, LC))
    return b


def _wring_init(self, n=2):
    self.wring = [(self.aa([128, 8, 512], BF16), Dep(), self.wring_ch[i]) for i in range(n)]
    self.wring_i = 0
    self.bank_i = 0


def _proj(self, l, chunks, blocks, evac, banks):
    k = self.k
    for (c0, ncol) in chunks:
        i = self.wring_i
        self.wring_i = (i + 1) % len(self.wring)
        wap, dw, chw = self.wring[i]
        k.dma("pool", wap[:, :, 0:ncol], self.W("w_in")[l, :, c0:c0 + ncol].rearrange("(kt p) c -> p kt c", p=128),
              chw, w=[dw])
        for cc in range(0, ncol, 128):
            m = min(128, ncol - cc)
            for (t0, n) in blocks:
                pi, hb = banks[self.bank_i % len(banks)]
                self.bank_i += 1
                pst = self.ps[pi][0:m, hb * 512: hb * 512 + n]
                dps = self.d_ps[pi][hb]
                hd = [self.d_H[t] for t in range(t0 // 128, (t0 + n + 127) // 128)]
                for dt in range(8):
                    k.pe(lambda e, dt=dt, pst=pst, wap=wap, cc=cc, m=m, t0=t0, n=n: e.matmul(
                        pst, wap[:, dt, cc:cc + m], self.H[:, dt, t0:t0 + n], start=(dt == 0), stop=(dt == 7)),
                        r=[dw] + hd, w=[dps])
                evac(c0 + cc, m, t0, n, pst, dps)


def _dwconv(self, eng, out_ap, in_ap, w3, ranges, r, w):
    k = self.k
    for (a, b) in ranges:
        k.op(eng, lambda e, a=a, b=b: e.tensor_scalar(out=out_ap[:, a:b], in0=in_ap[:, a:b], scalar1=w3[:, 1:2],
                                                      scalar2=None, op0=ALU.mult), r=r, w=w)
        k.op(eng, lambda e, a=a, b=b: e.scalar_tensor_tensor(out=out_ap[:, a + 1:b], in0=in_ap[:, a:b - 1], scalar=w3[:, 0:1],
                                                             in1=out_ap[:, a + 1:b], op0=ALU.mult, op1=ALU.add),
             r=list(r) + list(w), w=w)
        k.op(eng, lambda e, a=a, b=b: e.scalar_tensor_tensor(out=out_ap[:, a:b - 1], in0=in_ap[:, a + 1:b], scalar=w3[:, 2:3],
                                                             in1=out_ap[:, a:b - 1], op0=ALU.mult, op1=ALU.add),
             r=list(r) + list(w), w=w)


PROJ_BANKS = [(0, 0), (0, 1), (1, 0), (1, 1)]


def _mix_sc(self, l, with_ctx):
    k = self.k
    self.arena_reset()
    _wring_init(self)
    ranges = [(0, L)] + ([(L, T)] if with_ctx else [])
    blocks = _blocks(self, with_ctx)
    ntok = T if with_ctx else L
    ntt = ntok // 128
    pxb = self.aa([128, 6, T], BF16)
    d_px = [Dep() for _ in range(6)]

    def evac(c, m, t0, n, pst, dps):
        ct = (c - OFF_SC) // 128
        k.act(lambda e: e.activation(out=pxb[:, ct, t0:t0 + n], in_=pst, func=AF.Copy), r=[dps], w=[d_px[ct]])
    _proj(self, l, [(OFF_SC, 512), (OFF_SC + 512, 256)], blocks, evac, PROJ_BANKS)
    z = [self.aa([128, T]) for _ in range(2)]
    acc = [self.aa([128, T]) for _ in range(2)]
    scw = _ppv(self, l, "scw")
    for j in range(2):
        dz, dacc = Dep(), Dep()
        eng = "dve" if j == 0 else "pool"
        k.op(eng, lambda e, j=j: e.tensor_tensor(out=z[j][:, 0:ntok], in0=pxb[:, 2 + j, 0:ntok], in1=pxb[:, 4 + j, 0:ntok],
                                                 op=ALU.mult), r=[d_px[2 + j], d_px[4 + j]], w=[dz])
        _dwconv(self, "dve", acc[j], z[j], scw[:, j * 3:(j + 1) * 3], ranges, [dz, self.d_pp], [dacc])
        k.op(eng, lambda e, j=j: e.tensor_tensor(out=self.C[:, 2 + j, 0:ntok], in0=pxb[:, j, 0:ntok], in1=acc[j][:, 0:ntok],
                                                 op=ALU.mult), r=[d_px[j], dacc], w=[self.d_C[t] for t in range(ntt)])


def _mixers(self, l):
    k = self.k
    if self.inject_cat:
        ch = k.chan()
        for tt in range(NT):
            k.dma("sp", self.C[:, :, tt * 128:(tt + 1) * 128],
                  self.cat_in[:, tt * 128:(tt + 1) * 128].rearrange("(a p) t -> p a t", p=128), ch, w=[self.d_C[tt]])
        return
    _mix_common_init(self)
    if not hasattr(self, "wring_ch"):
        self.wring_ch = [k.chan() for _ in range(3)]
    with_ctx = l < DEPTH - 1
    if "sc" in self.mixers:
        _mix_sc(self, l, with_ctx)
    if "na" in self.mixers:
        _mix_na(self, l, with_ctx)
    if "hy" in self.mixers:
        _mix_hy(self, l, with_ctx)
    if "gdn" in self.mixers:
        _mix_gdn(self, l, with_ctx)


MK._mixers = _mixers


def const_inputs(inputs):
    out = {}
    g, mask = na_consts(inputs)
    out["na_rpbg"] = g
    out["na_mask"] = mask
    out.update(hy_consts())
    out.update(gdn_consts())
    return out


def _mix_na(self, l, with_ctx):
    k = self.k
    self.arena_reset()
    _wring_init(self)
    blocks = _blocks(self, True)
    qT = self.aa([128, 2, T], BF16)
    kT = self.aa([128, 2, T], BF16)
    d_q = [Dep() for _ in range(NT)]
    d_k = [Dep() for _ in range(NT)]

    def evac(c, m, t0, n, pst, dps):
        ct = (c - OFF_NA) // 128
        tts = range(t0 // 128, (t0 + n) // 128)
        if ct < 2:
            k.act(lambda e: e.activation(out=qT[:, ct, t0:t0 + n], in_=pst, func=AF.Copy, scale=0.125),
                  r=[dps], w=[d_q[t] for t in tts])
        else:
            k.dve(lambda e: e.tensor_copy(out=kT[:, ct - 2, t0:t0 + n], in_=pst), r=[dps], w=[d_k[t] for t in tts])
    _proj(self, l, [(OFF_NA, 512)], blocks, evac, PROJ_BANKS)
    Ve = self.aa([128, NT, 4, 65], BF16)
    Vo = self.aa([128, 15, 4, 65], BF16)
    d_Ve, d_Vo = Dep(), Dep()
    k.pool(lambda e: e.memset(Ve, 1.0), w=[d_Ve])
    k.pool(lambda e: e.memset(Vo, 1.0), w=[d_Vo])
    i = self.wring_i
    self.wring_i = (i + 1) % len(self.wring)
    wv, dwv, chv = self.wring[i]
    k.dma("pool", wv[:, :, 0:256], self.W("w_in")[l, :, OFF_NA + 512:OFF_NA + 768].rearrange("(kt p) c -> p kt c", p=128),
          chv, w=[dwv])
    nb = 0
    for (Vx, dV, ntl, off) in ((Ve, d_Ve, NT, 0), (Vo, d_Vo, 15, 64)):
        for j in range(ntl):
            pi, hb = PROJ_BANKS[nb % 4]
            nb += 1
            pst = self.ps[pi][:, hb * 512: hb * 512 + 256]
            dps = self.d_ps[pi][hb]
            a = off + j * 128
            hd = [self.d_H[t] for t in range(a // 128, (a + 255) // 128)]
            for dt in range(8):
                k.pe(lambda e, dt=dt, pst=pst, a=a: e.matmul(pst, self.H[:, dt, a:a + 128], wv[:, dt, 0:256],
                                                             start=(dt == 0), stop=(dt == 7)), r=[dwv] + hd, w=[dps])
            k.act(lambda e, Vx=Vx, j=j, pst=pst: e.activation(out=Vx[:, j, :, 0:64], in_=pst.rearrange("p (h d) -> p h d", h=4),
                                                              func=AF.Copy), r=[dps], w=[dV])
    T2 = self.aa([128, 4, 14, 64])
    msk = self.aa([128, 64])
    d_T2 = Dep()
    d_msk = d_T2
    if not hasattr(self, "na_rpbg_in"):
        self.na_rpbg_in = self.din("na_rpbg", [DEPTH, 128, 4 * 14 * 64])
        self.na_mask_in = self.din("na_mask", [128, 64])
        self.ch_na = self.k.chan()
    k.dma("sp", T2.rearrange("p a b c -> p (a b c)"), self.na_rpbg_in[l], self.ch_na, w=[d_T2])
    k.dma("sp", msk, self.na_mask_in[:, :], self.ch_na, w=[d_msk])
    k.dve(lambda e: e.tensor_tensor(out=T2.rearrange("p a b c -> p (a b) c"), in0=T2.rearrange("p a b c -> p (a b) c"),
                                    in1=msk.unsqueeze(1).to_broadcast([128, 56, 64]), op=ALU.add),
          r=[d_T2, d_msk], w=[d_T2])
    Sb = [self.aa([128, 4, 64]) for _ in range(2)]
    d_Sb = [Dep(), Dep()]
    E = [self.aa([128, 6, 64], BF16) for _ in range(3)]
    d_E = [Dep() for _ in range(3)]
    rs = [self.aa([64, 4]) for _ in range(2)]
    d_rs = [Dep(), Dep()]
    On = [self.aa([64, 256], BF16) for _ in range(2)]
    d_On = [Dep(), Dep()]
    SB = [(2, 0), (2, 1), (3, 0), (3, 1)]
    OB = [(0, 0), (0, 1)]
    TB = (1, 0)
    psT = self.ps[TB[0]][:, TB[1] * 512: TB[1] * 512 + 512].bitcast(BF16)
    d_psT = self.d_ps[TB[0]][TB[1]]
    n = 0
    for r in range(32):
        s = min(max(r - 4, 0), 24)
        base = s - r + 7
        opi, ohb = OB[r % 2]
        O_ps = self.ps[opi][0:64, ohb * 512: ohb * 512 + 260]
        d_O = self.d_ps[opi][ohb]
        tq = (64 * r) // 128
        for h in range(4):
            hp = slice(64 * (h % 2), 64 * (h % 2) + 64)
            hc = h // 2
            spi, shb = SB[n % 4]
            S_ps = self.ps[spi][:, shb * 512: shb * 512 + 384]
            d_S = self.d_ps[spi][shb]
            for kt in range(6):
                ks = 64 * s + 128 * kt if kt < 4 else L + 128 * (kt - 4)
                kd = [d_k[t] for t in range(ks // 128, (ks + 255) // 128)]
                k.pe(lambda e, kt=kt, ks=ks, S_ps=S_ps, hp=hp, hc=hc, r=r: e.matmul(
                    S_ps[:, kt * 64:(kt + 1) * 64], kT[hp, hc, ks:ks + 128], qT[hp, hc, 64 * r:64 * r + 64],
                    start=True, stop=True), r=kd + [d_q[tq]], w=[d_S])
            sb_i = n % 2
            e_i = n % 3
            k.dve(lambda e, sb_i=sb_i, S_ps=S_ps, h=h, base=base: e.tensor_tensor(
                out=Sb[sb_i], in0=S_ps[:, 0:256].rearrange("p (a b) -> p a b", a=4),
                in1=T2[:, h, base:base + 7:2, :], op=ALU.add), r=[d_S, d_T2], w=[d_Sb[sb_i]])
            k.act(lambda e, sb_i=sb_i, e_i=e_i: e.activation(out=E[e_i][:, 0:4, :], in_=Sb[sb_i], func=AF.Exp),
                  r=[d_Sb[sb_i]], w=[d_E[e_i]])
            k.act(lambda e, e_i=e_i, S_ps=S_ps: e.activation(out=E[e_i][:, 4:6, :],
                                                             in_=S_ps[:, 256:384].rearrange("p (a b) -> p a b", a=2),
                                                             func=AF.Exp), r=[d_S], w=[d_E[e_i]])
            for kt in range(6):
                if kt < 4:
                    if s % 2 == 0:
                        vt, dv = Ve[:, s // 2 + kt, h, :], d_Ve
                    else:
                        vt, dv = Vo[:, (s - 1) // 2 + kt, h, :], d_Vo
                else:
                    vt, dv = Ve[:, 16 + kt - 4, h, :], d_Ve
                k.pe(lambda e, kt=kt, vt=vt, e_i=e_i, O_ps=O_ps, h=h: e.matmul(
                    O_ps[:, h * 65:(h + 1) * 65], E[e_i][:, kt, :], vt, start=(kt == 0), stop=(kt == 5)),
                    r=[d_E[e_i], dv], w=[d_O])
            n += 1
        j = r % 2
        O3 = O_ps.rearrange("p (h d) -> p h d", h=4)
        k.dve(lambda e, j=j, O3=O3: e.reciprocal(out=rs[j], in_=O3[:, :, 64]), r=[d_O], w=[d_rs[j]])
        k.dve(lambda e, j=j, O3=O3: e.tensor_tensor(out=On[j].rearrange("p (h d) -> p h d", h=4), in0=O3[:, :, 0:64],
                                                    in1=rs[j].unsqueeze(2).to_broadcast([64, 4, 64]), op=ALU.mult),
              r=[d_O, d_rs[j]], w=[d_On[j]])
        for hc in range(2):
            k.pe(lambda e, j=j, hc=hc: e.transpose(out=psT[:, hc * 64:(hc + 1) * 64], in_=On[j][:, hc * 128:(hc + 1) * 128],
                                                   identity=self.ident_b[0:64, 0:64]), r=[d_On[j], self.d_const], w=[d_psT])
        k.act(lambda e, r=r: e.activation(out=self.C[:, 6:8, 64 * r:64 * r + 64],
                                          in_=psT[:, 0:128].rearrange("p (a b) -> p a b", a=2), func=AF.Copy),
              r=[d_psT], w=[self.d_C[tq]])
    if with_ctx:
        Ec = self.aa([128, 2, 256], BF16)
        d_Ec = Dep()
        Onc = self.aa([128, 256], BF16)
        d_Onc = Dep()
        rsc = self.aa([128, 4])
        d_rsc = Dep()
        for qt in range(2):
            opi, ohb = OB[qt % 2]
            O_ps = self.ps[opi][:, ohb * 512: ohb * 512 + 260]
            d_O = self.d_ps[opi][ohb]
            for h in range(4):
                hp = slice(64 * (h % 2), 64 * (h % 2) + 64)
                hc = h // 2
                spi, shb = SB[n % 4]
                n += 1
                S_ps = self.ps[spi][:, shb * 512: shb * 512 + 256]
                d_S = self.d_ps[spi][shb]
                for c in range(2):
                    k.pe(lambda e, c=c, S_ps=S_ps, hp=hp, hc=hc, qt=qt: e.matmul(
                        S_ps[:, c * 128:(c + 1) * 128], kT[hp, hc, L + 128 * c:L + 128 * c + 128],
                        qT[hp, hc, L + 128 * qt:L + 128 * qt + 128], start=True, stop=True),
                        r=[d_k[16 + c], d_q[16 + qt]], w=[d_S])
                k.act(lambda e, S_ps=S_ps: e.activation(out=Ec[:, :, 0:128], in_=S_ps.rearrange("p (a b) -> p a b", a=2),
                                                        func=AF.Exp), r=[d_S], w=[d_Ec])
                for c in range(2):
                    k.pe(lambda e, c=c, O_ps=O_ps, h=h: e.matmul(O_ps[:, h * 65:(h + 1) * 65], Ec[:, c, 0:128],
                                                                 Ve[:, 16 + c, h, :], start=(c == 0), stop=(c == 1)),
                         r=[d_Ec, d_Ve], w=[d_O])
            O3 = O_ps.rearrange("p (h d) -> p h d", h=4)
            k.dve(lambda e, O3=O3: e.reciprocal(out=rsc, in_=O3[:, :, 64]), r=[d_O], w=[d_rsc])
            k.dve(lambda e, O3=O3: e.tensor_tensor(out=Onc.rearrange("p (h d) -> p h d", h=4), in0=O3[:, :, 0:64],
                                                   in1=rsc.unsqueeze(2).to_broadcast([128, 4, 64]), op=ALU.mult),
                  r=[d_O, d_rsc], w=[d_Onc])
            for hc in range(2):
                k.pe(lambda e, hc=hc: e.transpose(out=psT[:, hc * 128:(hc + 1) * 128], in_=Onc[:, hc * 128:(hc + 1) * 128],
                                                  identity=self.ident_b[:]), r=[d_Onc, self.d_const], w=[d_psT])
            k.act(lambda e, qt=qt: e.activation(out=self.C[:, 6:8, L + 128 * qt:L + 128 * qt + 128],
                                                in_=psT[:, 0:256].rearrange("p (a b) -> p a b", a=2), func=AF.Copy),
                  r=[d_psT], w=[self.d_C[16 + qt]])


import math
HY_EMB = 33
HY_BANDS = 16


def hy_consts():
    bf = ml_dtypes.bfloat16
    out = {}
    max_decay = math.log(1e-2) / 0.3
    min_decay = math.log(1e-2) / 1.5
    deltas = np.abs(np.linspace(min_decay, max_decay, 256, dtype=np.float32))
    for tag, Ls in (("lat", L), ("ctx", LC)):
        nt = Ls // 128
        t = np.linspace(0.0, 1.0, Ls, dtype=np.float32)[:, None]
        bands = np.linspace(1e-4, HY_BANDS - 1, HY_BANDS, dtype=np.float32)
        ang = (np.float32(2.0 * math.pi / Ls)) * np.arange(Ls, dtype=np.float32)[:, None] * bands
        z = np.concatenate([t, np.cos(ang), -np.sin(ang)], axis=-1).astype(np.float32)
        out["hy_zT_" + tag] = np.ascontiguousarray(z.T)
        dec = np.exp(-t * deltas[None, :]).astype(np.float32)
        out["hy_dec_" + tag] = np.ascontiguousarray(dec.reshape(nt, 128, 256).transpose(1, 0, 2).reshape(128, nt * 256))
        N = 2 * Ls
        tt_ = np.arange(Ls, dtype=np.int64)
        ff = np.arange(Ls, dtype=np.int64)
        m = ((2 * ff[None, :] + 1) * tt_[:, None]) % (2 * N)
        th = m.astype(np.float64) * (math.pi / N)
        Cm = np.cos(th)
        Sm = -np.sin(th)
        for nm, M_ in (("C", Cm), ("S", Sm)):
            f4 = M_.reshape(nt, 128, nt, 128).transpose(2, 1, 0, 3).reshape(nt, 128, nt * 128)
            out["hy_%sf_%s" % (nm, tag)] = np.ascontiguousarray(f4).astype(bf)
            Wd = min(512, Ls)
            ntb = Ls // Wd
            i4 = M_.reshape(ntb, Wd, nt, 128).transpose(0, 3, 2, 1).reshape(ntb, 128, nt * Wd)
            out["hy_%si_%s" % (nm, tag)] = np.ascontiguousarray(i4).astype(bf)
    return out


def _arena_rewind(self, mark):
    self.k.barrier()
    self.ar_off = mark


def _hy_filters(self, l, Ls, tag, WHx, d_WHx):
    k = self.k
    nt = Ls // 128
    BW = min(512, Ls)
    nb = Ls // BW
    zin = self.din("hy_zT_" + tag, [HY_EMB, Ls]) if ("hy_zT_" + tag) not in self.inputs else self._hyin["hy_zT_" + tag]
    din_dec = self.din("hy_dec_" + tag, [128, nt * 256]) if ("hy_dec_" + tag) not in self.inputs else self._hyin["hy_dec_" + tag]
    self._hyin["hy_zT_" + tag] = zin
    self._hyin["hy_dec_" + tag] = din_dec
    zT = self.aa([HY_EMB, Ls])
    dec = self.aa([128, nt, 256])
    d_z = Dep()
    d_dec = d_z
    k.dma("sp", zT, zin[:, :], self.ch_hy, w=[d_z])
    k.dma("sp", dec.rearrange("p a b -> p (a b)"), din_dec[:, :], self.ch_hy, w=[d_dec])
    hb = [self.aa([64, Ls]) for _ in range(2)]
    d_hb = [Dep(), Dep()]
    v = self.aa([64, 512])
    ki = self.aa([64, 512], I32)
    kf = self.aa([64, 512])
    d_v, d_ki, d_kf = Dep(), Dep(), Dep()
    fb = self.aa([64, 3])
    d_fb = Dep()
    fr = _ppv(self, l, "hy_f", 64)
    bb = _ppv(self, l, "hy_b", 64)
    k.dve(lambda e: e.tensor_tensor(out=fb, in0=fr, in1=bb, op=ALU.mult), r=[self.d_pp], w=[d_fb])
    ws = [_ppv(self, l, "hy_w1", HY_EMB), _ppv(self, l, "hy_w2", 64), _ppv(self, l, "hy_w3", 64)]
    PB = [(2, 0), (2, 1)]
    nps = 0
    src, d_src = zT, d_z
    for li in range(3):
        dst, d_dst = hb[li % 2], d_hb[li % 2]
        for b in range(nb):
            pi, hbk = PB[nps % 2]
            nps += 1
            pst = self.ps[pi][0:64, hbk * 512: hbk * 512 + BW]
            dps = self.d_ps[pi][hbk]
            k.pe(lambda e, li=li, b=b, pst=pst, src=src: e.matmul(pst, ws[li], src[:, b * BW:(b + 1) * BW], start=True, stop=True),
                 r=[self.d_pp, d_src], w=[dps])
            k.dve(lambda e, li=li, pst=pst: e.tensor_scalar(out=v[:, 0:BW], in0=pst, scalar1=fr[:, li:li + 1], scalar2=fb[:, li:li + 1],
                                                            op0=ALU.mult, op1=ALU.add), r=[dps, d_fb, self.d_pp], w=[d_v])
            k.dve(lambda e: e.tensor_scalar(out=ki[:, 0:BW], in0=v[:, 0:BW], scalar1=1.0 / (2.0 * math.pi), scalar2=None, op0=ALU.mult),
                  r=[d_v], w=[d_ki])
            k.dve(lambda e: e.tensor_copy(out=kf[:, 0:BW], in_=ki[:, 0:BW]), r=[d_ki], w=[d_kf])
            k.dve(lambda e: e.scalar_tensor_tensor(out=v[:, 0:BW], in0=kf[:, 0:BW], scalar=-2.0 * math.pi, in1=v[:, 0:BW],
                                                   op0=ALU.mult, op1=ALU.add), r=[d_kf, d_v], w=[d_v])
            k.dve(lambda e: e.tensor_scalar(out=v[:, 0:BW], in0=v[:, 0:BW], scalar1=3.1415925, scalar2=-3.1415925,
                                            op0=ALU.min, op1=ALU.max), r=[d_v], w=[d_v])
            k.act(lambda e, dst=dst, b=b: e.activation(out=dst[:, b * BW:(b + 1) * BW], in_=v[:, 0:BW], func=AF.Sin),
                  r=[d_v], w=[d_dst])
        src, d_src = dst, d_dst
    h3, d_h3 = src, d_src
    w4 = _ppv(self, l, "hy_w4", 64)
    hd = [self.aa([128, 2, 256]) for _ in range(2)]
    d_hd = [Dep(), Dep()]
    ab = [self.aa([128, 512]) for _ in range(2)]
    d_ab = [Dep(), Dep()]
    nrm_ps = self.ps[3][:, 0:512]
    d_nrm = self.d_ps[3][0]
    rn = self.aa([128, 256])
    d_rn = Dep()
    for pss in range(2):
        for tt in range(nt):
            pi, hbk = PB[nps % 2]
            nps += 1
            pst = self.ps[pi][:, hbk * 512: hbk * 512 + 512]
            dps = self.d_ps[pi][hbk]
            k.pe(lambda e, tt=tt, pst=pst: e.matmul(pst, h3[:, tt * 128:(tt + 1) * 128], w4, start=True, stop=True),
                 r=[d_h3, self.d_pp], w=[dps])
            j = tt % 2
            k.dve(lambda e, j=j, tt=tt, pst=pst: e.tensor_tensor(out=hd[j], in0=pst.rearrange("p (a b) -> p a b", a=2),
                                                                 in1=dec[:, tt, :].unsqueeze(1).to_broadcast([128, 2, 256]),
                                                                 op=ALU.mult), r=[dps, d_dec], w=[d_hd[j]])
            if pss == 0:
                k.act(lambda e, j=j: e.activation(out=ab[j], in_=hd[j].rearrange("p a b -> p (a b)"), func=AF.Abs),
                      r=[d_hd[j]], w=[d_ab[j]])
                k.pe(lambda e, j=j, tt=tt: e.matmul(nrm_ps, self.ones_f[:], ab[j], start=(tt == 0), stop=(tt == nt - 1)),
                     r=[d_ab[j], self.d_const], w=[d_nrm])
            else:
                if tt == 0:
                    k.dve(lambda e, j=j: e.memset(hd[j][0:1, 1, :], 0.0), r=[d_hd[j]], w=[d_hd[j]])
                k.dve(lambda e, j=j: e.tensor_tensor(out=hd[j], in0=hd[j], in1=rn.unsqueeze(1).to_broadcast([128, 2, 256]),
                                                     op=ALU.mult), r=[d_hd[j], d_rn], w=[d_hd[j]])
                k.dve(lambda e, j=j, tt=tt: e.tensor_tensor(out=WHx[:, tt, 1, :], in0=hd[j][:, 0, :], in1=hd[j][:, 1, :], op=ALU.add),
                      r=[d_hd[j]], w=[d_WHx])
                k.pool(lambda e, j=j, tt=tt: e.tensor_tensor(out=WHx[:, tt, 2, :], in0=hd[j][:, 0, :], in1=hd[j][:, 1, :],
                                                             op=ALU.subtract), r=[d_hd[j]], w=[d_WHx])
        if pss == 0:
            k.dve(lambda e: e.tensor_copy(out=rn, in_=nrm_ps[:, 0:256]), r=[d_nrm], w=[d_rn])
            k.dve(lambda e: e.tensor_tensor(out=rn, in0=rn, in1=nrm_ps[:, 256:512], op=ALU.add), r=[d_nrm, d_rn], w=[d_rn])
            k.dve(lambda e: e.reciprocal(out=rn, in_=rn), r=[d_rn], w=[d_rn])
            k.dve(lambda e: e.tensor_scalar(out=rn, in0=rn, scalar1=2.0 / (2 * Ls), scalar2=None, op0=ALU.mult), r=[d_rn], w=[d_rn])


def _hy_dft(self, l, Ls, tag, toff, WHx, d_WHx, x0T, wTm, d_x0w, Yh, ring):
    k = self.k
    nt = Ls // 128
    Wd = min(512, Ls)
    ntb = Ls // Wd
    names = ["hy_Cf_", "hy_Sf_", "hy_Ci_", "hy_Si_"]
    tabs = []
    for nm in names:
        key = nm + tag
        if key not in self._hyin:
            shp = [nt, 128, nt * 128] if nm[4] == "f" else [ntb, 128, nt * Wd]
            self._hyin[key] = self.din(key, shp, BF16)
        tabs.append(self._hyin[key])
    Cf, Sf, Ci, Si = tabs
    d_Yh = Dep()
    Kr = self.aa([128, 4, 256])
    d_K = [Dep(), Dep()]
    tm = [self.aa([128, 256]) for _ in range(4)]
    d_tm = [Dep() for _ in range(4)]
    hybias = _ppv(self, l, "hybias")
    for j in range(nt):
        slot, dsl, chs = ring[self.hyring_i % len(ring)]
        self.hyring_i += 1
        cst = slot[:, 0:nt * 128].rearrange("p (a b) -> p a b", b=128)
        sst = slot[:, 2048:2048 + nt * 128].rearrange("p (a b) -> p a b", b=128)
        k.dma("sp", slot[:, 0:nt * 128], Cf[j], chs, w=[dsl])
        k.dma("sp", slot[:, 2048:2048 + nt * 128], Sf[j], chs, w=[dsl])
        ps_r = self.ps[0][:, 0:512]
        ps_i = self.ps[0][:, 512:1024]
        for tt in range(nt):
            k.pe(lambda e, tt=tt, cst=cst: e.matmul(ps_r, cst[:, tt, :], WHx[:, tt, 0:2, :], start=(tt == 0), stop=(tt == nt - 1)),
                 r=[dsl, d_WHx], w=[self.d_ps[0][0]])
        for tt in range(nt):
            k.pe(lambda e, tt=tt, sst=sst: e.matmul(ps_i, sst[:, tt, :], WHx[:, tt, 0:3:2, :], start=(tt == 0), stop=(tt == nt - 1)),
                 r=[dsl, d_WHx], w=[self.d_ps[0][1]])
        k.act(lambda e: e.activation(out=Kr[:, 0:2, :], in_=ps_r.rearrange("p (a b) -> p a b", a=2), func=AF.Copy),
              r=[self.d_ps[0][0]], w=[d_K[0]])
        k.act(lambda e: e.activation(out=Kr[:, 2:4, :], in_=ps_i.rearrange("p (a b) -> p a b", a=2), func=AF.Copy),
              r=[self.d_ps[0][1]], w=[d_K[1]])
        Ur, Kre, Ui, Kie = Kr[:, 0, :], Kr[:, 1, :], Kr[:, 2, :], Kr[:, 3, :]
        k.dve(lambda e: e.tensor_tensor(out=tm[0], in0=Ur, in1=Kre, op=ALU.mult), r=[d_K[0]], w=[d_tm[0]])
        k.pool(lambda e: e.tensor_tensor(out=tm[1], in0=Ui, in1=Kie, op=ALU.mult), r=[d_K[1]], w=[d_tm[1]])
        k.dve(lambda e, j=j: e.tensor_tensor(out=Yh[:, j, 0, :], in0=tm[0], in1=tm[1], op=ALU.subtract),
              r=[d_tm[0], d_tm[1]], w=[d_Yh])
        k.pool(lambda e: e.tensor_tensor(out=tm[2], in0=Ur, in1=Kie, op=ALU.mult), r=d_K, w=[d_tm[2]])
        k.dve(lambda e: e.tensor_tensor(out=tm[3], in0=Ui, in1=Kre, op=ALU.mult), r=d_K, w=[d_tm[3]])
        k.pool(lambda e, j=j: e.tensor_tensor(out=Yh[:, j, 1, :], in0=tm[2], in1=tm[3], op=ALU.add),
               r=[d_tm[2], d_tm[3]], w=[d_Yh])
    G = 4 if nt >= 4 else nt
    yt = [self.aa([128, 512]) for _ in range(2)]
    d_yt = [Dep(), Dep()]
    for tb in range(ntb):
        for g in range(nt // G):
            slot, dsl, chs = ring[self.hyring_i % len(ring)]
            self.hyring_i += 1
            k.dma("sp", slot[:, 0:G * Wd], Ci[tb, :, g * G * Wd:(g + 1) * G * Wd], chs, w=[dsl])
            k.dma("sp", slot[:, 2048:2048 + G * Wd], Si[tb, :, g * G * Wd:(g + 1) * G * Wd], chs, w=[dsl])
            cst = slot[:, 0:G * Wd].rearrange("p (a b) -> p a b", b=Wd)
            sst = slot[:, 2048:2048 + G * Wd].rearrange("p (a b) -> p a b", b=Wd)
            for fi in range(G):
                ft = g * G + fi
                for ct in range(2):
                    py = self.ps[1][:, ct * 512: ct * 512 + Wd]
                    k.pe(lambda e, fi=fi, ft=ft, ct=ct, py=py, cst=cst: e.matmul(py, Yh[:, ft, 0, ct * 128:(ct + 1) * 128], cst[:, fi, :],
                                                                               start=(ft == 0), stop=False),
                         r=[dsl, d_Yh], w=[self.d_ps[1][ct]])
                    k.pe(lambda e, fi=fi, ft=ft, ct=ct, py=py, sst=sst: e.matmul(py, Yh[:, ft, 1, ct * 128:(ct + 1) * 128], sst[:, fi, :],
                                                                               start=False, stop=(ft == nt - 1)),
                         r=[dsl, d_Yh], w=[self.d_ps[1][ct]])
        a = toff + tb * Wd
        tts = [self.d_C[t] for t in range(a // 128, (a + Wd) // 128)]
        for ct in range(2):
            py = self.ps[1][:, ct * 512: ct * 512 + Wd]
            k.dve(lambda e, ct=ct, py=py, a=a: e.scalar_tensor_tensor(out=yt[ct][:, 0:Wd], in0=wTm[:, ct, a:a + Wd], scalar=hybias[:, ct:ct + 1],
                                                                      in1=py, op0=ALU.mult, op1=ALU.add),
                  r=[self.d_ps[1][ct], d_x0w, self.d_pp], w=[d_yt[ct]])
            k.pool(lambda e, ct=ct, a=a: e.tensor_tensor(out=self.C[:, 4 + ct, a:a + Wd], in0=yt[ct][:, 0:Wd], in1=x0T[:, ct, a:a + Wd],
                                                         op=ALU.mult), r=[d_yt[ct], d_x0w], w=tts)


def _mix_hy(self, l, with_ctx):
    k = self.k
    self.arena_reset()
    if not hasattr(self, "_hyin"):
        self._hyin = {}
        self.ch_hy = k.chan()
        self.ch_hyring = [k.chan() for _ in range(3)]
    ntok = T if with_ctx else L
    WH = self.aa([128, 16, 3, 256], BF16)
    d_WH = Dep()
    if with_ctx:
        WHc = self.aa([128, 2, 3, 256], BF16)
        d_WHc = Dep()
    x0T = self.aa([128, 2, T], BF16)
    wTm = self.aa([128, 2, T], BF16)
    d_x0w = Dep()
    markA = self.ar_off
    _hy_filters(self, l, L, "lat", WH, d_WH)
    if with_ctx:
        _arena_rewind(self, markA)
        _hy_filters(self, l, LC, "ctx", WHc, d_WHc)
    _arena_rewind(self, markA)
    _wring_init(self)
    ranges = [(0, L)] + ([(L, T)] if with_ctx else [])
    blocks = _blocks(self, with_ctx)
    pin = [self.aa([128, T]) for _ in range(1)]
    d_pin = [Dep()]
    x1v = self.aa([128, 4, T], BF16)
    d_x1v = [Dep() for _ in range(4)]
    cacc = [self.aa([128, T]) for _ in range(1)]
    d_cacc = Dep()
    hyw = _ppv(self, l, "hyw")

    def evac(c, m, t0, n, pst, dps):
        ct = (c - OFF_HY) // 128
        j = 0
        k.act(lambda e: e.activation(out=pin[j][:, t0:t0 + n], in_=pst, func=AF.Copy), r=[dps], w=[d_pin[j]])
        if (t0, n) == blocks[-1]:
            _dwconv(self, "dve", cacc[0], pin[j], hyw[:, ct * 3:(ct + 1) * 3], ranges, [d_pin[j], self.d_pp], [d_cacc])
            if ct < 2:
                k.pool(lambda e: e.tensor_copy(out=x0T[:, ct, 0:ntok], in_=cacc[0][:, 0:ntok]), r=[d_cacc], w=[d_x0w])
            else:
                k.pool(lambda e: e.tensor_copy(out=x1v[:, ct - 2, 0:ntok], in_=cacc[0][:, 0:ntok]), r=[d_cacc], w=[d_x1v[ct - 2]])
    _proj(self, l, [(OFF_HY, 512), (OFF_HY + 512, 256)], blocks, evac, PROJ_BANKS)
    for ct in range(2):
        k.pool(lambda e, ct=ct: e.tensor_tensor(out=wTm[:, ct, 0:ntok], in0=x1v[:, ct, 0:ntok], in1=x1v[:, 2 + ct, 0:ntok], op=ALU.mult),
               r=[d_x1v[ct], d_x1v[2 + ct]], w=[d_x0w])
    psT = self.ps[2][:, 0:512].bitcast(BF16)
    d_psT = self.d_ps[2][0]
    for tt in range(ntok // 128):
        for ct in range(2):
            k.pe(lambda e, tt=tt, ct=ct: e.transpose(out=psT[:, ct * 128:(ct + 1) * 128], in_=wTm[:, ct, tt * 128:(tt + 1) * 128],
                                                     identity=self.ident_b[:]), r=[d_x0w, self.d_const], w=[d_psT])
        if tt < 16:
            k.act(lambda e, tt=tt: e.activation(out=WH[:, tt, 0, :], in_=psT[:, 0:256], func=AF.Copy), r=[d_psT], w=[d_WH])
        else:
            k.act(lambda e, tt=tt: e.activation(out=WHc[:, tt - 16, 0, :], in_=psT[:, 0:256], func=AF.Copy), r=[d_psT], w=[d_WHc])
    _arena_rewind(self, markA)
    Yh = self.aa([128, 16, 2, 256], BF16)
    ring = [(self.aa([128, 4096], BF16), Dep(), self.ch_hyring[i]) for i in range(3)]
    self.hyring_i = 0
    markD = self.ar_off
    _hy_dft(self, l, L, "lat", 0, WH, d_WH, x0T, wTm, d_x0w, Yh, ring)
    if with_ctx:
        _arena_rewind(self, markD)
        _hy_dft(self, l, LC, "ctx", L, WHc, d_WHc, x0T, wTm, d_x0w, Yh, ring)


def gdn_consts():
    out = {}
    m = np.arange(128)[:, None]
    i = np.arange(128)[None, :]
    LE = (m <= i).astype(np.float32)
    GE = (m >= i).astype(np.float32)
    GT = (m > i).astype(np.float32)
    LT = (m < i).astype(np.float32)
    NEGF = np.tile(np.where(i > m, -1e9, 0.0).astype(np.float32), (1, 4))
    NEGB = np.tile(np.where(i < m, -1e9, 0.0).astype(np.float32), (1, 4))
    out["gdn_gm"] = np.ascontiguousarray(np.concatenate([LE, GE, GT, LT, NEGF, NEGB], axis=1))
    pm = np.zeros((128, 128), np.float32)
    for mm in range(128):
        if (mm % 64) < 32:
            pm[mm + 32, mm] = -1.0
        else:
            pm[mm - 32, mm] = 1.0
    out["gdn_pm"] = pm
    ii = np.arange(128)[:, None]
    jj = np.arange(128)[None, :]
    lms, ums = [], []
    for s_ in range(7):
        b_ = 1 << s_
        same2 = (ii // (2 * b_)) == (jj // (2 * b_))
        diff1 = (ii // b_) != (jj // b_)
        lms.append(np.where(same2 & diff1 & (jj < ii), -1.0, 0.0))
        ums.append(np.where(same2 & diff1 & (jj > ii), -1.0, 0.0))
    out["gdn_lm"] = np.ascontiguousarray(np.concatenate(lms + ums, axis=1)).astype(ml_dtypes.bfloat16)
    pos = np.arange(L)
    row = (pos // 64).astype(np.float32)
    col = (pos % 64).astype(np.float32)
    inv = (10000.0 ** (-np.arange(16, dtype=np.float32) / 16)).astype(np.float32)
    ang = np.concatenate([row[:, None] * inv, col[:, None] * inv], axis=-1)
    idx = (np.arange(128) % 64) % 32
    cs = np.stack([np.cos(ang).T[idx], np.sin(ang).T[idx]], axis=1)
    out["gdn_rope"] = np.ascontiguousarray(cs.reshape(128, 2 * L)).astype(ml_dtypes.bfloat16)
    return out


def _mix_gdn(self, l, with_ctx):
    k = self.k
    self.arena_reset()
    if not hasattr(self, "gdn_gm_in"):
        self.gdn_gm_in = self.din("gdn_gm", [128, 1536])
        self.gdn_pm_in = self.din("gdn_pm", [128, 128])
        self.gdn_rope_in = self.din("gdn_rope", [128, 2 * L], BF16)
        self.ch_gdn = k.chan()
    otok = self.aa([128, NT, 256], BF16)
    markO = self.ar_off
    qT = self.aa([128, 2, T], BF16)
    kT = self.aa([128, 2, T], BF16)
    d_qk = Dep()
    vtok = self.aa([128, NT, 256], BF16)
    ktok = self.aa([128, NT, 256], BF16)
    d_vtok, d_ktok = Dep(), Dep()
    la = self.aa([128, NT, 8])
    beta = self.aa([128, NT, 8])
    d_la, d_beta = Dep(), Dep()
    GM = self.aa([128, 1536])
    d_GM = Dep()
    d_LMK = d_GM
    d_c1 = d_GM
    k.dma("sp", GM, self.gdn_gm_in[:, :], self.ch_gdn, w=[d_GM])
    LE, GE, GT, LT = (GM[:, i * 128:(i + 1) * 128] for i in range(4))
    NEGF, NEGB = GM[:, 512:1024], GM[:, 1024:1536]
    LMK = self.aa([128, 14 * 128], BF16)
    if not hasattr(self, "gdn_lm_in"):
        self.gdn_lm_in = self.din("gdn_lm", [128, 14 * 128], BF16)
    k.dma("sp", LMK, self.gdn_lm_in[:, :], self.ch_gdn, w=[d_LMK])
    markP = self.ar_off
    _wring_init(self, 1)
    pm = self.aa([128, 128])
    rope = self.aa([128, 2, L], BF16)
    blk64 = self.aa([128, 128])
    d_blk = Dep()
    k.dma("sp", pm, self.gdn_pm_in[:, :], self.ch_gdn, w=[d_c1])
    k.dma("sp", rope.rearrange("p a b -> p (a b)"), self.gdn_rope_in[:, :], self.ch_gdn, w=[d_c1])
    k.pool(lambda e: e.memset(blk64, 0.0), w=[d_blk])
    k.pool(lambda e: e.memset(blk64[0:64, 0:64], 1.0), w=[d_blk])
    k.pool(lambda e: e.memset(blk64[64:128, 64:128], 1.0), w=[d_blk])
    wab, dwab, chab = self.wring[0]
    k.dma("pool", wab[:, :, 0:16], self.W("w_in")[l, :, 1024:1040].rearrange("(kt p) c -> p kt c", p=128), chab, w=[dwab])
    ab_ps = self.ps[3][:, 0:NT * 16]
    d_ab = self.d_ps[3][0]
    for tt in range(NT):
        for dt in range(8):
            k.pe(lambda e, tt=tt, dt=dt: e.matmul(ab_ps[:, tt * 16:(tt + 1) * 16], self.H[:, dt, tt * 128:(tt + 1) * 128],
                                                  wab[:, dt, 0:16], start=(dt == 0), stop=(dt == 7)),
                 r=[dwab, self.d_H[tt]], w=[d_ab])
    ab3 = ab_ps.rearrange("p (t c) -> p t c", c=16)
    xa = self.aa([128, NT, 8])
    ea = self.aa([128, 8])
    d_xa, d_ea = Dep(), Dep()
    k.dve(lambda e: e.tensor_tensor(out=xa, in0=ab3[:, :, 0:8], in1=_ppv(self, l, "dtb").unsqueeze(1).to_broadcast([128, NT, 8]),
                                    op=ALU.add), r=[d_ab, self.d_pp], w=[d_xa])
    k.act(lambda e: e.activation(out=xa, in_=xa, func=AF.Exp), r=[d_xa], w=[d_xa])
    k.act(lambda e: e.activation(out=xa, in_=xa, func=AF.Ln, bias=1.0, scale=1.0), r=[d_xa], w=[d_xa])
    k.act(lambda e: e.activation(out=ea, in_=_ppv(self, l, "alog"), func=AF.Exp), r=[self.d_pp], w=[d_ea])
    k.dve(lambda e: e.scalar_tensor_tensor(out=la, in0=xa, scalar=-1.0, in1=ea.unsqueeze(1).to_broadcast([128, NT, 8]),
                                           op0=ALU.mult, op1=ALU.mult), r=[d_xa, d_ea], w=[d_la])
    k.act(lambda e: e.activation(out=beta, in_=ab3[:, :, 8:16], func=AF.Sigmoid), r=[d_ab], w=[d_beta])
    blocks = _blocks(self, True)
    ranges = [(0, L), (L, T)]
    pin = self.aa([128, T])
    cacc = self.aa([128, T])
    d_pin, d_cacc = Dep(), Dep()
    vTt = self.aa([128, T], BF16)
    d_vTt = Dep()
    rv = self.aa([128, 512])
    t1 = self.aa([128, 512])
    t2 = self.aa([128, 512])
    d_rv, d_t1, d_t2 = Dep(), Dep(), Dep()
    gdw = _ppv(self, l, "gdw")
    psT = self.ps[2][:, 0:512].bitcast(BF16)
    d_psT = self.d_ps[2][0]

    def finish_tile(ct):
        _dwconv(self, "dve", cacc, pin, gdw[:, ct * 3:(ct + 1) * 3], ranges, [d_pin, self.d_pp], [d_cacc])
        if ct >= 4:
            k.act(lambda e: e.activation(out=vTt, in_=cacc, func=AF.Silu), r=[d_cacc], w=[d_vTt])
            for tt in range(NT):
                k.pe(lambda e, tt=tt: e.transpose(out=psT[:, 0:128], in_=vTt[:, tt * 128:(tt + 1) * 128], identity=self.ident_b[:]),
                     r=[d_vTt, self.d_const], w=[d_psT])
                k.dve(lambda e, tt=tt: e.tensor_copy(out=vtok[:, tt, (ct - 4) * 128:(ct - 3) * 128], in_=psT[:, 0:128]),
                      r=[d_psT], w=[d_vtok])
            return
        isq = ct < 2
        dst = qT if isq else kT
        cc = ct % 2
        k.act(lambda e: e.activation(out=pin, in_=cacc, func=AF.Silu), r=[d_cacc, d_pin], w=[d_pin])
        k.act(lambda e: e.activation(out=cacc, in_=pin, func=AF.Square), r=[d_pin, d_cacc], w=[d_cacc])
        for (t0, n) in blocks:
            ss = self.ps[3][:, 512:512 + n]
            dss = self.d_ps[3][1]
            k.pe(lambda e, t0=t0, n=n, ss=ss: e.matmul(ss, blk64, cacc[:, t0:t0 + n], start=True, stop=True), r=[d_cacc, d_blk], w=[dss])
            sc_, bi_ = (64.0, 64.0 * EPS) if isq else (1.0, EPS)
            k.act(lambda e, n=n, ss=ss: e.activation(out=rv[:, 0:n], in_=ss, func=AF.Sqrt, bias=bi_, scale=sc_), r=[dss], w=[d_rv])
            k.dve(lambda e, n=n: e.reciprocal(out=rv[:, 0:n], in_=rv[:, 0:n]), r=[d_rv], w=[d_rv])
            k.dve(lambda e, t0=t0, n=n: e.tensor_tensor(out=pin[:, t0:t0 + n], in0=pin[:, t0:t0 + n], in1=rv[:, 0:n], op=ALU.mult),
                  r=[d_rv, d_pin], w=[d_pin])
            if t0 < L:
                pv = self.ps[2][:, 512:512 + n]
                dpv = self.d_ps[2][1]
                k.pe(lambda e, t0=t0, n=n, pv=pv: e.matmul(pv, pm, pin[:, t0:t0 + n], start=True, stop=True), r=[d_pin, d_c1], w=[dpv])
                k.dve(lambda e, t0=t0, n=n: e.tensor_tensor(out=t1[:, 0:n], in0=pin[:, t0:t0 + n], in1=rope[:, 0, t0:t0 + n], op=ALU.mult),
                      r=[d_pin, d_c1], w=[d_t1])
                k.dve(lambda e, t0=t0, n=n, pv=pv: e.tensor_tensor(out=t2[:, 0:n], in0=pv, in1=rope[:, 1, t0:t0 + n], op=ALU.mult),
                      r=[dpv, d_c1], w=[d_t2])
                k.pool(lambda e, t0=t0, n=n: e.tensor_tensor(out=dst[:, cc, t0:t0 + n], in0=t1[:, 0:n], in1=t2[:, 0:n], op=ALU.add),
                       r=[d_t1, d_t2], w=[d_qk])
            else:
                k.pool(lambda e, t0=t0, n=n: e.tensor_copy(out=dst[:, cc, t0:t0 + n], in_=pin[:, t0:t0 + n]), r=[d_pin], w=[d_qk])
        if not isq:
            for tt in range(NT):
                k.pe(lambda e, tt=tt: e.transpose(out=psT[:, 0:128], in_=kT[:, cc, tt * 128:(tt + 1) * 128], identity=self.ident_b[:]),
                     r=[d_qk, self.d_const], w=[d_psT])
                k.dve(lambda e, tt=tt: e.tensor_copy(out=ktok[:, tt, cc * 128:(cc + 1) * 128], in_=psT[:, 0:128]),
                      r=[d_psT], w=[d_ktok])

    def evac(c, m, t0, n, pst, dps):
        ct = c // 128
        k.act(lambda e: e.activation(out=pin[:, t0:t0 + n], in_=pst, func=AF.Copy), r=[dps], w=[d_pin])
        if (t0, n) == blocks[-1]:
            finish_tile(ct)
    _proj(self, l, [(512, 256), (256, 256), (0, 256)], blocks, evac, PROJ_BANKS)
    import os
    _stop = os.environ.get("GDN_STOP", "")
    if "gdn_g1" in self.dbg:
        self.k.barrier()
        chd = k.chan()
        for nm, ap_, n_ in (("qT", qT.rearrange("p a b -> p (a b)"), 2 * T), ("kT", kT.rearrange("p a b -> p (a b)"), 2 * T),
                            ("vtok", vtok.rearrange("p a b -> p (a b)"), NT * 256), ("ktok", ktok.rearrange("p a b -> p (a b)"), NT * 256)):
            o_ = self.dout("dbg_" + nm, [128, n_], BF16)
            k.dma("sp", o_[:, :], ap_, chd)
        for nm, ap_ in (("la", la), ("beta", beta)):
            o_ = self.dout("dbg_" + nm, [128, NT * 8])
            k.dma("sp", o_[:, :], ap_.rearrange("p a b -> p (a b)"), chd)
        self.k.barrier()
    if _stop == "g1":
        return
    _arena_rewind(self, markP)
    d_otok = [Dep() for _ in range(NT)]
    hm = self.aa([128, 4])
    d_hm = Dep()
    k.pool(lambda e: e.memset(hm, 0.0), w=[d_hm])
    k.pool(lambda e: e.memset(hm[0:64, 0:4:2], 1.0), w=[d_hm])
    k.pool(lambda e: e.memset(hm[64:128, 1:4:2], 1.0), w=[d_hm])
    first_visit = [True] * NT
    orderF = [16, 17] + list(range(16))
    orderB = [17, 16] + list(range(15, -1, -1))
    W_ = {}
    for dd in range(2):
        w = {}
        w["Z"] = self.aa([128, 4, 128]); w["D"] = self.aa([128, 4, 128], BF16); w["Ds"] = w["Z"]
        w["E"] = [self.aa([128, 4, 128], BF16) for _ in range(2)]
        w["ET"] = [self.aa([128, 4, 128], BF16) for _ in range(2)]
        w["d_E"] = [Dep(), Dep()]; w["d_ET"] = [Dep(), Dep()]
        w["A"] = [self.aa([128, 4, 128], BF16) for _ in range(2)]
        w["AT"] = [self.aa([128, 4, 128], BF16) for _ in range(2)]
        w["X"] = [self.aa([128, 4, 128], BF16) for _ in range(2)]
        w["QKm"] = self.aa([128, 4, 128], BF16)
        w["R"] = self.aa([128, 4, 128], BF16)
        w["u"] = self.aa([128, 4, 64], BF16)
        w["d_R"], w["d_u"] = Dep(), Dep()
        w["QKT"] = self.aa([128, 4, 128], BF16)
        w["wT"] = self.aa([64, 4, 128], BF16)
        w["kd"] = self.aa([128, 4, 128], BF16)
        w["kc"] = self.aa([128, 2, 2, 128], BF16)
        w["d_kc"] = Dep()
        k.pool(lambda e, w=w: e.memset(w["kc"], 0.0), w=[w["d_kc"]])
        w["SbQ"] = self.aa([128, 4, 64], BF16)
        w["d_SbQ"] = Dep()
        k.pool(lambda e, w=w: e.memset(w["SbQ"], 0.0), w=[w["d_SbQ"]])
        w["ev"] = self.aa([128, 12])
        w["beg"] = self.aa([128, 4])
        w["vn"] = self.aa([128, 4, 64], BF16)
        w["o2"] = self.aa([128, 4, 64], BF16)
        w["ot"] = self.aa([128, 4, 64], BF16)
        w["S"] = self.aa([128, 4, 64])
        w["Sb"] = self.aa([128, 4, 64], BF16)
        for nm in ("Z", "D", "Ds", "QKm", "QKT", "wT", "kd", "ev", "beg", "vn", "o2", "ot", "S", "Sb"):
            w["d_" + nm] = Dep()
        w["d_Ds"] = w["d_Z"]
        w["d_A"] = [Dep(), Dep()]; w["d_AT"] = [Dep(), Dep()]; w["d_X"] = [Dep(), Dep()]
        k.dve(lambda e, w=w: e.memset(w["S"], 0.0), w=[w["d_S"]])
        k.pool(lambda e, w=w: e.memset(w["Sb"], 0.0), w=[w["d_Sb"]])
        W_[dd] = w

    def bank(dd, i, half=None):
        t = self.ps[2 * dd + i // 2]
        hb = i % 2
        return t[:, hb * 512:(hb + 1) * 512], self.d_ps[2 * dd + i // 2][hb]

    def unit_pre(dd, n):
        w = W_[dd]
        c0 = n * 128
        lacol = la[:, n, dd * 4:dd * 4 + 4]
        becol = beta[:, n, dd * 4:dd * 4 + 4]
        Mz = GT if dd == 0 else LT
        Um = LE if dd == 0 else GE
        Ugt = GT if dd == 0 else LT
        NEG = NEGF if dd == 0 else NEGB
        strict = GT if dd == 0 else LT
        B0, dB0 = bank(dd, 0)
        B1, dB1 = bank(dd, 1)
        B2, dB2 = bank(dd, 2)
        B3, dB3 = bank(dd, 3)
        k.dve(lambda e: e.tensor_tensor(out=w["Z"], in0=Mz.unsqueeze(1).to_broadcast([128, 4, 128]),
                                        in1=lacol.unsqueeze(2).to_broadcast([128, 4, 128]), op=ALU.mult),
              r=[d_GM, d_la], w=[w["d_Z"]])
        k.pe(lambda e: e.matmul(B0, Um, w["Z"].rearrange("p a b -> p (a b)"), start=True, stop=False), r=[d_GM, w["d_Z"]], w=[dB0])
        k.pe(lambda e: e.matmul(B0, self.ident_f[:], NEG, start=False, stop=True), r=[d_GM, self.d_const], w=[dB0])
        k.act(lambda e: e.activation(out=w["D"].rearrange("p a b -> p (a b)"), in_=B0, func=AF.Exp), r=[dB0], w=[w["d_D"]])
        if _stop == "pre1":
            return None, None
        k.pe(lambda e: e.matmul(B3[:, 0:4], Um, lacol, start=True, stop=True), r=[d_GM, d_la], w=[dB3])
        k.pe(lambda e: e.matmul(B3[:, 4:8], Ugt, lacol, start=True, stop=True), r=[d_GM, d_la], w=[dB3])
        k.pe(lambda e: e.matmul(B3[:, 8:12], self.ones_f[:], lacol, start=True, stop=True), r=[self.d_const, d_la], w=[dB3])
        k.act(lambda e: e.activation(out=w["ev"], in_=B3[:, 0:12], func=AF.Exp), r=[dB3], w=[w["d_ev"]])
        if _stop == "pre2":
            return None, None
        k.dve(lambda e: e.tensor_tensor(out=w["Ds"], in0=w["D"], in1=strict.unsqueeze(1).to_broadcast([128, 4, 128]), op=ALU.mult),
              r=[w["d_D"], d_GM], w=[w["d_Ds"]])
        if _stop == "pre2a":
            return None, None
        for h in range(4 if _stop != "pre2h" else 1):
            hp = slice(64 * (h % 2), 64 * (h % 2) + 64)
            if h == 0:
                k.act(lambda e: e.activation(out=w["kc"][0:64, :, 0, :], in_=kT[0:64, :, c0:c0 + 128], func=AF.Copy), r=[d_qk], w=[w["d_kc"]])
                k.act(lambda e: e.activation(out=w["kc"][64:128, :, 1, :], in_=kT[64:128, :, c0:c0 + 128], func=AF.Copy), r=[d_qk], w=[w["d_kc"]])
            k.pe(lambda e, h=h, hp=hp: e.matmul(B1[:, h * 128:(h + 1) * 128], kT[:, h // 2, c0:c0 + 128], w["kc"][:, h // 2, h % 2, :],
                                                start=True, stop=True), r=[d_qk, w["d_kc"]], w=[dB1])
        if _stop in ("pre2b", "pre2h"):
            return None, None
        k.dve(lambda e: e.tensor_tensor(out=w["Ds"].rearrange("p a b -> p (a b)"), in0=B1, in1=w["Ds"].rearrange("p a b -> p (a b)"),
                                        op=ALU.mult), r=[dB1, w["d_Ds"]], w=[w["d_Ds"]])
        k.dve(lambda e: e.tensor_tensor(out=w["A"][0], in0=w["Ds"], in1=becol.unsqueeze(2).to_broadcast([128, 4, 128]), op=ALU.mult),
              r=[w["d_Ds"], d_beta], w=[w["d_A"][0]])
        if _stop == "pre3":
            return None, None
        for h in range(4):
            hp = slice(64 * (h % 2), 64 * (h % 2) + 64)
            k.pe(lambda e, h=h, hp=hp: e.matmul(B2[:, h * 128:(h + 1) * 128], qT[:, h // 2, c0:c0 + 128], w["kc"][:, h // 2, h % 2, :],
                                                start=True, stop=True), r=[d_qk, w["d_kc"]], w=[dB2])
        k.dve(lambda e: e.tensor_tensor(out=w["QKm"].rearrange("p a b -> p (a b)"), in0=B2, in1=w["D"].rearrange("p a b -> p (a b)"),
                                        op=ALU.mult), r=[dB2, w["d_D"]], w=[w["d_QKm"]])
        if _stop == "pre4":
            return None, None
        B1b = B1.bitcast(BF16)
        B2b = B2.bitcast(BF16)
        for h in range(4):
            k.pe(lambda e, h=h: e.transpose(out=B1b[:, h * 128:(h + 1) * 128], in_=w["A"][0][:, h, :], identity=self.ident_b[:]),
                 r=[w["d_A"][0], self.d_const], w=[dB1])
        k.act(lambda e: e.activation(out=w["AT"][0].rearrange("p a b -> p (a b)"), in_=B1b[:, 0:512], func=AF.Copy),
              r=[dB1], w=[w["d_AT"][0]])
        for h in range(4):
            k.pe(lambda e, h=h: e.transpose(out=B2b[:, h * 128:(h + 1) * 128], in_=w["QKm"][:, h, :], identity=self.ident_b[:]),
                 r=[w["d_QKm"], self.d_const], w=[dB2])
        k.act(lambda e: e.activation(out=w["QKT"].rearrange("p a b -> p (a b)"), in_=B2b[:, 0:512], func=AF.Copy),
              r=[dB2], w=[w["d_QKT"]])
        if _stop == "pre5":
            return None, None
        k.dve(lambda e: e.tensor_tensor(out=w["beg"], in0=becol, in1=w["ev"][:, 0:4], op=ALU.mult), r=[d_beta, w["d_ev"]], w=[w["d_beg"]])
        X0 = w["R"].rearrange("p h (two d) -> p h two d", two=2)
        k.dve(lambda e: e.tensor_tensor(out=X0[:, :, 0, :], in0=vtok[:, n, :].rearrange("p (h d) -> p h d", h=4),
                                        in1=becol.unsqueeze(2).to_broadcast([128, 4, 64]), op=ALU.mult),
              r=[d_vtok, d_beta], w=[w["d_R"]])
        k.dve(lambda e: e.tensor_tensor(out=X0[:, :, 1, :], in0=ktok[:, n, :].rearrange("p (h d) -> p h d", h=4),
                                        in1=w["beg"].unsqueeze(2).to_broadcast([128, 4, 64]), op=ALU.mult),
              r=[d_ktok, w["d_beg"]], w=[w["d_R"]])
        for half in range(2):
            k.pool(lambda e, half=half: e.tensor_tensor(out=w["kd"][:, :, half * 64:(half + 1) * 64],
                                                        in0=ktok[:, n, :].rearrange("p (h d) -> p h d", h=4),
                                                        in1=w["ev"][:, 4:8].unsqueeze(2).to_broadcast([128, 4, 64]), op=ALU.mult),
                   r=[d_ktok, w["d_ev"]], w=[w["d_kd"]])
        if _stop == "pre6":
            return None, None
        A, AT = w["A"][0], w["AT"][0]
        dA, dAT = w["d_A"][0], w["d_AT"][0]
        Tm, TT = w["A"][1], w["AT"][1]
        dT, dTT = w["d_A"][1], w["d_AT"][1]
        M1, M1t = w["X"][0], w["X"][1]
        dM1, dM1t = w["d_X"][0], w["d_X"][1]
        mo = 0 if dd == 0 else 7
        mt = 7 if dd == 0 else 0
        I4 = self.ident_b[:].unsqueeze(1).to_broadcast([128, 4, 128])
        msk = lambda s_: LMK[:, (mo + s_) * 128:(mo + s_ + 1) * 128].unsqueeze(1).to_broadcast([128, 4, 128])
        mskT = lambda s_: LMK[:, (mt + s_) * 128:(mt + s_ + 1) * 128].unsqueeze(1).to_broadcast([128, 4, 128])
        k.pool(lambda e: e.tensor_tensor(out=Tm, in0=A, in1=msk(0), op=ALU.mult), r=[dA, d_LMK], w=[dT])
        k.dve(lambda e: e.tensor_tensor(out=Tm, in0=Tm, in1=I4, op=ALU.add), r=[dT, self.d_const], w=[dT])
        k.pool(lambda e: e.tensor_tensor(out=TT, in0=AT, in1=mskT(0), op=ALU.mult), r=[dAT, d_LMK], w=[dTT])
        k.dve(lambda e: e.tensor_tensor(out=TT, in0=TT, in1=I4, op=ALU.add), r=[dTT, self.d_const], w=[dTT])
        fl = lambda x: x.rearrange("p a b -> p (a b)")
        def mk_E(lev):
            j = lev % 2
            k.pool(lambda e, lev=lev, j=j: e.tensor_tensor(out=w["E"][j], in0=A, in1=msk(lev), op=ALU.mult), r=[dA, d_LMK], w=[w["d_E"][j]])
            k.dve(lambda e, lev=lev, j=j: e.tensor_tensor(out=w["ET"][j], in0=AT, in1=mskT(lev), op=ALU.mult), r=[dAT, d_LMK], w=[w["d_ET"][j]])
        mk_E(1)
        for lev in range(1, 7):
            j = lev % 2
            E, ET, dE, dET = w["E"][j], w["ET"][j], w["d_E"][j], w["d_ET"][j]
            for h in range(4):
                k.pe(lambda e, h=h, ET=ET: e.matmul(B0[:, h * 128:(h + 1) * 128], ET[:, h, :], Tm[:, h, :], start=True, stop=True),
                     r=[dET, dT], w=[dB0])
            for h in range(4):
                k.pe(lambda e, h=h, E=E: e.matmul(B1[:, h * 128:(h + 1) * 128], E[:, h, :], TT[:, h, :], start=True, stop=True),
                     r=[dE, dTT], w=[dB1])
            if lev < 6:
                mk_E(lev + 1)
            k.act(lambda e: e.activation(out=fl(M1), in_=B0, func=AF.Copy), r=[dB0], w=[dM1])
            k.act(lambda e: e.activation(out=fl(M1t), in_=B1, func=AF.Copy), r=[dB1], w=[dM1t])
            for h in range(4):
                k.pe(lambda e, h=h: e.matmul(B2[:, h * 128:(h + 1) * 128], TT[:, h, :], M1[:, h, :], start=True, stop=True),
                     r=[dTT, dM1], w=[dB2])
            for h in range(4):
                k.pe(lambda e, h=h: e.matmul(B3[:, h * 128:(h + 1) * 128], Tm[:, h, :], M1t[:, h, :], start=True, stop=True),
                     r=[dT, dM1t], w=[dB3])
            k.dve(lambda e: e.tensor_tensor(out=fl(Tm), in0=fl(Tm), in1=B2, op=ALU.add), r=[dT, dB2], w=[dT])
            k.dve(lambda e: e.tensor_tensor(out=fl(TT), in0=fl(TT), in1=B3, op=ALU.add), r=[dTT, dB3], w=[dTT])
        Rm = w["R"].rearrange("p h (two d) -> p h two d", two=2)
        for h in range(4):
            k.pe(lambda e, h=h: e.matmul(B1[:, h * 64:(h + 1) * 64], TT[:, h, :], Rm[:, h, 0, :], start=True, stop=True),
                 r=[dTT, w["d_R"]], w=[dB1])
        k.act(lambda e: e.activation(out=w["u"].rearrange("p a b -> p (a b)"), in_=B1[:, 0:256], func=AF.Copy), r=[dB1], w=[w["d_u"]])
        for h in range(4):
            k.pe(lambda e, h=h: e.matmul(B2[0:64, h * 128:(h + 1) * 128], Rm[:, h, 1, :], TT[:, h, :], start=True, stop=True),
                 r=[dTT, w["d_R"]], w=[dB2])
        k.act(lambda e: e.activation(out=w["wT"].rearrange("p a b -> p (a b)"), in_=B2[0:64, :], func=AF.Copy),
              r=[dB2], w=[w["d_wT"]])
        return w["u"], w["d_u"]

    def unit_scan(dd, n, Xf, dXf, need_out):
        w = W_[dd]
        c0 = n * 128
        B0, dB0 = bank(dd, 0)
        B1, dB1 = bank(dd, 1)
        for h in range(4):
            k.pe(lambda e, h=h: e.matmul(B0[:, h * 64:(h + 1) * 64], w["wT"][:, h, :], w["Sb"][0:64, h, :], start=True, stop=True),
                 r=[w["d_wT"], w["d_Sb"]], w=[dB0])
        if need_out:
            for h in range(4):
                hp = slice(64 * (h % 2), 64 * (h % 2) + 64)
                k.pe(lambda e, h=h, hp=hp: e.matmul(B0[:, 256 + h * 64:256 + (h + 1) * 64], qT[:, h // 2, c0:c0 + 128],
                                                    w["SbQ"][:, h, :], start=True, stop=True),
                     r=[d_qk, w["d_SbQ"]], w=[dB0])
        k.dve(lambda e: e.tensor_tensor(out=w["vn"], in0=Xf, in1=B0[:, 0:256].rearrange("p (h d) -> p h d", h=4), op=ALU.subtract),
              r=[dXf, dB0], w=[w["d_vn"]])
        if need_out:
            for h in range(4):
                k.pe(lambda e, h=h: e.matmul(B1[:, h * 64:(h + 1) * 64], w["QKT"][:, h, :], w["vn"][:, h, :], start=True, stop=True),
                     r=[w["d_QKT"], w["d_vn"]], w=[dB1])
            k.act(lambda e: e.activation(out=w["o2"].rearrange("p a b -> p (a b)"), in_=B1[:, 0:256], func=AF.Copy), r=[dB1], w=[w["d_o2"]])
            k.dve(lambda e: e.tensor_tensor(out=w["ot"], in0=B0[:, 256:512].rearrange("p (h d) -> p h d", h=4),
                                            in1=w["ev"][:, 0:4].unsqueeze(2).to_broadcast([128, 4, 64]), op=ALU.mult),
                  r=[dB0, w["d_ev"]], w=[w["d_ot"]])
            k.pool(lambda e: e.tensor_tensor(out=w["ot"], in0=w["ot"], in1=w["o2"], op=ALU.add), r=[w["d_ot"], w["d_o2"]], w=[w["d_ot"]])
            if first_visit[n]:
                first_visit[n] = False
                k.pool(lambda e: e.tensor_copy(out=otok[:, n, :].rearrange("p (h d) -> p h d", h=4), in_=w["ot"]), r=[w["d_ot"]], w=[d_otok[n]])
            else:
                k.pool(lambda e: e.tensor_tensor(out=otok[:, n, :].rearrange("p (h d) -> p h d", h=4),
                                                 in0=otok[:, n, :].rearrange("p (h d) -> p h d", h=4), in1=w["ot"], op=ALU.add),
                       r=[w["d_ot"], d_otok[n]], w=[d_otok[n]])
        for h in range(4):
            k.pe(lambda e, h=h: e.matmul(B1[:, 256 + h * 64:256 + (h + 1) * 64], w["kd"][:, h, :], w["vn"][:, h, :], start=True, stop=True),
                 r=[w["d_kd"], w["d_vn"]], w=[dB1])
        k.dve(lambda e: e.tensor_tensor(out=w["S"], in0=w["S"], in1=w["ev"][:, 8:12].unsqueeze(2).to_broadcast([128, 4, 64]), op=ALU.mult),
              r=[w["d_S"], w["d_ev"]], w=[w["d_S"]])
        k.dve(lambda e: e.tensor_tensor(out=w["S"], in0=w["S"], in1=B1[:, 256:512].rearrange("p (h d) -> p h d", h=4), op=ALU.add),
              r=[w["d_S"], dB1], w=[w["d_S"]])
        k.act(lambda e: e.activation(out=w["Sb"], in_=w["S"], func=AF.Copy), r=[w["d_S"]], w=[w["d_Sb"]])
        k.pool(lambda e: e.tensor_tensor(out=w["SbQ"], in0=w["S"], in1=hm.unsqueeze(2).to_broadcast([128, 4, 64]), op=ALU.mult),
               r=[w["d_S"], d_hm], w=[w["d_SbQ"]])

    for step in range(1 if _stop.startswith("pre") else (int(_stop[4:].rstrip("p")) if _stop.startswith("step") else 18)):
        pend = []
        for dd in range(2):
            n = (orderF if dd == 0 else orderB)[step]
            need_out = (n < 16) or with_ctx
            Xf, dXf = unit_pre(dd, n)
            pend.append((dd, n, Xf, dXf, need_out))
        if _stop.startswith("pre") or (_stop.startswith("step") and _stop.endswith("p")):
            continue
        if "gdn_u0" in self.dbg and step == 0:
            self.k.barrier()
            chd = k.chan()
            w0 = W_[0]
            Xf0 = pend[0][2]
            for nm, ap_, n_, dt_ in (("D", w0["D"], 512, F32), ("X", Xf0, 256, BF16), ("QKT", w0["QKT"], 512, BF16),
                                     ("A0", w0["A"][1], 512, BF16), ("AT0", w0["AT"][1], 512, BF16), ("kd", w0["kd"], 512, BF16)):
                o_ = self.dout("dbg_" + nm, [128, n_], dt_)
                k.dma("sp", o_[:, :], ap_.rearrange("p a b -> p (a b)"), chd)
            o_ = self.dout("dbg_ev", [128, 12])
            k.dma("sp", o_[:, :], w0["ev"], chd)
            o_ = self.dout("dbg_wT", [64, 512], BF16)
            k.dma("sp", o_[:, :], w0["wT"].rearrange("p a b -> p (a b)"), chd)
            self.k.barrier()
        for (dd, n, Xf, dXf, need_out) in pend:
            unit_scan(dd, n, Xf, dXf, need_out)
    if "gdn_otok" in self.dbg:
        self.k.barrier()
        chd = k.chan()
        o_ = self.dout("dbg_otok", [128, NT * 256])
        k.dma("sp", o_[:, :], otok.rearrange("p a b -> p (a b)"), chd)
        self.k.barrier()
    if _stop:
        return
    _arena_rewind(self, markO)
    _gdn_out(self, l, with_ctx, otok, d_otok)


def _gdn_out(self, l, with_ctx, otok, d_otok):
    k = self.k
    ntt = NT if with_ctx else 16
    _wring_init(self, 1)
    gT = self.aa([128, 2, T], BF16)
    d_g = Dep()
    blocks = _blocks(self, with_ctx)

    def evac(c, m, t0, n, pst, dps):
        ct = (c - 768) // 128
        k.act(lambda e: e.activation(out=gT[:, ct, t0:t0 + n], in_=pst, func=AF.Silu), r=[dps], w=[d_g])
    _proj(self, l, [(768, 256)], blocks, evac, PROJ_BANKS)
    gn = _ppv(self, l, "gnorm")
    sq = self.aa([128, 4, 64])
    ss = self.aa([128, 4])
    on = [self.aa([128, 256], BF16) for _ in range(2)]
    d_sq, d_ss = Dep(), Dep()
    d_on = [Dep(), Dep()]
    psT = self.ps[2][:, 0:512].bitcast(BF16)
    d_psT = self.d_ps[2][0]
    for tt in range(ntt):
        o3 = otok[:, tt, :].rearrange("p (h d) -> p h d", h=4)
        j = tt % 2
        k.dve(lambda e, o3=o3: e.tensor_tensor(out=sq, in0=o3, in1=o3, op=ALU.mult), r=[d_otok[tt]], w=[d_sq])
        k.dve(lambda e: e.reduce_sum(out=ss, in_=sq, axis=mybir.AxisListType.X), r=[d_sq], w=[d_ss])
        k.act(lambda e: e.activation(out=ss, in_=ss, func=AF.Sqrt, bias=EPS, scale=1.0 / 64), r=[d_ss], w=[d_ss])
        k.dve(lambda e: e.reciprocal(out=ss, in_=ss), r=[d_ss], w=[d_ss])
        k.dve(lambda e, o3=o3, j=j: e.tensor_tensor(out=on[j].rearrange("p (h d) -> p h d", h=4), in0=o3,
                                                    in1=ss.unsqueeze(2).to_broadcast([128, 4, 64]), op=ALU.mult),
              r=[d_otok[tt], d_ss], w=[d_on[j]])
        for ct in range(2):
            k.pe(lambda e, j=j, ct=ct: e.transpose(out=psT[:, ct * 128:(ct + 1) * 128], in_=on[j][:, ct * 128:(ct + 1) * 128],
                                                   identity=self.ident_b[:]), r=[d_on[j], self.d_const], w=[d_psT])
        for ct in range(2):
            k.dve(lambda e, ct=ct, tt=tt: e.scalar_tensor_tensor(out=self.C[:, ct, tt * 128:(tt + 1) * 128], in0=psT[:, ct * 128:(ct + 1) * 128],
                                                                 scalar=gn[:, 0:1], in1=gT[:, ct, tt * 128:(tt + 1) * 128],
                                                                 op0=ALU.mult, op1=ALU.mult),
                  r=[d_psT, d_g, self.d_pp], w=[self.d_C[tt]])


from concourse.bass_utils import run_bass_kernel_spmd


def kernel(**inputs):
    inputs = {k_: np.asarray(v) for k_, v in inputs.items()}
    mk = MK()
    nc = mk.build()
    consts = const_inputs(inputs)
    pp = pp_host(inputs)

    def extra(b):
        e = {"pp": pp}
        e.update(consts)
        return e
    maps = host_inputs(mk, inputs, extra=extra)
    n = len(maps)
    res = run_bass_kernel_spmd(nc, maps, core_ids=list(range(n)))
    out = np.stack([np.asarray(res.results[b]["out"]) for b in range(n)], axis=0)
    return out.astype(np.float32)
```

```python
import numpy as np
import ml_dtypes
from contextlib import ExitStack
import concourse.bass as bass
import concourse.mybir as mybir

F32 = mybir.dt.float32
BF16 = mybir.dt.bfloat16
I32 = mybir.dt.int32
AF = mybir.ActivationFunctionType
ALU = mybir.AluOpType

D = 1024
L = 2048
LC = 256
T = L + LC
NT = T // 128
DEPTH = 2
EPS = 1e-6
IN_COLS = 3344
OFF_SC, OFF_HY, OFF_NA = 1040, 1808, 2576


class Dep:
    __slots__ = ("lw", "rd")

    def __init__(self):
        self.lw = None
        self.rd = []


class Chan:
    __slots__ = ("sem", "cnt", "key", "q")

    def __init__(self, sem, key):
        self.sem = sem
        self.cnt = 0
        self.key = key


class K:
    def __init__(self, nc, stack):
        self.nc = nc
        self.stack = stack
        self.eng = {"pe": nc.tensor, "act": nc.scalar, "dve": nc.vector, "pool": nc.gpsimd, "sp": nc.sync}
        self.sems = {}
        self.ecnt = {}
        self.waited = {e: {} for e in self.eng}
        for e in self.eng:
            self.sems["e_" + e] = stack.enter_context(nc.semaphore("e_" + e))
            self.ecnt[e] = 0
        self.chans = []
        self.bar_sem = stack.enter_context(nc.semaphore("bar"))
        self.bar_cnt = 0
        self.nops = 0

    def chan(self):
        key = "c%d" % len(self.chans)
        s = self.stack.enter_context(self.nc.semaphore(key))
        self.sems[key] = s
        c = Chan(s, key)
        c.q = None
        self.chans.append(c)
        return c

    def _wait(self, e, ev):
        key, val, src = ev
        if src == e and e == "pe":
            return
        w = self.waited[e]
        if w.get(key, 0) >= val:
            return
        w[key] = val
        self.eng[e].wait_ge(self.sems[key], val)

    def _deps(self, e, r, w):
        for d in r:
            if d.lw is not None:
                self._wait(e, d.lw)
        for d in w:
            if d.lw is not None and d.lw[2] != e:
                self._wait(e, d.lw)
            for ev in d.rd:
                if ev[2] != e:
                    self._wait(e, ev)

    def _commit(self, ev, r, w):
        for d in w:
            d.lw = ev
            d.rd = []
        for d in r:
            d.rd.append(ev)
            if len(d.rd) > 48:
                best = {}
                for x in d.rd:
                    if x[0] not in best or best[x[0]][1] < x[1]:
                        best[x[0]] = x
                d.rd = list(best.values())

    def op(self, e, fn, r=(), w=()):
        self._deps(e, r, w)
        ins = fn(self.eng[e])
        self.ecnt[e] += 1
        ins.then_inc(self.sems["e_" + e], 1)
        self._commit(("e_" + e, self.ecnt[e], e), r, w)
        self.nops += 1
        return ins

    def pe(self, fn, r=(), w=()):
        return self.op("pe", fn, r, w)

    def act(self, fn, r=(), w=()):
        return self.op("act", fn, r, w)

    def dve(self, fn, r=(), w=()):
        return self.op("dve", fn, r, w)

    def pool(self, fn, r=(), w=()):
        return self.op("pool", fn, r, w)

    def dma(self, q, out, in_, ch, r=(), w=(), **kw):
        self._deps(q, r, w)
        ins = self.eng[q].dma_start(out=out, in_=in_, **kw)
        ch.cnt += 16
        ch.q = q
        ins.then_inc(ch.sem, 16)
        self._commit((ch.key, ch.cnt, "dma"), r, w)
        self.nops += 1
        return ins

    def barrier(self):
        for e in self.eng:
            if self.ecnt[e] > 0:
                self._wait(e, ("e_" + e, self.ecnt[e], "self"))
        for c in self.chans:
            if c.cnt > 0:
                self._wait(c.q or "sp", (c.key, c.cnt, "dma"))
        for e in self.eng:
            self.eng[e].sem_inc(self.bar_sem, 1)
        self.bar_cnt += len(self.eng)
        for e in self.eng:
            self.eng[e].wait_ge(self.bar_sem, self.bar_cnt)
            for e2 in self.eng:
                self.waited[e]["e_" + e2] = self.ecnt[e2]
            for c in self.chans:
                self.waited[e][c.key] = c.cnt


class MK:
    def __init__(self, dbg=None, layers=(0, 1), inject_cat=False, mixers=("gdn", "sc", "hy", "na"), do_mlp=True,
                 phases=("mod", "p1", "mix", "p3", "p4"), inject_h=False):
        self.phases = phases
        self.inject_h = inject_h
        self.dbg = dbg or {}
        self.layers = layers
        self.inject_cat = inject_cat
        self.mixers = mixers
        self.do_mlp = do_mlp
        self.inputs = {}
        self.outputs = {}

    def din(self, name, shape, dtype=F32):
        t = self.nc.dram_tensor(name, list(shape), dtype, kind="ExternalInput").ap()
        self.inputs[name] = (tuple(shape), dtype)
        return t

    def dout(self, name, shape, dtype=F32):
        t = self.nc.dram_tensor(name, list(shape), dtype, kind="ExternalOutput").ap()
        self.outputs[name] = (tuple(shape), dtype)
        return t

    def W(self, name):
        if name not in self._w:
            self._w[name] = self.din(name, self._wshape[name])
        return self._w[name]

    def sb(self, st, name, shape, dtype=F32):
        self._n = getattr(self, "_n", 0) + 1
        return st.enter_context(self.nc.sbuf_tensor("%s_%d" % (name, self._n), list(shape), dtype))

    def build(self):
        nc = bass.Bass("TRN2", target_bir_lowering=False)
        self.nc = nc
        with ExitStack() as st:
            self.st = st
            self.k = K(nc, st)
            self._declare()
            self._consts()
            for l in self.layers:
                self._layer(l)
            self._finish()
        return nc

    def _declare(self):
        nc = self.nc
        self._wshape = {}
        self._w = {}
        self.x_in = self.din("x", [L, D])
        self.ctx_in = self.din("ctx", [LC, D])
        self.cvec_in = self.din("cvec", [128, 16])
        self._wshape["ada_w"] = [DEPTH, D, 6 * D]
        self.ada_bT = self.din("ada_bT", [128, DEPTH * 48])
        self.gains_in = self.din("gains", [128, 4 * DEPTH * 8])
        self._wshape["w_in"] = [DEPTH, D, IN_COLS]
        self._wshape["w_out"] = [DEPTH, D, D]
        self._wshape["mlp_w1"] = [DEPTH, D, 4 * D]
        self._wshape["mlp_w2"] = [DEPTH, 4 * D, D]
        self.ident_f_in = self.din("ident_f", [128, 128])
        self.ident_b_in = self.din("ident_b", [128, 128], BF16)
        self.out = self.dout("out", [L, D])
        self.xres = nc.dram_tensor("xres", [T, D], F32).ap()
        self.d_xres = [Dep() for _ in range(NT)]
        self.d_out = [Dep() for _ in range(NT)]
        if self.inject_cat:
            self.cat_in = self.din("cat_in", [D, T], BF16)
        self.ps = [self.st.enter_context(nc.psum_tensor("ps%d" % i, [128, 1024], F32)) for i in range(4)]
        self.d_ps = [[Dep(), Dep()] for _ in range(4)]

    def _consts(self):
        k, st = self.k, self.st
        sb = lambda n, s, d=F32: self.sb(st, n, s, d)
        self.ident_f = sb("ident_f", [128, 128])
        self.ident_b = sb("ident_b", [128, 128], BF16)
        self.ones_f = sb("ones_f", [128, 128])
        self.cvec = sb("cvec", [128, 16])
        self.adab = sb("adab", [128, DEPTH * 48])
        self.gains = sb("gains", [128, 4 * DEPTH * 8])
        self.d_const = Dep()
        ch = k.chan()
        k.dma("sp", self.ident_f[:], self.ident_f_in[:, :], ch, w=[self.d_const])
        k.dma("sp", self.ident_b[:], self.ident_b_in[:, :], ch, w=[self.d_const])
        k.dma("sp", self.cvec[:], self.cvec_in[:, :], ch, w=[self.d_const])
        k.dma("sp", self.adab[:], self.ada_bT[:, :], ch, w=[self.d_const])
        k.dma("sp", self.gains[:], self.gains_in[:, :], ch, w=[self.d_const])
        k.dve(lambda e: e.memset(self.ones_f[:], 1.0), w=[self.d_const])
        self.H = sb("H", [128, 8, T], BF16)
        self.C = sb("C", [128, 8, T], BF16)
        self.d_H = [Dep() for _ in range(NT)]
        self.d_C = [Dep() for _ in range(NT)]
        self.modT = sb("modT", [128, 48, 2])
        self.A1 = sb("A1", [128, 8, 2])
        self.A2 = sb("A2", [128, 8, 2])
        self.G1f = sb("G1f", [128, 8, 2])
        self.G2f = sb("G2f", [128, 8, 2])
        self.Gbc = sb("Gbc", [128, 2, 2, D])
        self.d_mod = Dep()
        self.d_gbc = Dep()
        self.NAR = 28416
        self.AR = sb("arena", [128, self.NAR])
        self.ar_off = 0
        self.NXT = 3
        self.d_xt = [Dep() for _ in range(self.NXT)]
        self.ch_xt_ld = [k.chan() for _ in range(self.NXT)]
        self.ch_xt_st = [k.chan() for _ in range(self.NXT)]
        self.d_xn = [Dep(), Dep()]
        self.d_junk = Dep()
        self.d_stat = [Dep() for _ in range(4)]
        self.d_tmpb = [Dep(), Dep()]
        self.d_tmpf = [Dep(), Dep()]
        self.xt_i = self.xn_i = self.stat_i = self.tmp_i = 0

    def arena_reset(self):
        self.k.barrier()
        self.ar_off = 0

    def aa(self, shape, dtype=F32):
        esz = 4 if dtype in (F32, I32) else 2
        n = int(np.prod(shape[1:]))
        nbytes = (n * esz + 31) // 32 * 32
        o = self.ar_off
        assert o + nbytes <= self.NAR * 4, "arena overflow %d" % (o + nbytes)
        self.ar_off = o + nbytes
        v = self.AR[:, o // 4:(o + nbytes) // 4]
        if dtype != F32:
            v = v.bitcast(dtype)
        v = v[:, 0:n]
        if len(shape) == 3:
            v = v.rearrange("p (a b) -> p a b", b=shape[2])
        elif len(shape) == 4:
            v = v.rearrange("p (a b c) -> p a b c", b=shape[2], c=shape[3])
        if shape[0] != 128:
            v = v[0:shape[0]]
        return v

    def _staging(self, norm=True):
        self.xt = [self.aa([128, D]) for i in range(self.NXT)]
        self.junk = self.aa([128, D], BF16)
        self.stat = [self.aa([128, 8]) for i in range(4)]
        self.tmpf = [self.aa([128, D]) for i in range(2)]
        if norm:
            self.xn = [self.aa([128, D], BF16) for i in range(2)]
            self.tmpb = [self.aa([128, D], BF16) for i in range(2)]

    def gain(self, kind, l):
        o = (kind * DEPTH + l) * 8
        return self.gains[:, o:o + 8]

    def _layer(self, l):
        ph = self.phases
        if "mod" in ph:
            self._modulation(l)
        if "p1" in ph:
            self.arena_reset()
            self._staging()
            for tt in range(NT):
                s = 0 if tt < 16 else 1
                xi = self._load_x(l, tt, first=True)
                self._norm_tile(xi, s, self.A1, self.modT[:, 0:8, :], self.H, tt, self.d_H[tt])
        elif self.inject_h:
            ch = self.k.chan()
            hin = self.din("h_in", [D, T], BF16)
            for tt in range(NT):
                self.k.dma("sp", self.H[:, :, tt * 128:(tt + 1) * 128],
                           hin[:, tt * 128:(tt + 1) * 128].rearrange("(a p) t -> p a t", p=128), ch, w=[self.d_H[tt]])
        if "hx" in self.dbg and self.dbg["hx"] == l:
            self._dump_feat("dbg_hx", self.H, self.d_H)
        if "mix" in ph:
            self._mixers(l)
        if "cat" in self.dbg and self.dbg["cat"] == l:
            self._dump_feat("dbg_cat", self.C, self.d_C)
        if "p3" in ph:
            self._p3(l)
        if "hx2" in self.dbg and self.dbg["hx2"] == l:
            self._dump_feat("dbg_hx2", self.C, self.d_C)
        if "p4" in ph:
            self._p4(l)

    def _xsrc(self, l, tt, first):
        if l == self.layers[0] and l == 0 and first:
            if tt < 16:
                return self.x_in[tt * 128:(tt + 1) * 128, :], None
            return self.ctx_in[(tt - 16) * 128:(tt - 15) * 128, :], None
        return self.xres[tt * 128:(tt + 1) * 128, :], self.d_xres[tt]

    def _load_x(self, l, tt, first):
        k = self.k
        i = self.xt_i
        self.xt_i = (i + 1) % self.NXT
        src, dep = self._xsrc(l, tt, first)
        k.dma("sp", self.xt[i][:], src, self.ch_xt_ld[i], r=[dep] if dep else [], w=[self.d_xt[i]])
        return i

    def _store_x(self, i, dst_ap, dst_dep):
        self.k.dma("sp", dst_ap, self.xt[i][:], self.ch_xt_st[i], r=[self.d_xt[i]], w=[dst_dep])

    def _rstd(self, src_ap, src_deps):
        k = self.k
        j = self.stat_i
        self.stat_i = (j + 1) % 4
        stt, dst = self.stat[j], self.d_stat[j]
        k.act(lambda e: e.activation(out=self.junk[:], in_=src_ap, func=AF.Square, accum_out=stt[:, 0:1]),
              r=src_deps, w=[self.d_junk, dst])
        k.act(lambda e: e.activation(out=stt[:, 1:2], in_=stt[:, 0:1], func=AF.Sqrt, bias=EPS, scale=1.0 / D),
              r=[dst], w=[dst])
        k.dve(lambda e: e.reciprocal(out=stt[:, 2:3], in_=stt[:, 1:2]), r=[dst], w=[dst])
        return stt[:, 2:3], dst

    def _norm_tile(self, xi, s, A, B, Hbuf, tt, dH):
        k = self.k
        xt, dxt = self.xt[xi], self.d_xt[xi]
        rs, drs = self._rstd(xt[:], [dxt])
        j = self.xn_i
        self.xn_i = 1 - j
        xn, dxn = self.xn[j], self.d_xn[j]
        k.dve(lambda e: e.tensor_scalar(out=xn[:], in0=xt[:], scalar1=rs, scalar2=None, op0=ALU.mult),
              r=[dxt, drs], w=[dxn])
        pi = 3
        psb = self.ps[pi][:, 0:512].bitcast(BF16)
        dps = self.d_ps[pi][0]
        for dt in range(8):
            k.pe(lambda e, dt=dt: e.transpose(out=psb[:, dt * 128:(dt + 1) * 128], in_=xn[:, dt * 128:(dt + 1) * 128],
                                              identity=self.ident_b[:]),
                 r=[dxn, self.d_const], w=[dps])
        ti = self.tmp_i
        self.tmp_i = 1 - ti
        tb, dtb = self.tmpb[ti], self.d_tmpb[ti]
        k.dve(lambda e: e.tensor_tensor(out=tb[:].rearrange("p (a b) -> p a b", a=8),
                                        in0=psb.rearrange("p (a b) -> p a b", a=8),
                                        in1=A[:, :, s:s + 1].to_broadcast([128, 8, 128]), op=ALU.mult),
              r=[dps, self.d_mod], w=[dtb])
        k.pool(lambda e: e.tensor_tensor(out=Hbuf[:, :, tt * 128:(tt + 1) * 128],
                                         in0=tb[:].rearrange("p (a b) -> p a b", a=8),
                                         in1=B[:, :, s:s + 1].to_broadcast([128, 8, 128]), op=ALU.add),
               r=[dtb, self.d_mod], w=[dH])

    def _modulation(self, l):
        k = self.k
        self.arena_reset()
        if True:
            sT = self.aa([128, 16], BF16)
            d_sT = Dep()
            k.act(lambda e: e.activation(out=sT[:], in_=self.cvec[:], func=AF.Silu), r=[self.d_const], w=[d_sT])
            wb = [self.aa([128, 8, 512], BF16) for i in range(2)]
            dwb = [Dep(), Dep()]
            chw = [k.chan(), k.chan()]
            mod_ps = self.ps[0][:, 0:96]
            dps = self.d_ps[0][0]
            for g in range(12):
                i = g % 2
                src = self.W("ada_w")[l, :, g * 512:(g + 1) * 512].rearrange("(kt p) c -> p kt c", p=128)
                k.dma("pool", wb[i][:], src, chw[i], w=[dwb[i]])
                for jj in range(4):
                    jt = g * 4 + jj
                    for kt in range(8):
                        k.pe(lambda e, i=i, jj=jj, jt=jt, kt=kt: e.matmul(
                            mod_ps[:, jt * 2:jt * 2 + 2], wb[i][:, kt, jj * 128:(jj + 1) * 128],
                            sT[:, kt * 2:kt * 2 + 2], start=(kt == 0), stop=(kt == 7)),
                            r=[dwb[i], d_sT], w=[dps])
            dm = self.d_mod
            k.dve(lambda e: e.tensor_tensor(out=self.modT[:], in0=mod_ps.rearrange("p (a b) -> p a b", b=2),
                                            in1=self.adab[:, l * 48:(l + 1) * 48].unsqueeze(2).to_broadcast([128, 48, 2]),
                                            op=ALU.add), r=[dps, self.d_const], w=[dm])
            g = lambda kind: self.gain(kind, l).unsqueeze(2).to_broadcast([128, 8, 2])
            k.dve(lambda e: e.scalar_tensor_tensor(out=self.A1[:], in0=self.modT[:, 8:16, :], scalar=1.0, in1=g(0),
                                                   op0=ALU.add, op1=ALU.mult), r=[dm, self.d_const], w=[dm])
            k.dve(lambda e: e.scalar_tensor_tensor(out=self.A2[:], in0=self.modT[:, 32:40, :], scalar=1.0, in1=g(2),
                                                   op0=ALU.add, op1=ALU.mult), r=[dm, self.d_const], w=[dm])
            k.dve(lambda e: e.tensor_tensor(out=self.G1f[:], in0=self.modT[:, 16:24, :], in1=g(1), op=ALU.mult),
                  r=[dm, self.d_const], w=[dm])
            k.dve(lambda e: e.tensor_tensor(out=self.G2f[:], in0=self.modT[:, 40:48, :], in1=g(3), op=ALU.mult),
                  r=[dm, self.d_const], w=[dm])
            diag = [self.aa([128, 128]) for i in range(2)]
            ddiag = [Dep(), Dep()]
            n = 0
            for kind, Gf in enumerate((self.G1f, self.G2f)):
                for s in range(2):
                    for dt in range(8):
                        i = n % 2
                        n += 1
                        k.dve(lambda e, i=i, Gf=Gf, dt=dt, s=s: e.tensor_scalar(
                            out=diag[i][:], in0=self.ident_f[:], scalar1=Gf[:, dt, s:s + 1], scalar2=None, op0=ALU.mult),
                            r=[dm, self.d_const], w=[ddiag[i]])
                        pi = 1 + (n % 2)
                        pst = self.ps[pi][:, 0:128]
                        k.pe(lambda e, i=i, pst=pst: e.matmul(pst, self.ones_f[:], diag[i][:], start=True, stop=True),
                             r=[ddiag[i], self.d_const], w=[self.d_ps[pi][0]])
                        k.act(lambda e, pst=pst, kind=kind, s=s, dt=dt: e.activation(
                            out=self.Gbc[:, kind, s, dt * 128:(dt + 1) * 128], in_=pst, func=AF.Copy),
                            r=[self.d_ps[pi][0]], w=[self.d_gbc])
        if "mod" in self.dbg and self.dbg["mod"] == l:
            o = self.dout("dbg_mod", [128, 96])
            ch = k.chan()
            k.dma("sp", o[:, :], self.modT[:].rearrange("p a b -> p (a b)"), ch, r=[self.d_mod])
            o2 = self.dout("dbg_gbc", [128, 4 * D])
            k.dma("sp", o2[:, :], self.Gbc[:].rearrange("p a b c -> p (a b c)"), ch, r=[self.d_gbc])

    def _mixers(self, l):
        k = self.k
        if self.inject_cat:
            ch = k.chan()
            for tt in range(NT):
                k.dma("sp", self.C[:, :, tt * 128:(tt + 1) * 128],
                      self.cat_in[:, tt * 128:(tt + 1) * 128].rearrange("(a p) t -> p a t", p=128), ch, w=[self.d_C[tt]])
            return
        raise NotImplementedError

    def _p3(self, l):
        k = self.k
        last = (l == DEPTH - 1)
        ntt = 16 if last else NT
        self.arena_reset()
        self._staging()
        if True:
            wo = self.aa([128, 8, D], BF16)
            dwo = Dep()
            ch = k.chan()
            k.dma("pool", wo[:], self.W("w_out")[l].rearrange("(kt p) c -> p kt c", p=128), ch, w=[dwo])
            for tt in range(ntt):
                s = 0 if tt < 16 else 1
                pi = tt % 3
                yps = self.ps[pi]
                for half in range(2):
                    for mt in range(8):
                        k.pe(lambda e, half=half, mt=mt, yps=yps, tt=tt: e.matmul(
                            yps[:, half * 512:(half + 1) * 512], self.C[:, mt, tt * 128:(tt + 1) * 128],
                            wo[:, mt, half * 512:(half + 1) * 512], start=(mt == 0), stop=(mt == 7)),
                            r=[self.d_C[tt], dwo], w=[self.d_ps[pi][half]])
                xi = self._load_x(l, tt, first=True)
                self._resid_update(xi, yps[:], self.d_ps[pi], 0, s)
                self._store_x(xi, self.xres[tt * 128:(tt + 1) * 128, :], self.d_xres[tt])
                self._norm_tile(xi, s, self.A2, self.modT[:, 24:32, :], self.C, tt, self.d_C[tt])

    def _resid_update(self, xi, y_ap, y_deps, kind, s):
        k = self.k
        rs, drs = self._rstd(y_ap, list(y_deps))
        ti = self.tmp_i
        self.tmp_i = 1 - ti
        tf, dtf = self.tmpf[ti], self.d_tmpf[ti]
        k.dve(lambda e: e.scalar_tensor_tensor(out=tf[:], in0=y_ap, scalar=rs, in1=self.Gbc[:, kind, s, :],
                                               op0=ALU.mult, op1=ALU.mult),
              r=list(y_deps) + [drs, self.d_gbc], w=[dtf])
        xt, dxt = self.xt[xi], self.d_xt[xi]
        k.dve(lambda e: e.tensor_tensor(out=xt[:], in0=xt[:], in1=tf[:], op=ALU.add), r=[dtf, dxt], w=[dxt])

    def _p4(self, l):
        k = self.k
        last = (l == DEPTH - 1)
        if last:
            sblocks = [(0, 768), (768, 768), (1536, 512)]
        else:
            sblocks = [(0, 768), (768, 768), (1536, 768)]
        self.arena_reset()
        self._staging(norm=False)
        if True:
            hT = self.aa([128, 32, 768], BF16)
            d_hT = [[Dep() for _ in range(2)] for _ in range(32)]
            HF = self.H[:].rearrange("p a t -> p (a t)")
            w1c = [HF[:, i * 4096:(i + 1) * 4096].rearrange("p (k c) -> p k c", c=512) for i in range(3)]
            d_w1c = [Dep(), Dep(), Dep()]
            ch_w1 = [k.chan(), k.chan(), k.chan()]
            w2c = [HF[:, 12288 + i * 2048: 12288 + (i + 1) * 2048].rearrange("p (k c) -> p k c", c=512) for i in range(3)]
            d_w2c = [Dep() for _ in range(3)]
            ch_w2 = [k.chan() for _ in range(3)]
            rl = [self.aa([128, 384], BF16) for i in range(2)]
            d_rl = [Dep(), Dep()]
            ytok = self.aa([128, 6, D])
            d_ytok = [Dep() for _ in range(6)]
            n_w1 = 0
            n_w2 = 0
            n_rl = 0
            for (t0, n) in sblocks:
                n2 = n // 2
                ntl = n // 128
                for ffc in range(8):
                    i = n_w1 % 3
                    n_w1 += 1
                    k.dma("pool", w1c[i], self.W("mlp_w1")[l, :, ffc * 512:(ffc + 1) * 512].rearrange("(kt p) c -> p kt c", p=128),
                          ch_w1[i], w=[d_w1c[i]])
                    for f in range(4):
                        fft = ffc * 4 + f
                        for sbk in range(2):
                            hps = self.ps[3][:, sbk * 512: sbk * 512 + n2]
                            dhps = self.d_ps[3][sbk]
                            tts = range((t0 + sbk * n2) // 128, (t0 + (sbk + 1) * n2 + 127) // 128)
                            rdeps = [self.d_C[t] for t in tts]
                            for dt in range(8):
                                k.pe(lambda e, i=i, f=f, dt=dt, hps=hps, sbk=sbk: e.matmul(
                                    hps, w1c[i][:, dt, f * 128:(f + 1) * 128],
                                    self.C[:, dt, t0 + sbk * n2: t0 + (sbk + 1) * n2], start=(dt == 0), stop=(dt == 7)),
                                    r=[d_w1c[i]] + rdeps, w=[dhps])
                            j = n_rl % 2
                            n_rl += 1
                            k.act(lambda e, j=j, hps=hps: e.activation(out=rl[j][:, 0:n2], in_=hps, func=AF.Relu),
                                  r=[dhps], w=[d_rl[j]])
                            k.dve(lambda e, j=j, fft=fft, sbk=sbk: e.tensor_tensor(
                                out=hT[:, fft, sbk * n2:(sbk + 1) * n2], in0=rl[j][:, 0:n2], in1=rl[j][:, 0:n2], op=ALU.mult),
                                r=[d_rl[j]], w=[d_hT[fft][sbk]])
                for dh in range(2):
                    for ffc in range(8):
                        i = n_w2 % 3
                        n_w2 += 1
                        k.dma("pool", w2c[i],
                              self.W("mlp_w2")[l, ffc * 512:(ffc + 1) * 512, dh * 512:(dh + 1) * 512].rearrange("(f p) c -> p f c", p=128),
                              ch_w2[i], w=[d_w2c[i]])
                        for f in range(4):
                            fft = ffc * 4 + f
                            for tl in range(ntl):
                                pi, hb = tl // 2, tl % 2
                                sbk = (tl * 128) // n2
                                k.pe(lambda e, i=i, f=f, fft=fft, tl=tl, pi=pi, hb=hb: e.matmul(
                                    self.ps[pi][:, hb * 512:(hb + 1) * 512], hT[:, fft, tl * 128:(tl + 1) * 128],
                                    w2c[i][:, f, :], start=(fft == 0), stop=(fft == 31)),
                                    r=[d_w2c[i], d_hT[fft][sbk]], w=[self.d_ps[pi][hb]])
                    for tl in range(ntl):
                        pi, hb = tl // 2, tl % 2
                        k.act(lambda e, tl=tl, pi=pi, hb=hb, dh=dh: e.activation(
                            out=ytok[:, tl, dh * 512:(dh + 1) * 512], in_=self.ps[pi][:, hb * 512:(hb + 1) * 512], func=AF.Copy),
                            r=[self.d_ps[pi][hb]], w=[d_ytok[tl]])
                for tl in range(ntl):
                    tt = t0 // 128 + tl
                    s = 0 if tt < 16 else 1
                    xi = self._load_x(l, tt, first=False)
                    self._resid_update(xi, ytok[:, tl, :], [d_ytok[tl]], 1, s)
                    if last:
                        self._store_x(xi, self.out[tt * 128:(tt + 1) * 128, :], self.d_out[tt])
                    else:
                        self._store_x(xi, self.xres[tt * 128:(tt + 1) * 128, :], self.d_xres[tt])

    def _dump_feat(self, name, buf, deps):
        k = self.k
        o = self.dout(name, [D, T])
        self.arena_reset()
        if True:
            stg = self.aa([128, 8, 128])
            dst = Dep()
            ch = k.chan()
            for tt in range(NT):
                k.dve(lambda e, tt=tt: e.tensor_copy(out=stg[:], in_=buf[:, :, tt * 128:(tt + 1) * 128]), r=[deps[tt]], w=[dst])
                k.dma("sp", o[:, tt * 128:(tt + 1) * 128].rearrange("(a p) t -> p a t", p=128), stg[:], ch, r=[dst])

    def _finish(self):
        k = self.k
        if "xres" in self.dbg:
            self.arena_reset()
            self._staging()
            o = self.dout("dbg_xres", [T, D])
            ch = k.chan()
            for tt in range(NT):
                xi = self._load_x(1, tt, first=False)
                self._store_x(xi, o[tt * 128:(tt + 1) * 128, :], Dep())
        k.barrier()


def host_inputs(mk, inputs, extra=None):
    bf = ml_dtypes.bfloat16
    f32 = np.float32
    shared = {}
    shared["ada_w"] = np.ascontiguousarray(inputs["ada_w"], dtype=f32)
    shared["ada_bT"] = np.ascontiguousarray(
        inputs["ada_b"].reshape(DEPTH, 48, 128).transpose(2, 0, 1).reshape(128, DEPTH * 48), dtype=f32)
    g = np.stack([inputs["norm_pre_mix"], inputs["norm_post_mix"], inputs["norm_pre_mlp"], inputs["norm_post_mlp"]])
    shared["gains"] = np.ascontiguousarray(g.reshape(4, DEPTH, 8, 128).transpose(3, 0, 1, 2).reshape(128, -1), dtype=f32)
    for n in ("w_in", "w_out", "mlp_w1", "mlp_w2"):
        shared[n] = np.ascontiguousarray(inputs[n], dtype=f32)
    shared["ident_f"] = np.eye(128, dtype=f32)
    shared["ident_b"] = np.eye(128, dtype=f32).astype(bf)
    maps = []
    for b in range(inputs["x"].shape[0]):
        m = dict(shared)
        m["x"] = np.ascontiguousarray(inputs["x"][b], dtype=f32)
        m["ctx"] = np.ascontiguousarray(inputs["ctx"][b], dtype=f32)
        cv = np.stack([inputs["c"][b].reshape(8, 128), inputs["c_ctx"].reshape(8, 128)], axis=-1)
        m["cvec"] = np.ascontiguousarray(cv.transpose(1, 0, 2).reshape(128, 16), dtype=f32)
        if extra:
            m.update(extra(b))
        maps.append({kk: v for kk, v in m.items() if kk in mk.inputs})
    return maps


PP_ENTRIES = [("scw", 6), ("hyw", 18), ("gdw", 18), ("hybias", 2), ("gnorm", 1), ("hy_w1", 64), ("hy_w2", 64),
              ("hy_w3", 64), ("hy_w4", 512), ("hy_b", 3), ("hy_f", 3), ("alog", 8), ("dtb", 8)]
PP_OFF = {}
_o = 0
for _n, _w in PP_ENTRIES:
    PP_OFF[_n] = (_o, _w)
    _o += _w
PP_W = _o


def pp_host(inputs):
    pp = np.zeros((128, DEPTH * PP_W), np.float32)
    for l in range(DEPTH):
        def put(name, arr):
            o, w = PP_OFF[name]
            arr = np.asarray(arr, np.float32)
            assert arr.shape[1] == w, (name, arr.shape)
            pp[:arr.shape[0], l * PP_W + o: l * PP_W + o + w] = arr
        put("scw", inputs["sc_conv"][l].reshape(3, 2, 128).transpose(2, 1, 0).reshape(128, 6))
        put("hyw", inputs["hy_conv"][l].reshape(3, 6, 128).transpose(2, 1, 0).reshape(128, 18))
        put("gdw", inputs["gdn_conv"][l].reshape(3, 6, 128).transpose(2, 1, 0).reshape(128, 18))
        put("hybias", inputs["hy_bias"][l].reshape(2, 128).T)
        put("gnorm", np.tile(inputs["gdn_norm"][l], 2).reshape(128, 1))
        put("hy_w1", inputs["hy_w1"][l])
        put("hy_w2", inputs["hy_w2"][l])
        put("hy_w3", inputs["hy_w3"][l])
        put("hy_w4", inputs["hy_w4"][l])
        put("hy_b", np.stack([inputs["hy_b1"][l], inputs["hy_b2"][l], inputs["hy_b3"][l]], axis=1))
        put("hy_f", inputs["hy_freq"][l].T)
        put("alog", np.tile(inputs["gdn_a_log"][l].reshape(1, 8), (128, 1)))
        put("dtb", np.tile(inputs["gdn_dt_bias"][l].reshape(1, 8), (128, 1)))
    return pp


def na_consts(inputs):
    rpb = np.asarray(inputs["na_rpb"], np.float32)
    par = np.arange(2)[:, None, None, None]
    kc = np.arange(64)[None, :, None, None]
    i = np.arange(14)[None, None, :, None]
    qc = np.arange(64)[None, None, None, :]
    dc = np.clip(kc - qc, -15, 15) + 15
    di = np.broadcast_to(i + par, (2, 64, 14, 64))
    dcb = np.broadcast_to(dc, (2, 64, 14, 64))
    g = rpb[:, :, di, dcb]
    g = g.transpose(0, 2, 3, 1, 4, 5).reshape(DEPTH, 128, 4 * 14 * 64)
    cs = np.clip(np.arange(64) - 8, 0, 48)
    kcv = np.arange(64)[:, None]
    valid = (kcv >= cs[None, :]) & (kcv < cs[None, :] + 16)
    m = np.where(valid, 0.0, -1e30).astype(np.float32)
    mask = np.concatenate([m, m], axis=0)
    return np.ascontiguousarray(g), np.ascontiguousarray(mask)


def _mix_common_init(self):
    if getattr(self, "pp", None) is not None:
        return
    k = self.k
    self.pp_in = self.din("pp", [128, DEPTH * PP_W])
    self.pp = self.sb(self.st, "pp", [128, DEPTH * PP_W])
    self.d_pp = Dep()
    ch = k.chan()
    k.dma("sp", self.pp[:], self.pp_in[:, :], ch, w=[self.d_pp])


def _ppv(self, l, name, rows=128):
    o, w = PP_OFF[name]
    return self.pp[0:rows, l * PP_W + o: l * PP_W + o + w]


def _blocks(self, with_ctx):
    b = [(i * 512, 512) for i in range(4)]
    if with_ctx:
        b.append((L, LC))
    return b


def _wring_init(self, n=2):
    self.wring = [(self.aa([128, 8, 512], BF16), Dep(), self.wring_ch[i]) for i in range(n)]
    self.wring_i = 0
    self.bank_i = 0


def _proj(self, l, chunks, blocks, evac, banks):
    k = self.k
    for (c0, ncol) in chunks:
        i = self.wring_i
        self.wring_i = (i + 1) % len(self.wring)
        wap, dw, chw = self.wring[i]
        k.dma("pool", wap[:, :, 0:ncol], self.W("w_in")[l, :, c0:c0 + ncol].rearrange("(kt p) c -> p kt c", p=128),
              chw, w=[dw])
        for cc in range(0, ncol, 128):
            m = min(128, ncol - cc)
            for (t0, n) in blocks:
                pi, hb = banks[self.bank_i % len(banks)]
                self.bank_i += 1
                pst = self.ps[pi][0:m, hb * 512: hb * 512 + n]
                dps = self.d_ps[pi][hb]
                hd = [self.d_H[t] for t in range(t0 // 128, (t0 + n + 127) // 128)]
                for dt in range(8):
                    k.pe(lambda e, dt=dt, pst=pst, wap=wap, cc=cc, m=m, t0=t0, n=n: e.matmul(
                        pst, wap[:, dt, cc:cc + m], self.H[:, dt, t0:t0 + n], start=(dt == 0), stop=(dt == 7)),
                        r=[dw] + hd, w=[dps])
                evac(c0 + cc, m, t0, n, pst, dps)


def _dwconv(self, eng, out_ap, in_ap, w3, ranges, r, w):
    k = self.k
    for (a, b) in ranges:
        k.op(eng, lambda e, a=a, b=b: e.tensor_scalar(out=out_ap[:, a:b], in0=in_ap[:, a:b], scalar1=w3[:, 1:2],
                                                      scalar2=None, op0=ALU.mult), r=r, w=w)
        k.op(eng, lambda e, a=a, b=b: e.scalar_tensor_tensor(out=out_ap[:, a + 1:b], in0=in_ap[:, a:b - 1], scalar=w3[:, 0:1],
                                                             in1=out_ap[:, a + 1:b], op0=ALU.mult, op1=ALU.add),
             r=list(r) + list(w), w=w)
        k.op(eng, lambda e, a=a, b=b: e.scalar_tensor_tensor(out=out_ap[:, a:b - 1], in0=in_ap[:, a + 1:b], scalar=w3[:, 2:3],
                                                             in1=out_ap[:, a:b - 1], op0=ALU.mult, op1=ALU.add),
             r=list(r) + list(w), w=w)


PROJ_BANKS = [(0, 0), (0, 1), (1, 0), (1, 1)]


def _mix_sc(self, l, with_ctx):
    k = self.k
    self.arena_reset()
    _wring_init(self)
    ranges = [(0, L)] + ([(L, T)] if with_ctx else [])
    blocks = _blocks(self, with_ctx)
    ntok = T if with_ctx else L
    ntt = ntok // 128
    pxb = self.aa([128, 6, T], BF16)
    d_px = [Dep() for _ in range(6)]

    def evac(c, m, t0, n, pst, dps):
        ct = (c - OFF_SC) // 128
        k.act(lambda e: e.activation(out=pxb[:, ct, t0:t0 + n], in_=pst, func=AF.Copy), r=[dps], w=[d_px[ct]])
    _proj(self, l, [(OFF_SC, 512), (OFF_SC + 512, 256)], blocks, evac, PROJ_BANKS)
    z = [self.aa([128, T]) for _ in range(2)]
    acc = [self.aa([128, T]) for _ in range(2)]
    scw = _ppv(self, l, "scw")
    for j in range(2):
        dz, dacc = Dep(), Dep()
        eng = "dve" if j == 0 else "pool"
        k.op(eng, lambda e, j=j: e.tensor_tensor(out=z[j][:, 0:ntok], in0=pxb[:, 2 + j, 0:ntok], in1=pxb[:, 4 + j, 0:ntok],
                                                 op=ALU.mult), r=[d_px[2 + j], d_px[4 + j]], w=[dz])
        _dwconv(self, "dve", acc[j], z[j], scw[:, j * 3:(j + 1) * 3], ranges, [dz, self.d_pp], [dacc])
        k.op(eng, lambda e, j=j: e.tensor_tensor(out=self.C[:, 2 + j, 0:ntok], in0=pxb[:, j, 0:ntok], in1=acc[j][:, 0:ntok],
                                                 op=ALU.mult), r=[d_px[j], dacc], w=[self.d_C[t] for t in range(ntt)])


def _mixers(self, l):
    k = self.k
    if self.inject_cat:
        ch = k.chan()
        for tt in range(NT):
            k.dma("sp", self.C[:, :, tt * 128:(tt + 1) * 128],
                  self.cat_in[:, tt * 128:(tt + 1) * 128].rearrange("(a p) t -> p a t", p=128), ch, w=[self.d_C[tt]])
        return
    _mix_common_init(self)
    if not hasattr(self, "wring_ch"):
        self.wring_ch = [k.chan() for _ in range(3)]
    with_ctx = l < DEPTH - 1
    if "sc" in self.mixers:
        _mix_sc(self, l, with_ctx)
    if "na" in self.mixers:
        _mix_na(self, l, with_ctx)
    if "hy" in self.mixers:
        _mix_hy(self, l, with_ctx)
    if "gdn" in self.mixers:
        _mix_gdn(self, l, with_ctx)


MK._mixers = _mixers


def const_inputs(inputs):
    out = {}
    g, mask = na_consts(inputs)
    out["na_rpbg"] = g
    out["na_mask"] = mask
    out.update(hy_consts())
    out.update(gdn_consts())
    return out


def _mix_na(self, l, with_ctx):
    k = self.k
    self.arena_reset()
    _wring_init(self)
    blocks = _blocks(self, True)
    qT = self.aa([128, 2, T], BF16)
    kT = self.aa([128, 2, T], BF16)
    d_q = [Dep() for _ in range(NT)]
    d_k = [Dep() for _ in range(NT)]

    def evac(c, m, t0, n, pst, dps):
        ct = (c - OFF_NA) // 128
        tts = range(t0 // 128, (t0 + n) // 128)
        if ct < 2:
            k.act(lambda e: e.activation(out=qT[:, ct, t0:t0 + n], in_=pst, func=AF.Copy, scale=0.125),
                  r=[dps], w=[d_q[t] for t in tts])
        else:
            k.dve(lambda e: e.tensor_copy(out=kT[:, ct - 2, t0:t0 + n], in_=pst), r=[dps], w=[d_k[t] for t in tts])
    _proj(self, l, [(OFF_NA, 512)], blocks, evac, PROJ_BANKS)
    Ve = self.aa([128, NT, 4, 65], BF16)
    Vo = self.aa([128, 15, 4, 65], BF16)
    d_Ve, d_Vo = Dep(), Dep()
    k.pool(lambda e: e.memset(Ve, 1.0), w=[d_Ve])
    k.pool(lambda e: e.memset(Vo, 1.0), w=[d_Vo])
    i = self.wring_i
    self.wring_i = (i + 1) % len(self.wring)
    wv, dwv, chv = self.wring[i]
    k.dma("pool", wv[:, :, 0:256], self.W("w_in")[l, :, OFF_NA + 512:OFF_NA + 768].rearrange("(kt p) c -> p kt c", p=128),
          chv, w=[dwv])
    nb = 0
    for (Vx, dV, ntl, off) in ((Ve, d_Ve, NT, 0), (Vo, d_Vo, 15, 64)):
        for j in range(ntl):
            pi, hb = PROJ_BANKS[nb % 4]
            nb += 1
            pst = self.ps[pi][:, hb * 512: hb * 512 + 256]
            dps = self.d_ps[pi][hb]
            a = off + j * 128
            hd = [self.d_H[t] for t in range(a // 128, (a + 255) // 128)]
            for dt in range(8):
                k.pe(lambda e, dt=dt, pst=pst, a=a: e.matmul(pst, self.H[:, dt, a:a + 128], wv[:, dt, 0:256],
                                                             start=(dt == 0), stop=(dt == 7)), r=[dwv] + hd, w=[dps])
            k.act(lambda e, Vx=Vx, j=j, pst=pst: e.activation(out=Vx[:, j, :, 0:64], in_=pst.rearrange("p (h d) -> p h d", h=4),
                                                              func=AF.Copy), r=[dps], w=[dV])
    T2 = self.aa([128, 4, 14, 64])
    msk = self.aa([128, 64])
    d_T2 = Dep()
    d_msk = d_T2
    if not hasattr(self, "na_rpbg_in"):
        self.na_rpbg_in = self.din("na_rpbg", [DEPTH, 128, 4 * 14 * 64])
        self.na_mask_in = self.din("na_mask", [128, 64])
        self.ch_na = self.k.chan()
    k.dma("sp", T2.rearrange("p a b c -> p (a b c)"), self.na_rpbg_in[l], self.ch_na, w=[d_T2])
    k.dma("sp", msk, self.na_mask_in[:, :], self.ch_na, w=[d_msk])
    k.dve(lambda e: e.tensor_tensor(out=T2.rearrange("p a b c -> p (a b) c"), in0=T2.rearrange("p a b c -> p (a b) c"),
                                    in1=msk.unsqueeze(1).to_broadcast([128, 56, 64]), op=ALU.add),
          r=[d_T2, d_msk], w=[d_T2])
    Sb = [self.aa([128, 4, 64]) for _ in range(2)]
    d_Sb = [Dep(), Dep()]
    E = [self.aa([128, 6, 64], BF16) for _ in range(3)]
    d_E = [Dep() for _ in range(3)]
    rs = [self.aa([64, 4]) for _ in range(2)]
    d_rs = [Dep(), Dep()]
    On = [self.aa([64, 256], BF16) for _ in range(2)]
    d_On = [Dep(), Dep()]
    SB = [(2, 0), (2, 1), (3, 0), (3, 1)]
    OB = [(0, 0), (0, 1)]
    TB = (1, 0)
    psT = self.ps[TB[0]][:, TB[1] * 512: TB[1] * 512 + 512].bitcast(BF16)
    d_psT = self.d_ps[TB[0]][TB[1]]
    n = 0
    for r in range(32):
        s = min(max(r - 4, 0), 24)
        base = s - r + 7
        opi, ohb = OB[r % 2]
        O_ps = self.ps[opi][0:64, ohb * 512: ohb * 512 + 260]
        d_O = self.d_ps[opi][ohb]
        tq = (64 * r) // 128
        for h in range(4):
            hp = slice(64 * (h % 2), 64 * (h % 2) + 64)
            hc = h // 2
            spi, shb = SB[n % 4]
            S_ps = self.ps[spi][:, shb * 512: shb * 512 + 384]
            d_S = self.d_ps[spi][shb]
            for kt in range(6):
                ks = 64 * s + 128 * kt if kt < 4 else L + 128 * (kt - 4)
                kd = [d_k[t] for t in range(ks // 128, (ks + 255) // 128)]
                k.pe(lambda e, kt=kt, ks=ks, S_ps=S_ps, hp=hp, hc=hc, r=r: e.matmul(
                    S_ps[:, kt * 64:(kt + 1) * 64], kT[hp, hc, ks:ks + 128], qT[hp, hc, 64 * r:64 * r + 64],
                    start=True, stop=True), r=kd + [d_q[tq]], w=[d_S])
            sb_i = n % 2
            e_i = n % 3
            k.dve(lambda e, sb_i=sb_i, S_ps=S_ps, h=h, base=base: e.tensor_tensor(
                out=Sb[sb_i], in0=S_ps[:, 0:256].rearrange("p (a b) -> p a b", a=4),
                in1=T2[:, h, base:base + 7:2, :], op=ALU.add), r=[d_S, d_T2], w=[d_Sb[sb_i]])
            k.act(lambda e, sb_i=sb_i, e_i=e_i: e.activation(out=E[e_i][:, 0:4, :], in_=Sb[sb_i], func=AF.Exp),
                  r=[d_Sb[sb_i]], w=[d_E[e_i]])
            k.act(lambda e, e_i=e_i, S_ps=S_ps: e.activation(out=E[e_i][:, 4:6, :],
                                                             in_=S_ps[:, 256:384].rearrange("p (a b) -> p a b", a=2),
                                                             func=AF.Exp), r=[d_S], w=[d_E[e_i]])
            for kt in range(6):
                if kt < 4:
                    if s % 2 == 0:
                        vt, dv = Ve[:, s // 2 + kt, h, :], d_Ve
                    else:
                        vt, dv = Vo[:, (s - 1) // 2 + kt, h, :], d_Vo
                else:
                    vt, dv = Ve[:, 16 + kt - 4, h, :], d_Ve
                k.pe(lambda e, kt=kt, vt=vt, e_i=e_i, O_ps=O_ps, h=h: e.matmul(
                    O_ps[:, h * 65:(h + 1) * 65], E[e_i][:, kt, :], vt, start=(kt == 0), stop=(kt == 5)),
                    r=[d_E[e_i], dv], w=[d_O])
            n += 1
        j = r % 2
        O3 = O_ps.rearrange("p (h d) -> p h d", h=4)
        k.dve(lambda e, j=j, O3=O3: e.reciprocal(out=rs[j], in_=O3[:, :, 64]), r=[d_O], w=[d_rs[j]])
        k.dve(lambda e, j=j, O3=O3: e.tensor_tensor(out=On[j].rearrange("p (h d) -> p h d", h=4), in0=O3[:, :, 0:64],
                                                    in1=rs[j].unsqueeze(2).to_broadcast([64, 4, 64]), op=ALU.mult),
              r=[d_O, d_rs[j]], w=[d_On[j]])
        for hc in range(2):
            k.pe(lambda e, j=j, hc=hc: e.transpose(out=psT[:, hc * 64:(hc + 1) * 64], in_=On[j][:, hc * 128:(hc + 1) * 128],
                                                   identity=self.ident_b[0:64, 0:64]), r=[d_On[j], self.d_const], w=[d_psT])
        k.act(lambda e, r=r: e.activation(out=self.C[:, 6:8, 64 * r:64 * r + 64],
                                          in_=psT[:, 0:128].rearrange("p (a b) -> p a b", a=2), func=AF.Copy),
              r=[d_psT], w=[self.d_C[tq]])
    if with_ctx:
        Ec = self.aa([128, 2, 256], BF16)
        d_Ec = Dep()
        Onc = self.aa([128, 256], BF16)
        d_Onc = Dep()
        rsc = self.aa([128, 4])
        d_rsc = Dep()
        for qt in range(2):
            opi, ohb = OB[qt % 2]
            O_ps = self.ps[opi][:, ohb * 512: ohb * 512 + 260]
            d_O = self.d_ps[opi][ohb]
            for h in range(4):
                hp = slice(64 * (h % 2), 64 * (h % 2) + 64)
                hc = h // 2
                spi, shb = SB[n % 4]
                n += 1
                S_ps = self.ps[spi][:, shb * 512: shb * 512 + 256]
                d_S = self.d_ps[spi][shb]
                for c in range(2):
                    k.pe(lambda e, c=c, S_ps=S_ps, hp=hp, hc=hc, qt=qt: e.matmul(
                        S_ps[:, c * 128:(c + 1) * 128], kT[hp, hc, L + 128 * c:L + 128 * c + 128],
                        qT[hp, hc, L + 128 * qt:L + 128 * qt + 128], start=True, stop=True),
                        r=[d_k[16 + c], d_q[16 + qt]], w=[d_S])
                k.act(lambda e, S_ps=S_ps: e.activation(out=Ec[:, :, 0:128], in_=S_ps.rearrange("p (a b) -> p a b", a=2),
                                                        func=AF.Exp), r=[d_S], w=[d_Ec])
                for c in range(2):
                    k.pe(lambda e, c=c, O_ps=O_ps, h=h: e.matmul(O_ps[:, h * 65:(h + 1) * 65], Ec[:, c, 0:128],
                                                                 Ve[:, 16 + c, h, :], start=(c == 0), stop=(c == 1)),
                         r=[d_Ec, d_Ve], w=[d_O])
            O3 = O_ps.rearrange("p (h d) -> p h d", h=4)
            k.dve(lambda e, O3=O3: e.reciprocal(out=rsc, in_=O3[:, :, 64]), r=[d_O], w=[d_rsc])
            k.dve(lambda e, O3=O3: e.tensor_tensor(out=Onc.rearrange("p (h d) -> p h d", h=4), in0=O3[:, :, 0:64],
                                                   in1=rsc.unsqueeze(2).to_broadcast([128, 4, 64]), op=ALU.mult),
                  r=[d_O, d_rsc], w=[d_Onc])
            for hc in range(2):
                k.pe(lambda e, hc=hc: e.transpose(out=psT[:, hc * 128:(hc + 1) * 128], in_=Onc[:, hc * 128:(hc + 1) * 128],
                                                  identity=self.ident_b[:]), r=[d_Onc, self.d_const], w=[d_psT])
            k.act(lambda e, qt=qt: e.activation(out=self.C[:, 6:8, L + 128 * qt:L + 128 * qt + 128],
                                                in_=psT[:, 0:256].rearrange("p (a b) -> p a b", a=2), func=AF.Copy),
                  r=[d_psT], w=[self.d_C[16 + qt]])


import math
HY_EMB = 33
HY_BANDS = 16


def hy_consts():
    bf = ml_dtypes.bfloat16
    out = {}
    max_decay = math.log(1e-2) / 0.3
    min_decay = math.log(1e-2) / 1.5
    deltas = np.abs(np.linspace(min_decay, max_decay, 256, dtype=np.float32))
    for tag, Ls in (("lat", L), ("ctx", LC)):
        nt = Ls // 128
        t = np.linspace(0.0, 1.0, Ls, dtype=np.float32)[:, None]
        bands = np.linspace(1e-4, HY_BANDS - 1, HY_BANDS, dtype=np.float32)
        ang = (np.float32(2.0 * math.pi / Ls)) * np.arange(Ls, dtype=np.float32)[:, None] * bands
        z = np.concatenate([t, np.cos(ang), -np.sin(ang)], axis=-1).astype(np.float32)
        out["hy_zT_" + tag] = np.ascontiguousarray(z.T)
        dec = np.exp(-t * deltas[None, :]).astype(np.float32)
        out["hy_dec_" + tag] = np.ascontiguousarray(dec.reshape(nt, 128, 256).transpose(1, 0, 2).reshape(128, nt * 256))
        N = 2 * Ls
        tt_ = np.arange(Ls, dtype=np.int64)
        ff = np.arange(Ls, dtype=np.int64)
        m = ((2 * ff[None, :] + 1) * tt_[:, None]) % (2 * N)
        th = m.astype(np.float64) * (math.pi / N)
        Cm = np.cos(th)
        Sm = -np.sin(th)
        for nm, M_ in (("C", Cm), ("S", Sm)):
            f4 = M_.reshape(nt, 128, nt, 128).transpose(2, 1, 0, 3).reshape(nt, 128, nt * 128)
            out["hy_%sf_%s" % (nm, tag)] = np.ascontiguousarray(f4).astype(bf)
            Wd = min(512, Ls)
            ntb = Ls // Wd
            i4 = M_.reshape(ntb, Wd, nt, 128).transpose(0, 3, 2, 1).reshape(ntb, 128, nt * Wd)
            out["hy_%si_%s" % (nm, tag)] = np.ascontiguousarray(i4).astype(bf)
    return out


def _arena_rewind(self, mark):
    self.k.barrier()
    self.ar_off = mark


def _hy_filters(self, l, Ls, tag, WHx, d_WHx):
    k = self.k
    nt = Ls // 128
    BW = min(512, Ls)
    nb = Ls // BW
    zin = self.din("hy_zT_" + tag, [HY_EMB, Ls]) if ("hy_zT_" + tag) not in self.inputs else self._hyin["hy_zT_" + tag]
    din_dec = self.din("hy_dec_" + tag, [128, nt * 256]) if ("hy_dec_" + tag) not in self.inputs else self._hyin["hy_dec_" + tag]
    self._hyin["hy_zT_" + tag] = zin
    self._hyin["hy_dec_" + tag] = din_dec
    zT = self.aa([HY_EMB, Ls])
    dec = self.aa([128, nt, 256])
    d_z = Dep()
    d_dec = d_z
    k.dma("sp", zT, zin[:, :], self.ch_hy, w=[d_z])
    k.dma("sp", dec.rearrange("p a b -> p (a b)"), din_dec[:, :], self.ch_hy, w=[d_dec])
    hb = [self.aa([64, Ls]) for _ in range(2)]
    d_hb = [Dep(), Dep()]
    v = self.aa([64, 512])
    ki = self.aa([64, 512], I32)
    kf = self.aa([64, 512])
    d_v, d_ki, d_kf = Dep(), Dep(), Dep()
    fb = self.aa([64, 3])
    d_fb = Dep()
    fr = _ppv(self, l, "hy_f", 64)
    bb = _ppv(self, l, "hy_b", 64)
    k.dve(lambda e: e.tensor_tensor(out=fb, in0=fr, in1=bb, op=ALU.mult), r=[self.d_pp], w=[d_fb])
    ws = [_ppv(self, l, "hy_w1", HY_EMB), _ppv(self, l, "hy_w2", 64), _ppv(self, l, "hy_w3", 64)]
    PB = [(2, 0), (2, 1)]
    nps = 0
    src, d_src = zT, d_z
    for li in range(3):
        dst, d_dst = hb[li % 2], d_hb[li % 2]
        for b in range(nb):
            pi, hbk = PB[nps % 2]
            nps += 1
            pst = self.ps[pi][0:64, hbk * 512: hbk * 512 + BW]
            dps = self.d_ps[pi][hbk]
            k.pe(lambda e, li=li, b=b, pst=pst, src=src: e.matmul(pst, ws[li], src[:, b * BW:(b + 1) * BW], start=True, stop=True),
                 r=[self.d_pp, d_src], w=[dps])
            k.dve(lambda e, li=li, pst=pst: e.tensor_scalar(out=v[:, 0:BW], in0=pst, scalar1=fr[:, li:li + 1], scalar2=fb[:, li:li + 1],
                                                            op0=ALU.mult, op1=ALU.add), r=[dps, d_fb, self.d_pp], w=[d_v])
            k.dve(lambda e: e.tensor_scalar(out=ki[:, 0:BW], in0=v[:, 0:BW], scalar1=1.0 / (2.0 * math.pi), scalar2=None, op0=ALU.mult),
                  r=[d_v], w=[d_ki])
            k.dve(lambda e: e.tensor_copy(out=kf[:, 0:BW], in_=ki[:, 0:BW]), r=[d_ki], w=[d_kf])
            k.dve(lambda e: e.scalar_tensor_tensor(out=v[:, 0:BW], in0=kf[:, 0:BW], scalar=-2.0 * math.pi, in1=v[:, 0:BW],
                                                   op0=ALU.mult, op1=ALU.add), r=[d_kf, d_v], w=[d_v])
            k.dve(lambda e: e.tensor_scalar(out=v[:, 0:BW], in0=v[:, 0:BW], scalar1=3.1415925, scalar2=-3.1415925,
                                            op0=ALU.min, op1=ALU.max), r=[d_v], w=[d_v])
            k.act(lambda e, dst=dst, b=b: e.activation(out=dst[:, b * BW:(b + 1) * BW], in_=v[:, 0:BW], func=AF.Sin),
                  r=[d_v], w=[d_dst])
        src, d_src = dst, d_dst
    h3, d_h3 = src, d_src
    w4 = _ppv(self, l, "hy_w4", 64)
    hd = [self.aa([128, 2, 256]) for _ in range(2)]
    d_hd = [Dep(), Dep()]
    ab = [self.aa([128, 512]) for _ in range(2)]
    d_ab = [Dep(), Dep()]
    nrm_ps = self.ps[3][:, 0:512]
    d_nrm = self.d_ps[3][0]
    rn = self.aa([128, 256])
    d_rn = Dep()
    for pss in range(2):
        for tt in range(nt):
            pi, hbk = PB[nps % 2]
            nps += 1
            pst = self.ps[pi][:, hbk * 512: hbk * 512 + 512]
            dps = self.d_ps[pi][hbk]
            k.pe(lambda e, tt=tt, pst=pst: e.matmul(pst, h3[:, tt * 128:(tt + 1) * 128], w4, start=True, stop=True),
                 r=[d_h3, self.d_pp], w=[dps])
            j = tt % 2
            k.dve(lambda e, j=j, tt=tt, pst=pst: e.tensor_tensor(out=hd[j], in0=pst.rearrange("p (a b) -> p a b", a=2),
                                                                 in1=dec[:, tt, :].unsqueeze(1).to_broadcast([128, 2, 256]),
                                                                 op=ALU.mult), r=[dps, d_dec], w=[d_hd[j]])
            if pss == 0:
                k.act(lambda e, j=j: e.activation(out=ab[j], in_=hd[j].rearrange("p a b -> p (a b)"), func=AF.Abs),
                      r=[d_hd[j]], w=[d_ab[j]])
                k.pe(lambda e, j=j, tt=tt: e.matmul(nrm_ps, self.ones_f[:], ab[j], start=(tt == 0), stop=(tt == nt - 1)),
                     r=[d_ab[j], self.d_const], w=[d_nrm])
            else:
                if tt == 0:
                    k.dve(lambda e, j=j: e.memset(hd[j][0:1, 1, :], 0.0), r=[d_hd[j]], w=[d_hd[j]])
                k.dve(lambda e, j=j: e.tensor_tensor(out=hd[j], in0=hd[j], in1=rn.unsqueeze(1).to_broadcast([128, 2, 256]),
                                                     op=ALU.mult), r=[d_hd[j], d_rn], w=[d_hd[j]])
                k.dve(lambda e, j=j, tt=tt: e.tensor_tensor(out=WHx[:, tt, 1, :], in0=hd[j][:, 0, :], in1=hd[j][:, 1, :], op=ALU.add),
                      r=[d_hd[j]], w=[d_WHx])
                k.pool(lambda e, j=j, tt=tt: e.tensor_tensor(out=WHx[:, tt, 2, :], in0=hd[j][:, 0, :], in1=hd[j][:, 1, :],
                                                             op=ALU.subtract), r=[d_hd[j]], w=[d_WHx])
        if pss == 0:
            k.dve(lambda e: e.tensor_copy(out=rn, in_=nrm_ps[:, 0:256]), r=[d_nrm], w=[d_rn])
            k.dve(lambda e: e.tensor_tensor(out=rn, in0=rn, in1=nrm_ps[:, 256:512], op=ALU.add), r=[d_nrm, d_rn], w=[d_rn])
            k.dve(lambda e: e.reciprocal(out=rn, in_=rn), r=[d_rn], w=[d_rn])
            k.dve(lambda e: e.tensor_scalar(out=rn, in0=rn, scalar1=2.0 / (2 * Ls), scalar2=None, op0=ALU.mult), r=[d_rn], w=[d_rn])


def _hy_dft(self, l, Ls, tag, toff, WHx, d_WHx, x0T, wTm, d_x0w, Yh, ring):
    k = self.k
    nt = Ls // 128
    Wd = min(512, Ls)
    ntb = Ls // Wd
    names = ["hy_Cf_", "hy_Sf_", "hy_Ci_", "hy_Si_"]
    tabs = []
    for nm in names:
        key = nm + tag
        if key not in self._hyin:
            shp = [nt, 128, nt * 128] if nm[4] == "f" else [ntb, 128, nt * Wd]
            self._hyin[key] = self.din(key, shp, BF16)
        tabs.append(self._hyin[key])
    Cf, Sf, Ci, Si = tabs
    d_Yh = Dep()
    Kr = self.aa([128, 4, 256])
    d_K = [Dep(), Dep()]
    tm = [self.aa([128, 256]) for _ in range(4)]
    d_tm = [Dep() for _ in range(4)]
    hybias = _ppv(self, l, "hybias")
    for j in range(nt):
        slot, dsl, chs = ring[self.hyring_i % len(ring)]
        self.hyring_i += 1
        cst = slot[:, 0:nt * 128].rearrange("p (a b) -> p a b", b=128)
        sst = slot[:, 2048:2048 + nt * 128].rearrange("p (a b) -> p a b", b=128)
        k.dma("sp", slot[:, 0:nt * 128], Cf[j], chs, w=[dsl])
        k.dma("sp", slot[:, 2048:2048 + nt * 128], Sf[j], chs, w=[dsl])
        ps_r = self.ps[0][:, 0:512]
        ps_i = self.ps[0][:, 512:1024]
        for tt in range(nt):
            k.pe(lambda e, tt=tt, cst=cst: e.matmul(ps_r, cst[:, tt, :], WHx[:, tt, 0:2, :], start=(tt == 0), stop=(tt == nt - 1)),
                 r=[dsl, d_WHx], w=[self.d_ps[0][0]])
        for tt in range(nt):
            k.pe(lambda e, tt=tt, sst=sst: e.matmul(ps_i, sst[:, tt, :], WHx[:, tt, 0:3:2, :], start=(tt == 0), stop=(tt == nt - 1)),
                 r=[dsl, d_WHx], w=[self.d_ps[0][1]])
        k.act(lambda e: e.activation(out=Kr[:, 0:2, :], in_=ps_r.rearrange("p (a b) -> p a b", a=2), func=AF.Copy),
              r=[self.d_ps[0][0]], w=[d_K[0]])
        k.act(lambda e: e.activation(out=Kr[:, 2:4, :], in_=ps_i.rearrange("p (a b) -> p a b", a=2), func=AF.Copy),
              r=[self.d_ps[0][1]], w=[d_K[1]])
        Ur, Kre, Ui, Kie = Kr[:, 0, :], Kr[:, 1, :], Kr[:, 2, :], Kr[:, 3, :]
        k.dve(lambda e: e.tensor_tensor(out=tm[0], in0=Ur, in1=Kre, op=ALU.mult), r=[d_K[0]], w=[d_tm[0]])
        k.pool(lambda e: e.tensor_tensor(out=tm[1], in0=Ui, in1=Kie, op=ALU.mult), r=[d_K[1]], w=[d_tm[1]])
        k.dve(lambda e, j=j: e.tensor_tensor(out=Yh[:, j, 0, :], in0=tm[0], in1=tm[1], op=ALU.subtract),
              r=[d_tm[0], d_tm[1]], w=[d_Yh])
        k.pool(lambda e: e.tensor_tensor(out=tm[2], in0=Ur, in1=Kie, op=ALU.mult), r=d_K, w=[d_tm[2]])
        k.dve(lambda e: e.tensor_tensor(out=tm[3], in0=Ui, in1=Kre, op=ALU.mult), r=d_K, w=[d_tm[3]])
        k.pool(lambda e, j=j: e.tensor_tensor(out=Yh[:, j, 1, :], in0=tm[2], in1=tm[3], op=ALU.add),
               r=[d_tm[2], d_tm[3]], w=[d_Yh])
    G = 4 if nt >= 4 else nt
    yt = [self.aa([128, 512]) for _ in range(2)]
    d_yt = [Dep(), Dep()]
    for tb in range(ntb):
        for g in range(nt // G):
            slot, dsl, chs = ring[self.hyring_i % len(ring)]
            self.hyring_i += 1
            k.dma("sp", slot[:, 0:G * Wd], Ci[tb, :, g * G * Wd:(g + 1) * G * Wd], chs, w=[dsl])
            k.dma("sp", slot[:, 2048:2048 + G * Wd], Si[tb, :, g * G * Wd:(g + 1) * G * Wd], chs, w=[dsl])
            cst = slot[:, 0:G * Wd].rearrange("p (a b) -> p a b", b=Wd)
            sst = slot[:, 2048:2048 + G * Wd].rearrange("p (a b) -> p a b", b=Wd)
            for fi in range(G):
                ft = g * G + fi
                for ct in range(2):
                    py = self.ps[1][:, ct * 512: ct * 512 + Wd]
                    k.pe(lambda e, fi=fi, ft=ft, ct=ct, py=py, cst=cst: e.matmul(py, Yh[:, ft, 0, ct * 128:(ct + 1) * 128], cst[:, fi, :],
                                                                               start=(ft == 0), stop=False),
                         r=[dsl, d_Yh], w=[self.d_ps[1][ct]])
                    k.pe(lambda e, fi=fi, ft=ft, ct=ct, py=py, sst=sst: e.matmul(py, Yh[:, ft, 1, ct * 128:(ct + 1) * 128], sst[:, fi, :],
                                                                               start=False, stop=(ft == nt - 1)),
                         r=[dsl, d_Yh], w=[self.d_ps[1][ct]])
        a = toff + tb * Wd
        tts = [self.d_C[t] for t in range(a // 128, (a + Wd) // 128)]
        for ct in range(2):
            py = self.ps[1][:, ct * 512: ct * 512 + Wd]
            k.dve(lambda e, ct=ct, py=py, a=a: e.scalar_tensor_tensor(out=yt[ct][:, 0:Wd], in0=wTm[:, ct, a:a + Wd], scalar=hybias[:, ct:ct + 1],
                                                                      in1=py, op0=ALU.mult, op1=ALU.add),
                  r=[self.d_ps[1][ct], d_x0w, self.d_pp], w=[d_yt[ct]])
            k.pool(lambda e, ct=ct, a=a: e.tensor_tensor(out=self.C[:, 4 + ct, a:a + Wd], in0=yt[ct][:, 0:Wd], in1=x0T[:, ct, a:a + Wd],
                                                         op=ALU.mult), r=[d_yt[ct], d_x0w], w=tts)


def _mix_hy(self, l, with_ctx):
    k = self.k
    self.arena_reset()
    if not hasattr(self, "_hyin"):
        self._hyin = {}
        self.ch_hy = k.chan()
        self.ch_hyring = [k.chan() for _ in range(3)]
    ntok = T if with_ctx else L
    WH = self.aa([128, 16, 3, 256], BF16)
    d_WH = Dep()
    if with_ctx:
        WHc = self.aa([128, 2, 3, 256], BF16)
        d_WHc = Dep()
    x0T = self.aa([128, 2, T], BF16)
    wTm = self.aa([128, 2, T], BF16)
    d_x0w = Dep()
    markA = self.ar_off
    _hy_filters(self, l, L, "lat", WH, d_WH)
    if with_ctx:
        _arena_rewind(self, markA)
        _hy_filters(self, l, LC, "ctx", WHc, d_WHc)
    _arena_rewind(self, markA)
    _wring_init(self)
    ranges = [(0, L)] + ([(L, T)] if with_ctx else [])
    blocks = _blocks(self, with_ctx)
    pin = [self.aa([128, T]) for _ in range(1)]
    d_pin = [Dep()]
    x1v = self.aa([128, 4, T], BF16)
    d_x1v = [Dep() for _ in range(4)]
    cacc = [self.aa([128, T]) for _ in range(1)]
    d_cacc = Dep()
    hyw = _ppv(self, l, "hyw")

    def evac(c, m, t0, n, pst, dps):
        ct = (c - OFF_HY) // 128
        j = 0
        k.act(lambda e: e.activation(out=pin[j][:, t0:t0 + n], in_=pst, func=AF.Copy), r=[dps], w=[d_pin[j]])
        if (t0, n) == blocks[-1]:
            _dwconv(self, "dve", cacc[0], pin[j], hyw[:, ct * 3:(ct + 1) * 3], ranges, [d_pin[j], self.d_pp], [d_cacc])
            if ct < 2:
                k.pool(lambda e: e.tensor_copy(out=x0T[:, ct, 0:ntok], in_=cacc[0][:, 0:ntok]), r=[d_cacc], w=[d_x0w])
            else:
                k.pool(lambda e: e.tensor_copy(out=x1v[:, ct - 2, 0:ntok], in_=cacc[0][:, 0:ntok]), r=[d_cacc], w=[d_x1v[ct - 2]])
    _proj(self, l, [(OFF_HY, 512), (OFF_HY + 512, 256)], blocks, evac, PROJ_BANKS)
    for ct in range(2):
        k.pool(lambda e, ct=ct: e.tensor_tensor(out=wTm[:, ct, 0:ntok], in0=x1v[:, ct, 0:ntok], in1=x1v[:, 2 + ct, 0:ntok], op=ALU.mult),
               r=[d_x1v[ct], d_x1v[2 + ct]], w=[d_x0w])
    psT = self.ps[2][:, 0:512].bitcast(BF16)
    d_psT = self.d_ps[2][0]
    for tt in range(ntok // 128):
        for ct in range(2):
            k.pe(lambda e, tt=tt, ct=ct: e.transpose(out=psT[:, ct * 128:(ct + 1) * 128], in_=wTm[:, ct, tt * 128:(tt + 1) * 128],
                                                     identity=self.ident_b[:]), r=[d_x0w, self.d_const], w=[d_psT])
        if tt < 16:
            k.act(lambda e, tt=tt: e.activation(out=WH[:, tt, 0, :], in_=psT[:, 0:256], func=AF.Copy), r=[d_psT], w=[d_WH])
        else:
            k.act(lambda e, tt=tt: e.activation(out=WHc[:, tt - 16, 0, :], in_=psT[:, 0:256], func=AF.Copy), r=[d_psT], w=[d_WHc])
    _arena_rewind(self, markA)
    Yh = self.aa([128, 16, 2, 256], BF16)
    ring = [(self.aa([128, 4096], BF16), Dep(), self.ch_hyring[i]) for i in range(3)]
    self.hyring_i = 0
    markD = self.ar_off
    _hy_dft(self, l, L, "lat", 0, WH, d_WH, x0T, wTm, d_x0w, Yh, ring)
    if with_ctx:
        _arena_rewind(self, markD)
        _hy_dft(self, l, LC, "ctx", L, WHc, d_WHc, x0T, wTm, d_x0w, Yh, ring)


def gdn_consts():
    out = {}
    m = np.arange(128)[:, None]
    i = np.arange(128)[None, :]
    LE = (m <= i).astype(np.float32)
    GE = (m >= i).astype(np.float32)
    GT = (m > i).astype(np.float32)
    LT = (m < i).astype(np.float32)
    NEGF = np.tile(np.where(i > m, -1e9, 0.0).astype(np.float32), (1, 4))
    NEGB = np.tile(np.where(i < m, -1e9, 0.0).astype(np.float32), (1, 4))
    out["gdn_gm"] = np.ascontiguousarray(np.concatenate([LE, GE, GT, LT, NEGF, NEGB], axis=1))
    pm = np.zeros((128, 128), np.float32)
    for mm in range(128):
        if (mm % 64) < 32:
            pm[mm + 32, mm] = -1.0
        else:
            pm[mm - 32, mm] = 1.0
    out["gdn_pm"] = pm
    ii = np.arange(128)[:, None]
    jj = np.arange(128)[None, :]
    lms, ums = [], []
    for s_ in range(7):
        b_ = 1 << s_
        same2 = (ii // (2 * b_)) == (jj // (2 * b_))
        diff1 = (ii // b_) != (jj // b_)
        lms.append(np.where(same2 & diff1 & (jj < ii), -1.0, 0.0))
        ums.append(np.where(same2 & diff1 & (jj > ii), -1.0, 0.0))
    out["gdn_lm"] = np.ascontiguousarray(np.concatenate(lms + ums, axis=1)).astype(ml_dtypes.bfloat16)
    pos = np.arange(L)
    row = (pos // 64).astype(np.float32)
    col = (pos % 64).astype(np.float32)
    inv = (10000.0 ** (-np.arange(16, dtype=np.float32) / 16)).astype(np.float32)
    ang = np.concatenate([row[:, None] * inv, col[:, None] * inv], axis=-1)
    idx = (np.arange(128) % 64) % 32
    cs = np.stack([np.cos(ang).T[idx], np.sin(ang).T[idx]], axis=1)
    out["gdn_rope"] = np.ascontiguousarray(cs.reshape(128, 2 * L)).astype(ml_dtypes.bfloat16)
    return out


def _mix_gdn(self, l, with_ctx):
    k = self.k
    self.arena_reset()
    if not hasattr(self, "gdn_gm_in"):
        self.gdn_gm_in = self.din("gdn_gm", [128, 1536])
        self.gdn_pm_in = self.din("gdn_pm", [128, 128])
        self.gdn_rope_in = self.din("gdn_rope", [128, 2 * L], BF16)
        self.ch_gdn = k.chan()
    otok = self.aa([128, NT, 256], BF16)
    markO = self.ar_off
    qT = self.aa([128, 2, T], BF16)
    kT = self.aa([128, 2, T], BF16)
    d_qk = Dep()
    vtok = self.aa([128, NT, 256], BF16)
    ktok = self.aa([128, NT, 256], BF16)
    d_vtok, d_ktok = Dep(), Dep()
    la = self.aa([128, NT, 8])
    beta = self.aa([128, NT, 8])
    d_la, d_beta = Dep(), Dep()
    GM = self.aa([128, 1536])
    d_GM = Dep()
    d_LMK = d_GM
    d_c1 = d_GM
    k.dma("sp", GM, self.gdn_gm_in[:, :], self.ch_gdn, w=[d_GM])
    LE, GE, GT, LT = (GM[:, i * 128:(i + 1) * 128] for i in range(4))
    NEGF, NEGB = GM[:, 512:1024], GM[:, 1024:1536]
    LMK = self.aa([128, 14 * 128], BF16)
    if not hasattr(self, "gdn_lm_in"):
        self.gdn_lm_in = self.din("gdn_lm", [128, 14 * 128], BF16)
    k.dma("sp", LMK, self.gdn_lm_in[:, :], self.ch_gdn, w=[d_LMK])
    markP = self.ar_off
    _wring_init(self, 1)
    pm = self.aa([128, 128])
    rope = self.aa([128, 2, L], BF16)
    blk64 = self.aa([128, 128])
    d_blk = Dep()
    k.dma("sp", pm, self.gdn_pm_in[:, :], self.ch_gdn, w=[d_c1])
    k.dma("sp", rope.rearrange("p a b -> p (a b)"), self.gdn_rope_in[:, :], self.ch_gdn, w=[d_c1])
    k.pool(lambda e: e.memset(blk64, 0.0), w=[d_blk])
    k.pool(lambda e: e.memset(blk64[0:64, 0:64], 1.0), w=[d_blk])
    k.pool(lambda e: e.memset(blk64[64:128, 64:128], 1.0), w=[d_blk])
    wab, dwab, chab = self.wring[0]
    k.dma("pool", wab[:, :, 0:16], self.W("w_in")[l, :, 1024:1040].rearrange("(kt p) c -> p kt c", p=128), chab, w=[dwab])
    ab_ps = self.ps[3][:, 0:NT * 16]
    d_ab = self.d_ps[3][0]
    for tt in range(NT):
        for dt in range(8):
            k.pe(lambda e, tt=tt, dt=dt: e.matmul(ab_ps[:, tt * 16:(tt + 1) * 16], self.H[:, dt, tt * 128:(tt + 1) * 128],
                                                  wab[:, dt, 0:16], start=(dt == 0), stop=(dt == 7)),
                 r=[dwab, self.d_H[tt]], w=[d_ab])
    ab3 = ab_ps.rearrange("p (t c) -> p t c", c=16)
    xa = self.aa([128, NT, 8])
    ea = self.aa([128, 8])
    d_xa, d_ea = Dep(), Dep()
    k.dve(lambda e: e.tensor_tensor(out=xa, in0=ab3[:, :, 0:8], in1=_ppv(self, l, "dtb").unsqueeze(1).to_broadcast([128, NT, 8]),
                                    op=ALU.add), r=[d_ab, self.d_pp], w=[d_xa])
    k.act(lambda e: e.activation(out=xa, in_=xa, func=AF.Exp), r=[d_xa], w=[d_xa])
    k.act(lambda e: e.activation(out=xa, in_=xa, func=AF.Ln, bias=1.0, scale=1.0), r=[d_xa], w=[d_xa])
    k.act(lambda e: e.activation(out=ea, in_=_ppv(self, l, "alog"), func=AF.Exp), r=[self.d_pp], w=[d_ea])
    k.dve(lambda e: e.scalar_tensor_tensor(out=la, in0=xa, scalar=-1.0, in1=ea.unsqueeze(1).to_broadcast([128, NT, 8]),
                                           op0=ALU.mult, op1=ALU.mult), r=[d_xa, d_ea], w=[d_la])
    k.act(lambda e: e.activation(out=beta, in_=ab3[:, :, 8:16], func=AF.Sigmoid), r=[d_ab], w=[d_beta])
    blocks = _blocks(self, True)
    ranges = [(0, L), (L, T)]
    pin = self.aa([128, T])
    cacc = self.aa([128, T])
    d_pin, d_cacc = Dep(), Dep()
    vTt = self.aa([128, T], BF16)
    d_vTt = Dep()
    rv = self.aa([128, 512])
    t1 = self.aa([128, 512])
    t2 = self.aa([128, 512])
    d_rv, d_t1, d_t2 = Dep(), Dep(), Dep()
    gdw = _ppv(self, l, "gdw")
    psT = self.ps[2][:, 0:512].bitcast(BF16)
    d_psT = self.d_ps[2][0]

    def finish_tile(ct):
        _dwconv(self, "dve", cacc, pin, gdw[:, ct * 3:(ct + 1) * 3], ranges, [d_pin, self.d_pp], [d_cacc])
        if ct >= 4:
            k.act(lambda e: e.activation(out=vTt, in_=cacc, func=AF.Silu), r=[d_cacc], w=[d_vTt])
            for tt in range(NT):
                k.pe(lambda e, tt=tt: e.transpose(out=psT[:, 0:128], in_=vTt[:, tt * 128:(tt + 1) * 128], identity=self.ident_b[:]),
                     r=[d_vTt, self.d_const], w=[d_psT])
                k.dve(lambda e, tt=tt: e.tensor_copy(out=vtok[:, tt, (ct - 4) * 128:(ct - 3) * 128], in_=psT[:, 0:128]),
                      r=[d_psT], w=[d_vtok])
            return
        isq = ct < 2
        dst = qT if isq else kT
        cc = ct % 2
        k.act(lambda e: e.activation(out=pin, in_=cacc, func=AF.Silu), r=[d_cacc, d_pin], w=[d_pin])
        k.act(lambda e: e.activation(out=cacc, in_=pin, func=AF.Square), r=[d_pin, d_cacc], w=[d_cacc])
        for (t0, n) in blocks:
            ss = self.ps[3][:, 512:512 + n]
            dss = self.d_ps[3][1]
            k.pe(lambda e, t0=t0, n=n, ss=ss: e.matmul(ss, blk64, cacc[:, t0:t0 + n], start=True, stop=True), r=[d_cacc, d_blk], w=[dss])
            sc_, bi_ = (64.0, 64.0 * EPS) if isq else (1.0, EPS)
            k.act(lambda e, n=n, ss=ss: e.activation(out=rv[:, 0:n], in_=ss, func=AF.Sqrt, bias=bi_, scale=sc_), r=[dss], w=[d_rv])
            k.dve(lambda e, n=n: e.reciprocal(out=rv[:, 0:n], in_=rv[:, 0:n]), r=[d_rv], w=[d_rv])
            k.dve(lambda e, t0=t0, n=n: e.tensor_tensor(out=pin[:, t0:t0 + n], in0=pin[:, t0:t0 + n], in1=rv[:, 0:n], op=ALU.mult),
                  r=[d_rv, d_pin], w=[d_pin])
            if t0 < L:
                pv = self.ps[2][:, 512:512 + n]
                dpv = self.d_ps[2][1]
                k.pe(lambda e, t0=t0, n=n, pv=pv: e.matmul(pv, pm, pin[:, t0:t0 + n], start=True, stop=True), r=[d_pin, d_c1], w=[dpv])
                k.dve(lambda e, t0=t0, n=n: e.tensor_tensor(out=t1[:, 0:n], in0=pin[:, t0:t0 + n], in1=rope[:, 0, t0:t0 + n], op=ALU.mult),
                      r=[d_pin, d_c1], w=[d_t1])
                k.dve(lambda e, t0=t0, n=n, pv=pv: e.tensor_tensor(out=t2[:, 0:n], in0=pv, in1=rope[:, 1, t0:t0 + n], op=ALU.mult),
                      r=[dpv, d_c1], w=[d_t2])
                k.pool(lambda e, t0=t0, n=n: e.tensor_tensor(out=dst[:, cc, t0:t0 + n], in0=t1[:, 0:n], in1=t2[:, 0:n], op=ALU.add),
                       r=[d_t1, d_t2], w=[d_qk])
            else:
                k.pool(lambda e, t0=t0, n=n: e.tensor_copy(out=dst[:, cc, t0:t0 + n], in_=pin[:, t0:t0 + n]), r=[d_pin], w=[d_qk])
        if not isq:
            for tt in range(NT):
                k.pe(lambda e, tt=tt: e.transpose(out=psT[:, 0:128], in_=kT[:, cc, tt * 128:(tt + 1) * 128], identity=self.ident_b[:]),
                     r=[d_qk, self.d_const], w=[d_psT])
                k.dve(lambda e, tt=tt: e.tensor_copy(out=ktok[:, tt, cc * 128:(cc + 1) * 128], in_=psT[:, 0:128]),
                      r=[d_psT], w=[d_ktok])

    def evac(c, m, t0, n, pst, dps):
        ct = c // 128
        k.act(lambda e: e.activation(out=pin[:, t0:t0 + n], in_=pst, func=AF.Copy), r=[dps], w=[d_pin])
        if (t0, n) == blocks[-1]:
            finish_tile(ct)
    _proj(self, l, [(512, 256), (256, 256), (0, 256)], blocks, evac, PROJ_BANKS)
    import os
    _stop = os.environ.get("GDN_STOP", "")
    if "gdn_g1" in self.dbg:
        self.k.barrier()
        chd = k.chan()
        for nm, ap_, n_ in (("qT", qT.rearrange("p a b -> p (a b)"), 2 * T), ("kT", kT.rearrange("p a b -> p (a b)"), 2 * T),
                            ("vtok", vtok.rearrange("p a b -> p (a b)"), NT * 256), ("ktok", ktok.rearrange("p a b -> p (a b)"), NT * 256)):
            o_ = self.dout("dbg_" + nm, [128, n_], BF16)
            k.dma("sp", o_[:, :], ap_, chd)
        for nm, ap_ in (("la", la), ("beta", beta)):
            o_ = self.dout("dbg_" + nm, [128, NT * 8])
            k.dma("sp", o_[:, :], ap_.rearrange("p a b -> p (a b)"), chd)
        self.k.barrier()
    if _stop == "g1":
        return
    _arena_rewind(self, markP)
    d_otok = [Dep() for _ in range(NT)]
    hm = self.aa([128, 4])
    d_hm = Dep()
    k.pool(lambda e: e.memset(hm, 0.0), w=[d_hm])
    k.pool(lambda e: e.memset(hm[0:64, 0:4:2], 1.0), w=[d_hm])
    k.pool(lambda e: e.memset(hm[64:128, 1:4:2], 1.0), w=[d_hm])
    first_visit = [True] * NT
    orderF = [16, 17] + list(range(16))
    orderB = [17, 16] + list(range(15, -1, -1))
    W_ = {}
    for dd in range(2):
        w = {}
        w["Z"] = self.aa([128, 4, 128]); w["D"] = self.aa([128, 4, 128], BF16); w["Ds"] = w["Z"]
        w["E"] = [self.aa([128, 4, 128], BF16) for _ in range(2)]
        w["ET"] = [self.aa([128, 4, 128], BF16) for _ in range(2)]
        w["d_E"] = [Dep(), Dep()]; w["d_ET"] = [Dep(), Dep()]
        w["A"] = [self.aa([128, 4, 128], BF16) for _ in range(2)]
        w["AT"] = [self.aa([128, 4, 128], BF16) for _ in range(2)]
        w["X"] = [self.aa([128, 4, 128], BF16) for _ in range(2)]
        w["QKm"] = self.aa([128, 4, 128], BF16)
        w["R"] = self.aa([128, 4, 128], BF16)
        w["u"] = self.aa([128, 4, 64], BF16)
        w["d_R"], w["d_u"] = Dep(), Dep()
        w["QKT"] = self.aa([128, 4, 128], BF16)
        w["wT"] = self.aa([64, 4, 128], BF16)
        w["kd"] = self.aa([128, 4, 128], BF16)
        w["kc"] = self.aa([128, 2, 2, 128], BF16)
        w["d_kc"] = Dep()
        k.pool(lambda e, w=w: e.memset(w["kc"], 0.0), w=[w["d_kc"]])
        w["SbQ"] = self.aa([128, 4, 64], BF16)
        w["d_SbQ"] = Dep()
        k.pool(lambda e, w=w: e.memset(w["SbQ"], 0.0), w=[w["d_SbQ"]])
        w["ev"] = self.aa([128, 12])
        w["beg"] = self.aa([128, 4])
        w["vn"] = self.aa([128, 4, 64], BF16)
        w["o2"] = self.aa([128, 4, 64], BF16)
        w["ot"] = self.aa([128, 4, 64], BF16)
        w["S"] = self.aa([128, 4, 64])
        w["Sb"] = self.aa([128, 4, 64], BF16)
        for nm in ("Z", "D", "Ds", "QKm", "QKT", "wT", "kd", "ev", "beg", "vn", "o2", "ot", "S", "Sb"):
            w["d_" + nm] = Dep()
        w["d_Ds"] = w["d_Z"]
        w["d_A"] = [Dep(), Dep()]; w["d_AT"] = [Dep(), Dep()]; w["d_X"] = [Dep(), Dep()]
        k.dve(lambda e, w=w: e.memset(w["S"], 0.0), w=[w["d_S"]])
        k.pool(lambda e, w=w: e.memset(w["Sb"], 0.0), w=[w["d_Sb"]])
        W_[dd] = w

    def bank(dd, i, half=None):
        t = self.ps[2 * dd + i // 2]
        hb = i % 2
        return t[:, hb * 512:(hb + 1) * 512], self.d_ps[2 * dd + i // 2][hb]

    def unit_pre(dd, n):
        w = W_[dd]
        c0 = n * 128
        lacol = la[:, n, dd * 4:dd * 4 + 4]
        becol = beta[:, n, dd * 4:dd * 4 + 4]
        Mz = GT if dd == 0 else LT
        Um = LE if dd == 0 else GE
        Ugt = GT if dd == 0 else LT
        NEG = NEGF if dd == 0 else NEGB
        strict = GT if dd == 0 else LT
        B0, dB0 = bank(dd, 0)
        B1, dB1 = bank(dd, 1)
        B2, dB2 = bank(dd, 2)
        B3, dB3 = bank(dd, 3)
        k.dve(lambda e: e.tensor_tensor(out=w["Z"], in0=Mz.unsqueeze(1).to_broadcast([128, 4, 128]),
                                        in1=lacol.unsqueeze(2).to_broadcast([128, 4, 128]), op=ALU.mult),
              r=[d_GM, d_la], w=[w["d_Z"]])
        k.pe(lambda e: e.matmul(B0, Um, w["Z"].rearrange("p a b -> p (a b)"), start=True, stop=False), r=[d_GM, w["d_Z"]], w=[dB0])
        k.pe(lambda e: e.matmul(B0, self.ident_f[:], NEG, start=False, stop=True), r=[d_GM, self.d_const], w=[dB0])
        k.act(lambda e: e.activation(out=w["D"].rearrange("p a b -> p (a b)"), in_=B0, func=AF.Exp), r=[dB0], w=[w["d_D"]])
        yield
        k.pe(lambda e: e.matmul(B3[:, 0:4], Um, lacol, start=True, stop=True), r=[d_GM, d_la], w=[dB3])
        k.pe(lambda e: e.matmul(B3[:, 4:8], Ugt, lacol, start=True, stop=True), r=[d_GM, d_la], w=[dB3])
        k.pe(lambda e: e.matmul(B3[:, 8:12], self.ones_f[:], lacol, start=True, stop=True), r=[self.d_const, d_la], w=[dB3])
        yield
        k.act(lambda e: e.activation(out=w["ev"], in_=B3[:, 0:12], func=AF.Exp), r=[dB3], w=[w["d_ev"]])
        yield
        k.dve(lambda e: e.tensor_tensor(out=w["Ds"], in0=w["D"], in1=strict.unsqueeze(1).to_broadcast([128, 4, 128]), op=ALU.mult),
              r=[w["d_D"], d_GM], w=[w["d_Ds"]])
        yield
        for h in range(4):
            hp = slice(64 * (h % 2), 64 * (h % 2) + 64)
            if h == 0:
                k.act(lambda e: e.activation(out=w["kc"][0:64, :, 0, :], in_=kT[0:64, :, c0:c0 + 128], func=AF.Copy), r=[d_qk], w=[w["d_kc"]])
                k.act(lambda e: e.activation(out=w["kc"][64:128, :, 1, :], in_=kT[64:128, :, c0:c0 + 128], func=AF.Copy), r=[d_qk], w=[w["d_kc"]])
            k.pe(lambda e, h=h, hp=hp: e.matmul(B1[:, h * 128:(h + 1) * 128], kT[:, h // 2, c0:c0 + 128], w["kc"][:, h // 2, h % 2, :],
                                                start=True, stop=True), r=[d_qk, w["d_kc"]], w=[dB1])
        yield
        k.dve(lambda e: e.tensor_tensor(out=w["Ds"].rearrange("p a b -> p (a b)"), in0=B1, in1=w["Ds"].rearrange("p a b -> p (a b)"),
                                        op=ALU.mult), r=[dB1, w["d_Ds"]], w=[w["d_Ds"]])
        k.dve(lambda e: e.tensor_tensor(out=w["A"][0], in0=w["Ds"], in1=becol.unsqueeze(2).to_broadcast([128, 4, 128]), op=ALU.mult),
              r=[w["d_Ds"], d_beta], w=[w["d_A"][0]])
        yield
        for h in range(4):
            hp = slice(64 * (h % 2), 64 * (h % 2) + 64)
            k.pe(lambda e, h=h, hp=hp: e.matmul(B2[:, h * 128:(h + 1) * 128], qT[:, h // 2, c0:c0 + 128], w["kc"][:, h // 2, h % 2, :],
                                                start=True, stop=True), r=[d_qk, w["d_kc"]], w=[dB2])
        yield
        k.dve(lambda e: e.tensor_tensor(out=w["QKm"].rearrange("p a b -> p (a b)"), in0=B2, in1=w["D"].rearrange("p a b -> p (a b)"),
                                        op=ALU.mult), r=[dB2, w["d_D"]], w=[w["d_QKm"]])
        yield
        B1b = B1.bitcast(BF16)
        B2b = B2.bitcast(BF16)
        for h in range(4):
            k.pe(lambda e, h=h: e.transpose(out=B1b[:, h * 128:(h + 1) * 128], in_=w["A"][0][:, h, :], identity=self.ident_b[:]),
                 r=[w["d_A"][0], self.d_const], w=[dB1])
        yield
        k.act(lambda e: e.activation(out=w["AT"][0].rearrange("p a b -> p (a b)"), in_=B1b[:, 0:512], func=AF.Copy),
              r=[dB1], w=[w["d_AT"][0]])
        for h in range(4):
            k.pe(lambda e, h=h: e.transpose(out=B2b[:, h * 128:(h + 1) * 128], in_=w["QKm"][:, h, :], identity=self.ident_b[:]),
                 r=[w["d_QKm"], self.d_const], w=[dB2])
        yield
        k.act(lambda e: e.activation(out=w["QKT"].rearrange("p a b -> p (a b)"), in_=B2b[:, 0:512], func=AF.Copy),
              r=[dB2], w=[w["d_QKT"]])
        yield
        k.dve(lambda e: e.tensor_tensor(out=w["beg"], in0=becol, in1=w["ev"][:, 0:4], op=ALU.mult), r=[d_beta, w["d_ev"]], w=[w["d_beg"]])
        X0 = w["R"].rearrange("p h (two d) -> p h two d", two=2)
        k.dve(lambda e: e.tensor_tensor(out=X0[:, :, 0, :], in0=vtok[:, n, :].rearrange("p (h d) -> p h d", h=4),
                                        in1=becol.unsqueeze(2).to_broadcast([128, 4, 64]), op=ALU.mult),
              r=[d_vtok, d_beta], w=[w["d_R"]])
        k.dve(lambda e: e.tensor_tensor(out=X0[:, :, 1, :], in0=ktok[:, n, :].rearrange("p (h d) -> p h d", h=4),
                                        in1=w["beg"].unsqueeze(2).to_broadcast([128, 4, 64]), op=ALU.mult),
              r=[d_ktok, w["d_beg"]], w=[w["d_R"]])
        for half in range(2):
            k.pool(lambda e, half=half: e.tensor_tensor(out=w["kd"][:, :, half * 64:(half + 1) * 64],
                                                        in0=ktok[:, n, :].rearrange("p (h d) -> p h d", h=4),
                                                        in1=w["ev"][:, 4:8].unsqueeze(2).to_broadcast([128, 4, 64]), op=ALU.mult),
                   r=[d_ktok, w["d_ev"]], w=[w["d_kd"]])
        yield
        A, AT = w["A"][0], w["AT"][0]
        dA, dAT = w["d_A"][0], w["d_AT"][0]
        Tm, TT = w["A"][1], w["AT"][1]
        dT, dTT = w["d_A"][1], w["d_AT"][1]
        M1, M1t = w["X"][0], w["X"][1]
        dM1, dM1t = w["d_X"][0], w["d_X"][1]
        mo = 0 if dd == 0 else 7
        mt = 7 if dd == 0 else 0
        I4 = self.ident_b[:].unsqueeze(1).to_broadcast([128, 4, 128])
        msk = lambda s_: LMK[:, (mo + s_) * 128:(mo + s_ + 1) * 128].unsqueeze(1).to_broadcast([128, 4, 128])
        mskT = lambda s_: LMK[:, (mt + s_) * 128:(mt + s_ + 1) * 128].unsqueeze(1).to_broadcast([128, 4, 128])
        k.pool(lambda e: e.tensor_tensor(out=Tm, in0=A, in1=msk(0), op=ALU.mult), r=[dA, d_LMK], w=[dT])
        k.dve(lambda e: e.tensor_tensor(out=Tm, in0=Tm, in1=I4, op=ALU.add), r=[dT, self.d_const], w=[dT])
        k.pool(lambda e: e.tensor_tensor(out=TT, in0=AT, in1=mskT(0), op=ALU.mult), r=[dAT, d_LMK], w=[dTT])
        k.dve(lambda e: e.tensor_tensor(out=TT, in0=TT, in1=I4, op=ALU.add), r=[dTT, self.d_const], w=[dTT])
        yield
        fl = lambda x: x.rearrange("p a b -> p (a b)")
        def mk_E(lev):
            j = lev % 2
            k.pool(lambda e, lev=lev, j=j: e.tensor_tensor(out=w["E"][j], in0=A, in1=msk(lev), op=ALU.mult), r=[dA, d_LMK], w=[w["d_E"][j]])
            k.dve(lambda e, lev=lev, j=j: e.tensor_tensor(out=w["ET"][j], in0=AT, in1=mskT(lev), op=ALU.mult), r=[dAT, d_LMK], w=[w["d_ET"][j]])
        mk_E(1)
        for lev in range(1, 7):
            j = lev % 2
            E, ET, dE, dET = w["E"][j], w["ET"][j], w["d_E"][j], w["d_ET"][j]
            for h in range(4):
                k.pe(lambda e, h=h, ET=ET: e.matmul(B0[:, h * 128:(h + 1) * 128], ET[:, h, :], Tm[:, h, :], start=True, stop=True),
                     r=[dET, dT], w=[dB0])
            for h in range(4):
                k.pe(lambda e, h=h, E=E: e.matmul(B1[:, h * 128:(h + 1) * 128], E[:, h, :], TT[:, h, :], start=True, stop=True),
                     r=[dE, dTT], w=[dB1])
            yield
            if lev < 6:
                mk_E(lev + 1)
            k.act(lambda e: e.activation(out=fl(M1), in_=B0, func=AF.Copy), r=[dB0], w=[dM1])
            k.act(lambda e: e.activation(out=fl(M1t), in_=B1, func=AF.Copy), r=[dB1], w=[dM1t])
            yield
            for h in range(4):
                k.pe(lambda e, h=h: e.matmul(B2[:, h * 128:(h + 1) * 128], TT[:, h, :], M1[:, h, :], start=True, stop=True),
                     r=[dTT, dM1], w=[dB2])
            for h in range(4):
                k.pe(lambda e, h=h: e.matmul(B3[:, h * 128:(h + 1) * 128], Tm[:, h, :], M1t[:, h, :], start=True, stop=True),
                     r=[dT, dM1t], w=[dB3])
            yield
            k.dve(lambda e: e.tensor_tensor(out=fl(Tm), in0=fl(Tm), in1=B2, op=ALU.add), r=[dT, dB2], w=[dT])
            k.dve(lambda e: e.tensor_tensor(out=fl(TT), in0=fl(TT), in1=B3, op=ALU.add), r=[dTT, dB3], w=[dTT])
        yield
        Rm = w["R"].rearrange("p h (two d) -> p h two d", two=2)
        for h in range(4):
            k.pe(lambda e, h=h: e.matmul(B1[:, h * 64:(h + 1) * 64], TT[:, h, :], Rm[:, h, 0, :], start=True, stop=True),
                 r=[dTT, w["d_R"]], w=[dB1])
        k.act(lambda e: e.activation(out=w["u"].rearrange("p a b -> p (a b)"), in_=B1[:, 0:256], func=AF.Copy), r=[dB1], w=[w["d_u"]])
        yield
        for h in range(4):
            k.pe(lambda e, h=h: e.matmul(B2[0:64, h * 128:(h + 1) * 128], Rm[:, h, 1, :], TT[:, h, :], start=True, stop=True),
                 r=[dTT, w["d_R"]], w=[dB2])
        k.act(lambda e: e.activation(out=w["wT"].rearrange("p a b -> p (a b)"), in_=B2[0:64, :], func=AF.Copy),
              r=[dB2], w=[w["d_wT"]])
        yield

    def unit_scan(dd, n, need_out):
        w = W_[dd]
        Xf, dXf = w["u"], w["d_u"]
        c0 = n * 128
        B0, dB0 = bank(dd, 0)
        B1, dB1 = bank(dd, 1)
        for h in range(4):
            k.pe(lambda e, h=h: e.matmul(B0[:, h * 64:(h + 1) * 64], w["wT"][:, h, :], w["Sb"][0:64, h, :], start=True, stop=True),
                 r=[w["d_wT"], w["d_Sb"]], w=[dB0])
        if need_out:
            for h in range(4):
                hp = slice(64 * (h % 2), 64 * (h % 2) + 64)
                k.pe(lambda e, h=h, hp=hp: e.matmul(B0[:, 256 + h * 64:256 + (h + 1) * 64], qT[:, h // 2, c0:c0 + 128],
                                                    w["SbQ"][:, h, :], start=True, stop=True),
                     r=[d_qk, w["d_SbQ"]], w=[dB0])
        yield
        k.dve(lambda e: e.tensor_tensor(out=w["vn"], in0=Xf, in1=B0[:, 0:256].rearrange("p (h d) -> p h d", h=4), op=ALU.subtract),
              r=[dXf, dB0], w=[w["d_vn"]])
        if need_out:
            for h in range(4):
                k.pe(lambda e, h=h: e.matmul(B1[:, h * 64:(h + 1) * 64], w["QKT"][:, h, :], w["vn"][:, h, :], start=True, stop=True),
                     r=[w["d_QKT"], w["d_vn"]], w=[dB1])
            k.act(lambda e: e.activation(out=w["o2"].rearrange("p a b -> p (a b)"), in_=B1[:, 0:256], func=AF.Copy), r=[dB1], w=[w["d_o2"]])
            k.dve(lambda e: e.tensor_tensor(out=w["ot"], in0=B0[:, 256:512].rearrange("p (h d) -> p h d", h=4),
                                            in1=w["ev"][:, 0:4].unsqueeze(2).to_broadcast([128, 4, 64]), op=ALU.mult),
                  r=[dB0, w["d_ev"]], w=[w["d_ot"]])
            k.pool(lambda e: e.tensor_tensor(out=w["ot"], in0=w["ot"], in1=w["o2"], op=ALU.add), r=[w["d_ot"], w["d_o2"]], w=[w["d_ot"]])
            if first_visit[n]:
                first_visit[n] = False
                k.pool(lambda e: e.tensor_copy(out=otok[:, n, :].rearrange("p (h d) -> p h d", h=4), in_=w["ot"]), r=[w["d_ot"]], w=[d_otok[n]])
            else:
                k.pool(lambda e: e.tensor_tensor(out=otok[:, n, :].rearrange("p (h d) -> p h d", h=4),
                                                 in0=otok[:, n, :].rearrange("p (h d) -> p h d", h=4), in1=w["ot"], op=ALU.add),
                       r=[w["d_ot"], d_otok[n]], w=[d_otok[n]])
        yield
        for h in range(4):
            k.pe(lambda e, h=h: e.matmul(B1[:, 256 + h * 64:256 + (h + 1) * 64], w["kd"][:, h, :], w["vn"][:, h, :], start=True, stop=True),
                 r=[w["d_kd"], w["d_vn"]], w=[dB1])
        yield
        k.dve(lambda e: e.tensor_tensor(out=w["S"], in0=w["S"], in1=w["ev"][:, 8:12].unsqueeze(2).to_broadcast([128, 4, 64]), op=ALU.mult),
              r=[w["d_S"], w["d_ev"]], w=[w["d_S"]])
        k.dve(lambda e: e.tensor_tensor(out=w["S"], in0=w["S"], in1=B1[:, 256:512].rearrange("p (h d) -> p h d", h=4), op=ALU.add),
              r=[w["d_S"], dB1], w=[w["d_S"]])
        k.act(lambda e: e.activation(out=w["Sb"], in_=w["S"], func=AF.Copy), r=[w["d_S"]], w=[w["d_Sb"]])
        k.pool(lambda e: e.tensor_tensor(out=w["SbQ"], in0=w["S"], in1=hm.unsqueeze(2).to_broadcast([128, 4, 64]), op=ALU.mult),
               r=[w["d_S"], d_hm], w=[w["d_SbQ"]])

        yield

    def round_robin(gens):
        alive = list(gens)
        while alive:
            for g in list(alive):
                try:
                    next(g)
                except StopIteration:
                    alive.remove(g)

    _mode = os.environ.get("GDN_RR", "rr")
    for step in range(18):
        cur = []
        pres = []
        for dd in range(2):
            n = (orderF if dd == 0 else orderB)[step]
            need_out = (n < 16) or with_ctx
            pres.append(unit_pre(dd, n))
            cur.append((dd, n, need_out))
        if _mode == "rr":
            round_robin(pres)
            round_robin([unit_scan(*c_) for c_ in cur])
        else:
            for g in pres:
                round_robin([g])
            for c_ in cur:
                round_robin([unit_scan(*c_)])
    if "gdn_otok" in self.dbg:
        self.k.barrier()
        chd = k.chan()
        o_ = self.dout("dbg_otok", [128, NT * 256])
        k.dma("sp", o_[:, :], otok.rearrange("p a b -> p (a b)"), chd)
        self.k.barrier()
    if _stop:
        return
    _arena_rewind(self, markO)
    _gdn_out(self, l, with_ctx, otok, d_otok)


def _gdn_out(self, l, with_ctx, otok, d_otok):
    k = self.k
    ntt = NT if with_ctx else 16
    _wring_init(self, 1)
    gT = self.aa([128, 2, T], BF16)
    d_g = Dep()
    blocks = _blocks(self, with_ctx)

    def evac(c, m, t0, n, pst, dps):
        ct = (c - 768) // 128
        k.act(lambda e: e.activation(out=gT[:, ct, t0:t0 + n], in_=pst, func=AF.Silu), r=[dps], w=[d_g])
    _proj(self, l, [(768, 256)], blocks, evac, PROJ_BANKS)
    gn = _ppv(self, l, "gnorm")
    sq = self.aa([128, 4, 64])
    ss = self.aa([128, 4])
    on = [self.aa([128, 256], BF16) for _ in range(2)]
    d_sq, d_ss = Dep(), Dep()
    d_on = [Dep(), Dep()]
    psT = self.ps[2][:, 0:512].bitcast(BF16)
    d_psT = self.d_ps[2][0]
    for tt in range(ntt):
        o3 = otok[:, tt, :].rearrange("p (h d) -> p h d", h=4)
        j = tt % 2
        k.dve(lambda e, o3=o3: e.tensor_tensor(out=sq, in0=o3, in1=o3, op=ALU.mult), r=[d_otok[tt]], w=[d_sq])
        k.dve(lambda e: e.reduce_sum(out=ss, in_=sq, axis=mybir.AxisListType.X), r=[d_sq], w=[d_ss])
        k.act(lambda e: e.activation(out=ss, in_=ss, func=AF.Sqrt, bias=EPS, scale=1.0 / 64), r=[d_ss], w=[d_ss])
        k.dve(lambda e: e.reciprocal(out=ss, in_=ss), r=[d_ss], w=[d_ss])
        k.dve(lambda e, o3=o3, j=j: e.tensor_tensor(out=on[j].rearrange("p (h d) -> p h d", h=4), in0=o3,
                                                    in1=ss.unsqueeze(2).to_broadcast([128, 4, 64]), op=ALU.mult),
              r=[d_otok[tt], d_ss], w=[d_on[j]])
        for ct in range(2):
            k.pe(lambda e, j=j, ct=ct: e.transpose(out=psT[:, ct * 128:(ct + 1) * 128], in_=on[j][:, ct * 128:(ct + 1) * 128],
                                                   identity=self.ident_b[:]), r=[d_on[j], self.d_const], w=[d_psT])
        for ct in range(2):
            k.dve(lambda e, ct=ct, tt=tt: e.scalar_tensor_tensor(out=self.C[:, ct, tt * 128:(tt + 1) * 128], in0=psT[:, ct * 128:(ct + 1) * 128],
                                                                 scalar=gn[:, 0:1], in1=gT[:, ct, tt * 128:(tt + 1) * 128],
                                                                 op0=ALU.mult, op1=ALU.mult),
                  r=[d_psT, d_g, self.d_pp], w=[self.d_C[tt]])


from concourse.bass_utils import run_bass_kernel_spmd


def kernel(**inputs):
    inputs = {k_: np.asarray(v) for k_, v in inputs.items()}
    mk = MK()
    nc = mk.build()
    consts = const_inputs(inputs)
    pp = pp_host(inputs)

    def extra(b):
        e = {"pp": pp}
        e.update(consts)
        return e
    maps = host_inputs(mk, inputs, extra=extra)
    n = len(maps)
    res = run_bass_kernel_spmd(nc, maps, core_ids=list(range(n)))
    out = np.stack([np.asarray(res.results[b]["out"]) for b in range(n)], axis=0)
    return out.astype(np.float32)
```

```python
import numpy as np
import ml_dtypes
from contextlib import ExitStack
import concourse.bass as bass
import concourse.mybir as mybir

F32 = mybir.dt.float32
BF16 = mybir.dt.bfloat16
I32 = mybir.dt.int32
AF = mybir.ActivationFunctionType
ALU = mybir.AluOpType

D = 1024
L = 2048
LC = 256
T = L + LC
NT = T // 128
DEPTH = 2
EPS = 1e-6
IN_COLS = 3344
OFF_SC, OFF_HY, OFF_NA = 1040, 1808, 2576


class Dep:
    __slots__ = ("lw", "rd")

    def __init__(self):
        self.lw = None
        self.rd = []


class Chan:
    __slots__ = ("sem", "cnt", "key", "q")

    def __init__(self, sem, key):
        self.sem = sem
        self.cnt = 0
        self.key = key


class K:
    def __init__(self, nc, stack):
        self.nc = nc
        self.stack = stack
        self.eng = {"pe": nc.tensor, "act": nc.scalar, "dve": nc.vector, "pool": nc.gpsimd, "sp": nc.sync}
        self.sems = {}
        self.ecnt = {}
        self.waited = {e: {} for e in self.eng}
        for e in self.eng:
            self.sems["e_" + e] = stack.enter_context(nc.semaphore("e_" + e))
            self.ecnt[e] = 0
        self.chans = []
        self.bar_sem = stack.enter_context(nc.semaphore("bar"))
        self.bar_cnt = 0
        self.nops = 0

    def chan(self):
        key = "c%d" % len(self.chans)
        s = self.stack.enter_context(self.nc.semaphore(key))
        self.sems[key] = s
        c = Chan(s, key)
        c.q = None
        self.chans.append(c)
        return c

    def _wait(self, e, ev):
        key, val, src = ev
        if src == e and e == "pe":
            return
        w = self.waited[e]
        if w.get(key, 0) >= val:
            return
        w[key] = val
        self.eng[e].wait_ge(self.sems[key], val)

    def _deps(self, e, r, w):
        for d in r:
            if d.lw is not None:
                self._wait(e, d.lw)
        for d in w:
            if d.lw is not None and d.lw[2] != e:
                self._wait(e, d.lw)
            for ev in d.rd:
                if ev[2] != e:
                    self._wait(e, ev)

    def _commit(self, ev, r, w):
        for d in w:
            d.lw = ev
            d.rd = []
        for d in r:
            d.rd.append(ev)
            if len(d.rd) > 48:
                best = {}
                for x in d.rd:
                    if x[0] not in best or best[x[0]][1] < x[1]:
                        best[x[0]] = x
                d.rd = list(best.values())

    def op(self, e, fn, r=(), w=()):
        self._deps(e, r, w)
        ins = fn(self.eng[e])
        self.ecnt[e] += 1
        ins.then_inc(self.sems["e_" + e], 1)
        self._commit(("e_" + e, self.ecnt[e], e), r, w)
        self.nops += 1
        return ins

    def pe(self, fn, r=(), w=()):
        return self.op("pe", fn, r, w)

    def act(self, fn, r=(), w=()):
        return self.op("act", fn, r, w)

    def dve(self, fn, r=(), w=()):
        return self.op("dve", fn, r, w)

    def pool(self, fn, r=(), w=()):
        return self.op("pool", fn, r, w)

    def dma(self, q, out, in_, ch, r=(), w=(), **kw):
        self._deps(q, r, w)
        ins = self.eng[q].dma_start(out=out, in_=in_, **kw)
        ch.cnt += 16
        ch.q = q
        ins.then_inc(ch.sem, 16)
        self._commit((ch.key, ch.cnt, "dma"), r, w)
        self.nops += 1
        return ins

    def barrier(self):
        last = getattr(self, "_bar_last", {})
        cur = {}
        for e in self.eng:
            cur["e_" + e] = (self.ecnt[e], e)
        for c in self.chans:
            cur[c.key] = (c.cnt, "dma")
        changed = [(key, v[0], v[1]) for key, v in cur.items() if v[0] > 0 and last.get(key, (0,))[0] != v[0]]
        for e in self.eng:
            for (key, val, src) in changed:
                w = self.waited[e]
                if w.get(key, 0) >= val:
                    continue
                w[key] = val
                self.eng[e].wait_ge(self.sems[key], val)
        self._bar_last = cur


class MK:
    def __init__(self, dbg=None, layers=(0, 1), inject_cat=False, mixers=("gdn", "sc", "hy", "na"), do_mlp=True,
                 phases=("mod", "p1", "mix", "p3", "p4"), inject_h=False):
        self.phases = phases
        self.inject_h = inject_h
        self.dbg = dbg or {}
        self.layers = layers
        self.inject_cat = inject_cat
        self.mixers = mixers
        self.do_mlp = do_mlp
        self.inputs = {}
        self.outputs = {}

    def din(self, name, shape, dtype=F32):
        t = self.nc.dram_tensor(name, list(shape), dtype, kind="ExternalInput").ap()
        self.inputs[name] = (tuple(shape), dtype)
        return t

    def dout(self, name, shape, dtype=F32):
        t = self.nc.dram_tensor(name, list(shape), dtype, kind="ExternalOutput").ap()
        self.outputs[name] = (tuple(shape), dtype)
        return t

    def W(self, name):
        if name not in self._w:
            self._w[name] = self.din(name, self._wshape[name])
        return self._w[name]

    def sb(self, st, name, shape, dtype=F32):
        self._n = getattr(self, "_n", 0) + 1
        return st.enter_context(self.nc.sbuf_tensor("%s_%d" % (name, self._n), list(shape), dtype))

    def build(self):
        nc = bass.Bass("TRN2", target_bir_lowering=False)
        self.nc = nc
        with ExitStack() as st:
            self.st = st
            self.k = K(nc, st)
            self._declare()
            self._consts()
            for l in self.layers:
                self._layer(l)
            self._finish()
        return nc

    def _declare(self):
        nc = self.nc
        self._wshape = {}
        self._w = {}
        self.x_in = self.din("x", [L, D])
        self.ctx_in = self.din("ctx", [LC, D])
        self.cvec_in = self.din("cvec", [128, 16])
        self._wshape["ada_w"] = [DEPTH, D, 6 * D]
        self.ada_bT = self.din("ada_bT", [128, DEPTH * 48])
        self.gains_in = self.din("gains", [128, 4 * DEPTH * 8])
        self._wshape["w_in"] = [DEPTH, D, IN_COLS]
        self._wshape["w_out"] = [DEPTH, D, D]
        self._wshape["mlp_w1"] = [DEPTH, D, 4 * D]
        self._wshape["mlp_w2"] = [DEPTH, 4 * D, D]
        self.ident_f_in = self.din("ident_f", [128, 128])
        self.ident_b_in = self.din("ident_b", [128, 128], BF16)
        self.out = self.dout("out", [L, D])
        self.xres = nc.dram_tensor("xres", [T, D], F32).ap()
        self.d_xres = [Dep() for _ in range(NT)]
        self.d_out = [Dep() for _ in range(NT)]
        if self.inject_cat:
            self.cat_in = self.din("cat_in", [D, T], BF16)
        self.ps = [self.st.enter_context(nc.psum_tensor("ps%d" % i, [128, 1024], F32)) for i in range(4)]
        self.d_ps = [[Dep(), Dep()] for _ in range(4)]

    def _consts(self):
        k, st = self.k, self.st
        sb = lambda n, s, d=F32: self.sb(st, n, s, d)
        self.ident_f = sb("ident_f", [128, 128])
        self.ident_b = sb("ident_b", [128, 128], BF16)
        self.ones_f = sb("ones_f", [128, 128])
        self.cvec = sb("cvec", [128, 16])
        self.adab = sb("adab", [128, DEPTH * 48])
        self.gains = sb("gains", [128, 4 * DEPTH * 8])
        self.d_const = Dep()
        ch = k.chan()
        k.dma("sp", self.ident_f[:], self.ident_f_in[:, :], ch, w=[self.d_const])
        k.dma("sp", self.ident_b[:], self.ident_b_in[:, :], ch, w=[self.d_const])
        k.dma("sp", self.cvec[:], self.cvec_in[:, :], ch, w=[self.d_const])
        k.dma("sp", self.adab[:], self.ada_bT[:, :], ch, w=[self.d_const])
        k.dma("sp", self.gains[:], self.gains_in[:, :], ch, w=[self.d_const])
        k.dve(lambda e: e.memset(self.ones_f[:], 1.0), w=[self.d_const])
        self.H = sb("H", [128, 8, T], BF16)
        self.C = sb("C", [128, 8, T], BF16)
        self.d_H = [Dep() for _ in range(NT)]
        self.d_C = [Dep() for _ in range(NT)]
        self.modT = sb("modT", [128, 48, 2])
        self.A1 = sb("A1", [128, 8, 2])
        self.A2 = sb("A2", [128, 8, 2])
        self.G1f = sb("G1f", [128, 8, 2])
        self.G2f = sb("G2f", [128, 8, 2])
        self.Gbc = sb("Gbc", [128, 2, 2, D])
        self.d_mod = Dep()
        self.d_gbc = Dep()
        self.NAR = 28416
        self.AR = sb("arena", [128, self.NAR])
        self.ar_off = 0
        self.NXT = 3
        self.d_xt = [Dep() for _ in range(self.NXT)]
        self.ch_xt_ld = [k.chan() for _ in range(self.NXT)]
        self.ch_xt_st = [k.chan() for _ in range(self.NXT)]
        self.d_xn = [Dep(), Dep()]
        self.d_junk = Dep()
        self.d_stat = [Dep() for _ in range(4)]
        self.d_tmpb = [Dep(), Dep()]
        self.d_tmpf = [Dep(), Dep()]
        self.xt_i = self.xn_i = self.stat_i = self.tmp_i = 0

    def arena_reset(self):
        self.k.barrier()
        self.ar_off = 0

    def aa(self, shape, dtype=F32):
        esz = 4 if dtype in (F32, I32) else 2
        n = int(np.prod(shape[1:]))
        nbytes = (n * esz + 31) // 32 * 32
        o = self.ar_off
        assert o + nbytes <= self.NAR * 4, "arena overflow %d" % (o + nbytes)
        self.ar_off = o + nbytes
        v = self.AR[:, o // 4:(o + nbytes) // 4]
        if dtype != F32:
            v = v.bitcast(dtype)
        v = v[:, 0:n]
        if len(shape) == 3:
            v = v.rearrange("p (a b) -> p a b", b=shape[2])
        elif len(shape) == 4:
            v = v.rearrange("p (a b c) -> p a b c", b=shape[2], c=shape[3])
        if shape[0] != 128:
            v = v[0:shape[0]]
        return v

    def _staging(self, norm=True):
        self.xt = [self.aa([128, D]) for i in range(self.NXT)]
        self.junk = self.aa([128, D], BF16)
        self.stat = [self.aa([128, 8]) for i in range(4)]
        self.tmpf = [self.aa([128, D]) for i in range(2)]
        if norm:
            self.xn = [self.aa([128, D], BF16) for i in range(2)]
            self.tmpb = [self.aa([128, D], BF16) for i in range(2)]

    def gain(self, kind, l):
        o = (kind * DEPTH + l) * 8
        return self.gains[:, o:o + 8]

    def _layer(self, l):
        ph = self.phases
        if "mod" in ph:
            self._modulation(l)
        if "p1" in ph:
            self.arena_reset()
            self._staging()
            for tt in range(NT):
                s = 0 if tt < 16 else 1
                xi = self._load_x(l, tt, first=True)
                self._norm_tile(xi, s, self.A1, self.modT[:, 0:8, :], self.H, tt, self.d_H[tt])
        elif self.inject_h:
            ch = self.k.chan()
            hin = self.din("h_in", [D, T], BF16)
            for tt in range(NT):
                self.k.dma("sp", self.H[:, :, tt * 128:(tt + 1) * 128],
                           hin[:, tt * 128:(tt + 1) * 128].rearrange("(a p) t -> p a t", p=128), ch, w=[self.d_H[tt]])
        if "hx" in self.dbg and self.dbg["hx"] == l:
            self._dump_feat("dbg_hx", self.H, self.d_H)
        if "mix" in ph:
            self._mixers(l)
        if "cat" in self.dbg and self.dbg["cat"] == l:
            self._dump_feat("dbg_cat", self.C, self.d_C)
        if "p3" in ph:
            self._p3(l)
        if "hx2" in self.dbg and self.dbg["hx2"] == l:
            self._dump_feat("dbg_hx2", self.C, self.d_C)
        if "p4" in ph:
            self._p4(l)

    def _xsrc(self, l, tt, first):
        if l == self.layers[0] and l == 0 and first:
            if tt < 16:
                return self.x_in[tt * 128:(tt + 1) * 128, :], None
            return self.ctx_in[(tt - 16) * 128:(tt - 15) * 128, :], None
        return self.xres[tt * 128:(tt + 1) * 128, :], self.d_xres[tt]

    def _load_x(self, l, tt, first):
        k = self.k
        i = self.xt_i
        self.xt_i = (i + 1) % self.NXT
        src, dep = self._xsrc(l, tt, first)
        k.dma("sp", self.xt[i][:], src, self.ch_xt_ld[i], r=[dep] if dep else [], w=[self.d_xt[i]])
        return i

    def _store_x(self, i, dst_ap, dst_dep):
        self.k.dma("sp", dst_ap, self.xt[i][:], self.ch_xt_st[i], r=[self.d_xt[i]], w=[dst_dep])

    def _rstd(self, src_ap, src_deps):
        k = self.k
        j = self.stat_i
        self.stat_i = (j + 1) % 4
        stt, dst = self.stat[j], self.d_stat[j]
        k.act(lambda e: e.activation(out=self.junk[:], in_=src_ap, func=AF.Square, accum_out=stt[:, 0:1]),
              r=src_deps, w=[self.d_junk, dst])
        k.act(lambda e: e.activation(out=stt[:, 1:2], in_=stt[:, 0:1], func=AF.Sqrt, bias=EPS, scale=1.0 / D),
              r=[dst], w=[dst])
        k.dve(lambda e: e.reciprocal(out=stt[:, 2:3], in_=stt[:, 1:2]), r=[dst], w=[dst])
        return stt[:, 2:3], dst

    def _norm_tile(self, xi, s, A, B, Hbuf, tt, dH):
        k = self.k
        xt, dxt = self.xt[xi], self.d_xt[xi]
        rs, drs = self._rstd(xt[:], [dxt])
        j = self.xn_i
        self.xn_i = 1 - j
        xn, dxn = self.xn[j], self.d_xn[j]
        k.dve(lambda e: e.tensor_scalar(out=xn[:], in0=xt[:], scalar1=rs, scalar2=None, op0=ALU.mult),
              r=[dxt, drs], w=[dxn])
        pi = 3
        psb = self.ps[pi][:, 0:512].bitcast(BF16)
        dps = self.d_ps[pi][0]
        for dt in range(8):
            k.pe(lambda e, dt=dt: e.transpose(out=psb[:, dt * 128:(dt + 1) * 128], in_=xn[:, dt * 128:(dt + 1) * 128],
                                              identity=self.ident_b[:]),
                 r=[dxn, self.d_const], w=[dps])
        ti = self.tmp_i
        self.tmp_i = 1 - ti
        tb, dtb = self.tmpb[ti], self.d_tmpb[ti]
        k.dve(lambda e: e.tensor_tensor(out=tb[:].rearrange("p (a b) -> p a b", a=8),
                                        in0=psb.rearrange("p (a b) -> p a b", a=8),
                                        in1=A[:, :, s:s + 1].to_broadcast([128, 8, 128]), op=ALU.mult),
              r=[dps, self.d_mod], w=[dtb])
        k.pool(lambda e: e.tensor_tensor(out=Hbuf[:, :, tt * 128:(tt + 1) * 128],
                                         in0=tb[:].rearrange("p (a b) -> p a b", a=8),
                                         in1=B[:, :, s:s + 1].to_broadcast([128, 8, 128]), op=ALU.add),
               r=[dtb, self.d_mod], w=[dH])

    def _modulation(self, l):
        k = self.k
        self.arena_reset()
        if True:
            sT = self.aa([128, 16], BF16)
            d_sT = Dep()
            k.act(lambda e: e.activation(out=sT[:], in_=self.cvec[:], func=AF.Silu), r=[self.d_const], w=[d_sT])
            wb = [self.aa([128, 8, 512], BF16) for i in range(2)]
            dwb = [Dep(), Dep()]
            chw = [k.chan(), k.chan()]
            mod_ps = self.ps[0][:, 0:96]
            dps = self.d_ps[0][0]
            for g in range(12):
                i = g % 2
                src = self.W("ada_w")[l, :, g * 512:(g + 1) * 512].rearrange("(kt p) c -> p kt c", p=128)
                k.dma("pool", wb[i][:], src, chw[i], w=[dwb[i]])
                for jj in range(4):
                    jt = g * 4 + jj
                    for kt in range(8):
                        k.pe(lambda e, i=i, jj=jj, jt=jt, kt=kt: e.matmul(
                            mod_ps[:, jt * 2:jt * 2 + 2], wb[i][:, kt, jj * 128:(jj + 1) * 128],
                            sT[:, kt * 2:kt * 2 + 2], start=(kt == 0), stop=(kt == 7)),
                            r=[dwb[i], d_sT], w=[dps])
            dm = self.d_mod
            k.dve(lambda e: e.tensor_tensor(out=self.modT[:], in0=mod_ps.rearrange("p (a b) -> p a b", b=2),
                                            in1=self.adab[:, l * 48:(l + 1) * 48].unsqueeze(2).to_broadcast([128, 48, 2]),
                                            op=ALU.add), r=[dps, self.d_const], w=[dm])
            g = lambda kind: self.gain(kind, l).unsqueeze(2).to_broadcast([128, 8, 2])
            k.dve(lambda e: e.scalar_tensor_tensor(out=self.A1[:], in0=self.modT[:, 8:16, :], scalar=1.0, in1=g(0),
                                                   op0=ALU.add, op1=ALU.mult), r=[dm, self.d_const], w=[dm])
            k.dve(lambda e: e.scalar_tensor_tensor(out=self.A2[:], in0=self.modT[:, 32:40, :], scalar=1.0, in1=g(2),
                                                   op0=ALU.add, op1=ALU.mult), r=[dm, self.d_const], w=[dm])
            k.dve(lambda e: e.tensor_tensor(out=self.G1f[:], in0=self.modT[:, 16:24, :], in1=g(1), op=ALU.mult),
                  r=[dm, self.d_const], w=[dm])
            k.dve(lambda e: e.tensor_tensor(out=self.G2f[:], in0=self.modT[:, 40:48, :], in1=g(3), op=ALU.mult),
                  r=[dm, self.d_const], w=[dm])
            diag = [self.aa([128, 128]) for i in range(2)]
            ddiag = [Dep(), Dep()]
            n = 0
            for kind, Gf in enumerate((self.G1f, self.G2f)):
                for s in range(2):
                    for dt in range(8):
                        i = n % 2
                        n += 1
                        k.dve(lambda e, i=i, Gf=Gf, dt=dt, s=s: e.tensor_scalar(
                            out=diag[i][:], in0=self.ident_f[:], scalar1=Gf[:, dt, s:s + 1], scalar2=None, op0=ALU.mult),
                            r=[dm, self.d_const], w=[ddiag[i]])
                        pi = 1 + (n % 2)
                        pst = self.ps[pi][:, 0:128]
                        k.pe(lambda e, i=i, pst=pst: e.matmul(pst, self.ones_f[:], diag[i][:], start=True, stop=True),
                             r=[ddiag[i], self.d_const], w=[self.d_ps[pi][0]])
                        k.act(lambda e, pst=pst, kind=kind, s=s, dt=dt: e.activation(
                            out=self.Gbc[:, kind, s, dt * 128:(dt + 1) * 128], in_=pst, func=AF.Copy),
                            r=[self.d_ps[pi][0]], w=[self.d_gbc])
        if "mod" in self.dbg and self.dbg["mod"] == l:
            o = self.dout("dbg_mod", [128, 96])
            ch = k.chan()
            k.dma("sp", o[:, :], self.modT[:].rearrange("p a b -> p (a b)"), ch, r=[self.d_mod])
            o2 = self.dout("dbg_gbc", [128, 4 * D])
            k.dma("sp", o2[:, :], self.Gbc[:].rearrange("p a b c -> p (a b c)"), ch, r=[self.d_gbc])

    def _mixers(self, l):
        k = self.k
        if self.inject_cat:
            ch = k.chan()
            for tt in range(NT):
                k.dma("sp", self.C[:, :, tt * 128:(tt + 1) * 128],
                      self.cat_in[:, tt * 128:(tt + 1) * 128].rearrange("(a p) t -> p a t", p=128), ch, w=[self.d_C[tt]])
            return
        raise NotImplementedError

    def _p3(self, l):
        k = self.k
        last = (l == DEPTH - 1)
        ntt = 16 if last else NT
        self.arena_reset()
        self._staging()
        if True:
            wo = self.aa([128, 8, D], BF16)
            dwo = Dep()
            ch = k.chan()
            k.dma("pool", wo[:], self.W("w_out")[l].rearrange("(kt p) c -> p kt c", p=128), ch, w=[dwo])
            for tt in range(ntt):
                s = 0 if tt < 16 else 1
                pi = tt % 3
                yps = self.ps[pi]
                for half in range(2):
                    for mt in range(8):
                        k.pe(lambda e, half=half, mt=mt, yps=yps, tt=tt: e.matmul(
                            yps[:, half * 512:(half + 1) * 512], self.C[:, mt, tt * 128:(tt + 1) * 128],
                            wo[:, mt, half * 512:(half + 1) * 512], start=(mt == 0), stop=(mt == 7)),
                            r=[self.d_C[tt], dwo], w=[self.d_ps[pi][half]])
                xi = self._load_x(l, tt, first=True)
                self._resid_update(xi, yps[:], self.d_ps[pi], 0, s)
                self._store_x(xi, self.xres[tt * 128:(tt + 1) * 128, :], self.d_xres[tt])
                self._norm_tile(xi, s, self.A2, self.modT[:, 24:32, :], self.C, tt, self.d_C[tt])

    def _resid_update(self, xi, y_ap, y_deps, kind, s):
        k = self.k
        rs, drs = self._rstd(y_ap, list(y_deps))
        ti = self.tmp_i
        self.tmp_i = 1 - ti
        tf, dtf = self.tmpf[ti], self.d_tmpf[ti]
        k.dve(lambda e: e.scalar_tensor_tensor(out=tf[:], in0=y_ap, scalar=rs, in1=self.Gbc[:, kind, s, :],
                                               op0=ALU.mult, op1=ALU.mult),
              r=list(y_deps) + [drs, self.d_gbc], w=[dtf])
        xt, dxt = self.xt[xi], self.d_xt[xi]
        k.dve(lambda e: e.tensor_tensor(out=xt[:], in0=xt[:], in1=tf[:], op=ALU.add), r=[dtf, dxt], w=[dxt])

    def _p4(self, l):
        k = self.k
        last = (l == DEPTH - 1)
        if last:
            sblocks = [(0, 768), (768, 768), (1536, 512)]
        else:
            sblocks = [(0, 768), (768, 768), (1536, 768)]
        self.arena_reset()
        self._staging(norm=False)
        if True:
            hT = self.aa([128, 32, 768], BF16)
            d_hT = [[Dep() for _ in range(2)] for _ in range(32)]
            HF = self.H[:].rearrange("p a t -> p (a t)")
            w1c = [HF[:, i * 4096:(i + 1) * 4096].rearrange("p (k c) -> p k c", c=512) for i in range(3)]
            d_w1c = [Dep(), Dep(), Dep()]
            ch_w1 = [k.chan(), k.chan(), k.chan()]
            w2c = [HF[:, 12288 + i * 2048: 12288 + (i + 1) * 2048].rearrange("p (k c) -> p k c", c=512) for i in range(3)]
            d_w2c = [Dep() for _ in range(3)]
            ch_w2 = [k.chan() for _ in range(3)]
            rl = [self.aa([128, 384], BF16) for i in range(2)]
            d_rl = [Dep(), Dep()]
            ytok = self.aa([128, 6, D])
            d_ytok = [Dep() for _ in range(6)]
            n_w1 = 0
            n_w2 = 0
            n_rl = 0
            for (t0, n) in sblocks:
                n2 = n // 2
                ntl = n // 128
                for ffc in range(8):
                    i = n_w1 % 3
                    n_w1 += 1
                    k.dma("pool", w1c[i], self.W("mlp_w1")[l, :, ffc * 512:(ffc + 1) * 512].rearrange("(kt p) c -> p kt c", p=128),
                          ch_w1[i], w=[d_w1c[i]])
                    for f in range(4):
                        fft = ffc * 4 + f
                        for sbk in range(2):
                            hps = self.ps[3][:, sbk * 512: sbk * 512 + n2]
                            dhps = self.d_ps[3][sbk]
                            tts = range((t0 + sbk * n2) // 128, (t0 + (sbk + 1) * n2 + 127) // 128)
                            rdeps = [self.d_C[t] for t in tts]
                            for dt in range(8):
                                k.pe(lambda e, i=i, f=f, dt=dt, hps=hps, sbk=sbk: e.matmul(
                                    hps, w1c[i][:, dt, f * 128:(f + 1) * 128],
                                    self.C[:, dt, t0 + sbk * n2: t0 + (sbk + 1) * n2], start=(dt == 0), stop=(dt == 7)),
                                    r=[d_w1c[i]] + rdeps, w=[dhps])
                            j = n_rl % 2
                            n_rl += 1
                            k.act(lambda e, j=j, hps=hps: e.activation(out=rl[j][:, 0:n2], in_=hps, func=AF.Relu),
                                  r=[dhps], w=[d_rl[j]])
                            k.dve(lambda e, j=j, fft=fft, sbk=sbk: e.tensor_tensor(
                                out=hT[:, fft, sbk * n2:(sbk + 1) * n2], in0=rl[j][:, 0:n2], in1=rl[j][:, 0:n2], op=ALU.mult),
                                r=[d_rl[j]], w=[d_hT[fft][sbk]])
                for dh in range(2):
                    for ffc in range(8):
                        i = n_w2 % 3
                        n_w2 += 1
                        k.dma("pool", w2c[i],
                              self.W("mlp_w2")[l, ffc * 512:(ffc + 1) * 512, dh * 512:(dh + 1) * 512].rearrange("(f p) c -> p f c", p=128),
                              ch_w2[i], w=[d_w2c[i]])
                        for f in range(4):
                            fft = ffc * 4 + f
                            for tl in range(ntl):
                                pi, hb = tl // 2, tl % 2
                                sbk = (tl * 128) // n2
                                k.pe(lambda e, i=i, f=f, fft=fft, tl=tl, pi=pi, hb=hb: e.matmul(
                                    self.ps[pi][:, hb * 512:(hb + 1) * 512], hT[:, fft, tl * 128:(tl + 1) * 128],
                                    w2c[i][:, f, :], start=(fft == 0), stop=(fft == 31)),
                                    r=[d_w2c[i], d_hT[fft][sbk]], w=[self.d_ps[pi][hb]])
                    for tl in range(ntl):
                        pi, hb = tl // 2, tl % 2
                        k.act(lambda e, tl=tl, pi=pi, hb=hb, dh=dh: e.activation(
                            out=ytok[:, tl, dh * 512:(dh + 1) * 512], in_=self.ps[pi][:, hb * 512:(hb + 1) * 512], func=AF.Copy),
                            r=[self.d_ps[pi][hb]], w=[d_ytok[tl]])
                for tl in range(ntl):
                    tt = t0 // 128 + tl
                    s = 0 if tt < 16 else 1
                    xi = self._load_x(l, tt, first=False)
                    self._resid_update(xi, ytok[:, tl, :], [d_ytok[tl]], 1, s)
                    if last:
                        self._store_x(xi, self.out[tt * 128:(tt + 1) * 128, :], self.d_out[tt])
                    else:
                        self._store_x(xi, self.xres[tt * 128:(tt + 1) * 128, :], self.d_xres[tt])

    def _dump_feat(self, name, buf, deps):
        k = self.k
        o = self.dout(name, [D, T])
        self.arena_reset()
        if True:
            stg = self.aa([128, 8, 128])
            dst = Dep()
            ch = k.chan()
            for tt in range(NT):
                k.dve(lambda e, tt=tt: e.tensor_copy(out=stg[:], in_=buf[:, :, tt * 128:(tt + 1) * 128]), r=[deps[tt]], w=[dst])
                k.dma("sp", o[:, tt * 128:(tt + 1) * 128].rearrange("(a p) t -> p a t", p=128), stg[:], ch, r=[dst])

    def _finish(self):
        k = self.k
        if "xres" in self.dbg:
            self.arena_reset()
            self._staging()
            o = self.dout("dbg_xres", [T, D])
            ch = k.chan()
            for tt in range(NT):
                xi = self._load_x(1, tt, first=False)
                self._store_x(xi, o[tt * 128:(tt + 1) * 128, :], Dep())
        k.barrier()


def host_inputs(mk, inputs, extra=None):
    bf = ml_dtypes.bfloat16
    f32 = np.float32
    shared = {}
    shared["ada_w"] = np.ascontiguousarray(inputs["ada_w"], dtype=f32)
    shared["ada_bT"] = np.ascontiguousarray(
        inputs["ada_b"].reshape(DEPTH, 48, 128).transpose(2, 0, 1).reshape(128, DEPTH * 48), dtype=f32)
    g = np.stack([inputs["norm_pre_mix"], inputs["norm_post_mix"], inputs["norm_pre_mlp"], inputs["norm_post_mlp"]])
    shared["gains"] = np.ascontiguousarray(g.reshape(4, DEPTH, 8, 128).transpose(3, 0, 1, 2).reshape(128, -1), dtype=f32)
    for n in ("w_in", "w_out", "mlp_w1", "mlp_w2"):
        shared[n] = np.ascontiguousarray(inputs[n], dtype=f32)
    shared["ident_f"] = np.eye(128, dtype=f32)
    shared["ident_b"] = np.eye(128, dtype=f32).astype(bf)
    maps = []
    for b in range(inputs["x"].shape[0]):
        m = dict(shared)
        m["x"] = np.ascontiguousarray(inputs["x"][b], dtype=f32)
        m["ctx"] = np.ascontiguousarray(inputs["ctx"][b], dtype=f32)
        cv = np.stack([inputs["c"][b].reshape(8, 128), inputs["c_ctx"].reshape(8, 128)], axis=-1)
        m["cvec"] = np.ascontiguousarray(cv.transpose(1, 0, 2).reshape(128, 16), dtype=f32)
        if extra:
            m.update(extra(b))
        maps.append({kk: v for kk, v in m.items() if kk in mk.inputs})
    return maps


PP_ENTRIES = [("scw", 6), ("hyw", 18), ("gdw", 18), ("hybias", 2), ("gnorm", 1), ("hy_w1", 64), ("hy_w2", 64),
              ("hy_w3", 64), ("hy_w4", 512), ("hy_b", 3), ("hy_f", 3), ("alog", 8), ("dtb", 8)]
PP_OFF = {}
_o = 0
for _n, _w in PP_ENTRIES:
    PP_OFF[_n] = (_o, _w)
    _o += _w
PP_W = _o


def pp_host(inputs):
    pp = np.zeros((128, DEPTH * PP_W), np.float32)
    for l in range(DEPTH):
        def put(name, arr):
            o, w = PP_OFF[name]
            arr = np.asarray(arr, np.float32)
            assert arr.shape[1] == w, (name, arr.shape)
            pp[:arr.shape[0], l * PP_W + o: l * PP_W + o + w] = arr
        put("scw", inputs["sc_conv"][l].reshape(3, 2, 128).transpose(2, 1, 0).reshape(128, 6))
        put("hyw", inputs["hy_conv"][l].reshape(3, 6, 128).transpose(2, 1, 0).reshape(128, 18))
        put("gdw", inputs["gdn_conv"][l].reshape(3, 6, 128).transpose(2, 1, 0).reshape(128, 18))
        put("hybias", inputs["hy_bias"][l].reshape(2, 128).T)
        put("gnorm", np.tile(inputs["gdn_norm"][l], 2).reshape(128, 1))
        put("hy_w1", inputs["hy_w1"][l])
        put("hy_w2", inputs["hy_w2"][l])
        put("hy_w3", inputs["hy_w3"][l])
        put("hy_w4", inputs["hy_w4"][l])
        put("hy_b", np.stack([inputs["hy_b1"][l], inputs["hy_b2"][l], inputs["hy_b3"][l]], axis=1))
        put("hy_f", inputs["hy_freq"][l].T)
        put("alog", np.tile(inputs["gdn_a_log"][l].reshape(1, 8), (128, 1)))
        put("dtb", np.tile(inputs["gdn_dt_bias"][l].reshape(1, 8), (128, 1)))
    return pp


def na_consts(inputs):
    rpb = np.asarray(inputs["na_rpb"], np.float32)
    par = np.arange(2)[:, None, None, None]
    kc = np.arange(64)[None, :, None, None]
    i = np.arange(14)[None, None, :, None]
    qc = np.arange(64)[None, None, None, :]
    dc = np.clip(kc - qc, -15, 15) + 15
    di = np.broadcast_to(i + par, (2, 64, 14, 64))
    dcb = np.broadcast_to(dc, (2, 64, 14, 64))
    g = rpb[:, :, di, dcb]
    g = g.transpose(0, 2, 3, 1, 4, 5).reshape(DEPTH, 128, 4 * 14 * 64)
    cs = np.clip(np.arange(64) - 8, 0, 48)
    kcv = np.arange(64)[:, None]
    valid = (kcv >= cs[None, :]) & (kcv < cs[None, :] + 16)
    m = np.where(valid, 0.0, -1e30).astype(np.float32)
    mask = np.concatenate([m, m], axis=0)
    return np.ascontiguousarray(g), np.ascontiguousarray(mask)


def _mix_common_init(self):
    if getattr(self, "pp", None) is not None:
        return
    k = self.k
    self.pp_in = self.din("pp", [128, DEPTH * PP_W])
    self.pp = self.sb(self.st, "pp", [128, DEPTH * PP_W])
    self.d_pp = Dep()
    ch = k.chan()
    k.dma("sp", self.pp[:], self.pp_in[:, :], ch, w=[self.d_pp])


def _ppv(self, l, name, rows=128):
    o, w = PP_OFF[name]
    return self.pp[0:rows, l * PP_W + o: l * PP_W + o + w]


def _blocks(self, with_ctx):
    b = [(i * 512, 512) for i in range(4)]
    if with_ctx:
        b.append((L, LC))
    return b


def _wring_init(self, n=2):
    self.wring = [(self.aa([128, 8, 512], BF16), Dep(), self.wring_ch[i]) for i in range(n)]
    self.wring_i = 0
    self.bank_i = 0


def _proj(self, l, chunks, blocks, evac, banks):
    k = self.k
    for (c0, ncol) in chunks:
        i = self.wring_i
        self.wring_i = (i + 1) % len(self.wring)
        wap, dw, chw = self.wring[i]
        k.dma("pool", wap[:, :, 0:ncol], self.W("w_in")[l, :, c0:c0 + ncol].rearrange("(kt p) c -> p kt c", p=128),
              chw, w=[dw])
        for cc in range(0, ncol, 128):
            m = min(128, ncol - cc)
            for (t0, n) in blocks:
                pi, hb = banks[self.bank_i % len(banks)]
                self.bank_i += 1
                pst = self.ps[pi][0:m, hb * 512: hb * 512 + n]
                dps = self.d_ps[pi][hb]
                hd = [self.d_H[t] for t in range(t0 // 128, (t0 + n + 127) // 128)]
                for dt in range(8):
                    k.pe(lambda e, dt=dt, pst=pst, wap=wap, cc=cc, m=m, t0=t0, n=n: e.matmul(
                        pst, wap[:, dt, cc:cc + m], self.H[:, dt, t0:t0 + n], start=(dt == 0), stop=(dt == 7)),
                        r=[dw] + hd, w=[dps])
                evac(c0 + cc, m, t0, n, pst, dps)


def _dwconv(self, eng, out_ap, in_ap, w3, ranges, r, w):
    k = self.k
    for (a, b) in ranges:
        k.op(eng, lambda e, a=a, b=b: e.tensor_scalar(out=out_ap[:, a:b], in0=in_ap[:, a:b], scalar1=w3[:, 1:2],
                                                      scalar2=None, op0=ALU.mult), r=r, w=w)
        k.op(eng, lambda e, a=a, b=b: e.scalar_tensor_tensor(out=out_ap[:, a + 1:b], in0=in_ap[:, a:b - 1], scalar=w3[:, 0:1],
                                                             in1=out_ap[:, a + 1:b], op0=ALU.mult, op1=ALU.add),
             r=list(r) + list(w), w=w)
        k.op(eng, lambda e, a=a, b=b: e.scalar_tensor_tensor(out=out_ap[:, a:b - 1], in0=in_ap[:, a + 1:b], scalar=w3[:, 2:3],
                                                             in1=out_ap[:, a:b - 1], op0=ALU.mult, op1=ALU.add),
             r=list(r) + list(w), w=w)


PROJ_BANKS = [(0, 0), (0, 1), (1, 0), (1, 1)]


def _mix_sc(self, l, with_ctx):
    k = self.k
    self.arena_reset()
    _wring_init(self)
    ranges = [(0, L)] + ([(L, T)] if with_ctx else [])
    blocks = _blocks(self, with_ctx)
    ntok = T if with_ctx else L
    ntt = ntok // 128
    pxb = self.aa([128, 6, T], BF16)
    d_px = [Dep() for _ in range(6)]

    def evac(c, m, t0, n, pst, dps):
        ct = (c - OFF_SC) // 128
        k.act(lambda e: e.activation(out=pxb[:, ct, t0:t0 + n], in_=pst, func=AF.Copy), r=[dps], w=[d_px[ct]])
    _proj(self, l, [(OFF_SC, 512), (OFF_SC + 512, 256)], blocks, evac, PROJ_BANKS)
    z = [self.aa([128, T]) for _ in range(2)]
    acc = [self.aa([128, T]) for _ in range(2)]
    scw = _ppv(self, l, "scw")
    for j in range(2):
        dz, dacc = Dep(), Dep()
        eng = "dve" if j == 0 else "pool"
        k.op(eng, lambda e, j=j: e.tensor_tensor(out=z[j][:, 0:ntok], in0=pxb[:, 2 + j, 0:ntok], in1=pxb[:, 4 + j, 0:ntok],
                                                 op=ALU.mult), r=[d_px[2 + j], d_px[4 + j]], w=[dz])
        _dwconv(self, "dve", acc[j], z[j], scw[:, j * 3:(j + 1) * 3], ranges, [dz, self.d_pp], [dacc])
        k.op(eng, lambda e, j=j: e.tensor_tensor(out=self.C[:, 2 + j, 0:ntok], in0=pxb[:, j, 0:ntok], in1=acc[j][:, 0:ntok],
                                                 op=ALU.mult), r=[d_px[j], dacc], w=[self.d_C[t] for t in range(ntt)])


def _mixers(self, l):
    k = self.k
    if self.inject_cat:
        ch = k.chan()
        for tt in range(NT):
            k.dma("sp", self.C[:, :, tt * 128:(tt + 1) * 128],
                  self.cat_in[:, tt * 128:(tt + 1) * 128].rearrange("(a p) t -> p a t", p=128), ch, w=[self.d_C[tt]])
        return
    _mix_common_init(self)
    if not hasattr(self, "wring_ch"):
        self.wring_ch = [k.chan() for _ in range(3)]
    with_ctx = l < DEPTH - 1
    if "sc" in self.mixers:
        _mix_sc(self, l, with_ctx)
    if "na" in self.mixers:
        _mix_na(self, l, with_ctx)
    if "hy" in self.mixers:
        _mix_hy(self, l, with_ctx)
    if "gdn" in self.mixers:
        _mix_gdn(self, l, with_ctx)


MK._mixers = _mixers


def const_inputs(inputs):
    out = {}
    g, mask = na_consts(inputs)
    out["na_rpbg"] = g
    out["na_mask"] = mask
    out.update(hy_consts())
    out.update(gdn_consts())
    return out


def _mix_na(self, l, with_ctx):
    k = self.k
    self.arena_reset()
    _wring_init(self)
    blocks = _blocks(self, True)
    qT = self.aa([128, 2, T], BF16)
    kT = self.aa([128, 2, T], BF16)
    d_q = [Dep() for _ in range(NT)]
    d_k = [Dep() for _ in range(NT)]

    def evac(c, m, t0, n, pst, dps):
        ct = (c - OFF_NA) // 128
        tts = range(t0 // 128, (t0 + n) // 128)
        if ct < 2:
            k.act(lambda e: e.activation(out=qT[:, ct, t0:t0 + n], in_=pst, func=AF.Copy, scale=0.125),
                  r=[dps], w=[d_q[t] for t in tts])
        else:
            k.dve(lambda e: e.tensor_copy(out=kT[:, ct - 2, t0:t0 + n], in_=pst), r=[dps], w=[d_k[t] for t in tts])
    _proj(self, l, [(OFF_NA, 512)], blocks, evac, PROJ_BANKS)
    Ve = self.aa([128, NT, 4, 65], BF16)
    Vo = self.aa([128, 15, 4, 65], BF16)
    d_Ve, d_Vo = Dep(), Dep()
    k.pool(lambda e: e.memset(Ve, 1.0), w=[d_Ve])
    k.pool(lambda e: e.memset(Vo, 1.0), w=[d_Vo])
    i = self.wring_i
    self.wring_i = (i + 1) % len(self.wring)
    wv, dwv, chv = self.wring[i]
    k.dma("pool", wv[:, :, 0:256], self.W("w_in")[l, :, OFF_NA + 512:OFF_NA + 768].rearrange("(kt p) c -> p kt c", p=128),
          chv, w=[dwv])
    nb = 0
    for (Vx, dV, ntl, off) in ((Ve, d_Ve, NT, 0), (Vo, d_Vo, 15, 64)):
        for j in range(ntl):
            pi, hb = PROJ_BANKS[nb % 4]
            nb += 1
            pst = self.ps[pi][:, hb * 512: hb * 512 + 256]
            dps = self.d_ps[pi][hb]
            a = off + j * 128
            hd = [self.d_H[t] for t in range(a // 128, (a + 255) // 128)]
            for dt in range(8):
                k.pe(lambda e, dt=dt, pst=pst, a=a: e.matmul(pst, self.H[:, dt, a:a + 128], wv[:, dt, 0:256],
                                                             start=(dt == 0), stop=(dt == 7)), r=[dwv] + hd, w=[dps])
            k.act(lambda e, Vx=Vx, j=j, pst=pst: e.activation(out=Vx[:, j, :, 0:64], in_=pst.rearrange("p (h d) -> p h d", h=4),
                                                              func=AF.Copy), r=[dps], w=[dV])
    T2 = self.aa([128, 4, 14, 64])
    msk = self.aa([128, 64])
    d_T2 = Dep()
    d_msk = d_T2
    if not hasattr(self, "na_rpbg_in"):
        self.na_rpbg_in = self.din("na_rpbg", [DEPTH, 128, 4 * 14 * 64])
        self.na_mask_in = self.din("na_mask", [128, 64])
        self.ch_na = self.k.chan()
    k.dma("sp", T2.rearrange("p a b c -> p (a b c)"), self.na_rpbg_in[l], self.ch_na, w=[d_T2])
    k.dma("sp", msk, self.na_mask_in[:, :], self.ch_na, w=[d_msk])
    k.dve(lambda e: e.tensor_tensor(out=T2.rearrange("p a b c -> p (a b) c"), in0=T2.rearrange("p a b c -> p (a b) c"),
                                    in1=msk.unsqueeze(1).to_broadcast([128, 56, 64]), op=ALU.add),
          r=[d_T2, d_msk], w=[d_T2])
    Sb = [self.aa([128, 4, 64]) for _ in range(2)]
    d_Sb = [Dep(), Dep()]
    E = [self.aa([128, 6, 64], BF16) for _ in range(3)]
    d_E = [Dep() for _ in range(3)]
    rs = [self.aa([64, 4]) for _ in range(2)]
    d_rs = [Dep(), Dep()]
    On = [self.aa([64, 256], BF16) for _ in range(2)]
    d_On = [Dep(), Dep()]
    SB = [(2, 0), (2, 1), (3, 0), (3, 1)]
    OB = [(0, 0), (0, 1)]
    TB = (1, 0)
    psT = self.ps[TB[0]][:, TB[1] * 512: TB[1] * 512 + 512].bitcast(BF16)
    d_psT = self.d_ps[TB[0]][TB[1]]
    n = 0
    for r in range(32):
        s = min(max(r - 4, 0), 24)
        base = s - r + 7
        opi, ohb = OB[r % 2]
        O_ps = self.ps[opi][0:64, ohb * 512: ohb * 512 + 260]
        d_O = self.d_ps[opi][ohb]
        tq = (64 * r) // 128
        for h in range(4):
            hp = slice(64 * (h % 2), 64 * (h % 2) + 64)
            hc = h // 2
            spi, shb = SB[n % 4]
            S_ps = self.ps[spi][:, shb * 512: shb * 512 + 384]
            d_S = self.d_ps[spi][shb]
            for kt in range(6):
                ks = 64 * s + 128 * kt if kt < 4 else L + 128 * (kt - 4)
                kd = [d_k[t] for t in range(ks // 128, (ks + 255) // 128)]
                k.pe(lambda e, kt=kt, ks=ks, S_ps=S_ps, hp=hp, hc=hc, r=r: e.matmul(
                    S_ps[:, kt * 64:(kt + 1) * 64], kT[hp, hc, ks:ks + 128], qT[hp, hc, 64 * r:64 * r + 64],
                    start=True, stop=True), r=kd + [d_q[tq]], w=[d_S])
            sb_i = n % 2
            e_i = n % 3
            k.dve(lambda e, sb_i=sb_i, S_ps=S_ps, h=h, base=base: e.tensor_tensor(
                out=Sb[sb_i], in0=S_ps[:, 0:256].rearrange("p (a b) -> p a b", a=4),
                in1=T2[:, h, base:base + 7:2, :], op=ALU.add), r=[d_S, d_T2], w=[d_Sb[sb_i]])
            k.act(lambda e, sb_i=sb_i, e_i=e_i: e.activation(out=E[e_i][:, 0:4, :], in_=Sb[sb_i], func=AF.Exp),
                  r=[d_Sb[sb_i]], w=[d_E[e_i]])
            k.act(lambda e, e_i=e_i, S_ps=S_ps: e.activation(out=E[e_i][:, 4:6, :],
                                                             in_=S_ps[:, 256:384].rearrange("p (a b) -> p a b", a=2),
                                                             func=AF.Exp), r=[d_S], w=[d_E[e_i]])
            for kt in range(6):
                if kt < 4:
                    if s % 2 == 0:
                        vt, dv = Ve[:, s // 2 + kt, h, :], d_Ve
                    else:
                        vt, dv = Vo[:, (s - 1) // 2 + kt, h, :], d_Vo
                else:
                    vt, dv = Ve[:, 16 + kt - 4, h, :], d_Ve
                k.pe(lambda e, kt=kt, vt=vt, e_i=e_i, O_ps=O_ps, h=h: e.matmul(
                    O_ps[:, h * 65:(h + 1) * 65], E[e_i][:, kt, :], vt, start=(kt == 0), stop=(kt == 5)),
                    r=[d_E[e_i], dv], w=[d_O])
            n += 1
        j = r % 2
        O3 = O_ps.rearrange("p (h d) -> p h d", h=4)
        k.dve(lambda e, j=j, O3=O3: e.reciprocal(out=rs[j], in_=O3[:, :, 64]), r=[d_O], w=[d_rs[j]])
        k.dve(lambda e, j=j, O3=O3: e.tensor_tensor(out=On[j].rearrange("p (h d) -> p h d", h=4), in0=O3[:, :, 0:64],
                                                    in1=rs[j].unsqueeze(2).to_broadcast([64, 4, 64]), op=ALU.mult),
              r=[d_O, d_rs[j]], w=[d_On[j]])
        for hc in range(2):
            k.pe(lambda e, j=j, hc=hc: e.transpose(out=psT[:, hc * 64:(hc + 1) * 64], in_=On[j][:, hc * 128:(hc + 1) * 128],
                                                   identity=self.ident_b[0:64, 0:64]), r=[d_On[j], self.d_const], w=[d_psT])
        k.act(lambda e, r=r: e.activation(out=self.C[:, 6:8, 64 * r:64 * r + 64],
                                          in_=psT[:, 0:128].rearrange("p (a b) -> p a b", a=2), func=AF.Copy),
              r=[d_psT], w=[self.d_C[tq]])
    if with_ctx:
        Ec = self.aa([128, 2, 256], BF16)
        d_Ec = Dep()
        Onc = self.aa([128, 256], BF16)
        d_Onc = Dep()
        rsc = self.aa([128, 4])
        d_rsc = Dep()
        for qt in range(2):
            opi, ohb = OB[qt % 2]
            O_ps = self.ps[opi][:, ohb * 512: ohb * 512 + 260]
            d_O = self.d_ps[opi][ohb]
            for h in range(4):
                hp = slice(64 * (h % 2), 64 * (h % 2) + 64)
                hc = h // 2
                spi, shb = SB[n % 4]
                n += 1
                S_ps = self.ps[spi][:, shb * 512: shb * 512 + 256]
                d_S = self.d_ps[spi][shb]
                for c in range(2):
                    k.pe(lambda e, c=c, S_ps=S_ps, hp=hp, hc=hc, qt=qt: e.matmul(
                        S_ps[:, c * 128:(c + 1) * 128], kT[hp, hc, L + 128 * c:L + 128 * c + 128],
                        qT[hp, hc, L + 128 * qt:L + 128 * qt + 128], start=True, stop=True),
                        r=[d_k[16 + c], d_q[16 + qt]], w=[d_S])
                k.act(lambda e, S_ps=S_ps: e.activation(out=Ec[:, :, 0:128], in_=S_ps.rearrange("p (a b) -> p a b", a=2),
                                                        func=AF.Exp), r=[d_S], w=[d_Ec])
                for c in range(2):
                    k.pe(lambda e, c=c, O_ps=O_ps, h=h: e.matmul(O_ps[:, h * 65:(h + 1) * 65], Ec[:, c, 0:128],
                                                                 Ve[:, 16 + c, h, :], start=(c == 0), stop=(c == 1)),
                         r=[d_Ec, d_Ve], w=[d_O])
            O3 = O_ps.rearrange("p (h d) -> p h d", h=4)
            k.dve(lambda e, O3=O3: e.reciprocal(out=rsc, in_=O3[:, :, 64]), r=[d_O], w=[d_rsc])
            k.dve(lambda e, O3=O3: e.tensor_tensor(out=Onc.rearrange("p (h d) -> p h d", h=4), in0=O3[:, :, 0:64],
                                                   in1=rsc.unsqueeze(2).to_broadcast([128, 4, 64]), op=ALU.mult),
                  r=[d_O, d_rsc], w=[d_Onc])
            for hc in range(2):
                k.pe(lambda e, hc=hc: e.transpose(out=psT[:, hc * 128:(hc + 1) * 128], in_=Onc[:, hc * 128:(hc + 1) * 128],
                                                  identity=self.ident_b[:]), r=[d_Onc, self.d_const], w=[d_psT])
            k.act(lambda e, qt=qt: e.activation(out=self.C[:, 6:8, L + 128 * qt:L + 128 * qt + 128],
                                                in_=psT[:, 0:256].rearrange("p (a b) -> p a b", a=2), func=AF.Copy),
                  r=[d_psT], w=[self.d_C[16 + qt]])


import math
HY_EMB = 33
HY_BANDS = 16


def hy_consts():
    bf = ml_dtypes.bfloat16
    out = {}
    max_decay = math.log(1e-2) / 0.3
    min_decay = math.log(1e-2) / 1.5
    deltas = np.abs(np.linspace(min_decay, max_decay, 256, dtype=np.float32))
    for tag, Ls in (("lat", L), ("ctx", LC)):
        nt = Ls // 128
        t = np.linspace(0.0, 1.0, Ls, dtype=np.float32)[:, None]
        bands = np.linspace(1e-4, HY_BANDS - 1, HY_BANDS, dtype=np.float32)
        ang = (np.float32(2.0 * math.pi / Ls)) * np.arange(Ls, dtype=np.float32)[:, None] * bands
        z = np.concatenate([t, np.cos(ang), -np.sin(ang)], axis=-1).astype(np.float32)
        out["hy_zT_" + tag] = np.ascontiguousarray(z.T)
        dec = np.exp(-t * deltas[None, :]).astype(np.float32)
        out["hy_dec_" + tag] = np.ascontiguousarray(dec.reshape(nt, 128, 256).transpose(1, 0, 2).reshape(128, nt * 256))
        N = 2 * Ls
        tt_ = np.arange(Ls, dtype=np.int64)
        ff = np.arange(Ls, dtype=np.int64)
        m = ((2 * ff[None, :] + 1) * tt_[:, None]) % (2 * N)
        th = m.astype(np.float64) * (math.pi / N)
        Cm = np.cos(th)
        Sm = -np.sin(th)
        for nm, M_ in (("C", Cm), ("S", Sm)):
            f4 = M_.reshape(nt, 128, nt, 128).transpose(2, 1, 0, 3).reshape(nt, 128, nt * 128)
            out["hy_%sf_%s" % (nm, tag)] = np.ascontiguousarray(f4).astype(bf)
            Wd = min(512, Ls)
            ntb = Ls // Wd
            i4 = M_.reshape(ntb, Wd, nt, 128).transpose(0, 3, 2, 1).reshape(ntb, 128, nt * Wd)
            out["hy_%si_%s" % (nm, tag)] = np.ascontiguousarray(i4).astype(bf)
    return out


def _arena_rewind(self, mark):
    self.k.barrier()
    self.ar_off = mark


def _hy_filters(self, l, Ls, tag, WHx, d_WHx):
    k = self.k
    nt = Ls // 128
    BW = min(512, Ls)
    nb = Ls // BW
    zin = self.din("hy_zT_" + tag, [HY_EMB, Ls]) if ("hy_zT_" + tag) not in self.inputs else self._hyin["hy_zT_" + tag]
    din_dec = self.din("hy_dec_" + tag, [128, nt * 256]) if ("hy_dec_" + tag) not in self.inputs else self._hyin["hy_dec_" + tag]
    self._hyin["hy_zT_" + tag] = zin
    self._hyin["hy_dec_" + tag] = din_dec
    zT = self.aa([HY_EMB, Ls])
    dec = self.aa([128, nt, 256])
    d_z = Dep()
    d_dec = d_z
    k.dma("sp", zT, zin[:, :], self.ch_hy, w=[d_z])
    k.dma("sp", dec.rearrange("p a b -> p (a b)"), din_dec[:, :], self.ch_hy, w=[d_dec])
    hb = [self.aa([64, Ls]) for _ in range(2)]
    d_hb = [Dep(), Dep()]
    v = self.aa([64, 512])
    ki = self.aa([64, 512], I32)
    kf = self.aa([64, 512])
    d_v, d_ki, d_kf = Dep(), Dep(), Dep()
    fb = self.aa([64, 3])
    d_fb = Dep()
    fr = _ppv(self, l, "hy_f", 64)
    bb = _ppv(self, l, "hy_b", 64)
    k.dve(lambda e: e.tensor_tensor(out=fb, in0=fr, in1=bb, op=ALU.mult), r=[self.d_pp], w=[d_fb])
    ws = [_ppv(self, l, "hy_w1", HY_EMB), _ppv(self, l, "hy_w2", 64), _ppv(self, l, "hy_w3", 64)]
    PB = [(2, 0), (2, 1)]
    nps = 0
    src, d_src = zT, d_z
    for li in range(3):
        dst, d_dst = hb[li % 2], d_hb[li % 2]
        for b in range(nb):
            pi, hbk = PB[nps % 2]
            nps += 1
            pst = self.ps[pi][0:64, hbk * 512: hbk * 512 + BW]
            dps = self.d_ps[pi][hbk]
            k.pe(lambda e, li=li, b=b, pst=pst, src=src: e.matmul(pst, ws[li], src[:, b * BW:(b + 1) * BW], start=True, stop=True),
                 r=[self.d_pp, d_src], w=[dps])
            k.dve(lambda e, li=li, pst=pst: e.tensor_scalar(out=v[:, 0:BW], in0=pst, scalar1=fr[:, li:li + 1], scalar2=fb[:, li:li + 1],
                                                            op0=ALU.mult, op1=ALU.add), r=[dps, d_fb, self.d_pp], w=[d_v])
            k.dve(lambda e: e.tensor_scalar(out=ki[:, 0:BW], in0=v[:, 0:BW], scalar1=1.0 / (2.0 * math.pi), scalar2=None, op0=ALU.mult),
                  r=[d_v], w=[d_ki])
            k.dve(lambda e: e.tensor_copy(out=kf[:, 0:BW], in_=ki[:, 0:BW]), r=[d_ki], w=[d_kf])
            k.dve(lambda e: e.scalar_tensor_tensor(out=v[:, 0:BW], in0=kf[:, 0:BW], scalar=-2.0 * math.pi, in1=v[:, 0:BW],
                                                   op0=ALU.mult, op1=ALU.add), r=[d_kf, d_v], w=[d_v])
            k.dve(lambda e: e.tensor_scalar(out=v[:, 0:BW], in0=v[:, 0:BW], scalar1=3.1415925, scalar2=-3.1415925,
                                            op0=ALU.min, op1=ALU.max), r=[d_v], w=[d_v])
            k.act(lambda e, dst=dst, b=b: e.activation(out=dst[:, b * BW:(b + 1) * BW], in_=v[:, 0:BW], func=AF.Sin),
                  r=[d_v], w=[d_dst])
        src, d_src = dst, d_dst
    h3, d_h3 = src, d_src
    w4 = _ppv(self, l, "hy_w4", 64)
    hd = [self.aa([128, 2, 256]) for _ in range(2)]
    d_hd = [Dep(), Dep()]
    ab = [self.aa([128, 512]) for _ in range(2)]
    d_ab = [Dep(), Dep()]
    nrm_ps = self.ps[3][:, 0:512]
    d_nrm = self.d_ps[3][0]
    rn = self.aa([128, 256])
    d_rn = Dep()
    for pss in range(2):
        for tt in range(nt):
            pi, hbk = PB[nps % 2]
            nps += 1
            pst = self.ps[pi][:, hbk * 512: hbk * 512 + 512]
            dps = self.d_ps[pi][hbk]
            k.pe(lambda e, tt=tt, pst=pst: e.matmul(pst, h3[:, tt * 128:(tt + 1) * 128], w4, start=True, stop=True),
                 r=[d_h3, self.d_pp], w=[dps])
            j = tt % 2
            k.dve(lambda e, j=j, tt=tt, pst=pst: e.tensor_tensor(out=hd[j], in0=pst.rearrange("p (a b) -> p a b", a=2),
                                                                 in1=dec[:, tt, :].unsqueeze(1).to_broadcast([128, 2, 256]),
                                                                 op=ALU.mult), r=[dps, d_dec], w=[d_hd[j]])
            if pss == 0:
                k.act(lambda e, j=j: e.activation(out=ab[j], in_=hd[j].rearrange("p a b -> p (a b)"), func=AF.Abs),
                      r=[d_hd[j]], w=[d_ab[j]])
                k.pe(lambda e, j=j, tt=tt: e.matmul(nrm_ps, self.ones_f[:], ab[j], start=(tt == 0), stop=(tt == nt - 1)),
                     r=[d_ab[j], self.d_const], w=[d_nrm])
            else:
                if tt == 0:
                    k.dve(lambda e, j=j: e.memset(hd[j][0:1, 1, :], 0.0), r=[d_hd[j]], w=[d_hd[j]])
                k.dve(lambda e, j=j: e.tensor_tensor(out=hd[j], in0=hd[j], in1=rn.unsqueeze(1).to_broadcast([128, 2, 256]),
                                                     op=ALU.mult), r=[d_hd[j], d_rn], w=[d_hd[j]])
                k.dve(lambda e, j=j, tt=tt: e.tensor_tensor(out=WHx[:, tt, 1, :], in0=hd[j][:, 0, :], in1=hd[j][:, 1, :], op=ALU.add),
                      r=[d_hd[j]], w=[d_WHx])
                k.pool(lambda e, j=j, tt=tt: e.tensor_tensor(out=WHx[:, tt, 2, :], in0=hd[j][:, 0, :], in1=hd[j][:, 1, :],
                                                             op=ALU.subtract), r=[d_hd[j]], w=[d_WHx])
        if pss == 0:
            k.dve(lambda e: e.tensor_copy(out=rn, in_=nrm_ps[:, 0:256]), r=[d_nrm], w=[d_rn])
            k.dve(lambda e: e.tensor_tensor(out=rn, in0=rn, in1=nrm_ps[:, 256:512], op=ALU.add), r=[d_nrm, d_rn], w=[d_rn])
            k.dve(lambda e: e.reciprocal(out=rn, in_=rn), r=[d_rn], w=[d_rn])
            k.dve(lambda e: e.tensor_scalar(out=rn, in0=rn, scalar1=2.0 / (2 * Ls), scalar2=None, op0=ALU.mult), r=[d_rn], w=[d_rn])


def _hy_dft(self, l, Ls, tag, toff, WHx, d_WHx, x0T, wTm, d_x0w, Yh, ring):
    k = self.k
    nt = Ls // 128
    Wd = min(512, Ls)
    ntb = Ls // Wd
    names = ["hy_Cf_", "hy_Sf_", "hy_Ci_", "hy_Si_"]
    tabs = []
    for nm in names:
        key = nm + tag
        if key not in self._hyin:
            shp = [nt, 128, nt * 128] if nm[4] == "f" else [ntb, 128, nt * Wd]
            self._hyin[key] = self.din(key, shp, BF16)
        tabs.append(self._hyin[key])
    Cf, Sf, Ci, Si = tabs
    d_Yh = Dep()
    Kr = self.aa([128, 4, 256])
    d_K = [Dep(), Dep()]
    tm = [self.aa([128, 256]) for _ in range(4)]
    d_tm = [Dep() for _ in range(4)]
    hybias = _ppv(self, l, "hybias")
    for j in range(nt):
        slot, dsl, chs = ring[self.hyring_i % len(ring)]
        self.hyring_i += 1
        cst = slot[:, 0:nt * 128].rearrange("p (a b) -> p a b", b=128)
        sst = slot[:, 2048:2048 + nt * 128].rearrange("p (a b) -> p a b", b=128)
        k.dma("sp", slot[:, 0:nt * 128], Cf[j], chs, w=[dsl])
        k.dma("sp", slot[:, 2048:2048 + nt * 128], Sf[j], chs, w=[dsl])
        ps_r = self.ps[0][:, 0:512]
        ps_i = self.ps[0][:, 512:1024]
        for tt in range(nt):
            k.pe(lambda e, tt=tt, cst=cst: e.matmul(ps_r, cst[:, tt, :], WHx[:, tt, 0:2, :], start=(tt == 0), stop=(tt == nt - 1)),
                 r=[dsl, d_WHx], w=[self.d_ps[0][0]])
        for tt in range(nt):
            k.pe(lambda e, tt=tt, sst=sst: e.matmul(ps_i, sst[:, tt, :], WHx[:, tt, 0:3:2, :], start=(tt == 0), stop=(tt == nt - 1)),
                 r=[dsl, d_WHx], w=[self.d_ps[0][1]])
        k.act(lambda e: e.activation(out=Kr[:, 0:2, :], in_=ps_r.rearrange("p (a b) -> p a b", a=2), func=AF.Copy),
              r=[self.d_ps[0][0]], w=[d_K[0]])
        k.act(lambda e: e.activation(out=Kr[:, 2:4, :], in_=ps_i.rearrange("p (a b) -> p a b", a=2), func=AF.Copy),
              r=[self.d_ps[0][1]], w=[d_K[1]])
        Ur, Kre, Ui, Kie = Kr[:, 0, :], Kr[:, 1, :], Kr[:, 2, :], Kr[:, 3, :]
        k.dve(lambda e: e.tensor_tensor(out=tm[0], in0=Ur, in1=Kre, op=ALU.mult), r=[d_K[0]], w=[d_tm[0]])
        k.pool(lambda e: e.tensor_tensor(out=tm[1], in0=Ui, in1=Kie, op=ALU.mult), r=[d_K[1]], w=[d_tm[1]])
        k.dve(lambda e, j=j: e.tensor_tensor(out=Yh[:, j, 0, :], in0=tm[0], in1=tm[1], op=ALU.subtract),
              r=[d_tm[0], d_tm[1]], w=[d_Yh])
        k.pool(lambda e: e.tensor_tensor(out=tm[2], in0=Ur, in1=Kie, op=ALU.mult), r=d_K, w=[d_tm[2]])
        k.dve(lambda e: e.tensor_tensor(out=tm[3], in0=Ui, in1=Kre, op=ALU.mult), r=d_K, w=[d_tm[3]])
        k.pool(lambda e, j=j: e.tensor_tensor(out=Yh[:, j, 1, :], in0=tm[2], in1=tm[3], op=ALU.add),
               r=[d_tm[2], d_tm[3]], w=[d_Yh])
    G = 4 if nt >= 4 else nt
    yt = [self.aa([128, 512]) for _ in range(2)]
    d_yt = [Dep(), Dep()]
    for tb in range(ntb):
        for g in range(nt // G):
            slot, dsl, chs = ring[self.hyring_i % len(ring)]
            self.hyring_i += 1
            k.dma("sp", slot[:, 0:G * Wd], Ci[tb, :, g * G * Wd:(g + 1) * G * Wd], chs, w=[dsl])
            k.dma("sp", slot[:, 2048:2048 + G * Wd], Si[tb, :, g * G * Wd:(g + 1) * G * Wd], chs, w=[dsl])
            cst = slot[:, 0:G * Wd].rearrange("p (a b) -> p a b", b=Wd)
            sst = slot[:, 2048:2048 + G * Wd].rearrange("p (a b) -> p a b", b=Wd)
            for fi in range(G):
                ft = g * G + fi
                for ct in range(2):
                    py = self.ps[1][:, ct * 512: ct * 512 + Wd]
                    k.pe(lambda e, fi=fi, ft=ft, ct=ct, py=py, cst=cst: e.matmul(py, Yh[:, ft, 0, ct * 128:(ct + 1) * 128], cst[:, fi, :],
                                                                               start=(ft == 0), stop=False),
                         r=[dsl, d_Yh], w=[self.d_ps[1][ct]])
                    k.pe(lambda e, fi=fi, ft=ft, ct=ct, py=py, sst=sst: e.matmul(py, Yh[:, ft, 1, ct * 128:(ct + 1) * 128], sst[:, fi, :],
                                                                               start=False, stop=(ft == nt - 1)),
                         r=[dsl, d_Yh], w=[self.d_ps[1][ct]])
        a = toff + tb * Wd
        tts = [self.d_C[t] for t in range(a // 128, (a + Wd) // 128)]
        for ct in range(2):
            py = self.ps[1][:, ct * 512: ct * 512 + Wd]
            k.dve(lambda e, ct=ct, py=py, a=a: e.scalar_tensor_tensor(out=yt[ct][:, 0:Wd], in0=wTm[:, ct, a:a + Wd], scalar=hybias[:, ct:ct + 1],
                                                                      in1=py, op0=ALU.mult, op1=ALU.add),
                  r=[self.d_ps[1][ct], d_x0w, self.d_pp], w=[d_yt[ct]])
            k.pool(lambda e, ct=ct, a=a: e.tensor_tensor(out=self.C[:, 4 + ct, a:a + Wd], in0=yt[ct][:, 0:Wd], in1=x0T[:, ct, a:a + Wd],
                                                         op=ALU.mult), r=[d_yt[ct], d_x0w], w=tts)


def _mix_hy(self, l, with_ctx):
    k = self.k
    self.arena_reset()
    if not hasattr(self, "_hyin"):
        self._hyin = {}
        self.ch_hy = k.chan()
        self.ch_hyring = [k.chan() for _ in range(3)]
    ntok = T if with_ctx else L
    WH = self.aa([128, 16, 3, 256], BF16)
    d_WH = Dep()
    if with_ctx:
        WHc = self.aa([128, 2, 3, 256], BF16)
        d_WHc = Dep()
    x0T = self.aa([128, 2, T], BF16)
    wTm = self.aa([128, 2, T], BF16)
    d_x0w = Dep()
    markA = self.ar_off
    _hy_filters(self, l, L, "lat", WH, d_WH)
    if with_ctx:
        _arena_rewind(self, markA)
        _hy_filters(self, l, LC, "ctx", WHc, d_WHc)
    _arena_rewind(self, markA)
    _wring_init(self)
    ranges = [(0, L)] + ([(L, T)] if with_ctx else [])
    blocks = _blocks(self, with_ctx)
    pin = [self.aa([128, T]) for _ in range(1)]
    d_pin = [Dep()]
    x1v = self.aa([128, 4, T], BF16)
    d_x1v = [Dep() for _ in range(4)]
    cacc = [self.aa([128, T]) for _ in range(1)]
    d_cacc = Dep()
    hyw = _ppv(self, l, "hyw")

    def evac(c, m, t0, n, pst, dps):
        ct = (c - OFF_HY) // 128
        j = 0
        k.act(lambda e: e.activation(out=pin[j][:, t0:t0 + n], in_=pst, func=AF.Copy), r=[dps], w=[d_pin[j]])
        if (t0, n) == blocks[-1]:
            _dwconv(self, "dve", cacc[0], pin[j], hyw[:, ct * 3:(ct + 1) * 3], ranges, [d_pin[j], self.d_pp], [d_cacc])
            if ct < 2:
                k.pool(lambda e: e.tensor_copy(out=x0T[:, ct, 0:ntok], in_=cacc[0][:, 0:ntok]), r=[d_cacc], w=[d_x0w])
            else:
                k.pool(lambda e: e.tensor_copy(out=x1v[:, ct - 2, 0:ntok], in_=cacc[0][:, 0:ntok]), r=[d_cacc], w=[d_x1v[ct - 2]])
    _proj(self, l, [(OFF_HY, 512), (OFF_HY + 512, 256)], blocks, evac, PROJ_BANKS)
    for ct in range(2):
        k.pool(lambda e, ct=ct: e.tensor_tensor(out=wTm[:, ct, 0:ntok], in0=x1v[:, ct, 0:ntok], in1=x1v[:, 2 + ct, 0:ntok], op=ALU.mult),
               r=[d_x1v[ct], d_x1v[2 + ct]], w=[d_x0w])
    psT = self.ps[2][:, 0:512].bitcast(BF16)
    d_psT = self.d_ps[2][0]
    for tt in range(ntok // 128):
        for ct in range(2):
            k.pe(lambda e, tt=tt, ct=ct: e.transpose(out=psT[:, ct * 128:(ct + 1) * 128], in_=wTm[:, ct, tt * 128:(tt + 1) * 128],
                                                     identity=self.ident_b[:]), r=[d_x0w, self.d_const], w=[d_psT])
        if tt < 16:
            k.act(lambda e, tt=tt: e.activation(out=WH[:, tt, 0, :], in_=psT[:, 0:256], func=AF.Copy), r=[d_psT], w=[d_WH])
        else:
            k.act(lambda e, tt=tt: e.activation(out=WHc[:, tt - 16, 0, :], in_=psT[:, 0:256], func=AF.Copy), r=[d_psT], w=[d_WHc])
    _arena_rewind(self, markA)
    Yh = self.aa([128, 16, 2, 256], BF16)
    ring = [(self.aa([128, 4096], BF16), Dep(), self.ch_hyring[i]) for i in range(3)]
    self.hyring_i = 0
    markD = self.ar_off
    _hy_dft(self, l, L, "lat", 0, WH, d_WH, x0T, wTm, d_x0w, Yh, ring)
    if with_ctx:
        _arena_rewind(self, markD)
        _hy_dft(self, l, LC, "ctx", L, WHc, d_WHc, x0T, wTm, d_x0w, Yh, ring)


def gdn_consts():
    out = {}
    m = np.arange(128)[:, None]
    i = np.arange(128)[None, :]
    LE = (m <= i).astype(np.float32)
    GE = (m >= i).astype(np.float32)
    GT = (m > i).astype(np.float32)
    LT = (m < i).astype(np.float32)
    NEGF = np.tile(np.where(i > m, -1e9, 0.0).astype(np.float32), (1, 4))
    NEGB = np.tile(np.where(i < m, -1e9, 0.0).astype(np.float32), (1, 4))
    out["gdn_gm"] = np.ascontiguousarray(np.concatenate([LE, GE, GT, LT, NEGF, NEGB], axis=1))
    pm = np.zeros((128, 128), np.float32)
    for mm in range(128):
        if (mm % 64) < 32:
            pm[mm + 32, mm] = -1.0
        else:
            pm[mm - 32, mm] = 1.0
    out["gdn_pm"] = pm
    ii = np.arange(128)[:, None]
    jj = np.arange(128)[None, :]
    lms, ums = [], []
    for s_ in range(7):
        b_ = 1 << s_
        same2 = (ii // (2 * b_)) == (jj // (2 * b_))
        diff1 = (ii // b_) != (jj // b_)
        lms.append(np.where(same2 & diff1 & (jj < ii), -1.0, 0.0))
        ums.append(np.where(same2 & diff1 & (jj > ii), -1.0, 0.0))
    out["gdn_lm"] = np.ascontiguousarray(np.concatenate(lms + ums, axis=1)).astype(ml_dtypes.bfloat16)
    pos = np.arange(L)
    row = (pos // 64).astype(np.float32)
    col = (pos % 64).astype(np.float32)
    inv = (10000.0 ** (-np.arange(16, dtype=np.float32) / 16)).astype(np.float32)
    ang = np.concatenate([row[:, None] * inv, col[:, None] * inv], axis=-1)
    idx = (np.arange(128) % 64) % 32
    cs = np.stack([np.cos(ang).T[idx], np.sin(ang).T[idx]], axis=1)
    out["gdn_rope"] = np.ascontiguousarray(cs.reshape(128, 2 * L)).astype(ml_dtypes.bfloat16)
    return out


def _mix_gdn(self, l, with_ctx):
    k = self.k
    self.arena_reset()
    if not hasattr(self, "gdn_gm_in"):
        self.gdn_gm_in = self.din("gdn_gm", [128, 1536])
        self.gdn_pm_in = self.din("gdn_pm", [128, 128])
        self.gdn_rope_in = self.din("gdn_rope", [128, 2 * L], BF16)
        self.ch_gdn = k.chan()
    otok = self.aa([128, NT, 256], BF16)
    markO = self.ar_off
    qT = self.aa([128, 2, T], BF16)
    kT = self.aa([128, 2, T], BF16)
    d_qk = Dep()
    vtok = self.aa([128, NT, 256], BF16)
    ktok = self.aa([128, NT, 256], BF16)
    d_vtok, d_ktok = Dep(), Dep()
    la = self.aa([128, NT, 8])
    beta = self.aa([128, NT, 8])
    d_la, d_beta = Dep(), Dep()
    GM = self.aa([128, 1536])
    d_GM = Dep()
    d_LMK = d_GM
    d_c1 = d_GM
    k.dma("sp", GM, self.gdn_gm_in[:, :], self.ch_gdn, w=[d_GM])
    LE, GE, GT, LT = (GM[:, i * 128:(i + 1) * 128] for i in range(4))
    NEGF, NEGB = GM[:, 512:1024], GM[:, 1024:1536]
    LMK = self.aa([128, 14 * 128], BF16)
    if not hasattr(self, "gdn_lm_in"):
        self.gdn_lm_in = self.din("gdn_lm", [128, 14 * 128], BF16)
    k.dma("sp", LMK, self.gdn_lm_in[:, :], self.ch_gdn, w=[d_LMK])
    markP = self.ar_off
    _wring_init(self, 1)
    pm = self.aa([128, 128])
    rope = self.aa([128, 2, L], BF16)
    blk64 = self.aa([128, 128])
    d_blk = Dep()
    k.dma("sp", pm, self.gdn_pm_in[:, :], self.ch_gdn, w=[d_c1])
    k.dma("sp", rope.rearrange("p a b -> p (a b)"), self.gdn_rope_in[:, :], self.ch_gdn, w=[d_c1])
    k.pool(lambda e: e.memset(blk64, 0.0), w=[d_blk])
    k.pool(lambda e: e.memset(blk64[0:64, 0:64], 1.0), w=[d_blk])
    k.pool(lambda e: e.memset(blk64[64:128, 64:128], 1.0), w=[d_blk])
    wab, dwab, chab = self.wring[0]
    k.dma("pool", wab[:, :, 0:16], self.W("w_in")[l, :, 1024:1040].rearrange("(kt p) c -> p kt c", p=128), chab, w=[dwab])
    ab_ps = self.ps[3][:, 0:NT * 16]
    d_ab = self.d_ps[3][0]
    for tt in range(NT):
        for dt in range(8):
            k.pe(lambda e, tt=tt, dt=dt: e.matmul(ab_ps[:, tt * 16:(tt + 1) * 16], self.H[:, dt, tt * 128:(tt + 1) * 128],
                                                  wab[:, dt, 0:16], start=(dt == 0), stop=(dt == 7)),
                 r=[dwab, self.d_H[tt]], w=[d_ab])
    ab3 = ab_ps.rearrange("p (t c) -> p t c", c=16)
    xa = self.aa([128, NT, 8])
    ea = self.aa([128, 8])
    d_xa, d_ea = Dep(), Dep()
    k.dve(lambda e: e.tensor_tensor(out=xa, in0=ab3[:, :, 0:8], in1=_ppv(self, l, "dtb").unsqueeze(1).to_broadcast([128, NT, 8]),
                                    op=ALU.add), r=[d_ab, self.d_pp], w=[d_xa])
    k.act(lambda e: e.activation(out=xa, in_=xa, func=AF.Exp), r=[d_xa], w=[d_xa])
    k.act(lambda e: e.activation(out=xa, in_=xa, func=AF.Ln, bias=1.0, scale=1.0), r=[d_xa], w=[d_xa])
    k.act(lambda e: e.activation(out=ea, in_=_ppv(self, l, "alog"), func=AF.Exp), r=[self.d_pp], w=[d_ea])
    k.dve(lambda e: e.scalar_tensor_tensor(out=la, in0=xa, scalar=-1.0, in1=ea.unsqueeze(1).to_broadcast([128, NT, 8]),
                                           op0=ALU.mult, op1=ALU.mult), r=[d_xa, d_ea], w=[d_la])
    k.act(lambda e: e.activation(out=beta, in_=ab3[:, :, 8:16], func=AF.Sigmoid), r=[d_ab], w=[d_beta])
    blocks = _blocks(self, True)
    ranges = [(0, L), (L, T)]
    pin = self.aa([128, T])
    cacc = self.aa([128, T])
    d_pin, d_cacc = Dep(), Dep()
    vTt = self.aa([128, T], BF16)
    d_vTt = Dep()
    rv = self.aa([128, 512])
    t1 = self.aa([128, 512])
    t2 = self.aa([128, 512])
    d_rv, d_t1, d_t2 = Dep(), Dep(), Dep()
    gdw = _ppv(self, l, "gdw")
    psT = self.ps[2][:, 0:512].bitcast(BF16)
    d_psT = self.d_ps[2][0]

    def finish_tile(ct):
        _dwconv(self, "dve", cacc, pin, gdw[:, ct * 3:(ct + 1) * 3], ranges, [d_pin, self.d_pp], [d_cacc])
        if ct >= 4:
            k.act(lambda e: e.activation(out=vTt, in_=cacc, func=AF.Silu), r=[d_cacc], w=[d_vTt])
            for tt in range(NT):
                k.pe(lambda e, tt=tt: e.transpose(out=psT[:, 0:128], in_=vTt[:, tt * 128:(tt + 1) * 128], identity=self.ident_b[:]),
                     r=[d_vTt, self.d_const], w=[d_psT])
                k.dve(lambda e, tt=tt: e.tensor_copy(out=vtok[:, tt, (ct - 4) * 128:(ct - 3) * 128], in_=psT[:, 0:128]),
                      r=[d_psT], w=[d_vtok])
            return
        isq = ct < 2
        dst = qT if isq else kT
        cc = ct % 2
        k.act(lambda e: e.activation(out=pin, in_=cacc, func=AF.Silu), r=[d_cacc, d_pin], w=[d_pin])
        k.act(lambda e: e.activation(out=cacc, in_=pin, func=AF.Square), r=[d_pin, d_cacc], w=[d_cacc])
        for (t0, n) in blocks:
            ss = self.ps[3][:, 512:512 + n]
            dss = self.d_ps[3][1]
            k.pe(lambda e, t0=t0, n=n, ss=ss: e.matmul(ss, blk64, cacc[:, t0:t0 + n], start=True, stop=True), r=[d_cacc, d_blk], w=[dss])
            sc_, bi_ = (64.0, 64.0 * EPS) if isq else (1.0, EPS)
            k.act(lambda e, n=n, ss=ss: e.activation(out=rv[:, 0:n], in_=ss, func=AF.Sqrt, bias=bi_, scale=sc_), r=[dss], w=[d_rv])
            k.dve(lambda e, n=n: e.reciprocal(out=rv[:, 0:n], in_=rv[:, 0:n]), r=[d_rv], w=[d_rv])
            k.dve(lambda e, t0=t0, n=n: e.tensor_tensor(out=pin[:, t0:t0 + n], in0=pin[:, t0:t0 + n], in1=rv[:, 0:n], op=ALU.mult),
                  r=[d_rv, d_pin], w=[d_pin])
            if t0 < L:
                pv = self.ps[2][:, 512:512 + n]
                dpv = self.d_ps[2][1]
                k.pe(lambda e, t0=t0, n=n, pv=pv: e.matmul(pv, pm, pin[:, t0:t0 + n], start=True, stop=True), r=[d_pin, d_c1], w=[dpv])
                k.dve(lambda e, t0=t0, n=n: e.tensor_tensor(out=t1[:, 0:n], in0=pin[:, t0:t0 + n], in1=rope[:, 0, t0:t0 + n], op=ALU.mult),
                      r=[d_pin, d_c1], w=[d_t1])
                k.dve(lambda e, t0=t0, n=n, pv=pv: e.tensor_tensor(out=t2[:, 0:n], in0=pv, in1=rope[:, 1, t0:t0 + n], op=ALU.mult),
                      r=[dpv, d_c1], w=[d_t2])
                k.pool(lambda e, t0=t0, n=n: e.tensor_tensor(out=dst[:, cc, t0:t0 + n], in0=t1[:, 0:n], in1=t2[:, 0:n], op=ALU.add),
                       r=[d_t1, d_t2], w=[d_qk])
            else:
                k.pool(lambda e, t0=t0, n=n: e.tensor_copy(out=dst[:, cc, t0:t0 + n], in_=pin[:, t0:t0 + n]), r=[d_pin], w=[d_qk])
        if not isq:
            for tt in range(NT):
                k.pe(lambda e, tt=tt: e.transpose(out=psT[:, 0:128], in_=kT[:, cc, tt * 128:(tt + 1) * 128], identity=self.ident_b[:]),
                     r=[d_qk, self.d_const], w=[d_psT])
                k.dve(lambda e, tt=tt: e.tensor_copy(out=ktok[:, tt, cc * 128:(cc + 1) * 128], in_=psT[:, 0:128]),
                      r=[d_psT], w=[d_ktok])

    def evac(c, m, t0, n, pst, dps):
        ct = c // 128
        k.act(lambda e: e.activation(out=pin[:, t0:t0 + n], in_=pst, func=AF.Copy), r=[dps], w=[d_pin])
        if (t0, n) == blocks[-1]:
            finish_tile(ct)
    _proj(self, l, [(512, 256), (256, 256), (0, 256)], blocks, evac, PROJ_BANKS)
    import os
    _stop = os.environ.get("GDN_STOP", "")
    if "gdn_g1" in self.dbg:
        self.k.barrier()
        chd = k.chan()
        for nm, ap_, n_ in (("qT", qT.rearrange("p a b -> p (a b)"), 2 * T), ("kT", kT.rearrange("p a b -> p (a b)"), 2 * T),
                            ("vtok", vtok.rearrange("p a b -> p (a b)"), NT * 256), ("ktok", ktok.rearrange("p a b -> p (a b)"), NT * 256)):
            o_ = self.dout("dbg_" + nm, [128, n_], BF16)
            k.dma("sp", o_[:, :], ap_, chd)
        for nm, ap_ in (("la", la), ("beta", beta)):
            o_ = self.dout("dbg_" + nm, [128, NT * 8])
            k.dma("sp", o_[:, :], ap_.rearrange("p a b -> p (a b)"), chd)
        self.k.barrier()
    if _stop == "g1":
        return
    _arena_rewind(self, markP)
    d_otok = [Dep() for _ in range(NT)]
    hm = self.aa([128, 4])
    d_hm = Dep()
    k.pool(lambda e: e.memset(hm, 0.0), w=[d_hm])
    k.pool(lambda e: e.memset(hm[0:64, 0:4:2], 1.0), w=[d_hm])
    k.pool(lambda e: e.memset(hm[64:128, 1:4:2], 1.0), w=[d_hm])
    first_visit = [True] * NT
    orderF = [16, 17] + list(range(16))
    orderB = [17, 16] + list(range(15, -1, -1))
    W_ = {}
    WP = {}
    for dd in range(2):
        w = {}
        w["Z"] = self.aa([128, 4, 128]); w["D"] = self.aa([128, 4, 128], BF16); w["Ds"] = w["Z"]
        w["E"] = [self.aa([128, 4, 128], BF16) for _ in range(2)]
        w["ET"] = [self.aa([128, 4, 128], BF16) for _ in range(2)]
        w["d_E"] = [Dep(), Dep()]; w["d_ET"] = [Dep(), Dep()]
        w["A"] = [self.aa([128, 4, 128], BF16) for _ in range(2)]
        w["AT"] = [self.aa([128, 4, 128], BF16) for _ in range(2)]
        w["X"] = [self.aa([128, 4, 128], BF16) for _ in range(2)]
        w["QKm"] = self.aa([128, 4, 128], BF16)
        w["R"] = self.aa([128, 4, 128], BF16)
        w["d_R"] = Dep()
        WP.setdefault(dd, [])
        for par in range(2):
            wp = {"QKT": self.aa([128, 4, 128], BF16), "wT": self.aa([64, 4, 128], BF16), "kd": self.aa([128, 4, 128], BF16),
                  "ev": self.aa([128, 12]), "u": self.aa([128, 4, 64], BF16)}
            for nm in ("QKT", "wT", "kd", "ev", "u"):
                wp["d_" + nm] = Dep()
            WP[dd].append(wp)
        w["kc"] = self.aa([128, 2, 2, 128], BF16)
        w["d_kc"] = Dep()
        k.pool(lambda e, w=w: e.memset(w["kc"], 0.0), w=[w["d_kc"]])
        w["SbQ"] = self.aa([128, 4, 64], BF16)
        w["d_SbQ"] = Dep()
        k.pool(lambda e, w=w: e.memset(w["SbQ"], 0.0), w=[w["d_SbQ"]])
        w["beg"] = self.aa([128, 4])
        w["vn"] = self.aa([128, 4, 64], BF16)
        w["o2"] = self.aa([128, 4, 64], BF16)
        w["ot"] = self.aa([128, 4, 64], BF16)
        w["S"] = self.aa([128, 4, 64])
        w["Sb"] = self.aa([128, 4, 64], BF16)
        for nm in ("Z", "D", "Ds", "QKm", "beg", "vn", "o2", "ot", "S", "Sb"):
            w["d_" + nm] = Dep()
        w["d_Ds"] = w["d_Z"]
        w["d_A"] = [Dep(), Dep()]; w["d_AT"] = [Dep(), Dep()]; w["d_X"] = [Dep(), Dep()]
        k.dve(lambda e, w=w: e.memset(w["S"], 0.0), w=[w["d_S"]])
        k.pool(lambda e, w=w: e.memset(w["Sb"], 0.0), w=[w["d_Sb"]])
        W_[dd] = w

    def bank(dd, i, half=None):
        t = self.ps[2 * dd + i // 2]
        hb = i % 2
        return t[:, hb * 512:(hb + 1) * 512], self.d_ps[2 * dd + i // 2][hb]

    def unit_pre(dd, n, par):
        w = dict(W_[dd])
        w.update(WP[dd][par])
        c0 = n * 128
        lacol = la[:, n, dd * 4:dd * 4 + 4]
        becol = beta[:, n, dd * 4:dd * 4 + 4]
        Mz = GT if dd == 0 else LT
        Um = LE if dd == 0 else GE
        Ugt = GT if dd == 0 else LT
        NEG = NEGF if dd == 0 else NEGB
        strict = GT if dd == 0 else LT
        B0, dB0 = bank(dd, 0)
        B1, dB1 = bank(dd, 1)
        B2, dB2 = bank(dd, 2)
        B3, dB3 = bank(dd, 3)
        k.dve(lambda e: e.tensor_tensor(out=w["Z"], in0=Mz.unsqueeze(1).to_broadcast([128, 4, 128]),
                                        in1=lacol.unsqueeze(2).to_broadcast([128, 4, 128]), op=ALU.mult),
              r=[d_GM, d_la], w=[w["d_Z"]])
        k.pe(lambda e: e.matmul(B0, Um, w["Z"].rearrange("p a b -> p (a b)"), start=True, stop=False), r=[d_GM, w["d_Z"]], w=[dB0])
        k.pe(lambda e: e.matmul(B0, self.ident_f[:], NEG, start=False, stop=True), r=[d_GM, self.d_const], w=[dB0])
        k.act(lambda e: e.activation(out=w["D"].rearrange("p a b -> p (a b)"), in_=B0, func=AF.Exp), r=[dB0], w=[w["d_D"]])
        yield
        k.pe(lambda e: e.matmul(B3[:, 0:4], Um, lacol, start=True, stop=True), r=[d_GM, d_la], w=[dB3])
        k.pe(lambda e: e.matmul(B3[:, 4:8], Ugt, lacol, start=True, stop=True), r=[d_GM, d_la], w=[dB3])
        k.pe(lambda e: e.matmul(B3[:, 8:12], self.ones_f[:], lacol, start=True, stop=True), r=[self.d_const, d_la], w=[dB3])
        yield
        k.act(lambda e: e.activation(out=w["ev"], in_=B3[:, 0:12], func=AF.Exp), r=[dB3], w=[w["d_ev"]])
        yield
        k.dve(lambda e: e.tensor_tensor(out=w["Ds"], in0=w["D"], in1=strict.unsqueeze(1).to_broadcast([128, 4, 128]), op=ALU.mult),
              r=[w["d_D"], d_GM], w=[w["d_Ds"]])
        yield
        for h in range(4):
            hp = slice(64 * (h % 2), 64 * (h % 2) + 64)
            if h == 0:
                k.act(lambda e: e.activation(out=w["kc"][0:64, :, 0, :], in_=kT[0:64, :, c0:c0 + 128], func=AF.Copy), r=[d_qk], w=[w["d_kc"]])
                k.act(lambda e: e.activation(out=w["kc"][64:128, :, 1, :], in_=kT[64:128, :, c0:c0 + 128], func=AF.Copy), r=[d_qk], w=[w["d_kc"]])
            k.pe(lambda e, h=h, hp=hp: e.matmul(B1[:, h * 128:(h + 1) * 128], kT[:, h // 2, c0:c0 + 128], w["kc"][:, h // 2, h % 2, :],
                                                start=True, stop=True), r=[d_qk, w["d_kc"]], w=[dB1])
        yield
        k.dve(lambda e: e.tensor_tensor(out=w["Ds"].rearrange("p a b -> p (a b)"), in0=B1, in1=w["Ds"].rearrange("p a b -> p (a b)"),
                                        op=ALU.mult), r=[dB1, w["d_Ds"]], w=[w["d_Ds"]])
        k.dve(lambda e: e.tensor_tensor(out=w["A"][0], in0=w["Ds"], in1=becol.unsqueeze(2).to_broadcast([128, 4, 128]), op=ALU.mult),
              r=[w["d_Ds"], d_beta], w=[w["d_A"][0]])
        yield
        for h in range(4):
            hp = slice(64 * (h % 2), 64 * (h % 2) + 64)
            k.pe(lambda e, h=h, hp=hp: e.matmul(B2[:, h * 128:(h + 1) * 128], qT[:, h // 2, c0:c0 + 128], w["kc"][:, h // 2, h % 2, :],
                                                start=True, stop=True), r=[d_qk, w["d_kc"]], w=[dB2])
        yield
        k.dve(lambda e: e.tensor_tensor(out=w["QKm"].rearrange("p a b -> p (a b)"), in0=B2, in1=w["D"].rearrange("p a b -> p (a b)"),
                                        op=ALU.mult), r=[dB2, w["d_D"]], w=[w["d_QKm"]])
        yield
        B1b = B1.bitcast(BF16)
        B2b = B2.bitcast(BF16)
        for h in range(4):
            k.pe(lambda e, h=h: e.transpose(out=B1b[:, h * 128:(h + 1) * 128], in_=w["A"][0][:, h, :], identity=self.ident_b[:]),
                 r=[w["d_A"][0], self.d_const], w=[dB1])
        yield
        k.act(lambda e: e.activation(out=w["AT"][0].rearrange("p a b -> p (a b)"), in_=B1b[:, 0:512], func=AF.Copy),
              r=[dB1], w=[w["d_AT"][0]])
        for h in range(4):
            k.pe(lambda e, h=h: e.transpose(out=B2b[:, h * 128:(h + 1) * 128], in_=w["QKm"][:, h, :], identity=self.ident_b[:]),
                 r=[w["d_QKm"], self.d_const], w=[dB2])
        yield
        k.act(lambda e: e.activation(out=w["QKT"].rearrange("p a b -> p (a b)"), in_=B2b[:, 0:512], func=AF.Copy),
              r=[dB2], w=[w["d_QKT"]])
        yield
        k.dve(lambda e: e.tensor_tensor(out=w["beg"], in0=becol, in1=w["ev"][:, 0:4], op=ALU.mult), r=[d_beta, w["d_ev"]], w=[w["d_beg"]])
        X0 = w["R"].rearrange("p h (two d) -> p h two d", two=2)
        k.dve(lambda e: e.tensor_tensor(out=X0[:, :, 0, :], in0=vtok[:, n, :].rearrange("p (h d) -> p h d", h=4),
                                        in1=becol.unsqueeze(2).to_broadcast([128, 4, 64]), op=ALU.mult),
              r=[d_vtok, d_beta], w=[w["d_R"]])
        k.dve(lambda e: e.tensor_tensor(out=X0[:, :, 1, :], in0=ktok[:, n, :].rearrange("p (h d) -> p h d", h=4),
                                        in1=w["beg"].unsqueeze(2).to_broadcast([128, 4, 64]), op=ALU.mult),
              r=[d_ktok, w["d_beg"]], w=[w["d_R"]])
        for half in range(2):
            k.pool(lambda e, half=half: e.tensor_tensor(out=w["kd"][:, :, half * 64:(half + 1) * 64],
                                                        in0=ktok[:, n, :].rearrange("p (h d) -> p h d", h=4),
                                                        in1=w["ev"][:, 4:8].unsqueeze(2).to_broadcast([128, 4, 64]), op=ALU.mult),
                   r=[d_ktok, w["d_ev"]], w=[w["d_kd"]])
        yield
        A, AT = w["A"][0], w["AT"][0]
        dA, dAT = w["d_A"][0], w["d_AT"][0]
        Tm, TT = w["A"][1], w["AT"][1]
        dT, dTT = w["d_A"][1], w["d_AT"][1]
        M1, M1t = w["X"][0], w["X"][1]
        dM1, dM1t = w["d_X"][0], w["d_X"][1]
        mo = 0 if dd == 0 else 7
        mt = 7 if dd == 0 else 0
        I4 = self.ident_b[:].unsqueeze(1).to_broadcast([128, 4, 128])
        msk = lambda s_: LMK[:, (mo + s_) * 128:(mo + s_ + 1) * 128].unsqueeze(1).to_broadcast([128, 4, 128])
        mskT = lambda s_: LMK[:, (mt + s_) * 128:(mt + s_ + 1) * 128].unsqueeze(1).to_broadcast([128, 4, 128])
        k.pool(lambda e: e.tensor_tensor(out=Tm, in0=A, in1=msk(0), op=ALU.mult), r=[dA, d_LMK], w=[dT])
        k.dve(lambda e: e.tensor_tensor(out=Tm, in0=Tm, in1=I4, op=ALU.add), r=[dT, self.d_const], w=[dT])
        k.pool(lambda e: e.tensor_tensor(out=TT, in0=AT, in1=mskT(0), op=ALU.mult), r=[dAT, d_LMK], w=[dTT])
        k.dve(lambda e: e.tensor_tensor(out=TT, in0=TT, in1=I4, op=ALU.add), r=[dTT, self.d_const], w=[dTT])
        yield
        fl = lambda x: x.rearrange("p a b -> p (a b)")
        def mk_E(lev):
            j = lev % 2
            k.pool(lambda e, lev=lev, j=j: e.tensor_tensor(out=w["E"][j], in0=A, in1=msk(lev), op=ALU.mult), r=[dA, d_LMK], w=[w["d_E"][j]])
            k.dve(lambda e, lev=lev, j=j: e.tensor_tensor(out=w["ET"][j], in0=AT, in1=mskT(lev), op=ALU.mult), r=[dAT, d_LMK], w=[w["d_ET"][j]])
        mk_E(1)
        for lev in range(1, 7):
            j = lev % 2
            E, ET, dE, dET = w["E"][j], w["ET"][j], w["d_E"][j], w["d_ET"][j]
            for h in range(4):
                k.pe(lambda e, h=h, ET=ET: e.matmul(B0[:, h * 128:(h + 1) * 128], ET[:, h, :], Tm[:, h, :], start=True, stop=True),
                     r=[dET, dT], w=[dB0])
            for h in range(4):
                k.pe(lambda e, h=h, E=E: e.matmul(B1[:, h * 128:(h + 1) * 128], E[:, h, :], TT[:, h, :], start=True, stop=True),
                     r=[dE, dTT], w=[dB1])
            yield
            if lev < 6:
                mk_E(lev + 1)
            k.act(lambda e: e.activation(out=fl(M1), in_=B0, func=AF.Copy), r=[dB0], w=[dM1])
            k.act(lambda e: e.activation(out=fl(M1t), in_=B1, func=AF.Copy), r=[dB1], w=[dM1t])
            yield
            for h in range(4):
                k.pe(lambda e, h=h: e.matmul(B2[:, h * 128:(h + 1) * 128], TT[:, h, :], M1[:, h, :], start=True, stop=True),
                     r=[dTT, dM1], w=[dB2])
            for h in range(4):
                k.pe(lambda e, h=h: e.matmul(B3[:, h * 128:(h + 1) * 128], Tm[:, h, :], M1t[:, h, :], start=True, stop=True),
                     r=[dT, dM1t], w=[dB3])
            yield
            k.dve(lambda e: e.tensor_tensor(out=fl(Tm), in0=fl(Tm), in1=B2, op=ALU.add), r=[dT, dB2], w=[dT])
            k.dve(lambda e: e.tensor_tensor(out=fl(TT), in0=fl(TT), in1=B3, op=ALU.add), r=[dTT, dB3], w=[dTT])
        yield
        Rm = w["R"].rearrange("p h (two d) -> p h two d", two=2)
        for h in range(4):
            k.pe(lambda e, h=h: e.matmul(B1[:, h * 64:(h + 1) * 64], TT[:, h, :], Rm[:, h, 0, :], start=True, stop=True),
                 r=[dTT, w["d_R"]], w=[dB1])
        k.act(lambda e: e.activation(out=w["u"].rearrange("p a b -> p (a b)"), in_=B1[:, 0:256], func=AF.Copy), r=[dB1], w=[w["d_u"]])
        yield
        for h in range(4):
            k.pe(lambda e, h=h: e.matmul(B2[0:64, h * 128:(h + 1) * 128], Rm[:, h, 1, :], TT[:, h, :], start=True, stop=True),
                 r=[dTT, w["d_R"]], w=[dB2])
        k.act(lambda e: e.activation(out=w["wT"].rearrange("p a b -> p (a b)"), in_=B2[0:64, :], func=AF.Copy),
              r=[dB2], w=[w["d_wT"]])
        yield

    def unit_scan(dd, n, par, need_out):
        w = dict(W_[dd])
        w.update(WP[dd][par])
        Xf, dXf = w["u"], w["d_u"]
        c0 = n * 128
        B0, dB0 = bank(dd, 0)
        B1, dB1 = bank(dd, 1)
        for h in range(4):
            k.pe(lambda e, h=h: e.matmul(B0[:, h * 64:(h + 1) * 64], w["wT"][:, h, :], w["Sb"][0:64, h, :], start=True, stop=True),
                 r=[w["d_wT"], w["d_Sb"]], w=[dB0])
        if need_out:
            for h in range(4):
                hp = slice(64 * (h % 2), 64 * (h % 2) + 64)
                k.pe(lambda e, h=h, hp=hp: e.matmul(B0[:, 256 + h * 64:256 + (h + 1) * 64], qT[:, h // 2, c0:c0 + 128],
                                                    w["SbQ"][:, h, :], start=True, stop=True),
                     r=[d_qk, w["d_SbQ"]], w=[dB0])
        yield
        k.dve(lambda e: e.tensor_tensor(out=w["vn"], in0=Xf, in1=B0[:, 0:256].rearrange("p (h d) -> p h d", h=4), op=ALU.subtract),
              r=[dXf, dB0], w=[w["d_vn"]])
        if need_out:
            for h in range(4):
                k.pe(lambda e, h=h: e.matmul(B1[:, h * 64:(h + 1) * 64], w["QKT"][:, h, :], w["vn"][:, h, :], start=True, stop=True),
                     r=[w["d_QKT"], w["d_vn"]], w=[dB1])
            k.act(lambda e: e.activation(out=w["o2"].rearrange("p a b -> p (a b)"), in_=B1[:, 0:256], func=AF.Copy), r=[dB1], w=[w["d_o2"]])
            k.dve(lambda e: e.tensor_tensor(out=w["ot"], in0=B0[:, 256:512].rearrange("p (h d) -> p h d", h=4),
                                            in1=w["ev"][:, 0:4].unsqueeze(2).to_broadcast([128, 4, 64]), op=ALU.mult),
                  r=[dB0, w["d_ev"]], w=[w["d_ot"]])
            k.pool(lambda e: e.tensor_tensor(out=w["ot"], in0=w["ot"], in1=w["o2"], op=ALU.add), r=[w["d_ot"], w["d_o2"]], w=[w["d_ot"]])
            if first_visit[n]:
                first_visit[n] = False
                k.pool(lambda e: e.tensor_copy(out=otok[:, n, :].rearrange("p (h d) -> p h d", h=4), in_=w["ot"]), r=[w["d_ot"]], w=[d_otok[n]])
            else:
                k.pool(lambda e: e.tensor_tensor(out=otok[:, n, :].rearrange("p (h d) -> p h d", h=4),
                                                 in0=otok[:, n, :].rearrange("p (h d) -> p h d", h=4), in1=w["ot"], op=ALU.add),
                       r=[w["d_ot"], d_otok[n]], w=[d_otok[n]])
        yield
        for h in range(4):
            k.pe(lambda e, h=h: e.matmul(B1[:, 256 + h * 64:256 + (h + 1) * 64], w["kd"][:, h, :], w["vn"][:, h, :], start=True, stop=True),
                 r=[w["d_kd"], w["d_vn"]], w=[dB1])
        yield
        k.dve(lambda e: e.tensor_tensor(out=w["S"], in0=w["S"], in1=w["ev"][:, 8:12].unsqueeze(2).to_broadcast([128, 4, 64]), op=ALU.mult),
              r=[w["d_S"], w["d_ev"]], w=[w["d_S"]])
        k.dve(lambda e: e.tensor_tensor(out=w["S"], in0=w["S"], in1=B1[:, 256:512].rearrange("p (h d) -> p h d", h=4), op=ALU.add),
              r=[w["d_S"], dB1], w=[w["d_S"]])
        k.act(lambda e: e.activation(out=w["Sb"], in_=w["S"], func=AF.Copy), r=[w["d_S"]], w=[w["d_Sb"]])
        k.pool(lambda e: e.tensor_tensor(out=w["SbQ"], in0=w["S"], in1=hm.unsqueeze(2).to_broadcast([128, 4, 64]), op=ALU.mult),
               r=[w["d_S"], d_hm], w=[w["d_SbQ"]])

        yield

    def round_robin(gens):
        alive = list(gens)
        while alive:
            for g in list(alive):
                try:
                    next(g)
                except StopIteration:
                    alive.remove(g)

    _mode = os.environ.get("GDN_RR", "pipe")
    prev = []
    for step in range(18):
        par = step % 2
        cur = []
        pres = []
        for dd in range(2):
            n = (orderF if dd == 0 else orderB)[step]
            need_out = (n < 16) or with_ctx
            pres.append(unit_pre(dd, n, par))
            cur.append((dd, n, par, need_out))
        if _mode == "pipe":
            round_robin(pres + prev)
            prev = [unit_scan(*c_) for c_ in cur]
        else:
            round_robin(pres)
            round_robin([unit_scan(*c_) for c_ in cur])
    round_robin(prev)
    if "gdn_otok" in self.dbg:
        self.k.barrier()
        chd = k.chan()
        o_ = self.dout("dbg_otok", [128, NT * 256])
        k.dma("sp", o_[:, :], otok.rearrange("p a b -> p (a b)"), chd)
        self.k.barrier()
    if _stop:
        return
    _arena_rewind(self, markO)
    _gdn_out(self, l, with_ctx, otok, d_otok)


def _gdn_out(self, l, with_ctx, otok, d_otok):
    k = self.k
    ntt = NT if with_ctx else 16
    _wring_init(self, 1)
    gT = self.aa([128, 2, T], BF16)
    d_g = Dep()
    blocks = _blocks(self, with_ctx)

    def evac(c, m, t0, n, pst, dps):
        ct = (c - 768) // 128
        k.act(lambda e: e.activation(out=gT[:, ct, t0:t0 + n], in_=pst, func=AF.Silu), r=[dps], w=[d_g])
    _proj(self, l, [(768, 256)], blocks, evac, PROJ_BANKS)
    gn = _ppv(self, l, "gnorm")
    sq = self.aa([128, 4, 64])
    ss = self.aa([128, 4])
    on = [self.aa([128, 256], BF16) for _ in range(2)]
    d_sq, d_ss = Dep(), Dep()
    d_on = [Dep(), Dep()]
    psT = self.ps[2][:, 0:512].bitcast(BF16)
    d_psT = self.d_ps[2][0]
    for tt in range(ntt):
        o3 = otok[:, tt, :].rearrange("p (h d) -> p h d", h=4)
        j = tt % 2
        k.dve(lambda e, o3=o3: e.tensor_tensor(out=sq, in0=o3, in1=o3, op=ALU.mult), r=[d_otok[tt]], w=[d_sq])
        k.dve(lambda e: e.reduce_sum(out=ss, in_=sq, axis=mybir.AxisListType.X), r=[d_sq], w=[d_ss])
        k.act(lambda e: e.activation(out=ss, in_=ss, func=AF.Sqrt, bias=EPS, scale=1.0 / 64), r=[d_ss], w=[d_ss])
        k.dve(lambda e: e.reciprocal(out=ss, in_=ss), r=[d_ss], w=[d_ss])
        k.dve(lambda e, o3=o3, j=j: e.tensor_tensor(out=on[j].rearrange("p (h d) -> p h d", h=4), in0=o3,
                                                    in1=ss.unsqueeze(2).to_broadcast([128, 4, 64]), op=ALU.mult),
              r=[d_otok[tt], d_ss], w=[d_on[j]])
        for ct in range(2):
            k.pe(lambda e, j=j, ct=ct: e.transpose(out=psT[:, ct * 128:(ct + 1) * 128], in_=on[j][:, ct * 128:(ct + 1) * 128],
                                                   identity=self.ident_b[:]), r=[d_on[j], self.d_const], w=[d_psT])
        for ct in range(2):
            k.dve(lambda e, ct=ct, tt=tt: e.scalar_tensor_tensor(out=self.C[:, ct, tt * 128:(tt + 1) * 128], in0=psT[:, ct * 128:(ct + 1) * 128],
                                                                 scalar=gn[:, 0:1], in1=gT[:, ct, tt * 128:(tt + 1) * 128],
                                                                 op0=ALU.mult, op1=ALU.mult),
                  r=[d_psT, d_g, self.d_pp], w=[self.d_C[tt]])


from concourse.bass_utils import run_bass_kernel_spmd


def kernel(**inputs):
    inputs = {k_: np.asarray(v) for k_, v in inputs.items()}
    mk = MK()
    nc = mk.build()
    consts = const_inputs(inputs)
    pp = pp_host(inputs)

    def extra(b):
        e = {"pp": pp}
        e.update(consts)
        return e
    maps = host_inputs(mk, inputs, extra=extra)
    n = len(maps)
    res = run_bass_kernel_spmd(nc, maps, core_ids=list(range(n)))
    out = np.stack([np.asarray(res.results[b]["out"]) for b in range(n)], axis=0)
    return out.astype(np.float32)
```

```python
import numpy as np
import ml_dtypes
from contextlib import ExitStack
import concourse.bass as bass
import concourse.mybir as mybir

F32 = mybir.dt.float32
BF16 = mybir.dt.bfloat16
I32 = mybir.dt.int32
AF = mybir.ActivationFunctionType
ALU = mybir.AluOpType

D = 1024
L = 2048
LC = 256
T = L + LC
NT = T // 128
DEPTH = 2
EPS = 1e-6
IN_COLS = 3344
OFF_SC, OFF_HY, OFF_NA = 1040, 1808, 2576


class Dep:
    __slots__ = ("lw", "rd")

    def __init__(self):
        self.lw = None
        self.rd = []


class Chan:
    __slots__ = ("sem", "cnt", "key", "q")

    def __init__(self, sem, key):
        self.sem = sem
        self.cnt = 0
        self.key = key


class K:
    def __init__(self, nc, stack):
        self.nc = nc
        self.stack = stack
        self.eng = {"pe": nc.tensor, "act": nc.scalar, "dve": nc.vector, "pool": nc.gpsimd, "sp": nc.sync}
        self.sems = {}
        self.ecnt = {}
        self.waited = {e: {} for e in self.eng}
        for e in self.eng:
            self.sems["e_" + e] = stack.enter_context(nc.semaphore("e_" + e))
            self.ecnt[e] = 0
        self.chans = []
        self.bar_sem = stack.enter_context(nc.semaphore("bar"))
        self.bar_cnt = 0
        self.nops = 0

    def chan(self):
        key = "c%d" % len(self.chans)
        s = self.stack.enter_context(self.nc.semaphore(key))
        self.sems[key] = s
        c = Chan(s, key)
        c.q = None
        self.chans.append(c)
        return c

    def _wait(self, e, ev):
        key, val, src = ev
        if src == e and e == "pe":
            return
        w = self.waited[e]
        if w.get(key, 0) >= val:
            return
        w[key] = val
        self.eng[e].wait_ge(self.sems[key], val)

    def _deps(self, e, r, w):
        for d in r:
            if d.lw is not None:
                self._wait(e, d.lw)
        for d in w:
            if d.lw is not None and d.lw[2] != e:
                self._wait(e, d.lw)
            for ev in d.rd:
                if ev[2] != e:
                    self._wait(e, ev)

    def _commit(self, ev, r, w):
        for d in w:
            d.lw = ev
            d.rd = []
        for d in r:
            d.rd.append(ev)
            if len(d.rd) > 48:
                best = {}
                for x in d.rd:
                    if x[0] not in best or best[x[0]][1] < x[1]:
                        best[x[0]] = x
                d.rd = list(best.values())

    def op(self, e, fn, r=(), w=()):
        self._deps(e, r, w)
        ins = fn(self.eng[e])
        self.ecnt[e] += 1
        ins.then_inc(self.sems["e_" + e], 1)
        self._commit(("e_" + e, self.ecnt[e], e), r, w)
        self.nops += 1
        return ins

    def pe(self, fn, r=(), w=()):
        return self.op("pe", fn, r, w)

    def act(self, fn, r=(), w=()):
        return self.op("act", fn, r, w)

    def dve(self, fn, r=(), w=()):
        return self.op("dve", fn, r, w)

    def pool(self, fn, r=(), w=()):
        return self.op("pool", fn, r, w)

    def dma(self, q, out, in_, ch, r=(), w=(), **kw):
        self._deps(q, r, w)
        ins = self.eng[q].dma_start(out=out, in_=in_, **kw)
        ch.cnt += 16
        ch.q = q
        ins.then_inc(ch.sem, 16)
        self._commit((ch.key, ch.cnt, "dma"), r, w)
        self.nops += 1
        return ins

    def barrier(self):
        last = getattr(self, "_bar_last", {})
        cur = {}
        for e in self.eng:
            cur["e_" + e] = (self.ecnt[e], e)
        for c in self.chans:
            cur[c.key] = (c.cnt, "dma")
        changed = [(key, v[0], v[1]) for key, v in cur.items() if v[0] > 0 and last.get(key, (0,))[0] != v[0]]
        for e in self.eng:
            for (key, val, src) in changed:
                w = self.waited[e]
                if w.get(key, 0) >= val:
                    continue
                w[key] = val
                self.eng[e].wait_ge(self.sems[key], val)
        self._bar_last = cur


class MK:
    def __init__(self, dbg=None, layers=(0, 1), inject_cat=False, mixers=("gdn", "sc", "hy", "na"), do_mlp=True,
                 phases=("mod", "p1", "mix", "p3", "p4"), inject_h=False):
        self.phases = phases
        self.inject_h = inject_h
        self.dbg = dbg or {}
        self.layers = layers
        self.inject_cat = inject_cat
        self.mixers = mixers
        self.do_mlp = do_mlp
        self.inputs = {}
        self.outputs = {}

    def din(self, name, shape, dtype=F32):
        t = self.nc.dram_tensor(name, list(shape), dtype, kind="ExternalInput").ap()
        self.inputs[name] = (tuple(shape), dtype)
        return t

    def dout(self, name, shape, dtype=F32):
        t = self.nc.dram_tensor(name, list(shape), dtype, kind="ExternalOutput").ap()
        self.outputs[name] = (tuple(shape), dtype)
        return t

    def W(self, name):
        if name not in self._w:
            self._w[name] = self.din(name, self._wshape[name])
        return self._w[name]

    def sb(self, st, name, shape, dtype=F32):
        self._n = getattr(self, "_n", 0) + 1
        return st.enter_context(self.nc.sbuf_tensor("%s_%d" % (name, self._n), list(shape), dtype))

    def build(self):
        nc = bass.Bass("TRN2", target_bir_lowering=False)
        self.nc = nc
        with ExitStack() as st:
            self.st = st
            self.k = K(nc, st)
            self._declare()
            self._consts()
            for l in self.layers:
                self._layer(l)
            self._finish()
        return nc

    def _declare(self):
        nc = self.nc
        self._wshape = {}
        self._w = {}
        self.x_in = self.din("x", [L, D])
        self.ctx_in = self.din("ctx", [LC, D])
        self.cvec_in = self.din("cvec", [128, 16])
        self._wshape["ada_w"] = [DEPTH, D, 6 * D]
        self.ada_bT = self.din("ada_bT", [128, DEPTH * 48])
        self.gains_in = self.din("gains", [128, 4 * DEPTH * 8])
        self._wshape["w_in"] = [DEPTH, D, IN_COLS]
        self._wshape["w_out"] = [DEPTH, D, D]
        self._wshape["mlp_w1"] = [DEPTH, D, 4 * D]
        self._wshape["mlp_w2"] = [DEPTH, 4 * D, D]
        self.ident_f_in = self.din("ident_f", [128, 128])
        self.ident_b_in = self.din("ident_b", [128, 128], BF16)
        self.out = self.dout("out", [L, D])
        self.xres = nc.dram_tensor("xres", [T, D], F32).ap()
        self.d_xres = [Dep() for _ in range(NT)]
        self.d_out = [Dep() for _ in range(NT)]
        if self.inject_cat:
            self.cat_in = self.din("cat_in", [D, T], BF16)
        self.ps = [self.st.enter_context(nc.psum_tensor("ps%d" % i, [128, 1024], F32)) for i in range(4)]
        self.d_ps = [[Dep(), Dep()] for _ in range(4)]

    def _consts(self):
        k, st = self.k, self.st
        sb = lambda n, s, d=F32: self.sb(st, n, s, d)
        self.ident_f = sb("ident_f", [128, 128])
        self.ident_b = sb("ident_b", [128, 128], BF16)
        self.ones_f = sb("ones_f", [128, 128])
        self.cvec = sb("cvec", [128, 16])
        self.adab = sb("adab", [128, DEPTH * 48])
        self.gains = sb("gains", [128, 4 * DEPTH * 8])
        self.d_const = Dep()
        ch = k.chan()
        k.dma("sp", self.ident_f[:], self.ident_f_in[:, :], ch, w=[self.d_const])
        k.dma("sp", self.ident_b[:], self.ident_b_in[:, :], ch, w=[self.d_const])
        k.dma("sp", self.cvec[:], self.cvec_in[:, :], ch, w=[self.d_const])
        k.dma("sp", self.adab[:], self.ada_bT[:, :], ch, w=[self.d_const])
        k.dma("sp", self.gains[:], self.gains_in[:, :], ch, w=[self.d_const])
        k.dve(lambda e: e.memset(self.ones_f[:], 1.0), w=[self.d_const])
        self.H = sb("H", [128, 8, T], BF16)
        self.C = sb("C", [128, 8, T], BF16)
        self.d_H = [Dep() for _ in range(NT)]
        self.d_C = [Dep() for _ in range(NT)]
        self.modT = sb("modT", [128, 48, 2])
        self.A1 = sb("A1", [128, 8, 2])
        self.A2 = sb("A2", [128, 8, 2])
        self.G1f = sb("G1f", [128, 8, 2])
        self.G2f = sb("G2f", [128, 8, 2])
        self.Gbc = sb("Gbc", [128, 2, 2, D])
        self.d_mod = Dep()
        self.d_gbc = Dep()
        self.NAR = 28416
        self.AR = sb("arena", [128, self.NAR])
        self.ar_off = 0
        self.NXT = 3
        self.d_xt = [Dep() for _ in range(self.NXT)]
        self.ch_xt_ld = [k.chan() for _ in range(self.NXT)]
        self.ch_xt_st = [k.chan() for _ in range(self.NXT)]
        self.d_xn = [Dep(), Dep()]
        self.d_junk = Dep()
        self.d_stat = [Dep() for _ in range(4)]
        self.d_tmpb = [Dep(), Dep()]
        self.d_tmpf = [Dep(), Dep()]
        self.xt_i = self.xn_i = self.stat_i = self.tmp_i = 0

    def arena_reset(self):
        self.k.barrier()
        self.ar_off = 0

    def aa(self, shape, dtype=F32):
        esz = 4 if dtype in (F32, I32) else 2
        n = int(np.prod(shape[1:]))
        nbytes = (n * esz + 31) // 32 * 32
        o = self.ar_off
        assert o + nbytes <= self.NAR * 4, "arena overflow %d" % (o + nbytes)
        self.ar_off = o + nbytes
        v = self.AR[:, o // 4:(o + nbytes) // 4]
        if dtype != F32:
            v = v.bitcast(dtype)
        v = v[:, 0:n]
        if len(shape) == 3:
            v = v.rearrange("p (a b) -> p a b", b=shape[2])
        elif len(shape) == 4:
            v = v.rearrange("p (a b c) -> p a b c", b=shape[2], c=shape[3])
        if shape[0] != 128:
            v = v[0:shape[0]]
        return v

    def _staging(self, norm=True):
        self.xt = [self.aa([128, D]) for i in range(self.NXT)]
        self.junk = self.aa([128, D], BF16)
        self.stat = [self.aa([128, 8]) for i in range(4)]
        self.tmpf = [self.aa([128, D]) for i in range(2)]
        if norm:
            self.xn = [self.aa([128, D], BF16) for i in range(2)]
            self.tmpb = [self.aa([128, D], BF16) for i in range(2)]

    def gain(self, kind, l):
        o = (kind * DEPTH + l) * 8
        return self.gains[:, o:o + 8]

    def _layer(self, l):
        ph = self.phases
        if "mod" in ph:
            self._modulation(l)
        if "p1" in ph:
            self.arena_reset()
            self._staging()
            for tt in range(NT):
                s = 0 if tt < 16 else 1
                xi = self._load_x(l, tt, first=True)
                self._norm_tile(xi, s, self.A1, self.modT[:, 0:8, :], self.H, tt, self.d_H[tt])
        elif self.inject_h:
            ch = self.k.chan()
            hin = self.din("h_in", [D, T], BF16)
            for tt in range(NT):
                self.k.dma("sp", self.H[:, :, tt * 128:(tt + 1) * 128],
                           hin[:, tt * 128:(tt + 1) * 128].rearrange("(a p) t -> p a t", p=128), ch, w=[self.d_H[tt]])
        if "hx" in self.dbg and self.dbg["hx"] == l:
            self._dump_feat("dbg_hx", self.H, self.d_H)
        if "mix" in ph:
            self._mixers(l)
        if "cat" in self.dbg and self.dbg["cat"] == l:
            self._dump_feat("dbg_cat", self.C, self.d_C)
        if "p3" in ph:
            self._p3(l)
        if "hx2" in self.dbg and self.dbg["hx2"] == l:
            self._dump_feat("dbg_hx2", self.C, self.d_C)
        if "p4" in ph:
            self._p4(l)

    def _xsrc(self, l, tt, first):
        if l == self.layers[0] and l == 0 and first:
            if tt < 16:
                return self.x_in[tt * 128:(tt + 1) * 128, :], None
            return self.ctx_in[(tt - 16) * 128:(tt - 15) * 128, :], None
        return self.xres[tt * 128:(tt + 1) * 128, :], self.d_xres[tt]

    def _load_x(self, l, tt, first):
        k = self.k
        i = self.xt_i
        self.xt_i = (i + 1) % self.NXT
        src, dep = self._xsrc(l, tt, first)
        k.dma("sp", self.xt[i][:], src, self.ch_xt_ld[i], r=[dep] if dep else [], w=[self.d_xt[i]])
        return i

    def _store_x(self, i, dst_ap, dst_dep):
        self.k.dma("sp", dst_ap, self.xt[i][:], self.ch_xt_st[i], r=[self.d_xt[i]], w=[dst_dep])

    def _rstd(self, src_ap, src_deps):
        k = self.k
        j = self.stat_i
        self.stat_i = (j + 1) % 4
        stt, dst = self.stat[j], self.d_stat[j]
        k.act(lambda e: e.activation(out=self.junk[:], in_=src_ap, func=AF.Square, accum_out=stt[:, 0:1]),
              r=src_deps, w=[self.d_junk, dst])
        k.act(lambda e: e.activation(out=stt[:, 1:2], in_=stt[:, 0:1], func=AF.Sqrt, bias=EPS, scale=1.0 / D),
              r=[dst], w=[dst])
        k.dve(lambda e: e.reciprocal(out=stt[:, 2:3], in_=stt[:, 1:2]), r=[dst], w=[dst])
        return stt[:, 2:3], dst

    def _norm_tile(self, xi, s, A, B, Hbuf, tt, dH):
        k = self.k
        xt, dxt = self.xt[xi], self.d_xt[xi]
        rs, drs = self._rstd(xt[:], [dxt])
        j = self.xn_i
        self.xn_i = 1 - j
        xn, dxn = self.xn[j], self.d_xn[j]
        k.dve(lambda e: e.tensor_scalar(out=xn[:], in0=xt[:], scalar1=rs, scalar2=None, op0=ALU.mult),
              r=[dxt, drs], w=[dxn])
        pi = 3
        psb = self.ps[pi][:, 0:512].bitcast(BF16)
        dps = self.d_ps[pi][0]
        for dt in range(8):
            k.pe(lambda e, dt=dt: e.transpose(out=psb[:, dt * 128:(dt + 1) * 128], in_=xn[:, dt * 128:(dt + 1) * 128],
                                              identity=self.ident_b[:]),
                 r=[dxn, self.d_const], w=[dps])
        ti = self.tmp_i
        self.tmp_i = 1 - ti
        tb, dtb = self.tmpb[ti], self.d_tmpb[ti]
        k.dve(lambda e: e.tensor_tensor(out=tb[:].rearrange("p (a b) -> p a b", a=8),
                                        in0=psb.rearrange("p (a b) -> p a b", a=8),
                                        in1=A[:, :, s:s + 1].to_broadcast([128, 8, 128]), op=ALU.mult),
              r=[dps, self.d_mod], w=[dtb])
        k.pool(lambda e: e.tensor_tensor(out=Hbuf[:, :, tt * 128:(tt + 1) * 128],
                                         in0=tb[:].rearrange("p (a b) -> p a b", a=8),
                                         in1=B[:, :, s:s + 1].to_broadcast([128, 8, 128]), op=ALU.add),
               r=[dtb, self.d_mod], w=[dH])

    def _modulation(self, l):
        k = self.k
        self.arena_reset()
        if True:
            sT = self.aa([128, 16], BF16)
            d_sT = Dep()
            k.act(lambda e: e.activation(out=sT[:], in_=self.cvec[:], func=AF.Silu), r=[self.d_const], w=[d_sT])
            wb = [self.aa([128, 8, 512], BF16) for i in range(2)]
            dwb = [Dep(), Dep()]
            chw = [k.chan(), k.chan()]
            mod_ps = self.ps[0][:, 0:96]
            dps = self.d_ps[0][0]
            for g in range(12):
                i = g % 2
                src = self.W("ada_w")[l, :, g * 512:(g + 1) * 512].rearrange("(kt p) c -> p kt c", p=128)
                k.dma("pool", wb[i][:], src, chw[i], w=[dwb[i]])
                for jj in range(4):
                    jt = g * 4 + jj
                    for kt in range(8):
                        k.pe(lambda e, i=i, jj=jj, jt=jt, kt=kt: e.matmul(
                            mod_ps[:, jt * 2:jt * 2 + 2], wb[i][:, kt, jj * 128:(jj + 1) * 128],
                            sT[:, kt * 2:kt * 2 + 2], start=(kt == 0), stop=(kt == 7)),
                            r=[dwb[i], d_sT], w=[dps])
            dm = self.d_mod
            k.dve(lambda e: e.tensor_tensor(out=self.modT[:], in0=mod_ps.rearrange("p (a b) -> p a b", b=2),
                                            in1=self.adab[:, l * 48:(l + 1) * 48].unsqueeze(2).to_broadcast([128, 48, 2]),
                                            op=ALU.add), r=[dps, self.d_const], w=[dm])
            g = lambda kind: self.gain(kind, l).unsqueeze(2).to_broadcast([128, 8, 2])
            k.dve(lambda e: e.scalar_tensor_tensor(out=self.A1[:], in0=self.modT[:, 8:16, :], scalar=1.0, in1=g(0),
                                                   op0=ALU.add, op1=ALU.mult), r=[dm, self.d_const], w=[dm])
            k.dve(lambda e: e.scalar_tensor_tensor(out=self.A2[:], in0=self.modT[:, 32:40, :], scalar=1.0, in1=g(2),
                                                   op0=ALU.add, op1=ALU.mult), r=[dm, self.d_const], w=[dm])
            k.dve(lambda e: e.tensor_tensor(out=self.G1f[:], in0=self.modT[:, 16:24, :], in1=g(1), op=ALU.mult),
                  r=[dm, self.d_const], w=[dm])
            k.dve(lambda e: e.tensor_tensor(out=self.G2f[:], in0=self.modT[:, 40:48, :], in1=g(3), op=ALU.mult),
                  r=[dm, self.d_const], w=[dm])
            diag = [self.aa([128, 128]) for i in range(2)]
            ddiag = [Dep(), Dep()]
            n = 0
            for kind, Gf in enumerate((self.G1f, self.G2f)):
                for s in range(2):
                    for dt in range(8):
                        i = n % 2
                        n += 1
                        k.dve(lambda e, i=i, Gf=Gf, dt=dt, s=s: e.tensor_scalar(
                            out=diag[i][:], in0=self.ident_f[:], scalar1=Gf[:, dt, s:s + 1], scalar2=None, op0=ALU.mult),
                            r=[dm, self.d_const], w=[ddiag[i]])
                        pi = 1 + (n % 2)
                        pst = self.ps[pi][:, 0:128]
                        k.pe(lambda e, i=i, pst=pst: e.matmul(pst, self.ones_f[:], diag[i][:], start=True, stop=True),
                             r=[ddiag[i], self.d_const], w=[self.d_ps[pi][0]])
                        k.act(lambda e, pst=pst, kind=kind, s=s, dt=dt: e.activation(
                            out=self.Gbc[:, kind, s, dt * 128:(dt + 1) * 128], in_=pst, func=AF.Copy),
                            r=[self.d_ps[pi][0]], w=[self.d_gbc])
        if "mod" in self.dbg and self.dbg["mod"] == l:
            o = self.dout("dbg_mod", [128, 96])
            ch = k.chan()
            k.dma("sp", o[:, :], self.modT[:].rearrange("p a b -> p (a b)"), ch, r=[self.d_mod])
            o2 = self.dout("dbg_gbc", [128, 4 * D])
            k.dma("sp", o2[:, :], self.Gbc[:].rearrange("p a b c -> p (a b c)"), ch, r=[self.d_gbc])

    def _mixers(self, l):
        k = self.k
        if self.inject_cat:
            ch = k.chan()
            for tt in range(NT):
                k.dma("sp", self.C[:, :, tt * 128:(tt + 1) * 128],
                      self.cat_in[:, tt * 128:(tt + 1) * 128].rearrange("(a p) t -> p a t", p=128), ch, w=[self.d_C[tt]])
            return
        raise NotImplementedError

    def _p3(self, l):
        k = self.k
        last = (l == DEPTH - 1)
        ntt = 16 if last else NT
        self.arena_reset()
        self._staging()
        if True:
            wo = self.aa([128, 8, D], BF16)
            dwo = Dep()
            ch = k.chan()
            k.dma("pool", wo[:], self.W("w_out")[l].rearrange("(kt p) c -> p kt c", p=128), ch, w=[dwo])
            for tt in range(ntt):
                s = 0 if tt < 16 else 1
                pi = tt % 3
                yps = self.ps[pi]
                for half in range(2):
                    for mt in range(8):
                        k.pe(lambda e, half=half, mt=mt, yps=yps, tt=tt: e.matmul(
                            yps[:, half * 512:(half + 1) * 512], self.C[:, mt, tt * 128:(tt + 1) * 128],
                            wo[:, mt, half * 512:(half + 1) * 512], start=(mt == 0), stop=(mt == 7)),
                            r=[self.d_C[tt], dwo], w=[self.d_ps[pi][half]])
                xi = self._load_x(l, tt, first=True)
                self._resid_update(xi, yps[:], self.d_ps[pi], 0, s)
                self._store_x(xi, self.xres[tt * 128:(tt + 1) * 128, :], self.d_xres[tt])
                self._norm_tile(xi, s, self.A2, self.modT[:, 24:32, :], self.C, tt, self.d_C[tt])

    def _resid_update(self, xi, y_ap, y_deps, kind, s):
        k = self.k
        rs, drs = self._rstd(y_ap, list(y_deps))
        ti = self.tmp_i
        self.tmp_i = 1 - ti
        tf, dtf = self.tmpf[ti], self.d_tmpf[ti]
        k.dve(lambda e: e.scalar_tensor_tensor(out=tf[:], in0=y_ap, scalar=rs, in1=self.Gbc[:, kind, s, :],
                                               op0=ALU.mult, op1=ALU.mult),
              r=list(y_deps) + [drs, self.d_gbc], w=[dtf])
        xt, dxt = self.xt[xi], self.d_xt[xi]
        k.dve(lambda e: e.tensor_tensor(out=xt[:], in0=xt[:], in1=tf[:], op=ALU.add), r=[dtf, dxt], w=[dxt])

    def _p4(self, l):
        k = self.k
        last = (l == DEPTH - 1)
        if last:
            sblocks = [(0, 768), (768, 768), (1536, 512)]
        else:
            sblocks = [(0, 768), (768, 768), (1536, 768)]
        self.arena_reset()
        self._staging(norm=False)
        if True:
            hT = self.aa([128, 32, 768], BF16)
            d_hT = [[Dep() for _ in range(2)] for _ in range(32)]
            HF = self.H[:].rearrange("p a t -> p (a t)")
            w1c = [HF[:, i * 4096:(i + 1) * 4096].rearrange("p (k c) -> p k c", c=512) for i in range(3)]
            d_w1c = [Dep(), Dep(), Dep()]
            ch_w1 = [k.chan(), k.chan(), k.chan()]
            w2c = [HF[:, 12288 + i * 2048: 12288 + (i + 1) * 2048].rearrange("p (k c) -> p k c", c=512) for i in range(3)]
            d_w2c = [Dep() for _ in range(3)]
            ch_w2 = [k.chan() for _ in range(3)]
            rl = [self.aa([128, 384], BF16) for i in range(2)]
            d_rl = [Dep(), Dep()]
            ytok = self.aa([128, 6, D])
            d_ytok = [Dep() for _ in range(6)]
            n_w1 = 0
            n_w2 = 0
            n_rl = 0
            for (t0, n) in sblocks:
                n2 = n // 2
                ntl = n // 128
                for ffc in range(8):
                    i = n_w1 % 3
                    n_w1 += 1
                    k.dma("pool", w1c[i], self.W("mlp_w1")[l, :, ffc * 512:(ffc + 1) * 512].rearrange("(kt p) c -> p kt c", p=128),
                          ch_w1[i], w=[d_w1c[i]])
                    for f in range(4):
                        fft = ffc * 4 + f
                        for sbk in range(2):
                            hps = self.ps[3][:, sbk * 512: sbk * 512 + n2]
                            dhps = self.d_ps[3][sbk]
                            tts = range((t0 + sbk * n2) // 128, (t0 + (sbk + 1) * n2 + 127) // 128)
                            rdeps = [self.d_C[t] for t in tts]
                            for dt in range(8):
                                k.pe(lambda e, i=i, f=f, dt=dt, hps=hps, sbk=sbk: e.matmul(
                                    hps, w1c[i][:, dt, f * 128:(f + 1) * 128],
                                    self.C[:, dt, t0 + sbk * n2: t0 + (sbk + 1) * n2], start=(dt == 0), stop=(dt == 7)),
                                    r=[d_w1c[i]] + rdeps, w=[dhps])
                            j = n_rl % 2
                            n_rl += 1
                            k.act(lambda e, j=j, hps=hps: e.activation(out=rl[j][:, 0:n2], in_=hps, func=AF.Relu),
                                  r=[dhps], w=[d_rl[j]])
                            k.dve(lambda e, j=j, fft=fft, sbk=sbk: e.tensor_tensor(
                                out=hT[:, fft, sbk * n2:(sbk + 1) * n2], in0=rl[j][:, 0:n2], in1=rl[j][:, 0:n2], op=ALU.mult),
                                r=[d_rl[j]], w=[d_hT[fft][sbk]])
                for dh in range(2):
                    for ffc in range(8):
                        i = n_w2 % 3
                        n_w2 += 1
                        k.dma("pool", w2c[i],
                              self.W("mlp_w2")[l, ffc * 512:(ffc + 1) * 512, dh * 512:(dh + 1) * 512].rearrange("(f p) c -> p f c", p=128),
                              ch_w2[i], w=[d_w2c[i]])
                        for f in range(4):
                            fft = ffc * 4 + f
                            for tl in range(ntl):
                                pi, hb = tl // 2, tl % 2
                                sbk = (tl * 128) // n2
                                k.pe(lambda e, i=i, f=f, fft=fft, tl=tl, pi=pi, hb=hb: e.matmul(
                                    self.ps[pi][:, hb * 512:(hb + 1) * 512], hT[:, fft, tl * 128:(tl + 1) * 128],
                                    w2c[i][:, f, :], start=(fft == 0), stop=(fft == 31)),
                                    r=[d_w2c[i], d_hT[fft][sbk]], w=[self.d_ps[pi][hb]])
                    for tl in range(ntl):
                        pi, hb = tl // 2, tl % 2
                        k.act(lambda e, tl=tl, pi=pi, hb=hb, dh=dh: e.activation(
                            out=ytok[:, tl, dh * 512:(dh + 1) * 512], in_=self.ps[pi][:, hb * 512:(hb + 1) * 512], func=AF.Copy),
                            r=[self.d_ps[pi][hb]], w=[d_ytok[tl]])
                for tl in range(ntl):
                    tt = t0 // 128 + tl
                    s = 0 if tt < 16 else 1
                    xi = self._load_x(l, tt, first=False)
                    self._resid_update(xi, ytok[:, tl, :], [d_ytok[tl]], 1, s)
                    if last:
                        self._store_x(xi, self.out[tt * 128:(tt + 1) * 128, :], self.d_out[tt])
                    else:
                        self._store_x(xi, self.xres[tt * 128:(tt + 1) * 128, :], self.d_xres[tt])

    def _dump_feat(self, name, buf, deps):
        k = self.k
        o = self.dout(name, [D, T])
        self.arena_reset()
        if True:
            stg = self.aa([128, 8, 128])
            dst = Dep()
            ch = k.chan()
            for tt in range(NT):
                k.dve(lambda e, tt=tt: e.tensor_copy(out=stg[:], in_=buf[:, :, tt * 128:(tt + 1) * 128]), r=[deps[tt]], w=[dst])
                k.dma("sp", o[:, tt * 128:(tt + 1) * 128].rearrange("(a p) t -> p a t", p=128), stg[:], ch, r=[dst])

    def _finish(self):
        k = self.k
        if "xres" in self.dbg:
            self.arena_reset()
            self._staging()
            o = self.dout("dbg_xres", [T, D])
            ch = k.chan()
            for tt in range(NT):
                xi = self._load_x(1, tt, first=False)
                self._store_x(xi, o[tt * 128:(tt + 1) * 128, :], Dep())
        k.barrier()


def host_inputs(mk, inputs, extra=None):
    bf = ml_dtypes.bfloat16
    f32 = np.float32
    shared = {}
    shared["ada_w"] = np.ascontiguousarray(inputs["ada_w"], dtype=f32)
    shared["ada_bT"] = np.ascontiguousarray(
        inputs["ada_b"].reshape(DEPTH, 48, 128).transpose(2, 0, 1).reshape(128, DEPTH * 48), dtype=f32)
    g = np.stack([inputs["norm_pre_mix"], inputs["norm_post_mix"], inputs["norm_pre_mlp"], inputs["norm_post_mlp"]])
    shared["gains"] = np.ascontiguousarray(g.reshape(4, DEPTH, 8, 128).transpose(3, 0, 1, 2).reshape(128, -1), dtype=f32)
    for n in ("w_in", "w_out", "mlp_w1", "mlp_w2"):
        shared[n] = np.ascontiguousarray(inputs[n], dtype=f32)
    shared["ident_f"] = np.eye(128, dtype=f32)
    shared["ident_b"] = np.eye(128, dtype=f32).astype(bf)
    maps = []
    for b in range(inputs["x"].shape[0]):
        m = dict(shared)
        m["x"] = np.ascontiguousarray(inputs["x"][b], dtype=f32)
        m["ctx"] = np.ascontiguousarray(inputs["ctx"][b], dtype=f32)
        cv = np.stack([inputs["c"][b].reshape(8, 128), inputs["c_ctx"].reshape(8, 128)], axis=-1)
        m["cvec"] = np.ascontiguousarray(cv.transpose(1, 0, 2).reshape(128, 16), dtype=f32)
        if extra:
            m.update(extra(b))
        maps.append({kk: v for kk, v in m.items() if kk in mk.inputs})
    return maps


PP_ENTRIES = [("scw", 6), ("hyw", 18), ("gdw", 18), ("hybias", 2), ("gnorm", 1), ("hy_w1", 64), ("hy_w2", 64),
              ("hy_w3", 64), ("hy_w4", 512), ("hy_b", 3), ("hy_f", 3), ("alog", 8), ("dtb", 8)]
PP_OFF = {}
_o = 0
for _n, _w in PP_ENTRIES:
    PP_OFF[_n] = (_o, _w)
    _o += _w
PP_W = _o


def pp_host(inputs):
    pp = np.zeros((128, DEPTH * PP_W), np.float32)
    for l in range(DEPTH):
        def put(name, arr):
            o, w = PP_OFF[name]
            arr = np.asarray(arr, np.float32)
            assert arr.shape[1] == w, (name, arr.shape)
            pp[:arr.shape[0], l * PP_W + o: l * PP_W + o + w] = arr
        put("scw", inputs["sc_conv"][l].reshape(3, 2, 128).transpose(2, 1, 0).reshape(128, 6))
        put("hyw", inputs["hy_conv"][l].reshape(3, 6, 128).transpose(2, 1, 0).reshape(128, 18))
        put("gdw", inputs["gdn_conv"][l].reshape(3, 6, 128).transpose(2, 1, 0).reshape(128, 18))
        put("hybias", inputs["hy_bias"][l].reshape(2, 128).T)
        put("gnorm", np.tile(inputs["gdn_norm"][l], 2).reshape(128, 1))
        put("hy_w1", inputs["hy_w1"][l])
        put("hy_w2", inputs["hy_w2"][l])
        put("hy_w3", inputs["hy_w3"][l])
        put("hy_w4", inputs["hy_w4"][l])
        put("hy_b", np.stack([inputs["hy_b1"][l], inputs["hy_b2"][l], inputs["hy_b3"][l]], axis=1))
        put("hy_f", inputs["hy_freq"][l].T)
        put("alog", np.tile(inputs["gdn_a_log"][l].reshape(1, 8), (128, 1)))
        put("dtb", np.tile(inputs["gdn_dt_bias"][l].reshape(1, 8), (128, 1)))
    return pp


def na_consts(inputs):
    rpb = np.asarray(inputs["na_rpb"], np.float32)
    par = np.arange(2)[:, None, None, None]
    kc = np.arange(64)[None, :, None, None]
    i = np.arange(14)[None, None, :, None]
    qc = np.arange(64)[None, None, None, :]
    dc = np.clip(kc - qc, -15, 15) + 15
    di = np.broadcast_to(i + par, (2, 64, 14, 64))
    dcb = np.broadcast_to(dc, (2, 64, 14, 64))
    g = rpb[:, :, di, dcb]
    g = g.transpose(0, 2, 3, 1, 4, 5).reshape(DEPTH, 128, 4 * 14 * 64)
    cs = np.clip(np.arange(64) - 8, 0, 48)
    kcv = np.arange(64)[:, None]
    valid = (kcv >= cs[None, :]) & (kcv < cs[None, :] + 16)
    m = np.where(valid, 0.0, -1e30).astype(np.float32)
    mask = np.concatenate([m, m], axis=0)
    return np.ascontiguousarray(g), np.ascontiguousarray(mask)


def _mix_common_init(self):
    if getattr(self, "pp", None) is not None:
        return
    k = self.k
    self.pp_in = self.din("pp", [128, DEPTH * PP_W])
    self.pp = self.sb(self.st, "pp", [128, DEPTH * PP_W])
    self.d_pp = Dep()
    ch = k.chan()
    k.dma("sp", self.pp[:], self.pp_in[:, :], ch, w=[self.d_pp])


def _ppv(self, l, name, rows=128):
    o, w = PP_OFF[name]
    return self.pp[0:rows, l * PP_W + o: l * PP_W + o + w]


def _blocks(self, with_ctx):
    b = [(i * 512, 512) for i in range(4)]
    if with_ctx:
        b.append((L, LC))
    return b


def _wring_init(self, n=2):
    self.wring = [(self.aa([128, 8, 512], BF16), Dep(), self.wring_ch[i]) for i in range(n)]
    self.wring_i = 0
    self.bank_i = 0


def _proj(self, l, chunks, blocks, evac, banks):
    k = self.k
    for (c0, ncol) in chunks:
        i = self.wring_i
        self.wring_i = (i + 1) % len(self.wring)
        wap, dw, chw = self.wring[i]
        k.dma("pool", wap[:, :, 0:ncol], self.W("w_in")[l, :, c0:c0 + ncol].rearrange("(kt p) c -> p kt c", p=128),
              chw, w=[dw])
        for cc in range(0, ncol, 128):
            m = min(128, ncol - cc)
            for (t0, n) in blocks:
                pi, hb = banks[self.bank_i % len(banks)]
                self.bank_i += 1
                pst = self.ps[pi][0:m, hb * 512: hb * 512 + n]
                dps = self.d_ps[pi][hb]
                hd = [self.d_H[t] for t in range(t0 // 128, (t0 + n + 127) // 128)]
                for dt in range(8):
                    k.pe(lambda e, dt=dt, pst=pst, wap=wap, cc=cc, m=m, t0=t0, n=n: e.matmul(
                        pst, wap[:, dt, cc:cc + m], self.H[:, dt, t0:t0 + n], start=(dt == 0), stop=(dt == 7)),
                        r=[dw] + hd, w=[dps])
                evac(c0 + cc, m, t0, n, pst, dps)


def _dwconv(self, eng, out_ap, in_ap, w3, ranges, r, w):
    k = self.k
    for (a, b) in ranges:
        k.op(eng, lambda e, a=a, b=b: e.tensor_scalar(out=out_ap[:, a:b], in0=in_ap[:, a:b], scalar1=w3[:, 1:2],
                                                      scalar2=None, op0=ALU.mult), r=r, w=w)
        k.op(eng, lambda e, a=a, b=b: e.scalar_tensor_tensor(out=out_ap[:, a + 1:b], in0=in_ap[:, a:b - 1], scalar=w3[:, 0:1],
                                                             in1=out_ap[:, a + 1:b], op0=ALU.mult, op1=ALU.add),
             r=list(r) + list(w), w=w)
        k.op(eng, lambda e, a=a, b=b: e.scalar_tensor_tensor(out=out_ap[:, a:b - 1], in0=in_ap[:, a + 1:b], scalar=w3[:, 2:3],
                                                             in1=out_ap[:, a:b - 1], op0=ALU.mult, op1=ALU.add),
             r=list(r) + list(w), w=w)


PROJ_BANKS = [(0, 0), (0, 1), (1, 0), (1, 1)]


def _mix_sc(self, l, with_ctx):
    k = self.k
    self.arena_reset()
    _wring_init(self)
    ranges = [(0, L)] + ([(L, T)] if with_ctx else [])
    blocks = _blocks(self, with_ctx)
    ntok = T if with_ctx else L
    ntt = ntok // 128
    pxb = self.aa([128, 6, T], BF16)
    d_px = [Dep() for _ in range(6)]

    def evac(c, m, t0, n, pst, dps):
        ct = (c - OFF_SC) // 128
        k.act(lambda e: e.activation(out=pxb[:, ct, t0:t0 + n], in_=pst, func=AF.Copy), r=[dps], w=[d_px[ct]])
    _proj(self, l, [(OFF_SC, 512), (OFF_SC + 512, 256)], blocks, evac, PROJ_BANKS)
    z = [self.aa([128, T]) for _ in range(2)]
    acc = [self.aa([128, T]) for _ in range(2)]
    scw = _ppv(self, l, "scw")
    for j in range(2):
        dz, dacc = Dep(), Dep()
        eng = "dve" if j == 0 else "pool"
        k.op(eng, lambda e, j=j: e.tensor_tensor(out=z[j][:, 0:ntok], in0=pxb[:, 2 + j, 0:ntok], in1=pxb[:, 4 + j, 0:ntok],
                                                 op=ALU.mult), r=[d_px[2 + j], d_px[4 + j]], w=[dz])
        _dwconv(self, "dve", acc[j], z[j], scw[:, j * 3:(j + 1) * 3], ranges, [dz, self.d_pp], [dacc])
        k.op(eng, lambda e, j=j: e.tensor_tensor(out=self.C[:, 2 + j, 0:ntok], in0=pxb[:, j, 0:ntok], in1=acc[j][:, 0:ntok],
                                                 op=ALU.mult), r=[d_px[j], dacc], w=[self.d_C[t] for t in range(ntt)])


def _mixers(self, l):
    k = self.k
    if self.inject_cat:
        ch = k.chan()
        for tt in range(NT):
            k.dma("sp", self.C[:, :, tt * 128:(tt + 1) * 128],
                  self.cat_in[:, tt * 128:(tt + 1) * 128].rearrange("(a p) t -> p a t", p=128), ch, w=[self.d_C[tt]])
        return
    _mix_common_init(self)
    if not hasattr(self, "wring_ch"):
        self.wring_ch = [k.chan() for _ in range(3)]
    with_ctx = l < DEPTH - 1
    if "sc" in self.mixers:
        _mix_sc(self, l, with_ctx)
    if "na" in self.mixers:
        _mix_na(self, l, with_ctx)
    if "hy" in self.mixers:
        _mix_hy(self, l, with_ctx)
    if "gdn" in self.mixers:
        _mix_gdn(self, l, with_ctx)


MK._mixers = _mixers


def const_inputs(inputs):
    out = {}
    g, mask = na_consts(inputs)
    out["na_rpbg"] = g
    out["na_mask"] = mask
    out.update(hy_consts())
    out.update(gdn_consts())
    return out


def _mix_na(self, l, with_ctx):
    k = self.k
    self.arena_reset()
    _wring_init(self)
    blocks = _blocks(self, True)
    qT = self.aa([128, 2, T], BF16)
    kT = self.aa([128, 2, T], BF16)
    d_q = [Dep() for _ in range(NT)]
    d_k = [Dep() for _ in range(NT)]

    def evac(c, m, t0, n, pst, dps):
        ct = (c - OFF_NA) // 128
        tts = range(t0 // 128, (t0 + n) // 128)
        if ct < 2:
            k.act(lambda e: e.activation(out=qT[:, ct, t0:t0 + n], in_=pst, func=AF.Copy, scale=0.125),
                  r=[dps], w=[d_q[t] for t in tts])
        else:
            k.dve(lambda e: e.tensor_copy(out=kT[:, ct - 2, t0:t0 + n], in_=pst), r=[dps], w=[d_k[t] for t in tts])
    _proj(self, l, [(OFF_NA, 512)], blocks, evac, PROJ_BANKS)
    Ve = self.aa([128, NT, 4, 65], BF16)
    Vo = self.aa([128, 15, 4, 65], BF16)
    d_Ve, d_Vo = Dep(), Dep()
    k.pool(lambda e: e.memset(Ve, 1.0), w=[d_Ve])
    k.pool(lambda e: e.memset(Vo, 1.0), w=[d_Vo])
    i = self.wring_i
    self.wring_i = (i + 1) % len(self.wring)
    wv, dwv, chv = self.wring[i]
    k.dma("pool", wv[:, :, 0:256], self.W("w_in")[l, :, OFF_NA + 512:OFF_NA + 768].rearrange("(kt p) c -> p kt c", p=128),
          chv, w=[dwv])
    nb = 0
    for (Vx, dV, ntl, off) in ((Ve, d_Ve, NT, 0), (Vo, d_Vo, 15, 64)):
        for j in range(ntl):
            pi, hb = PROJ_BANKS[nb % 4]
            nb += 1
            pst = self.ps[pi][:, hb * 512: hb * 512 + 256]
            dps = self.d_ps[pi][hb]
            a = off + j * 128
            hd = [self.d_H[t] for t in range(a // 128, (a + 255) // 128)]
            for dt in range(8):
                k.pe(lambda e, dt=dt, pst=pst, a=a: e.matmul(pst, self.H[:, dt, a:a + 128], wv[:, dt, 0:256],
                                                             start=(dt == 0), stop=(dt == 7)), r=[dwv] + hd, w=[dps])
            k.act(lambda e, Vx=Vx, j=j, pst=pst: e.activation(out=Vx[:, j, :, 0:64], in_=pst.rearrange("p (h d) -> p h d", h=4),
                                                              func=AF.Copy), r=[dps], w=[dV])
    T2 = self.aa([128, 4, 14, 64])
    msk = self.aa([128, 64])
    d_T2 = Dep()
    d_msk = d_T2
    if not hasattr(self, "na_rpbg_in"):
        self.na_rpbg_in = self.din("na_rpbg", [DEPTH, 128, 4 * 14 * 64])
        self.na_mask_in = self.din("na_mask", [128, 64])
        self.ch_na = self.k.chan()
    k.dma("sp", T2.rearrange("p a b c -> p (a b c)"), self.na_rpbg_in[l], self.ch_na, w=[d_T2])
    k.dma("sp", msk, self.na_mask_in[:, :], self.ch_na, w=[d_msk])
    k.dve(lambda e: e.tensor_tensor(out=T2.rearrange("p a b c -> p (a b) c"), in0=T2.rearrange("p a b c -> p (a b) c"),
                                    in1=msk.unsqueeze(1).to_broadcast([128, 56, 64]), op=ALU.add),
          r=[d_T2, d_msk], w=[d_T2])
    Sb = [self.aa([128, 4, 64]) for _ in range(2)]
    d_Sb = [Dep(), Dep()]
    E = [self.aa([128, 6, 64], BF16) for _ in range(3)]
    d_E = [Dep() for _ in range(3)]
    rs = [self.aa([64, 4]) for _ in range(2)]
    d_rs = [Dep(), Dep()]
    On = [self.aa([64, 256], BF16) for _ in range(2)]
    d_On = [Dep(), Dep()]
    SB = [(2, 0), (2, 1), (3, 0), (3, 1)]
    OB = [(0, 0), (0, 1)]
    TB = (1, 0)
    psT = self.ps[TB[0]][:, TB[1] * 512: TB[1] * 512 + 512].bitcast(BF16)
    d_psT = self.d_ps[TB[0]][TB[1]]
    n = 0
    for r in range(32):
        s = min(max(r - 4, 0), 24)
        base = s - r + 7
        opi, ohb = OB[r % 2]
        O_ps = self.ps[opi][0:64, ohb * 512: ohb * 512 + 260]
        d_O = self.d_ps[opi][ohb]
        tq = (64 * r) // 128
        for h in range(4):
            hp = slice(64 * (h % 2), 64 * (h % 2) + 64)
            hc = h // 2
            spi, shb = SB[n % 4]
            S_ps = self.ps[spi][:, shb * 512: shb * 512 + 384]
            d_S = self.d_ps[spi][shb]
            for kt in range(6):
                ks = 64 * s + 128 * kt if kt < 4 else L + 128 * (kt - 4)
                kd = [d_k[t] for t in range(ks // 128, (ks + 255) // 128)]
                k.pe(lambda e, kt=kt, ks=ks, S_ps=S_ps, hp=hp, hc=hc, r=r: e.matmul(
                    S_ps[:, kt * 64:(kt + 1) * 64], kT[hp, hc, ks:ks + 128], qT[hp, hc, 64 * r:64 * r + 64],
                    start=True, stop=True), r=kd + [d_q[tq]], w=[d_S])
            sb_i = n % 2
            e_i = n % 3
            k.dve(lambda e, sb_i=sb_i, S_ps=S_ps, h=h, base=base: e.tensor_tensor(
                out=Sb[sb_i], in0=S_ps[:, 0:256].rearrange("p (a b) -> p a b", a=4),
                in1=T2[:, h, base:base + 7:2, :], op=ALU.add), r=[d_S, d_T2], w=[d_Sb[sb_i]])
            k.act(lambda e, sb_i=sb_i, e_i=e_i: e.activation(out=E[e_i][:, 0:4, :], in_=Sb[sb_i], func=AF.Exp),
                  r=[d_Sb[sb_i]], w=[d_E[e_i]])
            k.act(lambda e, e_i=e_i, S_ps=S_ps: e.activation(out=E[e_i][:, 4:6, :],
                                                             in_=S_ps[:, 256:384].rearrange("p (a b) -> p a b", a=2),
                                                             func=AF.Exp), r=[d_S], w=[d_E[e_i]])
            for kt in range(6):
                if kt < 4:
                    if s % 2 == 0:
                        vt, dv = Ve[:, s // 2 + kt, h, :], d_Ve
                    else:
                        vt, dv = Vo[:, (s - 1) // 2 + kt, h, :], d_Vo
                else:
                    vt, dv = Ve[:, 16 + kt - 4, h, :], d_Ve
                k.pe(lambda e, kt=kt, vt=vt, e_i=e_i, O_ps=O_ps, h=h: e.matmul(
                    O_ps[:, h * 65:(h + 1) * 65], E[e_i][:, kt, :], vt, start=(kt == 0), stop=(kt == 5)),
                    r=[d_E[e_i], dv], w=[d_O])
            n += 1
        j = r % 2
        O3 = O_ps.rearrange("p (h d) -> p h d", h=4)
        k.dve(lambda e, j=j, O3=O3: e.reciprocal(out=rs[j], in_=O3[:, :, 64]), r=[d_O], w=[d_rs[j]])
        k.dve(lambda e, j=j, O3=O3: e.tensor_tensor(out=On[j].rearrange("p (h d) -> p h d", h=4), in0=O3[:, :, 0:64],
                                                    in1=rs[j].unsqueeze(2).to_broadcast([64, 4, 64]), op=ALU.mult),
              r=[d_O, d_rs[j]], w=[d_On[j]])
        for hc in range(2):
            k.pe(lambda e, j=j, hc=hc: e.transpose(out=psT[:, hc * 64:(hc + 1) * 64], in_=On[j][:, hc * 128:(hc + 1) * 128],
                                                   identity=self.ident_b[0:64, 0:64]), r=[d_On[j], self.d_const], w=[d_psT])
        k.act(lambda e, r=r: e.activation(out=self.C[:, 6:8, 64 * r:64 * r + 64],
                                          in_=psT[:, 0:128].rearrange("p (a b) -> p a b", a=2), func=AF.Copy),
              r=[d_psT], w=[self.d_C[tq]])
    if with_ctx:
        Ec = self.aa([128, 2, 256], BF16)
        d_Ec = Dep()
        Onc = self.aa([128, 256], BF16)
        d_Onc = Dep()
        rsc = self.aa([128, 4])
        d_rsc = Dep()
        for qt in range(2):
            opi, ohb = OB[qt % 2]
            O_ps = self.ps[opi][:, ohb * 512: ohb * 512 + 260]
            d_O = self.d_ps[opi][ohb]
            for h in range(4):
                hp = slice(64 * (h % 2), 64 * (h % 2) + 64)
                hc = h // 2
                spi, shb = SB[n % 4]
                n += 1
                S_ps = self.ps[spi][:, shb * 512: shb * 512 + 256]
                d_S = self.d_ps[spi][shb]
                for c in range(2):
                    k.pe(lambda e, c=c, S_ps=S_ps, hp=hp, hc=hc, qt=qt: e.matmul(
                        S_ps[:, c * 128:(c + 1) * 128], kT[hp, hc, L + 128 * c:L + 128 * c + 128],
                        qT[hp, hc, L + 128 * qt:L + 128 * qt + 128], start=True, stop=True),
                        r=[d_k[16 + c], d_q[16 + qt]], w=[d_S])
                k.act(lambda e, S_ps=S_ps: e.activation(out=Ec[:, :, 0:128], in_=S_ps.rearrange("p (a b) -> p a b", a=2),
                                                        func=AF.Exp), r=[d_S], w=[d_Ec])
                for c in range(2):
                    k.pe(lambda e, c=c, O_ps=O_ps, h=h: e.matmul(O_ps[:, h * 65:(h + 1) * 65], Ec[:, c, 0:128],
                                                                 Ve[:, 16 + c, h, :], start=(c == 0), stop=(c == 1)),
                         r=[d_Ec, d_Ve], w=[d_O])
            O3 = O_ps.rearrange("p (h d) -> p h d", h=4)
            k.dve(lambda e, O3=O3: e.reciprocal(out=rsc, in_=O3[:, :, 64]), r=[d_O], w=[d_rsc])
            k.dve(lambda e, O3=O3: e.tensor_tensor(out=Onc.rearrange("p (h d) -> p h d", h=4), in0=O3[:, :, 0:64],
                                                   in1=rsc.unsqueeze(2).to_broadcast([128, 4, 64]), op=ALU.mult),
                  r=[d_O, d_rsc], w=[d_Onc])
            for hc in range(2):
                k.pe(lambda e, hc=hc: e.transpose(out=psT[:, hc * 128:(hc + 1) * 128], in_=Onc[:, hc * 128:(hc + 1) * 128],
                                                  identity=self.ident_b[:]), r=[d_Onc, self.d_const], w=[d_psT])
            k.act(lambda e, qt=qt: e.activation(out=self.C[:, 6:8, L + 128 * qt:L + 128 * qt + 128],
                                                in_=psT[:, 0:256].rearrange("p (a b) -> p a b", a=2), func=AF.Copy),
                  r=[d_psT], w=[self.d_C[16 + qt]])


import math
HY_EMB = 33
HY_BANDS = 16


def hy_consts():
    bf = ml_dtypes.bfloat16
    out = {}
    max_decay = math.log(1e-2) / 0.3
    min_decay = math.log(1e-2) / 1.5
    deltas = np.abs(np.linspace(min_decay, max_decay, 256, dtype=np.float32))
    for tag, Ls in (("lat", L), ("ctx", LC)):
        nt = Ls // 128
        t = np.linspace(0.0, 1.0, Ls, dtype=np.float32)[:, None]
        bands = np.linspace(1e-4, HY_BANDS - 1, HY_BANDS, dtype=np.float32)
        ang = (np.float32(2.0 * math.pi / Ls)) * np.arange(Ls, dtype=np.float32)[:, None] * bands
        z = np.concatenate([t, np.cos(ang), -np.sin(ang)], axis=-1).astype(np.float32)
        out["hy_zT_" + tag] = np.ascontiguousarray(z.T)
        dec = np.exp(-t * deltas[None, :]).astype(np.float32)
        out["hy_dec_" + tag] = np.ascontiguousarray(dec.reshape(nt, 128, 256).transpose(1, 0, 2).reshape(128, nt * 256))
        N = 2 * Ls
        tt_ = np.arange(Ls, dtype=np.int64)
        ff = np.arange(Ls, dtype=np.int64)
        m = ((2 * ff[None, :] + 1) * tt_[:, None]) % (2 * N)
        th = m.astype(np.float64) * (math.pi / N)
        Cm = np.cos(th)
        Sm = -np.sin(th)
        for nm, M_ in (("C", Cm), ("S", Sm)):
            f4 = M_.reshape(nt, 128, nt, 128).transpose(2, 1, 0, 3).reshape(nt, 128, nt * 128)
            out["hy_%sf_%s" % (nm, tag)] = np.ascontiguousarray(f4).astype(bf)
            Wd = min(512, Ls)
            ntb = Ls // Wd
            i4 = M_.reshape(ntb, Wd, nt, 128).transpose(0, 3, 2, 1).reshape(ntb, 128, nt * Wd)
            out["hy_%si_%s" % (nm, tag)] = np.ascontiguousarray(i4).astype(bf)
    return out


def _arena_rewind(self, mark):
    self.k.barrier()
    self.ar_off = mark


def _hy_filters(self, l, Ls, tag, WHx, d_WHx):
    k = self.k
    nt = Ls // 128
    BW = min(512, Ls)
    nb = Ls // BW
    zin = self.din("hy_zT_" + tag, [HY_EMB, Ls]) if ("hy_zT_" + tag) not in self.inputs else self._hyin["hy_zT_" + tag]
    din_dec = self.din("hy_dec_" + tag, [128, nt * 256]) if ("hy_dec_" + tag) not in self.inputs else self._hyin["hy_dec_" + tag]
    self._hyin["hy_zT_" + tag] = zin
    self._hyin["hy_dec_" + tag] = din_dec
    zT = self.aa([HY_EMB, Ls])
    dec = self.aa([128, nt, 256])
    d_z = Dep()
    d_dec = d_z
    k.dma("sp", zT, zin[:, :], self.ch_hy, w=[d_z])
    k.dma("sp", dec.rearrange("p a b -> p (a b)"), din_dec[:, :], self.ch_hy, w=[d_dec])
    hb = [self.aa([64, Ls]) for _ in range(2)]
    d_hb = [[Dep() for _ in range(nb)] for _ in range(2)]
    vs = [self.aa([64, 512]) for _ in range(2)]
    kis = [self.aa([64, 512], I32) for _ in range(2)]
    kfs = [self.aa([64, 512]) for _ in range(2)]
    d_vs, d_kis, d_kfs = [Dep(), Dep()], [Dep(), Dep()], [Dep(), Dep()]
    fb = self.aa([64, 3])
    d_fb = Dep()
    fr = _ppv(self, l, "hy_f", 64)
    bb = _ppv(self, l, "hy_b", 64)
    k.dve(lambda e: e.tensor_tensor(out=fb, in0=fr, in1=bb, op=ALU.mult), r=[self.d_pp], w=[d_fb])
    ws = [_ppv(self, l, "hy_w1", HY_EMB), _ppv(self, l, "hy_w2", 64), _ppv(self, l, "hy_w3", 64)]
    PB = [(2, 0), (2, 1)]
    nps = 0
    src, d_srcs = zT, [d_z] * nb
    for li in range(3):
        dst, d_dsts = hb[li % 2], d_hb[li % 2]
        for b in range(nb):
            v, ki, kf = vs[b % 2], kis[b % 2], kfs[b % 2]
            d_v, d_ki, d_kf = d_vs[b % 2], d_kis[b % 2], d_kfs[b % 2]
            d_src, d_dst = d_srcs[b], d_dsts[b]
            pi, hbk = PB[nps % 2]
            nps += 1
            pst = self.ps[pi][0:64, hbk * 512: hbk * 512 + BW]
            dps = self.d_ps[pi][hbk]
            k.pe(lambda e, li=li, b=b, pst=pst, src=src: e.matmul(pst, ws[li], src[:, b * BW:(b + 1) * BW], start=True, stop=True),
                 r=[self.d_pp, d_src], w=[dps])
            k.dve(lambda e, li=li, pst=pst: e.tensor_scalar(out=v[:, 0:BW], in0=pst, scalar1=fr[:, li:li + 1], scalar2=fb[:, li:li + 1],
                                                            op0=ALU.mult, op1=ALU.add), r=[dps, d_fb, self.d_pp], w=[d_v])
            k.dve(lambda e: e.tensor_scalar(out=ki[:, 0:BW], in0=v[:, 0:BW], scalar1=1.0 / (2.0 * math.pi), scalar2=None, op0=ALU.mult),
                  r=[d_v], w=[d_ki])
            k.dve(lambda e: e.tensor_copy(out=kf[:, 0:BW], in_=ki[:, 0:BW]), r=[d_ki], w=[d_kf])
            k.dve(lambda e: e.scalar_tensor_tensor(out=v[:, 0:BW], in0=kf[:, 0:BW], scalar=-2.0 * math.pi, in1=v[:, 0:BW],
                                                   op0=ALU.mult, op1=ALU.add), r=[d_kf, d_v], w=[d_v])
            k.dve(lambda e: e.tensor_scalar(out=v[:, 0:BW], in0=v[:, 0:BW], scalar1=3.1415925, scalar2=-3.1415925,
                                            op0=ALU.min, op1=ALU.max), r=[d_v], w=[d_v])
            k.act(lambda e, dst=dst, b=b: e.activation(out=dst[:, b * BW:(b + 1) * BW], in_=v[:, 0:BW], func=AF.Sin),
                  r=[d_v], w=[d_dst])
        src, d_srcs = dst, d_dsts
    h3, d_h3s = src, d_srcs
    w4 = _ppv(self, l, "hy_w4", 64)
    hd = [self.aa([128, 2, 256]) for _ in range(2)]
    d_hd = [Dep(), Dep()]
    ab = [self.aa([128, 512]) for _ in range(2)]
    d_ab = [Dep(), Dep()]
    nrm_ps = self.ps[3][:, 0:512]
    d_nrm = self.d_ps[3][0]
    rn = self.aa([128, 256])
    d_rn = Dep()
    for pss in range(2):
        for tt in range(nt):
            pi, hbk = PB[nps % 2]
            nps += 1
            pst = self.ps[pi][:, hbk * 512: hbk * 512 + 512]
            dps = self.d_ps[pi][hbk]
            k.pe(lambda e, tt=tt, pst=pst: e.matmul(pst, h3[:, tt * 128:(tt + 1) * 128], w4, start=True, stop=True),
                 r=[d_h3s[(tt * 128) // BW], self.d_pp], w=[dps])
            j = tt % 2
            k.dve(lambda e, j=j, tt=tt, pst=pst: e.tensor_tensor(out=hd[j], in0=pst.rearrange("p (a b) -> p a b", a=2),
                                                                 in1=dec[:, tt, :].unsqueeze(1).to_broadcast([128, 2, 256]),
                                                                 op=ALU.mult), r=[dps, d_dec], w=[d_hd[j]])
            if pss == 0:
                k.act(lambda e, j=j: e.activation(out=ab[j], in_=hd[j].rearrange("p a b -> p (a b)"), func=AF.Abs),
                      r=[d_hd[j]], w=[d_ab[j]])
                k.pe(lambda e, j=j, tt=tt: e.matmul(nrm_ps, self.ones_f[:], ab[j], start=(tt == 0), stop=(tt == nt - 1)),
                     r=[d_ab[j], self.d_const], w=[d_nrm])
            else:
                if tt == 0:
                    k.dve(lambda e, j=j: e.memset(hd[j][0:1, 1, :], 0.0), r=[d_hd[j]], w=[d_hd[j]])
                k.dve(lambda e, j=j: e.tensor_tensor(out=hd[j], in0=hd[j], in1=rn.unsqueeze(1).to_broadcast([128, 2, 256]),
                                                     op=ALU.mult), r=[d_hd[j], d_rn], w=[d_hd[j]])
                k.dve(lambda e, j=j, tt=tt: e.tensor_tensor(out=WHx[:, tt, 1, :], in0=hd[j][:, 0, :], in1=hd[j][:, 1, :], op=ALU.add),
                      r=[d_hd[j]], w=[d_WHx])
                k.pool(lambda e, j=j, tt=tt: e.tensor_tensor(out=WHx[:, tt, 2, :], in0=hd[j][:, 0, :], in1=hd[j][:, 1, :],
                                                             op=ALU.subtract), r=[d_hd[j]], w=[d_WHx])
        if pss == 0:
            k.dve(lambda e: e.tensor_copy(out=rn, in_=nrm_ps[:, 0:256]), r=[d_nrm], w=[d_rn])
            k.dve(lambda e: e.tensor_tensor(out=rn, in0=rn, in1=nrm_ps[:, 256:512], op=ALU.add), r=[d_nrm, d_rn], w=[d_rn])
            k.dve(lambda e: e.reciprocal(out=rn, in_=rn), r=[d_rn], w=[d_rn])
            k.dve(lambda e: e.tensor_scalar(out=rn, in0=rn, scalar1=2.0 / (2 * Ls), scalar2=None, op0=ALU.mult), r=[d_rn], w=[d_rn])


def _hy_dft(self, l, Ls, tag, toff, WHx, d_WHx, x0T, wTm, d_x0w, Yh, ring):
    k = self.k
    nt = Ls // 128
    Wd = min(512, Ls)
    ntb = Ls // Wd
    names = ["hy_Cf_", "hy_Sf_", "hy_Ci_", "hy_Si_"]
    tabs = []
    for nm in names:
        key = nm + tag
        if key not in self._hyin:
            shp = [nt, 128, nt * 128] if nm[4] == "f" else [ntb, 128, nt * Wd]
            self._hyin[key] = self.din(key, shp, BF16)
        tabs.append(self._hyin[key])
    Cf, Sf, Ci, Si = tabs
    d_Yh = Dep()
    Kr = self.aa([128, 4, 256])
    d_K = [Dep(), Dep()]
    tm = [self.aa([128, 256]) for _ in range(4)]
    d_tm = [Dep() for _ in range(4)]
    hybias = _ppv(self, l, "hybias")
    for j in range(nt):
        slot, dsl, chs = ring[self.hyring_i % len(ring)]
        self.hyring_i += 1
        cst = slot[:, 0:nt * 128].rearrange("p (a b) -> p a b", b=128)
        sst = slot[:, 2048:2048 + nt * 128].rearrange("p (a b) -> p a b", b=128)
        k.dma("sp", slot[:, 0:nt * 128], Cf[j], chs, w=[dsl])
        k.dma("sp", slot[:, 2048:2048 + nt * 128], Sf[j], chs, w=[dsl])
        ps_r = self.ps[0][:, 0:512]
        ps_i = self.ps[0][:, 512:1024]
        for tt in range(nt):
            k.pe(lambda e, tt=tt, cst=cst: e.matmul(ps_r, cst[:, tt, :], WHx[:, tt, 0:2, :], start=(tt == 0), stop=(tt == nt - 1)),
                 r=[dsl, d_WHx], w=[self.d_ps[0][0]])
        for tt in range(nt):
            k.pe(lambda e, tt=tt, sst=sst: e.matmul(ps_i, sst[:, tt, :], WHx[:, tt, 0:3:2, :], start=(tt == 0), stop=(tt == nt - 1)),
                 r=[dsl, d_WHx], w=[self.d_ps[0][1]])
        k.act(lambda e: e.activation(out=Kr[:, 0:2, :], in_=ps_r.rearrange("p (a b) -> p a b", a=2), func=AF.Copy),
              r=[self.d_ps[0][0]], w=[d_K[0]])
        k.act(lambda e: e.activation(out=Kr[:, 2:4, :], in_=ps_i.rearrange("p (a b) -> p a b", a=2), func=AF.Copy),
              r=[self.d_ps[0][1]], w=[d_K[1]])
        Ur, Kre, Ui, Kie = Kr[:, 0, :], Kr[:, 1, :], Kr[:, 2, :], Kr[:, 3, :]
        k.dve(lambda e: e.tensor_tensor(out=tm[0], in0=Ur, in1=Kre, op=ALU.mult), r=[d_K[0]], w=[d_tm[0]])
        k.pool(lambda e: e.tensor_tensor(out=tm[1], in0=Ui, in1=Kie, op=ALU.mult), r=[d_K[1]], w=[d_tm[1]])
        k.dve(lambda e, j=j: e.tensor_tensor(out=Yh[:, j, 0, :], in0=tm[0], in1=tm[1], op=ALU.subtract),
              r=[d_tm[0], d_tm[1]], w=[d_Yh])
        k.pool(lambda e: e.tensor_tensor(out=tm[2], in0=Ur, in1=Kie, op=ALU.mult), r=d_K, w=[d_tm[2]])
        k.dve(lambda e: e.tensor_tensor(out=tm[3], in0=Ui, in1=Kre, op=ALU.mult), r=d_K, w=[d_tm[3]])
        k.pool(lambda e, j=j: e.tensor_tensor(out=Yh[:, j, 1, :], in0=tm[2], in1=tm[3], op=ALU.add),
               r=[d_tm[2], d_tm[3]], w=[d_Yh])
    G = 4 if nt >= 4 else nt
    yt = [self.aa([128, 512]) for _ in range(2)]
    d_yt = [Dep(), Dep()]
    for tb in range(ntb):
        for g in range(nt // G):
            slot, dsl, chs = ring[self.hyring_i % len(ring)]
            self.hyring_i += 1
            k.dma("sp", slot[:, 0:G * Wd], Ci[tb, :, g * G * Wd:(g + 1) * G * Wd], chs, w=[dsl])
            k.dma("sp", slot[:, 2048:2048 + G * Wd], Si[tb, :, g * G * Wd:(g + 1) * G * Wd], chs, w=[dsl])
            cst = slot[:, 0:G * Wd].rearrange("p (a b) -> p a b", b=Wd)
            sst = slot[:, 2048:2048 + G * Wd].rearrange("p (a b) -> p a b", b=Wd)
            for fi in range(G):
                ft = g * G + fi
                for ct in range(2):
                    py = self.ps[1][:, ct * 512: ct * 512 + Wd]
                    k.pe(lambda e, fi=fi, ft=ft, ct=ct, py=py, cst=cst: e.matmul(py, Yh[:, ft, 0, ct * 128:(ct + 1) * 128], cst[:, fi, :],
                                                                               start=(ft == 0), stop=False),
                         r=[dsl, d_Yh], w=[self.d_ps[1][ct]])
                    k.pe(lambda e, fi=fi, ft=ft, ct=ct, py=py, sst=sst: e.matmul(py, Yh[:, ft, 1, ct * 128:(ct + 1) * 128], sst[:, fi, :],
                                                                               start=False, stop=(ft == nt - 1)),
                         r=[dsl, d_Yh], w=[self.d_ps[1][ct]])
        a = toff + tb * Wd
        tts = [self.d_C[t] for t in range(a // 128, (a + Wd) // 128)]
        for ct in range(2):
            py = self.ps[1][:, ct * 512: ct * 512 + Wd]
            k.dve(lambda e, ct=ct, py=py, a=a: e.scalar_tensor_tensor(out=yt[ct][:, 0:Wd], in0=wTm[:, ct, a:a + Wd], scalar=hybias[:, ct:ct + 1],
                                                                      in1=py, op0=ALU.mult, op1=ALU.add),
                  r=[self.d_ps[1][ct], d_x0w, self.d_pp], w=[d_yt[ct]])
            k.pool(lambda e, ct=ct, a=a: e.tensor_tensor(out=self.C[:, 4 + ct, a:a + Wd], in0=yt[ct][:, 0:Wd], in1=x0T[:, ct, a:a + Wd],
                                                         op=ALU.mult), r=[d_yt[ct], d_x0w], w=tts)


def _mix_hy(self, l, with_ctx):
    k = self.k
    self.arena_reset()
    if not hasattr(self, "_hyin"):
        self._hyin = {}
        self.ch_hy = k.chan()
        self.ch_hyring = [k.chan() for _ in range(3)]
    ntok = T if with_ctx else L
    WH = self.aa([128, 16, 3, 256], BF16)
    d_WH = Dep()
    if with_ctx:
        WHc = self.aa([128, 2, 3, 256], BF16)
        d_WHc = Dep()
    x0T = self.aa([128, 2, T], BF16)
    wTm = self.aa([128, 2, T], BF16)
    d_x0w = Dep()
    markA = self.ar_off
    _hy_filters(self, l, L, "lat", WH, d_WH)
    if with_ctx:
        _arena_rewind(self, markA)
        _hy_filters(self, l, LC, "ctx", WHc, d_WHc)
    _arena_rewind(self, markA)
    _wring_init(self)
    ranges = [(0, L)] + ([(L, T)] if with_ctx else [])
    blocks = _blocks(self, with_ctx)
    pin = [self.aa([128, T]) for _ in range(1)]
    d_pin = [Dep()]
    x1v = self.aa([128, 4, T], BF16)
    d_x1v = [Dep() for _ in range(4)]
    cacc = [self.aa([128, T]) for _ in range(1)]
    d_cacc = Dep()
    hyw = _ppv(self, l, "hyw")

    def evac(c, m, t0, n, pst, dps):
        ct = (c - OFF_HY) // 128
        j = 0
        k.act(lambda e: e.activation(out=pin[j][:, t0:t0 + n], in_=pst, func=AF.Copy), r=[dps], w=[d_pin[j]])
        if (t0, n) == blocks[-1]:
            _dwconv(self, "dve", cacc[0], pin[j], hyw[:, ct * 3:(ct + 1) * 3], ranges, [d_pin[j], self.d_pp], [d_cacc])
            if ct < 2:
                k.pool(lambda e: e.tensor_copy(out=x0T[:, ct, 0:ntok], in_=cacc[0][:, 0:ntok]), r=[d_cacc], w=[d_x0w])
            else:
                k.pool(lambda e: e.tensor_copy(out=x1v[:, ct - 2, 0:ntok], in_=cacc[0][:, 0:ntok]), r=[d_cacc], w=[d_x1v[ct - 2]])
    _proj(self, l, [(OFF_HY, 512), (OFF_HY + 512, 256)], blocks, evac, PROJ_BANKS)
    for ct in range(2):
        k.pool(lambda e, ct=ct: e.tensor_tensor(out=wTm[:, ct, 0:ntok], in0=x1v[:, ct, 0:ntok], in1=x1v[:, 2 + ct, 0:ntok], op=ALU.mult),
               r=[d_x1v[ct], d_x1v[2 + ct]], w=[d_x0w])
    psT = self.ps[2][:, 0:512].bitcast(BF16)
    d_psT = self.d_ps[2][0]
    for tt in range(ntok // 128):
        for ct in range(2):
            k.pe(lambda e, tt=tt, ct=ct: e.transpose(out=psT[:, ct * 128:(ct + 1) * 128], in_=wTm[:, ct, tt * 128:(tt + 1) * 128],
                                                     identity=self.ident_b[:]), r=[d_x0w, self.d_const], w=[d_psT])
        if tt < 16:
            k.act(lambda e, tt=tt: e.activation(out=WH[:, tt, 0, :], in_=psT[:, 0:256], func=AF.Copy), r=[d_psT], w=[d_WH])
        else:
            k.act(lambda e, tt=tt: e.activation(out=WHc[:, tt - 16, 0, :], in_=psT[:, 0:256], func=AF.Copy), r=[d_psT], w=[d_WHc])
    _arena_rewind(self, markA)
    Yh = self.aa([128, 16, 2, 256], BF16)
    ring = [(self.aa([128, 4096], BF16), Dep(), self.ch_hyring[i]) for i in range(3)]
    self.hyring_i = 0
    markD = self.ar_off
    _hy_dft(self, l, L, "lat", 0, WH, d_WH, x0T, wTm, d_x0w, Yh, ring)
    if with_ctx:
        _arena_rewind(self, markD)
        _hy_dft(self, l, LC, "ctx", L, WHc, d_WHc, x0T, wTm, d_x0w, Yh, ring)


def gdn_consts():
    out = {}
    m = np.arange(128)[:, None]
    i = np.arange(128)[None, :]
    LE = (m <= i).astype(np.float32)
    GE = (m >= i).astype(np.float32)
    GT = (m > i).astype(np.float32)
    LT = (m < i).astype(np.float32)
    NEGF = np.tile(np.where(i > m, -1e9, 0.0).astype(np.float32), (1, 4))
    NEGB = np.tile(np.where(i < m, -1e9, 0.0).astype(np.float32), (1, 4))
    out["gdn_gm"] = np.ascontiguousarray(np.concatenate([LE, GE, GT, LT, NEGF, NEGB], axis=1))
    pm = np.zeros((128, 128), np.float32)
    for mm in range(128):
        if (mm % 64) < 32:
            pm[mm + 32, mm] = -1.0
        else:
            pm[mm - 32, mm] = 1.0
    out["gdn_pm"] = pm
    ii = np.arange(128)[:, None]
    jj = np.arange(128)[None, :]
    lms, ums = [], []
    for s_ in range(7):
        b_ = 1 << s_
        same2 = (ii // (2 * b_)) == (jj // (2 * b_))
        diff1 = (ii // b_) != (jj // b_)
        lms.append(np.where(same2 & diff1 & (jj < ii), -1.0, 0.0))
        ums.append(np.where(same2 & diff1 & (jj > ii), -1.0, 0.0))
    out["gdn_lm"] = np.ascontiguousarray(np.concatenate(lms + ums, axis=1)).astype(ml_dtypes.bfloat16)
    pos = np.arange(L)
    row = (pos // 64).astype(np.float32)
    col = (pos % 64).astype(np.float32)
    inv = (10000.0 ** (-np.arange(16, dtype=np.float32) / 16)).astype(np.float32)
    ang = np.concatenate([row[:, None] * inv, col[:, None] * inv], axis=-1)
    idx = (np.arange(128) % 64) % 32
    cs = np.stack([np.cos(ang).T[idx], np.sin(ang).T[idx]], axis=1)
    out["gdn_rope"] = np.ascontiguousarray(cs.reshape(128, 2 * L)).astype(ml_dtypes.bfloat16)
    return out


def _mix_gdn(self, l, with_ctx):
    k = self.k
    self.arena_reset()
    if not hasattr(self, "gdn_gm_in"):
        self.gdn_gm_in = self.din("gdn_gm", [128, 1536])
        self.gdn_pm_in = self.din("gdn_pm", [128, 128])
        self.gdn_rope_in = self.din("gdn_rope", [128, 2 * L], BF16)
        self.ch_gdn = k.chan()
    otok = self.aa([128, NT, 256], BF16)
    markO = self.ar_off
    qT = self.aa([128, 2, T], BF16)
    kT = self.aa([128, 2, T], BF16)
    d_qk = Dep()
    vtok = self.aa([128, NT, 256], BF16)
    ktok = self.aa([128, NT, 256], BF16)
    d_vtok, d_ktok = Dep(), Dep()
    la = self.aa([128, NT, 8])
    beta = self.aa([128, NT, 8])
    d_la, d_beta = Dep(), Dep()
    GM = self.aa([128, 1536])
    d_GM = Dep()
    d_LMK = d_GM
    d_c1 = d_GM
    k.dma("sp", GM, self.gdn_gm_in[:, :], self.ch_gdn, w=[d_GM])
    LE, GE, GT, LT = (GM[:, i * 128:(i + 1) * 128] for i in range(4))
    NEGF, NEGB = GM[:, 512:1024], GM[:, 1024:1536]
    LMK = self.aa([128, 14 * 128], BF16)
    if not hasattr(self, "gdn_lm_in"):
        self.gdn_lm_in = self.din("gdn_lm", [128, 14 * 128], BF16)
    k.dma("sp", LMK, self.gdn_lm_in[:, :], self.ch_gdn, w=[d_LMK])
    markP = self.ar_off
    _wring_init(self, 1)
    pm = self.aa([128, 128])
    rope = self.aa([128, 2, L], BF16)
    blk64 = self.aa([128, 128])
    d_blk = Dep()
    k.dma("sp", pm, self.gdn_pm_in[:, :], self.ch_gdn, w=[d_c1])
    k.dma("sp", rope.rearrange("p a b -> p (a b)"), self.gdn_rope_in[:, :], self.ch_gdn, w=[d_c1])
    k.pool(lambda e: e.memset(blk64, 0.0), w=[d_blk])
    k.pool(lambda e: e.memset(blk64[0:64, 0:64], 1.0), w=[d_blk])
    k.pool(lambda e: e.memset(blk64[64:128, 64:128], 1.0), w=[d_blk])
    wab, dwab, chab = self.wring[0]
    k.dma("pool", wab[:, :, 0:16], self.W("w_in")[l, :, 1024:1040].rearrange("(kt p) c -> p kt c", p=128), chab, w=[dwab])
    ab_ps = self.ps[3][:, 0:NT * 16]
    d_ab = self.d_ps[3][0]
    for tt in range(NT):
        for dt in range(8):
            k.pe(lambda e, tt=tt, dt=dt: e.matmul(ab_ps[:, tt * 16:(tt + 1) * 16], self.H[:, dt, tt * 128:(tt + 1) * 128],
                                                  wab[:, dt, 0:16], start=(dt == 0), stop=(dt == 7)),
                 r=[dwab, self.d_H[tt]], w=[d_ab])
    ab3 = ab_ps.rearrange("p (t c) -> p t c", c=16)
    xa = self.aa([128, NT, 8])
    ea = self.aa([128, 8])
    d_xa, d_ea = Dep(), Dep()
    k.dve(lambda e: e.tensor_tensor(out=xa, in0=ab3[:, :, 0:8], in1=_ppv(self, l, "dtb").unsqueeze(1).to_broadcast([128, NT, 8]),
                                    op=ALU.add), r=[d_ab, self.d_pp], w=[d_xa])
    k.act(lambda e: e.activation(out=xa, in_=xa, func=AF.Exp), r=[d_xa], w=[d_xa])
    k.act(lambda e: e.activation(out=xa, in_=xa, func=AF.Ln, bias=1.0, scale=1.0), r=[d_xa], w=[d_xa])
    k.act(lambda e: e.activation(out=ea, in_=_ppv(self, l, "alog"), func=AF.Exp), r=[self.d_pp], w=[d_ea])
    k.dve(lambda e: e.scalar_tensor_tensor(out=la, in0=xa, scalar=-1.0, in1=ea.unsqueeze(1).to_broadcast([128, NT, 8]),
                                           op0=ALU.mult, op1=ALU.mult), r=[d_xa, d_ea], w=[d_la])
    k.act(lambda e: e.activation(out=beta, in_=ab3[:, :, 8:16], func=AF.Sigmoid), r=[d_ab], w=[d_beta])
    blocks = _blocks(self, True)
    ranges = [(0, L), (L, T)]
    pin = self.aa([128, T])
    cacc = self.aa([128, T])
    d_pin, d_cacc = Dep(), Dep()
    vTt = self.aa([128, T], BF16)
    d_vTt = Dep()
    rv = self.aa([128, 512])
    t1 = self.aa([128, 512])
    t2 = self.aa([128, 512])
    d_rv, d_t1, d_t2 = Dep(), Dep(), Dep()
    gdw = _ppv(self, l, "gdw")
    psT = self.ps[2][:, 0:512].bitcast(BF16)
    d_psT = self.d_ps[2][0]

    def finish_tile(ct):
        _dwconv(self, "dve", cacc, pin, gdw[:, ct * 3:(ct + 1) * 3], ranges, [d_pin, self.d_pp], [d_cacc])
        if ct >= 4:
            k.act(lambda e: e.activation(out=vTt, in_=cacc, func=AF.Silu), r=[d_cacc], w=[d_vTt])
            for tt in range(NT):
                k.pe(lambda e, tt=tt: e.transpose(out=psT[:, 0:128], in_=vTt[:, tt * 128:(tt + 1) * 128], identity=self.ident_b[:]),
                     r=[d_vTt, self.d_const], w=[d_psT])
                k.dve(lambda e, tt=tt: e.tensor_copy(out=vtok[:, tt, (ct - 4) * 128:(ct - 3) * 128], in_=psT[:, 0:128]),
                      r=[d_psT], w=[d_vtok])
            return
        isq = ct < 2
        dst = qT if isq else kT
        cc = ct % 2
        k.act(lambda e: e.activation(out=pin, in_=cacc, func=AF.Silu), r=[d_cacc, d_pin], w=[d_pin])
        k.act(lambda e: e.activation(out=cacc, in_=pin, func=AF.Square), r=[d_pin, d_cacc], w=[d_cacc])
        for (t0, n) in blocks:
            ss = self.ps[3][:, 512:512 + n]
            dss = self.d_ps[3][1]
            k.pe(lambda e, t0=t0, n=n, ss=ss: e.matmul(ss, blk64, cacc[:, t0:t0 + n], start=True, stop=True), r=[d_cacc, d_blk], w=[dss])
            sc_, bi_ = (64.0, 64.0 * EPS) if isq else (1.0, EPS)
            k.act(lambda e, n=n, ss=ss: e.activation(out=rv[:, 0:n], in_=ss, func=AF.Sqrt, bias=bi_, scale=sc_), r=[dss], w=[d_rv])
            k.dve(lambda e, n=n: e.reciprocal(out=rv[:, 0:n], in_=rv[:, 0:n]), r=[d_rv], w=[d_rv])
            k.dve(lambda e, t0=t0, n=n: e.tensor_tensor(out=pin[:, t0:t0 + n], in0=pin[:, t0:t0 + n], in1=rv[:, 0:n], op=ALU.mult),
                  r=[d_rv, d_pin], w=[d_pin])
            if t0 < L:
                pv = self.ps[2][:, 512:512 + n]
                dpv = self.d_ps[2][1]
                k.pe(lambda e, t0=t0, n=n, pv=pv: e.matmul(pv, pm, pin[:, t0:t0 + n], start=True, stop=True), r=[d_pin, d_c1], w=[dpv])
                k.dve(lambda e, t0=t0, n=n: e.tensor_tensor(out=t1[:, 0:n], in0=pin[:, t0:t0 + n], in1=rope[:, 0, t0:t0 + n], op=ALU.mult),
                      r=[d_pin, d_c1], w=[d_t1])
                k.dve(lambda e, t0=t0, n=n, pv=pv: e.tensor_tensor(out=t2[:, 0:n], in0=pv, in1=rope[:, 1, t0:t0 + n], op=ALU.mult),
                      r=[dpv, d_c1], w=[d_t2])
                k.pool(lambda e, t0=t0, n=n: e.tensor_tensor(out=dst[:, cc, t0:t0 + n], in0=t1[:, 0:n], in1=t2[:, 0:n], op=ALU.add),
                       r=[d_t1, d_t2], w=[d_qk])
            else:
                k.pool(lambda e, t0=t0, n=n: e.tensor_copy(out=dst[:, cc, t0:t0 + n], in_=pin[:, t0:t0 + n]), r=[d_pin], w=[d_qk])
        if not isq:
            for tt in range(NT):
                k.pe(lambda e, tt=tt: e.transpose(out=psT[:, 0:128], in_=kT[:, cc, tt * 128:(tt + 1) * 128], identity=self.ident_b[:]),
                     r=[d_qk, self.d_const], w=[d_psT])
                k.dve(lambda e, tt=tt: e.tensor_copy(out=ktok[:, tt, cc * 128:(cc + 1) * 128], in_=psT[:, 0:128]),
                      r=[d_psT], w=[d_ktok])

    def evac(c, m, t0, n, pst, dps):
        ct = c // 128
        k.act(lambda e: e.activation(out=pin[:, t0:t0 + n], in_=pst, func=AF.Copy), r=[dps], w=[d_pin])
        if (t0, n) == blocks[-1]:
            finish_tile(ct)
    _proj(self, l, [(512, 256), (256, 256), (0, 256)], blocks, evac, PROJ_BANKS)
    import os
    _stop = os.environ.get("GDN_STOP", "")
    if "gdn_g1" in self.dbg:
        self.k.barrier()
        chd = k.chan()
        for nm, ap_, n_ in (("qT", qT.rearrange("p a b -> p (a b)"), 2 * T), ("kT", kT.rearrange("p a b -> p (a b)"), 2 * T),
                            ("vtok", vtok.rearrange("p a b -> p (a b)"), NT * 256), ("ktok", ktok.rearrange("p a b -> p (a b)"), NT * 256)):
            o_ = self.dout("dbg_" + nm, [128, n_], BF16)
            k.dma("sp", o_[:, :], ap_, chd)
        for nm, ap_ in (("la", la), ("beta", beta)):
            o_ = self.dout("dbg_" + nm, [128, NT * 8])
            k.dma("sp", o_[:, :], ap_.rearrange("p a b -> p (a b)"), chd)
        self.k.barrier()
    if _stop == "g1":
        return
    _arena_rewind(self, markP)
    d_otok = [Dep() for _ in range(NT)]
    hm = self.aa([128, 4])
    d_hm = Dep()
    k.pool(lambda e: e.memset(hm, 0.0), w=[d_hm])
    k.pool(lambda e: e.memset(hm[0:64, 0:4:2], 1.0), w=[d_hm])
    k.pool(lambda e: e.memset(hm[64:128, 1:4:2], 1.0), w=[d_hm])
    first_visit = [True] * NT
    orderF = [16, 17] + list(range(16))
    orderB = [17, 16] + list(range(15, -1, -1))
    W_ = {}
    WP = {}
    for dd in range(2):
        w = {}
        w["Z"] = self.aa([128, 4, 128]); w["D"] = self.aa([128, 4, 128], BF16); w["Ds"] = w["Z"]
        w["E"] = [self.aa([128, 4, 128], BF16) for _ in range(2)]
        w["ET"] = [self.aa([128, 4, 128], BF16) for _ in range(2)]
        w["d_E"] = [Dep(), Dep()]; w["d_ET"] = [Dep(), Dep()]
        w["A"] = [self.aa([128, 4, 128], BF16) for _ in range(2)]
        w["AT"] = [self.aa([128, 4, 128], BF16) for _ in range(2)]
        w["X"] = [self.aa([128, 4, 128], BF16) for _ in range(2)]
        w["QKm"] = self.aa([128, 4, 128], BF16)
        w["R"] = self.aa([128, 4, 128], BF16)
        w["d_R"] = Dep()
        WP.setdefault(dd, [])
        for par in range(2):
            wp = {"QKT": self.aa([128, 4, 128], BF16), "wT": self.aa([64, 4, 128], BF16), "kd": self.aa([128, 4, 128], BF16),
                  "ev": self.aa([128, 12]), "u": self.aa([128, 4, 64], BF16)}
            for nm in ("QKT", "wT", "kd", "ev", "u"):
                wp["d_" + nm] = Dep()
            WP[dd].append(wp)
        w["kc"] = self.aa([128, 2, 2, 128], BF16)
        w["d_kc"] = Dep()
        k.pool(lambda e, w=w: e.memset(w["kc"], 0.0), w=[w["d_kc"]])
        w["SbQ"] = self.aa([128, 4, 64], BF16)
        w["d_SbQ"] = Dep()
        k.pool(lambda e, w=w: e.memset(w["SbQ"], 0.0), w=[w["d_SbQ"]])
        w["beg"] = self.aa([128, 4])
        w["vn"] = self.aa([128, 4, 64], BF16)
        w["o2"] = self.aa([128, 4, 64], BF16)
        w["ot"] = self.aa([128, 4, 64], BF16)
        w["S"] = self.aa([128, 4, 64])
        w["Sb"] = self.aa([128, 4, 64], BF16)
        for nm in ("Z", "D", "Ds", "QKm", "beg", "vn", "o2", "ot", "S", "Sb"):
            w["d_" + nm] = Dep()
        w["d_Ds"] = w["d_Z"]
        w["d_A"] = [Dep(), Dep()]; w["d_AT"] = [Dep(), Dep()]; w["d_X"] = [Dep(), Dep()]
        k.dve(lambda e, w=w: e.memset(w["S"], 0.0), w=[w["d_S"]])
        k.pool(lambda e, w=w: e.memset(w["Sb"], 0.0), w=[w["d_Sb"]])
        W_[dd] = w

    def bank(dd, i, half=None):
        t = self.ps[2 * dd + i // 2]
        hb = i % 2
        return t[:, hb * 512:(hb + 1) * 512], self.d_ps[2 * dd + i // 2][hb]

    def unit_pre(dd, n, par):
        w = dict(W_[dd])
        w.update(WP[dd][par])
        c0 = n * 128
        lacol = la[:, n, dd * 4:dd * 4 + 4]
        becol = beta[:, n, dd * 4:dd * 4 + 4]
        Mz = GT if dd == 0 else LT
        Um = LE if dd == 0 else GE
        Ugt = GT if dd == 0 else LT
        NEG = NEGF if dd == 0 else NEGB
        strict = GT if dd == 0 else LT
        B0, dB0 = bank(dd, 0)
        B1, dB1 = bank(dd, 1)
        B2, dB2 = bank(dd, 2)
        B3, dB3 = bank(dd, 3)
        k.dve(lambda e: e.tensor_tensor(out=w["Z"], in0=Mz.unsqueeze(1).to_broadcast([128, 4, 128]),
                                        in1=lacol.unsqueeze(2).to_broadcast([128, 4, 128]), op=ALU.mult),
              r=[d_GM, d_la], w=[w["d_Z"]])
        k.pe(lambda e: e.matmul(B0, Um, w["Z"].rearrange("p a b -> p (a b)"), start=True, stop=False), r=[d_GM, w["d_Z"]], w=[dB0])
        k.pe(lambda e: e.matmul(B0, self.ident_f[:], NEG, start=False, stop=True), r=[d_GM, self.d_const], w=[dB0])
        k.act(lambda e: e.activation(out=w["D"].rearrange("p a b -> p (a b)"), in_=B0, func=AF.Exp), r=[dB0], w=[w["d_D"]])
        yield
        k.pe(lambda e: e.matmul(B3[:, 0:4], Um, lacol, start=True, stop=True), r=[d_GM, d_la], w=[dB3])
        k.pe(lambda e: e.matmul(B3[:, 4:8], Ugt, lacol, start=True, stop=True), r=[d_GM, d_la], w=[dB3])
        k.pe(lambda e: e.matmul(B3[:, 8:12], self.ones_f[:], lacol, start=True, stop=True), r=[self.d_const, d_la], w=[dB3])
        yield
        k.act(lambda e: e.activation(out=w["ev"], in_=B3[:, 0:12], func=AF.Exp), r=[dB3], w=[w["d_ev"]])
        yield
        k.dve(lambda e: e.tensor_tensor(out=w["Ds"], in0=w["D"], in1=strict.unsqueeze(1).to_broadcast([128, 4, 128]), op=ALU.mult),
              r=[w["d_D"], d_GM], w=[w["d_Ds"]])
        yield
        for h in range(4):
            hp = slice(64 * (h % 2), 64 * (h % 2) + 64)
            if h == 0:
                k.act(lambda e: e.activation(out=w["kc"][0:64, :, 0, :], in_=kT[0:64, :, c0:c0 + 128], func=AF.Copy), r=[d_qk], w=[w["d_kc"]])
                k.act(lambda e: e.activation(out=w["kc"][64:128, :, 1, :], in_=kT[64:128, :, c0:c0 + 128], func=AF.Copy), r=[d_qk], w=[w["d_kc"]])
            k.pe(lambda e, h=h, hp=hp: e.matmul(B1[:, h * 128:(h + 1) * 128], kT[:, h // 2, c0:c0 + 128], w["kc"][:, h // 2, h % 2, :],
                                                start=True, stop=True), r=[d_qk, w["d_kc"]], w=[dB1])
        yield
        k.dve(lambda e: e.tensor_tensor(out=w["Ds"].rearrange("p a b -> p (a b)"), in0=B1, in1=w["Ds"].rearrange("p a b -> p (a b)"),
                                        op=ALU.mult), r=[dB1, w["d_Ds"]], w=[w["d_Ds"]])
        k.dve(lambda e: e.tensor_tensor(out=w["A"][0], in0=w["Ds"], in1=becol.unsqueeze(2).to_broadcast([128, 4, 128]), op=ALU.mult),
              r=[w["d_Ds"], d_beta], w=[w["d_A"][0]])
        yield
        for h in range(4):
            hp = slice(64 * (h % 2), 64 * (h % 2) + 64)
            k.pe(lambda e, h=h, hp=hp: e.matmul(B2[:, h * 128:(h + 1) * 128], qT[:, h // 2, c0:c0 + 128], w["kc"][:, h // 2, h % 2, :],
                                                start=True, stop=True), r=[d_qk, w["d_kc"]], w=[dB2])
        yield
        k.dve(lambda e: e.tensor_tensor(out=w["QKm"].rearrange("p a b -> p (a b)"), in0=B2, in1=w["D"].rearrange("p a b -> p (a b)"),
                                        op=ALU.mult), r=[dB2, w["d_D"]], w=[w["d_QKm"]])
        yield
        B1b = B1.bitcast(BF16)
        B2b = B2.bitcast(BF16)
        for h in range(4):
            k.pe(lambda e, h=h: e.transpose(out=B1b[:, h * 128:(h + 1) * 128], in_=w["A"][0][:, h, :], identity=self.ident_b[:]),
                 r=[w["d_A"][0], self.d_const], w=[dB1])
        yield
        k.act(lambda e: e.activation(out=w["AT"][0].rearrange("p a b -> p (a b)"), in_=B1b[:, 0:512], func=AF.Copy),
              r=[dB1], w=[w["d_AT"][0]])
        for h in range(4):
            k.pe(lambda e, h=h: e.transpose(out=B2b[:, h * 128:(h + 1) * 128], in_=w["QKm"][:, h, :], identity=self.ident_b[:]),
                 r=[w["d_QKm"], self.d_const], w=[dB2])
        yield
        k.act(lambda e: e.activation(out=w["QKT"].rearrange("p a b -> p (a b)"), in_=B2b[:, 0:512], func=AF.Copy),
              r=[dB2], w=[w["d_QKT"]])
        yield
        k.dve(lambda e: e.tensor_tensor(out=w["beg"], in0=becol, in1=w["ev"][:, 0:4], op=ALU.mult), r=[d_beta, w["d_ev"]], w=[w["d_beg"]])
        X0 = w["R"].rearrange("p h (two d) -> p h two d", two=2)
        k.dve(lambda e: e.tensor_tensor(out=X0[:, :, 0, :], in0=vtok[:, n, :].rearrange("p (h d) -> p h d", h=4),
                                        in1=becol.unsqueeze(2).to_broadcast([128, 4, 64]), op=ALU.mult),
              r=[d_vtok, d_beta], w=[w["d_R"]])
        k.dve(lambda e: e.tensor_tensor(out=X0[:, :, 1, :], in0=ktok[:, n, :].rearrange("p (h d) -> p h d", h=4),
                                        in1=w["beg"].unsqueeze(2).to_broadcast([128, 4, 64]), op=ALU.mult),
              r=[d_ktok, w["d_beg"]], w=[w["d_R"]])
        for half in range(2):
            k.pool(lambda e, half=half: e.tensor_tensor(out=w["kd"][:, :, half * 64:(half + 1) * 64],
                                                        in0=ktok[:, n, :].rearrange("p (h d) -> p h d", h=4),
                                                        in1=w["ev"][:, 4:8].unsqueeze(2).to_broadcast([128, 4, 64]), op=ALU.mult),
                   r=[d_ktok, w["d_ev"]], w=[w["d_kd"]])
        yield
        A, AT = w["A"][0], w["AT"][0]
        dA, dAT = w["d_A"][0], w["d_AT"][0]
        Tm, TT = w["A"][1], w["AT"][1]
        dT, dTT = w["d_A"][1], w["d_AT"][1]
        M1, M1t = w["X"][0], w["X"][1]
        dM1, dM1t = w["d_X"][0], w["d_X"][1]
        mo = 0 if dd == 0 else 7
        mt = 7 if dd == 0 else 0
        I4 = self.ident_b[:].unsqueeze(1).to_broadcast([128, 4, 128])
        msk = lambda s_: LMK[:, (mo + s_) * 128:(mo + s_ + 1) * 128].unsqueeze(1).to_broadcast([128, 4, 128])
        mskT = lambda s_: LMK[:, (mt + s_) * 128:(mt + s_ + 1) * 128].unsqueeze(1).to_broadcast([128, 4, 128])
        k.pool(lambda e: e.tensor_tensor(out=Tm, in0=A, in1=msk(0), op=ALU.mult), r=[dA, d_LMK], w=[dT])
        k.dve(lambda e: e.tensor_tensor(out=Tm, in0=Tm, in1=I4, op=ALU.add), r=[dT, self.d_const], w=[dT])
        k.pool(lambda e: e.tensor_tensor(out=TT, in0=AT, in1=mskT(0), op=ALU.mult), r=[dAT, d_LMK], w=[dTT])
        k.dve(lambda e: e.tensor_tensor(out=TT, in0=TT, in1=I4, op=ALU.add), r=[dTT, self.d_const], w=[dTT])
        yield
        fl = lambda x: x.rearrange("p a b -> p (a b)")
        def mk_E(lev):
            j = lev % 2
            k.pool(lambda e, lev=lev, j=j: e.tensor_tensor(out=w["E"][j], in0=A, in1=msk(lev), op=ALU.mult), r=[dA, d_LMK], w=[w["d_E"][j]])
            k.op("dve" if lev % 2 == 0 else "pool", lambda e, lev=lev, j=j: e.tensor_tensor(out=w["ET"][j], in0=AT, in1=mskT(lev), op=ALU.mult),
                 r=[dAT, d_LMK], w=[w["d_ET"][j]])
        mk_E(1)
        for lev in range(1, 7):
            j = lev % 2
            E, ET, dE, dET = w["E"][j], w["ET"][j], w["d_E"][j], w["d_ET"][j]
            for h in range(4):
                k.pe(lambda e, h=h, ET=ET: e.matmul(B0[:, h * 128:(h + 1) * 128], ET[:, h, :], Tm[:, h, :], start=True, stop=True),
                     r=[dET, dT], w=[dB0])
            for h in range(4):
                k.pe(lambda e, h=h, E=E: e.matmul(B1[:, h * 128:(h + 1) * 128], E[:, h, :], TT[:, h, :], start=True, stop=True),
                     r=[dE, dTT], w=[dB1])
            yield
            if lev < 6:
                mk_E(lev + 1)
            k.act(lambda e: e.activation(out=fl(M1), in_=B0, func=AF.Copy), r=[dB0], w=[dM1])
            k.act(lambda e: e.activation(out=fl(M1t), in_=B1, func=AF.Copy), r=[dB1], w=[dM1t])
            yield
            for h in range(4):
                k.pe(lambda e, h=h: e.matmul(B2[:, h * 128:(h + 1) * 128], TT[:, h, :], M1[:, h, :], start=True, stop=True),
                     r=[dTT, dM1], w=[dB2])
            for h in range(4):
                k.pe(lambda e, h=h: e.matmul(B3[:, h * 128:(h + 1) * 128], Tm[:, h, :], M1t[:, h, :], start=True, stop=True),
                     r=[dT, dM1t], w=[dB3])
            yield
            k.dve(lambda e: e.tensor_tensor(out=fl(Tm), in0=fl(Tm), in1=B2, op=ALU.add), r=[dT, dB2], w=[dT])
            k.dve(lambda e: e.tensor_tensor(out=fl(TT), in0=fl(TT), in1=B3, op=ALU.add), r=[dTT, dB3], w=[dTT])
        yield
        Rm = w["R"].rearrange("p h (two d) -> p h two d", two=2)
        for h in range(4):
            k.pe(lambda e, h=h: e.matmul(B1[:, h * 64:(h + 1) * 64], TT[:, h, :], Rm[:, h, 0, :], start=True, stop=True),
                 r=[dTT, w["d_R"]], w=[dB1])
        k.act(lambda e: e.activation(out=w["u"].rearrange("p a b -> p (a b)"), in_=B1[:, 0:256], func=AF.Copy), r=[dB1], w=[w["d_u"]])
        yield
        for h in range(4):
            k.pe(lambda e, h=h: e.matmul(B2[0:64, h * 128:(h + 1) * 128], Rm[:, h, 1, :], TT[:, h, :], start=True, stop=True),
                 r=[dTT, w["d_R"]], w=[dB2])
        k.act(lambda e: e.activation(out=w["wT"].rearrange("p a b -> p (a b)"), in_=B2[0:64, :], func=AF.Copy),
              r=[dB2], w=[w["d_wT"]])
        yield

    def unit_scan(dd, n, par, need_out):
        w = dict(W_[dd])
        w.update(WP[dd][par])
        Xf, dXf = w["u"], w["d_u"]
        c0 = n * 128
        B0, dB0 = bank(dd, 0)
        B1, dB1 = bank(dd, 1)
        for h in range(4):
            k.pe(lambda e, h=h: e.matmul(B0[:, h * 64:(h + 1) * 64], w["wT"][:, h, :], w["Sb"][0:64, h, :], start=True, stop=True),
                 r=[w["d_wT"], w["d_Sb"]], w=[dB0])
        if need_out:
            for h in range(4):
                hp = slice(64 * (h % 2), 64 * (h % 2) + 64)
                k.pe(lambda e, h=h, hp=hp: e.matmul(B0[:, 256 + h * 64:256 + (h + 1) * 64], qT[:, h // 2, c0:c0 + 128],
                                                    w["SbQ"][:, h, :], start=True, stop=True),
                     r=[d_qk, w["d_SbQ"]], w=[dB0])
        yield
        k.dve(lambda e: e.tensor_tensor(out=w["vn"], in0=Xf, in1=B0[:, 0:256].rearrange("p (h d) -> p h d", h=4), op=ALU.subtract),
              r=[dXf, dB0], w=[w["d_vn"]])
        if need_out:
            for h in range(4):
                k.pe(lambda e, h=h: e.matmul(B1[:, h * 64:(h + 1) * 64], w["QKT"][:, h, :], w["vn"][:, h, :], start=True, stop=True),
                     r=[w["d_QKT"], w["d_vn"]], w=[dB1])
            k.act(lambda e: e.activation(out=w["o2"].rearrange("p a b -> p (a b)"), in_=B1[:, 0:256], func=AF.Copy), r=[dB1], w=[w["d_o2"]])
            k.dve(lambda e: e.tensor_tensor(out=w["ot"], in0=B0[:, 256:512].rearrange("p (h d) -> p h d", h=4),
                                            in1=w["ev"][:, 0:4].unsqueeze(2).to_broadcast([128, 4, 64]), op=ALU.mult),
                  r=[dB0, w["d_ev"]], w=[w["d_ot"]])
            k.pool(lambda e: e.tensor_tensor(out=w["ot"], in0=w["ot"], in1=w["o2"], op=ALU.add), r=[w["d_ot"], w["d_o2"]], w=[w["d_ot"]])
            if first_visit[n]:
                first_visit[n] = False
                k.pool(lambda e: e.tensor_copy(out=otok[:, n, :].rearrange("p (h d) -> p h d", h=4), in_=w["ot"]), r=[w["d_ot"]], w=[d_otok[n]])
            else:
                k.pool(lambda e: e.tensor_tensor(out=otok[:, n, :].rearrange("p (h d) -> p h d", h=4),
                                                 in0=otok[:, n, :].rearrange("p (h d) -> p h d", h=4), in1=w["ot"], op=ALU.add),
                       r=[w["d_ot"], d_otok[n]], w=[d_otok[n]])
        yield
        for h in range(4):
            k.pe(lambda e, h=h: e.matmul(B1[:, 256 + h * 64:256 + (h + 1) * 64], w["kd"][:, h, :], w["vn"][:, h, :], start=True, stop=True),
                 r=[w["d_kd"], w["d_vn"]], w=[dB1])
        yield
        k.dve(lambda e: e.tensor_tensor(out=w["S"], in0=w["S"], in1=w["ev"][:, 8:12].unsqueeze(2).to_broadcast([128, 4, 64]), op=ALU.mult),
              r=[w["d_S"], w["d_ev"]], w=[w["d_S"]])
        k.dve(lambda e: e.tensor_tensor(out=w["S"], in0=w["S"], in1=B1[:, 256:512].rearrange("p (h d) -> p h d", h=4), op=ALU.add),
              r=[w["d_S"], dB1], w=[w["d_S"]])
        k.act(lambda e: e.activation(out=w["Sb"], in_=w["S"], func=AF.Copy), r=[w["d_S"]], w=[w["d_Sb"]])
        k.pool(lambda e: e.tensor_tensor(out=w["SbQ"], in0=w["S"], in1=hm.unsqueeze(2).to_broadcast([128, 4, 64]), op=ALU.mult),
               r=[w["d_S"], d_hm], w=[w["d_SbQ"]])

        yield

    def round_robin(gens):
        alive = list(gens)
        while alive:
            for g in list(alive):
                try:
                    next(g)
                except StopIteration:
                    alive.remove(g)

    _mode = os.environ.get("GDN_RR", "pipe")
    prev = []
    for step in range(18):
        par = step % 2
        cur = []
        pres = []
        for dd in range(2):
            n = (orderF if dd == 0 else orderB)[step]
            need_out = (n < 16) or with_ctx
            pres.append(unit_pre(dd, n, par))
            cur.append((dd, n, par, need_out))
        if _mode == "pipe":
            round_robin(pres + prev)
            prev = [unit_scan(*c_) for c_ in cur]
        else:
            round_robin(pres)
            round_robin([unit_scan(*c_) for c_ in cur])
    round_robin(prev)
    if "gdn_otok" in self.dbg:
        self.k.barrier()
        chd = k.chan()
        o_ = self.dout("dbg_otok", [128, NT * 256])
        k.dma("sp", o_[:, :], otok.rearrange("p a b -> p (a b)"), chd)
        self.k.barrier()
    if _stop:
        return
    _arena_rewind(self, markO)
    _gdn_out(self, l, with_ctx, otok, d_otok)


def _gdn_out(self, l, with_ctx, otok, d_otok):
    k = self.k
    ntt = NT if with_ctx else 16
    _wring_init(self, 1)
    gT = self.aa([128, 2, T], BF16)
    d_g = Dep()
    blocks = _blocks(self, with_ctx)

    def evac(c, m, t0, n, pst, dps):
        ct = (c - 768) // 128
        k.act(lambda e: e.activation(out=gT[:, ct, t0:t0 + n], in_=pst, func=AF.Silu), r=[dps], w=[d_g])
    _proj(self, l, [(768, 256)], blocks, evac, PROJ_BANKS)
    gn = _ppv(self, l, "gnorm")
    sq = self.aa([128, 4, 64])
    ss = self.aa([128, 4])
    on = [self.aa([128, 256], BF16) for _ in range(2)]
    d_sq, d_ss = Dep(), Dep()
    d_on = [Dep(), Dep()]
    psT = self.ps[2][:, 0:512].bitcast(BF16)
    d_psT = self.d_ps[2][0]
    for tt in range(ntt):
        o3 = otok[:, tt, :].rearrange("p (h d) -> p h d", h=4)
        j = tt % 2
        k.dve(lambda e, o3=o3: e.tensor_tensor(out=sq, in0=o3, in1=o3, op=ALU.mult), r=[d_otok[tt]], w=[d_sq])
        k.dve(lambda e: e.reduce_sum(out=ss, in_=sq, axis=mybir.AxisListType.X), r=[d_sq], w=[d_ss])
        k.act(lambda e: e.activation(out=ss, in_=ss, func=AF.Sqrt, bias=EPS, scale=1.0 / 64), r=[d_ss], w=[d_ss])
        k.dve(lambda e: e.reciprocal(out=ss, in_=ss), r=[d_ss], w=[d_ss])
        k.dve(lambda e, o3=o3, j=j: e.tensor_tensor(out=on[j].rearrange("p (h d) -> p h d", h=4), in0=o3,
                                                    in1=ss.unsqueeze(2).to_broadcast([128, 4, 64]), op=ALU.mult),
              r=[d_otok[tt], d_ss], w=[d_on[j]])
        for ct in range(2):
            k.pe(lambda e, j=j, ct=ct: e.transpose(out=psT[:, ct * 128:(ct + 1) * 128], in_=on[j][:, ct * 128:(ct + 1) * 128],
                                                   identity=self.ident_b[:]), r=[d_on[j], self.d_const], w=[d_psT])
        for ct in range(2):
            k.dve(lambda e, ct=ct, tt=tt: e.scalar_tensor_tensor(out=self.C[:, ct, tt * 128:(tt + 1) * 128], in0=psT[:, ct * 128:(ct + 1) * 128],
                                                                 scalar=gn[:, 0:1], in1=gT[:, ct, tt * 128:(tt + 1) * 128],
                                                                 op0=ALU.mult, op1=ALU.mult),
                  r=[d_psT, d_g, self.d_pp], w=[self.d_C[tt]])


from concourse.bass_utils import run_bass_kernel_spmd


def kernel(**inputs):
    inputs = {k_: np.asarray(v) for k_, v in inputs.items()}
    mk = MK()
    nc = mk.build()
    consts = const_inputs(inputs)
    pp = pp_host(inputs)

    def extra(b):
        e = {"pp": pp}
        e.update(consts)
        return e
    maps = host_inputs(mk, inputs, extra=extra)
    n = len(maps)
    res = run_bass_kernel_spmd(nc, maps, core_ids=list(range(n)))
    out = np.stack([np.asarray(res.results[b]["out"]) for b in range(n)], axis=0)
    return out.astype(np.float32)
```

```python
import numpy as np
import ml_dtypes
from contextlib import ExitStack
import concourse.bass as bass
import concourse.mybir as mybir

F32 = mybir.dt.float32
BF16 = mybir.dt.bfloat16
I32 = mybir.dt.int32
AF = mybir.ActivationFunctionType
ALU = mybir.AluOpType

D = 1024
L = 2048
LC = 256
T = L + LC
NT = T // 128
DEPTH = 2
EPS = 1e-6
IN_COLS = 3344
OFF_SC, OFF_HY, OFF_NA = 1040, 1808, 2576


class Dep:
    __slots__ = ("lw", "rd")

    def __init__(self):
        self.lw = None
        self.rd = []


class Chan:
    __slots__ = ("sem", "cnt", "key", "q")

    def __init__(self, sem, key):
        self.sem = sem
        self.cnt = 0
        self.key = key


class K:
    def __init__(self, nc, stack):
        self.nc = nc
        self.stack = stack
        self.eng = {"pe": nc.tensor, "act": nc.scalar, "dve": nc.vector, "pool": nc.gpsimd, "sp": nc.sync}
        self.sems = {}
        self.ecnt = {}
        self.waited = {e: {} for e in self.eng}
        for e in self.eng:
            self.sems["e_" + e] = stack.enter_context(nc.semaphore("e_" + e))
            self.ecnt[e] = 0
        self.chans = []
        self.bar_sem = stack.enter_context(nc.semaphore("bar"))
        self.bar_cnt = 0
        self.nops = 0

    def chan(self):
        key = "c%d" % len(self.chans)
        s = self.stack.enter_context(self.nc.semaphore(key))
        self.sems[key] = s
        c = Chan(s, key)
        c.q = None
        self.chans.append(c)
        return c

    def _wait(self, e, ev):
        key, val, src = ev
        if src == e and e == "pe":
            return
        w = self.waited[e]
        if w.get(key, 0) >= val:
            return
        w[key] = val
        self.eng[e].wait_ge(self.sems[key], val)

    def _deps(self, e, r, w):
        for d in r:
            if d.lw is not None:
                self._wait(e, d.lw)
        for d in w:
            if d.lw is not None and d.lw[2] != e:
                self._wait(e, d.lw)
            for ev in d.rd:
                if ev[2] != e:
                    self._wait(e, ev)

    def _commit(self, ev, r, w):
        for d in w:
            d.lw = ev
            d.rd = []
        for d in r:
            d.rd.append(ev)
            if len(d.rd) > 48:
                best = {}
                for x in d.rd:
                    if x[0] not in best or best[x[0]][1] < x[1]:
                        best[x[0]] = x
                d.rd = list(best.values())

    def op(self, e, fn, r=(), w=()):
        self._deps(e, r, w)
        ins = fn(self.eng[e])
        self.ecnt[e] += 1
        ins.then_inc(self.sems["e_" + e], 1)
        self._commit(("e_" + e, self.ecnt[e], e), r, w)
        self.nops += 1
        return ins

    def pe(self, fn, r=(), w=()):
        return self.op("pe", fn, r, w)

    def act(self, fn, r=(), w=()):
        return self.op("act", fn, r, w)

    def dve(self, fn, r=(), w=()):
        return self.op("dve", fn, r, w)

    def pool(self, fn, r=(), w=()):
        return self.op("pool", fn, r, w)

    def dma(self, q, out, in_, ch, r=(), w=(), **kw):
        self._deps(q, r, w)
        ins = self.eng[q].dma_start(out=out, in_=in_, **kw)
        ch.cnt += 16
        ch.q = q
        ins.then_inc(ch.sem, 16)
        self._commit((ch.key, ch.cnt, "dma"), r, w)
        self.nops += 1
        return ins

    def barrier(self):
        last = getattr(self, "_bar_last", {})
        cur = {}
        for e in self.eng:
            cur["e_" + e] = (self.ecnt[e], e)
        for c in self.chans:
            cur[c.key] = (c.cnt, "dma")
        changed = [(key, v[0], v[1]) for key, v in cur.items() if v[0] > 0 and last.get(key, (0,))[0] != v[0]]
        for e in self.eng:
            for (key, val, src) in changed:
                w = self.waited[e]
                if w.get(key, 0) >= val:
                    continue
                w[key] = val
                self.eng[e].wait_ge(self.sems[key], val)
        self._bar_last = cur


class MK:
    def __init__(self, dbg=None, layers=(0, 1), inject_cat=False, mixers=("gdn", "sc", "hy", "na"), do_mlp=True,
                 phases=("mod", "p1", "mix", "p3", "p4"), inject_h=False):
        self.phases = phases
        self.inject_h = inject_h
        self.dbg = dbg or {}
        self.layers = layers
        self.inject_cat = inject_cat
        self.mixers = mixers
        self.do_mlp = do_mlp
        self.inputs = {}
        self.outputs = {}

    def din(self, name, shape, dtype=F32):
        t = self.nc.dram_tensor(name, list(shape), dtype, kind="ExternalInput").ap()
        self.inputs[name] = (tuple(shape), dtype)
        return t

    def dout(self, name, shape, dtype=F32):
        t = self.nc.dram_tensor(name, list(shape), dtype, kind="ExternalOutput").ap()
        self.outputs[name] = (tuple(shape), dtype)
        return t

    def W(self, name):
        if name not in self._w:
            self._w[name] = self.din(name, self._wshape[name])
        return self._w[name]

    def sb(self, st, name, shape, dtype=F32):
        self._n = getattr(self, "_n", 0) + 1
        return st.enter_context(self.nc.sbuf_tensor("%s_%d" % (name, self._n), list(shape), dtype))

    def build(self):
        nc = bass.Bass("TRN2", target_bir_lowering=False)
        self.nc = nc
        with ExitStack() as st:
            self.st = st
            self.k = K(nc, st)
            self._declare()
            self._consts()
            for l in self.layers:
                self._layer(l)
            self._finish()
        return nc

    def _declare(self):
        nc = self.nc
        self._wshape = {}
        self._w = {}
        self.x_in = self.din("x", [L, D])
        self.ctx_in = self.din("ctx", [LC, D])
        self.cvec_in = self.din("cvec", [128, 16])
        self._wshape["ada_w"] = [DEPTH, D, 6 * D]
        self.ada_bT = self.din("ada_bT", [128, DEPTH * 48])
        self.gains_in = self.din("gains", [128, 4 * DEPTH * 8])
        self._wshape["w_in"] = [DEPTH, D, IN_COLS]
        self._wshape["w_out"] = [DEPTH, D, D]
        self._wshape["mlp_w1"] = [DEPTH, D, 4 * D]
        self._wshape["mlp_w2"] = [DEPTH, 4 * D, D]
        self.ident_f_in = self.din("ident_f", [128, 128])
        self.ident_b_in = self.din("ident_b", [128, 128], BF16)
        self.out = self.dout("out", [L, D])
        self.xres = nc.dram_tensor("xres", [T, D], F32).ap()
        self.d_xres = [Dep() for _ in range(NT)]
        self.d_out = [Dep() for _ in range(NT)]
        if self.inject_cat:
            self.cat_in = self.din("cat_in", [D, T], BF16)
        self.ps = [self.st.enter_context(nc.psum_tensor("ps%d" % i, [128, 1024], F32)) for i in range(4)]
        self.d_ps = [[Dep(), Dep()] for _ in range(4)]

    def _consts(self):
        k, st = self.k, self.st
        sb = lambda n, s, d=F32: self.sb(st, n, s, d)
        self.ident_f = sb("ident_f", [128, 128])
        self.ident_b = sb("ident_b", [128, 128], BF16)
        self.ones_f = sb("ones_f", [128, 128])
        self.cvec = sb("cvec", [128, 16])
        self.adab = sb("adab", [128, DEPTH * 48])
        self.gains = sb("gains", [128, 4 * DEPTH * 8])
        self.d_const = Dep()
        ch = k.chan()
        k.dma("sp", self.ident_f[:], self.ident_f_in[:, :], ch, w=[self.d_const])
        k.dma("sp", self.ident_b[:], self.ident_b_in[:, :], ch, w=[self.d_const])
        k.dma("sp", self.cvec[:], self.cvec_in[:, :], ch, w=[self.d_const])
        k.dma("sp", self.adab[:], self.ada_bT[:, :], ch, w=[self.d_const])
        k.dma("sp", self.gains[:], self.gains_in[:, :], ch, w=[self.d_const])
        k.dve(lambda e: e.memset(self.ones_f[:], 1.0), w=[self.d_const])
        self.H = sb("H", [128, 8, T], BF16)
        self.C = sb("C", [128, 8, T], BF16)
        self.d_H = [Dep() for _ in range(NT)]
        self.d_C = [Dep() for _ in range(NT)]
        self.modT = sb("modT", [128, 48, 2])
        self.A1 = sb("A1", [128, 8, 2])
        self.A2 = sb("A2", [128, 8, 2])
        self.G1f = sb("G1f", [128, 8, 2])
        self.G2f = sb("G2f", [128, 8, 2])
        self.Gbc = sb("Gbc", [128, 2, 2, D])
        self.d_mod = Dep()
        self.d_gbc = Dep()
        self.NAR = 28416
        self.AR = sb("arena", [128, self.NAR])
        self.ar_off = 0
        self.NXT = 3
        self.d_xt = [Dep() for _ in range(self.NXT)]
        self.ch_xt_ld = [k.chan() for _ in range(self.NXT)]
        self.ch_xt_st = [k.chan() for _ in range(self.NXT)]
        self.d_xn = [Dep(), Dep()]
        self.d_junk = Dep()
        self.d_stat = [Dep() for _ in range(4)]
        self.d_tmpb = [Dep(), Dep()]
        self.d_tmpf = [Dep(), Dep()]
        self.xt_i = self.xn_i = self.stat_i = self.tmp_i = 0

    def arena_reset(self):
        self.k.barrier()
        self.ar_off = 0

    def aa(self, shape, dtype=F32):
        esz = 4 if dtype in (F32, I32) else 2
        n = int(np.prod(shape[1:]))
        nbytes = (n * esz + 31) // 32 * 32
        o = self.ar_off
        assert o + nbytes <= self.NAR * 4, "arena overflow %d" % (o + nbytes)
        self.ar_off = o + nbytes
        v = self.AR[:, o // 4:(o + nbytes) // 4]
        if dtype != F32:
            v = v.bitcast(dtype)
        v = v[:, 0:n]
        if len(shape) == 3:
            v = v.rearrange("p (a b) -> p a b", b=shape[2])
        elif len(shape) == 4:
            v = v.rearrange("p (a b c) -> p a b c", b=shape[2], c=shape[3])
        if shape[0] != 128:
            v = v[0:shape[0]]
        return v

    def _staging(self, norm=True):
        self.xt = [self.aa([128, D]) for i in range(self.NXT)]
        self.junk = self.aa([128, D], BF16)
        self.stat = [self.aa([128, 8]) for i in range(4)]
        self.tmpf = [self.aa([128, D]) for i in range(2)]
        if norm:
            self.xn = [self.aa([128, D], BF16) for i in range(2)]
            self.tmpb = [self.aa([128, D], BF16) for i in range(2)]

    def gain(self, kind, l):
        o = (kind * DEPTH + l) * 8
        return self.gains[:, o:o + 8]

    def _layer(self, l):
        ph = self.phases
        if "mod" in ph:
            self._modulation(l)
        if "p1" in ph:
            self.arena_reset()
            self._staging()
            for tt in range(NT):
                s = 0 if tt < 16 else 1
                xi = self._load_x(l, tt, first=True)
                self._norm_tile(xi, s, self.A1, self.modT[:, 0:8, :], self.H, tt, self.d_H[tt])
        elif self.inject_h:
            ch = self.k.chan()
            hin = self.din("h_in", [D, T], BF16)
            for tt in range(NT):
                self.k.dma("sp", self.H[:, :, tt * 128:(tt + 1) * 128],
                           hin[:, tt * 128:(tt + 1) * 128].rearrange("(a p) t -> p a t", p=128), ch, w=[self.d_H[tt]])
        if "hx" in self.dbg and self.dbg["hx"] == l:
            self._dump_feat("dbg_hx", self.H, self.d_H)
        if "mix" in ph:
            self._mixers(l)
        if "cat" in self.dbg and self.dbg["cat"] == l:
            self._dump_feat("dbg_cat", self.C, self.d_C)
        if "p3" in ph:
            self._p3(l)
        if "hx2" in self.dbg and self.dbg["hx2"] == l:
            self._dump_feat("dbg_hx2", self.C, self.d_C)
        if "p4" in ph:
            self._p4(l)

    def _xsrc(self, l, tt, first):
        if l == self.layers[0] and l == 0 and first:
            if tt < 16:
                return self.x_in[tt * 128:(tt + 1) * 128, :], None
            return self.ctx_in[(tt - 16) * 128:(tt - 15) * 128, :], None
        return self.xres[tt * 128:(tt + 1) * 128, :], self.d_xres[tt]

    def _load_x(self, l, tt, first):
        k = self.k
        i = self.xt_i
        self.xt_i = (i + 1) % self.NXT
        src, dep = self._xsrc(l, tt, first)
        k.dma("sp", self.xt[i][:], src, self.ch_xt_ld[i], r=[dep] if dep else [], w=[self.d_xt[i]])
        return i

    def _store_x(self, i, dst_ap, dst_dep):
        self.k.dma("sp", dst_ap, self.xt[i][:], self.ch_xt_st[i], r=[self.d_xt[i]], w=[dst_dep])

    def _rstd(self, src_ap, src_deps):
        k = self.k
        j = self.stat_i
        self.stat_i = (j + 1) % 4
        stt, dst = self.stat[j], self.d_stat[j]
        k.act(lambda e: e.activation(out=self.junk[:], in_=src_ap, func=AF.Square, accum_out=stt[:, 0:1]),
              r=src_deps, w=[self.d_junk, dst])
        k.act(lambda e: e.activation(out=stt[:, 1:2], in_=stt[:, 0:1], func=AF.Sqrt, bias=EPS, scale=1.0 / D),
              r=[dst], w=[dst])
        k.dve(lambda e: e.reciprocal(out=stt[:, 2:3], in_=stt[:, 1:2]), r=[dst], w=[dst])
        return stt[:, 2:3], dst

    def _norm_tile(self, xi, s, A, B, Hbuf, tt, dH):
        k = self.k
        xt, dxt = self.xt[xi], self.d_xt[xi]
        rs, drs = self._rstd(xt[:], [dxt])
        j = self.xn_i
        self.xn_i = 1 - j
        xn, dxn = self.xn[j], self.d_xn[j]
        k.dve(lambda e: e.tensor_scalar(out=xn[:], in0=xt[:], scalar1=rs, scalar2=None, op0=ALU.mult),
              r=[dxt, drs], w=[dxn])
        pi = 3
        psb = self.ps[pi][:, 0:512].bitcast(BF16)
        dps = self.d_ps[pi][0]
        for dt in range(8):
            k.pe(lambda e, dt=dt: e.transpose(out=psb[:, dt * 128:(dt + 1) * 128], in_=xn[:, dt * 128:(dt + 1) * 128],
                                              identity=self.ident_b[:]),
                 r=[dxn, self.d_const], w=[dps])
        ti = self.tmp_i
        self.tmp_i = 1 - ti
        tb, dtb = self.tmpb[ti], self.d_tmpb[ti]
        k.dve(lambda e: e.tensor_tensor(out=tb[:].rearrange("p (a b) -> p a b", a=8),
                                        in0=psb.rearrange("p (a b) -> p a b", a=8),
                                        in1=A[:, :, s:s + 1].to_broadcast([128, 8, 128]), op=ALU.mult),
              r=[dps, self.d_mod], w=[dtb])
        k.pool(lambda e: e.tensor_tensor(out=Hbuf[:, :, tt * 128:(tt + 1) * 128],
                                         in0=tb[:].rearrange("p (a b) -> p a b", a=8),
                                         in1=B[:, :, s:s + 1].to_broadcast([128, 8, 128]), op=ALU.add),
               r=[dtb, self.d_mod], w=[dH])

    def _modulation(self, l):
        k = self.k
        self.arena_reset()
        if True:
            sT = self.aa([128, 16], BF16)
            d_sT = Dep()
            k.act(lambda e: e.activation(out=sT[:], in_=self.cvec[:], func=AF.Silu), r=[self.d_const], w=[d_sT])
            wb = [self.aa([128, 8, 512], BF16) for i in range(2)]
            dwb = [Dep(), Dep()]
            chw = [k.chan(), k.chan()]
            mod_ps = self.ps[0][:, 0:96]
            dps = self.d_ps[0][0]
            for g in range(12):
                i = g % 2
                src = self.W("ada_w")[l, :, g * 512:(g + 1) * 512].rearrange("(kt p) c -> p kt c", p=128)
                k.dma("pool", wb[i][:], src, chw[i], w=[dwb[i]])
                for jj in range(4):
                    jt = g * 4 + jj
                    for kt in range(8):
                        k.pe(lambda e, i=i, jj=jj, jt=jt, kt=kt: e.matmul(
                            mod_ps[:, jt * 2:jt * 2 + 2], wb[i][:, kt, jj * 128:(jj + 1) * 128],
                            sT[:, kt * 2:kt * 2 + 2], start=(kt == 0), stop=(kt == 7)),
                            r=[dwb[i], d_sT], w=[dps])
            dm = self.d_mod
            k.dve(lambda e: e.tensor_tensor(out=self.modT[:], in0=mod_ps.rearrange("p (a b) -> p a b", b=2),
                                            in1=self.adab[:, l * 48:(l + 1) * 48].unsqueeze(2).to_broadcast([128, 48, 2]),
                                            op=ALU.add), r=[dps, self.d_const], w=[dm])
            g = lambda kind: self.gain(kind, l).unsqueeze(2).to_broadcast([128, 8, 2])
            k.dve(lambda e: e.scalar_tensor_tensor(out=self.A1[:], in0=self.modT[:, 8:16, :], scalar=1.0, in1=g(0),
                                                   op0=ALU.add, op1=ALU.mult), r=[dm, self.d_const], w=[dm])
            k.dve(lambda e: e.scalar_tensor_tensor(out=self.A2[:], in0=self.modT[:, 32:40, :], scalar=1.0, in1=g(2),
                                                   op0=ALU.add, op1=ALU.mult), r=[dm, self.d_const], w=[dm])
            k.dve(lambda e: e.tensor_tensor(out=self.G1f[:], in0=self.modT[:, 16:24, :], in1=g(1), op=ALU.mult),
                  r=[dm, self.d_const], w=[dm])
            k.dve(lambda e: e.tensor_tensor(out=self.G2f[:], in0=self.modT[:, 40:48, :], in1=g(3), op=ALU.mult),
                  r=[dm, self.d_const], w=[dm])
            diag = [self.aa([128, 128]) for i in range(2)]
            ddiag = [Dep(), Dep()]
            n = 0
            for kind, Gf in enumerate((self.G1f, self.G2f)):
                for s in range(2):
                    for dt in range(8):
                        i = n % 2
                        n += 1
                        k.dve(lambda e, i=i, Gf=Gf, dt=dt, s=s: e.tensor_scalar(
                            out=diag[i][:], in0=self.ident_f[:], scalar1=Gf[:, dt, s:s + 1], scalar2=None, op0=ALU.mult),
                            r=[dm, self.d_const], w=[ddiag[i]])
                        pi = 1 + (n % 2)
                        pst = self.ps[pi][:, 0:128]
                        k.pe(lambda e, i=i, pst=pst: e.matmul(pst, self.ones_f[:], diag[i][:], start=True, stop=True),
                             r=[ddiag[i], self.d_const], w=[self.d_ps[pi][0]])
                        k.act(lambda e, pst=pst, kind=kind, s=s, dt=dt: e.activation(
                            out=self.Gbc[:, kind, s, dt * 128:(dt + 1) * 128], in_=pst, func=AF.Copy),
                            r=[self.d_ps[pi][0]], w=[self.d_gbc])
        if "mod" in self.dbg and self.dbg["mod"] == l:
            o = self.dout("dbg_mod", [128, 96])
            ch = k.chan()
            k.dma("sp", o[:, :], self.modT[:].rearrange("p a b -> p (a b)"), ch, r=[self.d_mod])
            o2 = self.dout("dbg_gbc", [128, 4 * D])
            k.dma("sp", o2[:, :], self.Gbc[:].rearrange("p a b c -> p (a b c)"), ch, r=[self.d_gbc])

    def _mixers(self, l):
        k = self.k
        if self.inject_cat:
            ch = k.chan()
            for tt in range(NT):
                k.dma("sp", self.C[:, :, tt * 128:(tt + 1) * 128],
                      self.cat_in[:, tt * 128:(tt + 1) * 128].rearrange("(a p) t -> p a t", p=128), ch, w=[self.d_C[tt]])
            return
        raise NotImplementedError

    def _p3(self, l):
        k = self.k
        last = (l == DEPTH - 1)
        ntt = 16 if last else NT
        self.arena_reset()
        self._staging()
        if True:
            wo = self.aa([128, 8, D], BF16)
            dwo = Dep()
            ch = k.chan()
            k.dma("pool", wo[:], self.W("w_out")[l].rearrange("(kt p) c -> p kt c", p=128), ch, w=[dwo])
            for tt in range(ntt):
                s = 0 if tt < 16 else 1
                pi = tt % 3
                yps = self.ps[pi]
                for half in range(2):
                    for mt in range(8):
                        k.pe(lambda e, half=half, mt=mt, yps=yps, tt=tt: e.matmul(
                            yps[:, half * 512:(half + 1) * 512], self.C[:, mt, tt * 128:(tt + 1) * 128],
                            wo[:, mt, half * 512:(half + 1) * 512], start=(mt == 0), stop=(mt == 7)),
                            r=[self.d_C[tt], dwo], w=[self.d_ps[pi][half]])
                xi = self._load_x(l, tt, first=True)
                self._resid_update(xi, yps[:], self.d_ps[pi], 0, s)
                self._store_x(xi, self.xres[tt * 128:(tt + 1) * 128, :], self.d_xres[tt])
                self._norm_tile(xi, s, self.A2, self.modT[:, 24:32, :], self.C, tt, self.d_C[tt])

    def _resid_update(self, xi, y_ap, y_deps, kind, s):
        k = self.k
        rs, drs = self._rstd(y_ap, list(y_deps))
        ti = self.tmp_i
        self.tmp_i = 1 - ti
        tf, dtf = self.tmpf[ti], self.d_tmpf[ti]
        k.dve(lambda e: e.scalar_tensor_tensor(out=tf[:], in0=y_ap, scalar=rs, in1=self.Gbc[:, kind, s, :],
                                               op0=ALU.mult, op1=ALU.mult),
              r=list(y_deps) + [drs, self.d_gbc], w=[dtf])
        xt, dxt = self.xt[xi], self.d_xt[xi]
        k.dve(lambda e: e.tensor_tensor(out=xt[:], in0=xt[:], in1=tf[:], op=ALU.add), r=[dtf, dxt], w=[dxt])

    def _p4(self, l):
        k = self.k
        last = (l == DEPTH - 1)
        if last:
            sblocks = [(0, 768), (768, 768), (1536, 512)]
        else:
            sblocks = [(0, 768), (768, 768), (1536, 768)]
        self.arena_reset()
        self._staging(norm=False)
        if True:
            hT = self.aa([128, 32, 768], BF16)
            d_hT = [[Dep() for _ in range(2)] for _ in range(32)]
            HF = self.H[:].rearrange("p a t -> p (a t)")
            w1c = [HF[:, i * 4096:(i + 1) * 4096].rearrange("p (k c) -> p k c", c=512) for i in range(3)]
            d_w1c = [Dep(), Dep(), Dep()]
            ch_w1 = [k.chan(), k.chan(), k.chan()]
            w2c = [HF[:, 12288 + i * 2048: 12288 + (i + 1) * 2048].rearrange("p (k c) -> p k c", c=512) for i in range(3)]
            d_w2c = [Dep() for _ in range(3)]
            ch_w2 = [k.chan() for _ in range(3)]
            rl = [self.aa([128, 384], BF16) for i in range(2)]
            d_rl = [Dep(), Dep()]
            ytok = self.aa([128, 6, D])
            d_ytok = [Dep() for _ in range(6)]
            n_w1 = 0
            n_w2 = 0
            n_rl = 0
            for (t0, n) in sblocks:
                n2 = n // 2
                ntl = n // 128
                for ffc in range(8):
                    i = n_w1 % 3
                    n_w1 += 1
                    k.dma("pool", w1c[i], self.W("mlp_w1")[l, :, ffc * 512:(ffc + 1) * 512].rearrange("(kt p) c -> p kt c", p=128),
                          ch_w1[i], w=[d_w1c[i]])
                    for f in range(4):
                        fft = ffc * 4 + f
                        for sbk in range(2):
                            hps = self.ps[3][:, sbk * 512: sbk * 512 + n2]
                            dhps = self.d_ps[3][sbk]
                            tts = range((t0 + sbk * n2) // 128, (t0 + (sbk + 1) * n2 + 127) // 128)
                            rdeps = [self.d_C[t] for t in tts]
                            for dt in range(8):
                                k.pe(lambda e, i=i, f=f, dt=dt, hps=hps, sbk=sbk: e.matmul(
                                    hps, w1c[i][:, dt, f * 128:(f + 1) * 128],
                                    self.C[:, dt, t0 + sbk * n2: t0 + (sbk + 1) * n2], start=(dt == 0), stop=(dt == 7)),
                                    r=[d_w1c[i]] + rdeps, w=[dhps])
                            j = n_rl % 2
                            n_rl += 1
                            k.act(lambda e, j=j, hps=hps: e.activation(out=rl[j][:, 0:n2], in_=hps, func=AF.Relu),
                                  r=[dhps], w=[d_rl[j]])
                            k.dve(lambda e, j=j, fft=fft, sbk=sbk: e.tensor_tensor(
                                out=hT[:, fft, sbk * n2:(sbk + 1) * n2], in0=rl[j][:, 0:n2], in1=rl[j][:, 0:n2], op=ALU.mult),
                                r=[d_rl[j]], w=[d_hT[fft][sbk]])
                for dh in range(2):
                    for ffc in range(8):
                        i = n_w2 % 3
                        n_w2 += 1
                        k.dma("pool", w2c[i],
                              self.W("mlp_w2")[l, ffc * 512:(ffc + 1) * 512, dh * 512:(dh + 1) * 512].rearrange("(f p) c -> p f c", p=128),
                              ch_w2[i], w=[d_w2c[i]])
                        for f in range(4):
                            fft = ffc * 4 + f
                            for tl in range(ntl):
                                pi, hb = tl // 2, tl % 2
                                sbk = (tl * 128) // n2
                                k.pe(lambda e, i=i, f=f, fft=fft, tl=tl, pi=pi, hb=hb: e.matmul(
                                    self.ps[pi][:, hb * 512:(hb + 1) * 512], hT[:, fft, tl * 128:(tl + 1) * 128],
                                    w2c[i][:, f, :], start=(fft == 0), stop=(fft == 31)),
                                    r=[d_w2c[i], d_hT[fft][sbk]], w=[self.d_ps[pi][hb]])
                    for tl in range(ntl):
                        pi, hb = tl // 2, tl % 2
                        k.act(lambda e, tl=tl, pi=pi, hb=hb, dh=dh: e.activation(
                            out=ytok[:, tl, dh * 512:(dh + 1) * 512], in_=self.ps[pi][:, hb * 512:(hb + 1) * 512], func=AF.Copy),
                            r=[self.d_ps[pi][hb]], w=[d_ytok[tl]])
                for tl in range(ntl):
                    tt = t0 // 128 + tl
                    s = 0 if tt < 16 else 1
                    xi = self._load_x(l, tt, first=False)
                    self._resid_update(xi, ytok[:, tl, :], [d_ytok[tl]], 1, s)
                    if last:
                        self._store_x(xi, self.out[tt * 128:(tt + 1) * 128, :], self.d_out[tt])
                    else:
                        self._store_x(xi, self.xres[tt * 128:(tt + 1) * 128, :], self.d_xres[tt])

    def _dump_feat(self, name, buf, deps):
        k = self.k
        o = self.dout(name, [D, T])
        self.arena_reset()
        if True:
            stg = self.aa([128, 8, 128])
            dst = Dep()
            ch = k.chan()
            for tt in range(NT):
                k.dve(lambda e, tt=tt: e.tensor_copy(out=stg[:], in_=buf[:, :, tt * 128:(tt + 1) * 128]), r=[deps[tt]], w=[dst])
                k.dma("sp", o[:, tt * 128:(tt + 1) * 128].rearrange("(a p) t -> p a t", p=128), stg[:], ch, r=[dst])

    def _finish(self):
        k = self.k
        if "xres" in self.dbg:
            self.arena_reset()
            self._staging()
            o = self.dout("dbg_xres", [T, D])
            ch = k.chan()
            for tt in range(NT):
                xi = self._load_x(1, tt, first=False)
                self._store_x(xi, o[tt * 128:(tt + 1) * 128, :], Dep())
        k.barrier()


def host_inputs(mk, inputs, extra=None):
    bf = ml_dtypes.bfloat16
    f32 = np.float32
    shared = {}
    shared["ada_w"] = np.ascontiguousarray(inputs["ada_w"], dtype=f32)
    shared["ada_bT"] = np.ascontiguousarray(
        inputs["ada_b"].reshape(DEPTH, 48, 128).transpose(2, 0, 1).reshape(128, DEPTH * 48), dtype=f32)
    g = np.stack([inputs["norm_pre_mix"], inputs["norm_post_mix"], inputs["norm_pre_mlp"], inputs["norm_post_mlp"]])
    shared["gains"] = np.ascontiguousarray(g.reshape(4, DEPTH, 8, 128).transpose(3, 0, 1, 2).reshape(128, -1), dtype=f32)
    for n in ("w_in", "w_out", "mlp_w1", "mlp_w2"):
        shared[n] = np.ascontiguousarray(inputs[n], dtype=f32)
    shared["ident_f"] = np.eye(128, dtype=f32)
    shared["ident_b"] = np.eye(128, dtype=f32).astype(bf)
    maps = []
    for b in range(inputs["x"].shape[0]):
        m = dict(shared)
        m["x"] = np.ascontiguousarray(inputs["x"][b], dtype=f32)
        m["ctx"] = np.ascontiguousarray(inputs["ctx"][b], dtype=f32)
        cv = np.stack([inputs["c"][b].reshape(8, 128), inputs["c_ctx"].reshape(8, 128)], axis=-1)
        m["cvec"] = np.ascontiguousarray(cv.transpose(1, 0, 2).reshape(128, 16), dtype=f32)
        if extra:
            m.update(extra(b))
        maps.append({kk: v for kk, v in m.items() if kk in mk.inputs})
    return maps


PP_ENTRIES = [("scw", 6), ("hyw", 18), ("gdw", 18), ("hybias", 2), ("gnorm", 1), ("hy_w1", 64), ("hy_w2", 64),
              ("hy_w3", 64), ("hy_w4", 512), ("hy_b", 3), ("hy_f", 3), ("alog", 8), ("dtb", 8)]
PP_OFF = {}
_o = 0
for _n, _w in PP_ENTRIES:
    PP_OFF[_n] = (_o, _w)
    _o += _w
PP_W = _o


def pp_host(inputs):
    pp = np.zeros((128, DEPTH * PP_W), np.float32)
    for l in range(DEPTH):
        def put(name, arr):
            o, w = PP_OFF[name]
            arr = np.asarray(arr, np.float32)
            assert arr.shape[1] == w, (name, arr.shape)
            pp[:arr.shape[0], l * PP_W + o: l * PP_W + o + w] = arr
        put("scw", inputs["sc_conv"][l].reshape(3, 2, 128).transpose(2, 1, 0).reshape(128, 6))
        put("hyw", inputs["hy_conv"][l].reshape(3, 6, 128).transpose(2, 1, 0).reshape(128, 18))
        put("gdw", inputs["gdn_conv"][l].reshape(3, 6, 128).transpose(2, 1, 0).reshape(128, 18))
        put("hybias", inputs["hy_bias"][l].reshape(2, 128).T)
        put("gnorm", np.tile(inputs["gdn_norm"][l], 2).reshape(128, 1))
        put("hy_w1", inputs["hy_w1"][l])
        put("hy_w2", inputs["hy_w2"][l])
        put("hy_w3", inputs["hy_w3"][l])
        put("hy_w4", inputs["hy_w4"][l])
        put("hy_b", np.stack([inputs["hy_b1"][l], inputs["hy_b2"][l], inputs["hy_b3"][l]], axis=1))
        put("hy_f", inputs["hy_freq"][l].T)
        put("alog", np.tile(inputs["gdn_a_log"][l].reshape(1, 8), (128, 1)))
        put("dtb", np.tile(inputs["gdn_dt_bias"][l].reshape(1, 8), (128, 1)))
    return pp


def na_consts(inputs):
    rpb = np.asarray(inputs["na_rpb"], np.float32)
    par = np.arange(2)[:, None, None, None]
    kc = np.arange(64)[None, :, None, None]
    i = np.arange(14)[None, None, :, None]
    qc = np.arange(64)[None, None, None, :]
    dc = np.clip(kc - qc, -15, 15) + 15
    di = np.broadcast_to(i + par, (2, 64, 14, 64))
    dcb = np.broadcast_to(dc, (2, 64, 14, 64))
    g = rpb[:, :, di, dcb]
    g = g.transpose(0, 2, 3, 1, 4, 5).reshape(DEPTH, 128, 4 * 14 * 64)
    cs = np.clip(np.arange(64) - 8, 0, 48)
    kcv = np.arange(64)[:, None]
    valid = (kcv >= cs[None, :]) & (kcv < cs[None, :] + 16)
    m = np.where(valid, 0.0, -1e30).astype(np.float32)
    mask = np.concatenate([m, m], axis=0)
    return np.ascontiguousarray(g), np.ascontiguousarray(mask)


def _mix_common_init(self):
    if getattr(self, "pp", None) is not None:
        return
    k = self.k
    self.pp_in = self.din("pp", [128, DEPTH * PP_W])
    self.pp = self.sb(self.st, "pp", [128, DEPTH * PP_W])
    self.d_pp = Dep()
    ch = k.chan()
    k.dma("sp", self.pp[:], self.pp_in[:, :], ch, w=[self.d_pp])


def _ppv(self, l, name, rows=128):
    o, w = PP_OFF[name]
    return self.pp[0:rows, l * PP_W + o: l * PP_W + o + w]


def _blocks(self, with_ctx):
    b = [(i * 512, 512) for i in range(4)]
    if with_ctx:
        b.append((L, LC))
    return b


def _wring_init(self, n=2):
    self.wring = [(self.aa([128, 8, 512], BF16), Dep(), self.wring_ch[i]) for i in range(n)]
    self.wring_i = 0
    self.bank_i = 0


def _proj(self, l, chunks, blocks, evac, banks):
    k = self.k
    for (c0, ncol) in chunks:
        i = self.wring_i
        self.wring_i = (i + 1) % len(self.wring)
        wap, dw, chw = self.wring[i]
        k.dma("pool", wap[:, :, 0:ncol], self.W("w_in")[l, :, c0:c0 + ncol].rearrange("(kt p) c -> p kt c", p=128),
              chw, w=[dw])
        for cc in range(0, ncol, 128):
            m = min(128, ncol - cc)
            for (t0, n) in blocks:
                pi, hb = banks[self.bank_i % len(banks)]
                self.bank_i += 1
                pst = self.ps[pi][0:m, hb * 512: hb * 512 + n]
                dps = self.d_ps[pi][hb]
                hd = [self.d_H[t] for t in range(t0 // 128, (t0 + n + 127) // 128)]
                for dt in range(8):
                    k.pe(lambda e, dt=dt, pst=pst, wap=wap, cc=cc, m=m, t0=t0, n=n: e.matmul(
                        pst, wap[:, dt, cc:cc + m], self.H[:, dt, t0:t0 + n], start=(dt == 0), stop=(dt == 7)),
                        r=[dw] + hd, w=[dps])
                evac(c0 + cc, m, t0, n, pst, dps)


def _dwconv(self, eng, out_ap, in_ap, w3, ranges, r, w):
    k = self.k
    for (a, b) in ranges:
        k.op(eng, lambda e, a=a, b=b: e.tensor_scalar(out=out_ap[:, a:b], in0=in_ap[:, a:b], scalar1=w3[:, 1:2],
                                                      scalar2=None, op0=ALU.mult), r=r, w=w)
        k.op(eng, lambda e, a=a, b=b: e.scalar_tensor_tensor(out=out_ap[:, a + 1:b], in0=in_ap[:, a:b - 1], scalar=w3[:, 0:1],
                                                             in1=out_ap[:, a + 1:b], op0=ALU.mult, op1=ALU.add),
             r=list(r) + list(w), w=w)
        k.op(eng, lambda e, a=a, b=b: e.scalar_tensor_tensor(out=out_ap[:, a:b - 1], in0=in_ap[:, a + 1:b], scalar=w3[:, 2:3],
                                                             in1=out_ap[:, a:b - 1], op0=ALU.mult, op1=ALU.add),
             r=list(r) + list(w), w=w)


PROJ_BANKS = [(0, 0), (0, 1), (1, 0), (1, 1)]


def _mix_sc(self, l, with_ctx):
    k = self.k
    self.arena_reset()
    _wring_init(self)
    ranges = [(0, L)] + ([(L, T)] if with_ctx else [])
    blocks = _blocks(self, with_ctx)
    ntok = T if with_ctx else L
    ntt = ntok // 128
    pxb = self.aa([128, 6, T], BF16)
    d_px = [Dep() for _ in range(6)]

    def evac(c, m, t0, n, pst, dps):
        ct = (c - OFF_SC) // 128
        k.act(lambda e: e.activation(out=pxb[:, ct, t0:t0 + n], in_=pst, func=AF.Copy), r=[dps], w=[d_px[ct]])
    _proj(self, l, [(OFF_SC, 512), (OFF_SC + 512, 256)], blocks, evac, PROJ_BANKS)
    z = [self.aa([128, T]) for _ in range(2)]
    acc = [self.aa([128, T]) for _ in range(2)]
    scw = _ppv(self, l, "scw")
    for j in range(2):
        dz, dacc = Dep(), Dep()
        eng = "dve" if j == 0 else "pool"
        k.op(eng, lambda e, j=j: e.tensor_tensor(out=z[j][:, 0:ntok], in0=pxb[:, 2 + j, 0:ntok], in1=pxb[:, 4 + j, 0:ntok],
                                                 op=ALU.mult), r=[d_px[2 + j], d_px[4 + j]], w=[dz])
        _dwconv(self, "dve", acc[j], z[j], scw[:, j * 3:(j + 1) * 3], ranges, [dz, self.d_pp], [dacc])
        k.op(eng, lambda e, j=j: e.tensor_tensor(out=self.C[:, 2 + j, 0:ntok], in0=pxb[:, j, 0:ntok], in1=acc[j][:, 0:ntok],
                                                 op=ALU.mult), r=[d_px[j], dacc], w=[self.d_C[t] for t in range(ntt)])


def _mixers(self, l):
    k = self.k
    if self.inject_cat:
        ch = k.chan()
        for tt in range(NT):
            k.dma("sp", self.C[:, :, tt * 128:(tt + 1) * 128],
                  self.cat_in[:, tt * 128:(tt + 1) * 128].rearrange("(a p) t -> p a t", p=128), ch, w=[self.d_C[tt]])
        return
    _mix_common_init(self)
    if not hasattr(self, "wring_ch"):
        self.wring_ch = [k.chan() for _ in range(3)]
    with_ctx = l < DEPTH - 1
    if "sc" in self.mixers:
        _mix_sc(self, l, with_ctx)
    if "na" in self.mixers:
        _mix_na(self, l, with_ctx)
    if "hy" in self.mixers:
        _mix_hy(self, l, with_ctx)
    if "gdn" in self.mixers:
        _mix_gdn(self, l, with_ctx)


MK._mixers = _mixers


def const_inputs(inputs):
    out = {}
    g, mask = na_consts(inputs)
    out["na_rpbg"] = g
    out["na_mask"] = mask
    out.update(hy_consts())
    out.update(gdn_consts())
    return out


def _mix_na(self, l, with_ctx):
    k = self.k
    self.arena_reset()
    _wring_init(self)
    blocks = _blocks(self, True)
    qT = self.aa([128, 2, T], BF16)
    kT = self.aa([128, 2, T], BF16)
    d_q = [Dep() for _ in range(NT)]
    d_k = [Dep() for _ in range(NT)]

    def evac(c, m, t0, n, pst, dps):
        ct = (c - OFF_NA) // 128
        tts = range(t0 // 128, (t0 + n) // 128)
        if ct < 2:
            k.act(lambda e: e.activation(out=qT[:, ct, t0:t0 + n], in_=pst, func=AF.Copy, scale=0.125),
                  r=[dps], w=[d_q[t] for t in tts])
        else:
            k.dve(lambda e: e.tensor_copy(out=kT[:, ct - 2, t0:t0 + n], in_=pst), r=[dps], w=[d_k[t] for t in tts])
    _proj(self, l, [(OFF_NA, 512)], blocks, evac, PROJ_BANKS)
    Ve = self.aa([128, NT, 4, 65], BF16)
    Vo = self.aa([128, 15, 4, 65], BF16)
    d_Ve, d_Vo = Dep(), Dep()
    k.pool(lambda e: e.memset(Ve, 1.0), w=[d_Ve])
    k.pool(lambda e: e.memset(Vo, 1.0), w=[d_Vo])
    i = self.wring_i
    self.wring_i = (i + 1) % len(self.wring)
    wv, dwv, chv = self.wring[i]
    k.dma("pool", wv[:, :, 0:256], self.W("w_in")[l, :, OFF_NA + 512:OFF_NA + 768].rearrange("(kt p) c -> p kt c", p=128),
          chv, w=[dwv])
    nb = 0
    for (Vx, dV, ntl, off) in ((Ve, d_Ve, NT, 0), (Vo, d_Vo, 15, 64)):
        for j in range(ntl):
            pi, hb = PROJ_BANKS[nb % 4]
            nb += 1
            pst = self.ps[pi][:, hb * 512: hb * 512 + 256]
            dps = self.d_ps[pi][hb]
            a = off + j * 128
            hd = [self.d_H[t] for t in range(a // 128, (a + 255) // 128)]
            for dt in range(8):
                k.pe(lambda e, dt=dt, pst=pst, a=a: e.matmul(pst, self.H[:, dt, a:a + 128], wv[:, dt, 0:256],
                                                             start=(dt == 0), stop=(dt == 7)), r=[dwv] + hd, w=[dps])
            k.act(lambda e, Vx=Vx, j=j, pst=pst: e.activation(out=Vx[:, j, :, 0:64], in_=pst.rearrange("p (h d) -> p h d", h=4),
                                                              func=AF.Copy), r=[dps], w=[dV])
    T2 = self.aa([128, 4, 14, 64])
    msk = self.aa([128, 64])
    d_T2 = Dep()
    d_msk = d_T2
    if not hasattr(self, "na_rpbg_in"):
        self.na_rpbg_in = self.din("na_rpbg", [DEPTH, 128, 4 * 14 * 64])
        self.na_mask_in = self.din("na_mask", [128, 64])
        self.ch_na = self.k.chan()
    k.dma("sp", T2.rearrange("p a b c -> p (a b c)"), self.na_rpbg_in[l], self.ch_na, w=[d_T2])
    k.dma("sp", msk, self.na_mask_in[:, :], self.ch_na, w=[d_msk])
    k.dve(lambda e: e.tensor_tensor(out=T2.rearrange("p a b c -> p (a b) c"), in0=T2.rearrange("p a b c -> p (a b) c"),
                                    in1=msk.unsqueeze(1).to_broadcast([128, 56, 64]), op=ALU.add),
          r=[d_T2, d_msk], w=[d_T2])
    Sb = [self.aa([128, 4, 64]) for _ in range(2)]
    d_Sb = [Dep(), Dep()]
    E = [self.aa([128, 6, 64], BF16) for _ in range(3)]
    d_E = [Dep() for _ in range(3)]
    rs = [self.aa([64, 4]) for _ in range(2)]
    d_rs = [Dep(), Dep()]
    On = [self.aa([64, 256], BF16) for _ in range(2)]
    d_On = [Dep(), Dep()]
    SB = [(2, 0), (2, 1), (3, 0), (3, 1)]
    OB = [(0, 0), (0, 1)]
    TB = (1, 0)
    psT = self.ps[TB[0]][:, TB[1] * 512: TB[1] * 512 + 512].bitcast(BF16)
    d_psT = self.d_ps[TB[0]][TB[1]]
    def unit_S(r, h, n):
        s = min(max(r - 4, 0), 24)
        hp = slice(64 * (h % 2), 64 * (h % 2) + 64)
        hc = h // 2
        tq = (64 * r) // 128
        spi, shb = SB[n % 4]
        S_ps = self.ps[spi][:, shb * 512: shb * 512 + 384]
        d_S = self.d_ps[spi][shb]
        for kt in range(6):
            ks = 64 * s + 128 * kt if kt < 4 else L + 128 * (kt - 4)
            kd = [d_k[t] for t in range(ks // 128, (ks + 255) // 128)]
            k.pe(lambda e, kt=kt, ks=ks: e.matmul(
                S_ps[:, kt * 64:(kt + 1) * 64], kT[hp, hc, ks:ks + 128], qT[hp, hc, 64 * r:64 * r + 64],
                start=True, stop=True), r=kd + [d_q[tq]], w=[d_S])

    def unit_BE(r, h, n):
        s = min(max(r - 4, 0), 24)
        base = s - r + 7
        spi, shb = SB[n % 4]
        S_ps = self.ps[spi][:, shb * 512: shb * 512 + 384]
        d_S = self.d_ps[spi][shb]
        sb_i = n % 2
        e_i = n % 3
        k.dve(lambda e: e.tensor_tensor(
            out=Sb[sb_i], in0=S_ps[:, 0:256].rearrange("p (a b) -> p a b", a=4),
            in1=T2[:, h, base:base + 7:2, :], op=ALU.add), r=[d_S, d_T2], w=[d_Sb[sb_i]])
        k.act(lambda e: e.activation(out=E[e_i][:, 0:4, :], in_=Sb[sb_i], func=AF.Exp),
              r=[d_Sb[sb_i]], w=[d_E[e_i]])
        k.act(lambda e: e.activation(out=E[e_i][:, 4:6, :], in_=S_ps[:, 256:384].rearrange("p (a b) -> p a b", a=2),
                                     func=AF.Exp), r=[d_S], w=[d_E[e_i]])

    def unit_PV(r, h, n):
        s = min(max(r - 4, 0), 24)
        e_i = n % 3
        opi, ohb = OB[r % 2]
        O_ps = self.ps[opi][0:64, ohb * 512: ohb * 512 + 260]
        d_O = self.d_ps[opi][ohb]
        for kt in range(6):
            if kt < 4:
                if s % 2 == 0:
                    vt, dv = Ve[:, s // 2 + kt, h, :], d_Ve
                else:
                    vt, dv = Vo[:, (s - 1) // 2 + kt, h, :], d_Vo
            else:
                vt, dv = Ve[:, 16 + kt - 4, h, :], d_Ve
            k.pe(lambda e, kt=kt, vt=vt: e.matmul(
                O_ps[:, h * 65:(h + 1) * 65], E[e_i][:, kt, :], vt, start=(kt == 0), stop=(kt == 5)),
                r=[d_E[e_i], dv], w=[d_O])

    def unit_FIN(r):
        tq = (64 * r) // 128
        opi, ohb = OB[r % 2]
        O_ps = self.ps[opi][0:64, ohb * 512: ohb * 512 + 260]
        d_O = self.d_ps[opi][ohb]
        j = r % 2
        O3 = O_ps.rearrange("p (h d) -> p h d", h=4)
        k.dve(lambda e: e.reciprocal(out=rs[j], in_=O3[:, :, 64]), r=[d_O], w=[d_rs[j]])
        k.dve(lambda e: e.tensor_tensor(out=On[j].rearrange("p (h d) -> p h d", h=4), in0=O3[:, :, 0:64],
                                        in1=rs[j].unsqueeze(2).to_broadcast([64, 4, 64]), op=ALU.mult),
              r=[d_O, d_rs[j]], w=[d_On[j]])
        for hc in range(2):
            k.pe(lambda e, hc=hc: e.transpose(out=psT[:, hc * 64:(hc + 1) * 64], in_=On[j][:, hc * 128:(hc + 1) * 128],
                                              identity=self.ident_b[0:64, 0:64]), r=[d_On[j], self.d_const], w=[d_psT])
        k.act(lambda e: e.activation(out=self.C[:, 6:8, 64 * r:64 * r + 64],
                                     in_=psT[:, 0:128].rearrange("p (a b) -> p a b", a=2), func=AF.Copy),
              r=[d_psT], w=[self.d_C[tq]])

    units = [(r, h) for r in range(32) for h in range(4)]
    unit_S(units[0][0], units[0][1], 0)
    for i, (r, h) in enumerate(units):
        unit_BE(r, h, i)
        if i + 1 < len(units):
            unit_S(units[i + 1][0], units[i + 1][1], i + 1)
        unit_PV(r, h, i)
        if h == 3:
            unit_FIN(r)
    n = len(units)
    if with_ctx:
        Ec = self.aa([128, 2, 256], BF16)
        d_Ec = Dep()
        Onc = self.aa([128, 256], BF16)
        d_Onc = Dep()
        rsc = self.aa([128, 4])
        d_rsc = Dep()
        for qt in range(2):
            opi, ohb = OB[qt % 2]
            O_ps = self.ps[opi][:, ohb * 512: ohb * 512 + 260]
            d_O = self.d_ps[opi][ohb]
            for h in range(4):
                hp = slice(64 * (h % 2), 64 * (h % 2) + 64)
                hc = h // 2
                spi, shb = SB[n % 4]
                n += 1
                S_ps = self.ps[spi][:, shb * 512: shb * 512 + 256]
                d_S = self.d_ps[spi][shb]
                for c in range(2):
                    k.pe(lambda e, c=c, S_ps=S_ps, hp=hp, hc=hc, qt=qt: e.matmul(
                        S_ps[:, c * 128:(c + 1) * 128], kT[hp, hc, L + 128 * c:L + 128 * c + 128],
                        qT[hp, hc, L + 128 * qt:L + 128 * qt + 128], start=True, stop=True),
                        r=[d_k[16 + c], d_q[16 + qt]], w=[d_S])
                k.act(lambda e, S_ps=S_ps: e.activation(out=Ec[:, :, 0:128], in_=S_ps.rearrange("p (a b) -> p a b", a=2),
                                                        func=AF.Exp), r=[d_S], w=[d_Ec])
                for c in range(2):
                    k.pe(lambda e, c=c, O_ps=O_ps, h=h: e.matmul(O_ps[:, h * 65:(h + 1) * 65], Ec[:, c, 0:128],
                                                                 Ve[:, 16 + c, h, :], start=(c == 0), stop=(c == 1)),
                         r=[d_Ec, d_Ve], w=[d_O])
            O3 = O_ps.rearrange("p (h d) -> p h d", h=4)
            k.dve(lambda e, O3=O3: e.reciprocal(out=rsc, in_=O3[:, :, 64]), r=[d_O], w=[d_rsc])
            k.dve(lambda e, O3=O3: e.tensor_tensor(out=Onc.rearrange("p (h d) -> p h d", h=4), in0=O3[:, :, 0:64],
                                                   in1=rsc.unsqueeze(2).to_broadcast([128, 4, 64]), op=ALU.mult),
                  r=[d_O, d_rsc], w=[d_Onc])
            for hc in range(2):
                k.pe(lambda e, hc=hc: e.transpose(out=psT[:, hc * 128:(hc + 1) * 128], in_=Onc[:, hc * 128:(hc + 1) * 128],
                                                  identity=self.ident_b[:]), r=[d_Onc, self.d_const], w=[d_psT])
            k.act(lambda e, qt=qt: e.activation(out=self.C[:, 6:8, L + 128 * qt:L + 128 * qt + 128],
                                                in_=psT[:, 0:256].rearrange("p (a b) -> p a b", a=2), func=AF.Copy),
                  r=[d_psT], w=[self.d_C[16 + qt]])


import math
HY_EMB = 33
HY_BANDS = 16


def hy_consts():
    bf = ml_dtypes.bfloat16
    out = {}
    max_decay = math.log(1e-2) / 0.3
    min_decay = math.log(1e-2) / 1.5
    deltas = np.abs(np.linspace(min_decay, max_decay, 256, dtype=np.float32))
    for tag, Ls in (("lat", L), ("ctx", LC)):
        nt = Ls // 128
        t = np.linspace(0.0, 1.0, Ls, dtype=np.float32)[:, None]
        bands = np.linspace(1e-4, HY_BANDS - 1, HY_BANDS, dtype=np.float32)
        ang = (np.float32(2.0 * math.pi / Ls)) * np.arange(Ls, dtype=np.float32)[:, None] * bands
        z = np.concatenate([t, np.cos(ang), -np.sin(ang)], axis=-1).astype(np.float32)
        out["hy_zT_" + tag] = np.ascontiguousarray(z.T)
        dec = np.exp(-t * deltas[None, :]).astype(np.float32)
        out["hy_dec_" + tag] = np.ascontiguousarray(dec.reshape(nt, 128, 256).transpose(1, 0, 2).reshape(128, nt * 256))
        N = 2 * Ls
        tt_ = np.arange(Ls, dtype=np.int64)
        ff = np.arange(Ls, dtype=np.int64)
        m = ((2 * ff[None, :] + 1) * tt_[:, None]) % (2 * N)
        th = m.astype(np.float64) * (math.pi / N)
        Cm = np.cos(th)
        Sm = -np.sin(th)
        for nm, M_ in (("C", Cm), ("S", Sm)):
            f4 = M_.reshape(nt, 128, nt, 128).transpose(2, 1, 0, 3).reshape(nt, 128, nt * 128)
            out["hy_%sf_%s" % (nm, tag)] = np.ascontiguousarray(f4).astype(bf)
            Wd = min(512, Ls)
            ntb = Ls // Wd
            i4 = M_.reshape(ntb, Wd, nt, 128).transpose(0, 3, 2, 1).reshape(ntb, 128, nt * Wd)
            out["hy_%si_%s" % (nm, tag)] = np.ascontiguousarray(i4).astype(bf)
    return out


def _arena_rewind(self, mark):
    self.k.barrier()
    self.ar_off = mark


def _hy_filters(self, l, Ls, tag, WHx, d_WHx):
    k = self.k
    nt = Ls // 128
    BW = min(512, Ls)
    nb = Ls // BW
    zin = self.din("hy_zT_" + tag, [HY_EMB, Ls]) if ("hy_zT_" + tag) not in self.inputs else self._hyin["hy_zT_" + tag]
    din_dec = self.din("hy_dec_" + tag, [128, nt * 256]) if ("hy_dec_" + tag) not in self.inputs else self._hyin["hy_dec_" + tag]
    self._hyin["hy_zT_" + tag] = zin
    self._hyin["hy_dec_" + tag] = din_dec
    zT = self.aa([HY_EMB, Ls])
    dec = self.aa([128, nt, 256])
    d_z = Dep()
    d_dec = d_z
    k.dma("sp", zT, zin[:, :], self.ch_hy, w=[d_z])
    k.dma("sp", dec.rearrange("p a b -> p (a b)"), din_dec[:, :], self.ch_hy, w=[d_dec])
    hb = [self.aa([64, Ls]) for _ in range(2)]
    d_hb = [[Dep() for _ in range(nb)] for _ in range(2)]
    vs = [self.aa([64, 512]) for _ in range(2)]
    kis = [self.aa([64, 512], I32) for _ in range(2)]
    kfs = [self.aa([64, 512]) for _ in range(2)]
    d_vs, d_kis, d_kfs = [Dep(), Dep()], [Dep(), Dep()], [Dep(), Dep()]
    fb = self.aa([64, 3])
    d_fb = Dep()
    fr = _ppv(self, l, "hy_f", 64)
    bb = _ppv(self, l, "hy_b", 64)
    k.dve(lambda e: e.tensor_tensor(out=fb, in0=fr, in1=bb, op=ALU.mult), r=[self.d_pp], w=[d_fb])
    ws = [_ppv(self, l, "hy_w1", HY_EMB), _ppv(self, l, "hy_w2", 64), _ppv(self, l, "hy_w3", 64)]
    PB = [(2, 0), (2, 1)]
    nps = 0
    src, d_srcs = zT, [d_z] * nb
    for li in range(3):
        dst, d_dsts = hb[li % 2], d_hb[li % 2]
        for b in range(nb):
            v, ki, kf = vs[b % 2], kis[b % 2], kfs[b % 2]
            d_v, d_ki, d_kf = d_vs[b % 2], d_kis[b % 2], d_kfs[b % 2]
            d_src, d_dst = d_srcs[b], d_dsts[b]
            pi, hbk = PB[nps % 2]
            nps += 1
            pst = self.ps[pi][0:64, hbk * 512: hbk * 512 + BW]
            dps = self.d_ps[pi][hbk]
            k.pe(lambda e, li=li, b=b, pst=pst, src=src: e.matmul(pst, ws[li], src[:, b * BW:(b + 1) * BW], start=True, stop=True),
                 r=[self.d_pp, d_src], w=[dps])
            k.dve(lambda e, li=li, pst=pst: e.tensor_scalar(out=v[:, 0:BW], in0=pst, scalar1=fr[:, li:li + 1], scalar2=fb[:, li:li + 1],
                                                            op0=ALU.mult, op1=ALU.add), r=[dps, d_fb, self.d_pp], w=[d_v])
            k.dve(lambda e: e.tensor_scalar(out=ki[:, 0:BW], in0=v[:, 0:BW], scalar1=1.0 / (2.0 * math.pi), scalar2=None, op0=ALU.mult),
                  r=[d_v], w=[d_ki])
            k.dve(lambda e: e.tensor_copy(out=kf[:, 0:BW], in_=ki[:, 0:BW]), r=[d_ki], w=[d_kf])
            k.dve(lambda e: e.scalar_tensor_tensor(out=v[:, 0:BW], in0=kf[:, 0:BW], scalar=-2.0 * math.pi, in1=v[:, 0:BW],
                                                   op0=ALU.mult, op1=ALU.add), r=[d_kf, d_v], w=[d_v])
            k.dve(lambda e: e.tensor_scalar(out=v[:, 0:BW], in0=v[:, 0:BW], scalar1=3.1415925, scalar2=-3.1415925,
                                            op0=ALU.min, op1=ALU.max), r=[d_v], w=[d_v])
            k.act(lambda e, dst=dst, b=b: e.activation(out=dst[:, b * BW:(b + 1) * BW], in_=v[:, 0:BW], func=AF.Sin),
                  r=[d_v], w=[d_dst])
        src, d_srcs = dst, d_dsts
    h3, d_h3s = src, d_srcs
    w4 = _ppv(self, l, "hy_w4", 64)
    hd = [self.aa([128, 2, 256]) for _ in range(2)]
    d_hd = [Dep(), Dep()]
    ab = [self.aa([128, 512]) for _ in range(2)]
    d_ab = [Dep(), Dep()]
    nrm_ps = self.ps[3][:, 0:512]
    d_nrm = self.d_ps[3][0]
    rn = self.aa([128, 256])
    d_rn = Dep()
    for pss in range(2):
        for tt in range(nt):
            pi, hbk = PB[nps % 2]
            nps += 1
            pst = self.ps[pi][:, hbk * 512: hbk * 512 + 512]
            dps = self.d_ps[pi][hbk]
            k.pe(lambda e, tt=tt, pst=pst: e.matmul(pst, h3[:, tt * 128:(tt + 1) * 128], w4, start=True, stop=True),
                 r=[d_h3s[(tt * 128) // BW], self.d_pp], w=[dps])
            j = tt % 2
            k.dve(lambda e, j=j, tt=tt, pst=pst: e.tensor_tensor(out=hd[j], in0=pst.rearrange("p (a b) -> p a b", a=2),
                                                                 in1=dec[:, tt, :].unsqueeze(1).to_broadcast([128, 2, 256]),
                                                                 op=ALU.mult), r=[dps, d_dec], w=[d_hd[j]])
            if pss == 0:
                k.act(lambda e, j=j: e.activation(out=ab[j], in_=hd[j].rearrange("p a b -> p (a b)"), func=AF.Abs),
                      r=[d_hd[j]], w=[d_ab[j]])
                k.pe(lambda e, j=j, tt=tt: e.matmul(nrm_ps, self.ones_f[:], ab[j], start=(tt == 0), stop=(tt == nt - 1)),
                     r=[d_ab[j], self.d_const], w=[d_nrm])
            else:
                if tt == 0:
                    k.dve(lambda e, j=j: e.memset(hd[j][0:1, 1, :], 0.0), r=[d_hd[j]], w=[d_hd[j]])
                k.dve(lambda e, j=j: e.tensor_tensor(out=hd[j], in0=hd[j], in1=rn.unsqueeze(1).to_broadcast([128, 2, 256]),
                                                     op=ALU.mult), r=[d_hd[j], d_rn], w=[d_hd[j]])
                k.dve(lambda e, j=j, tt=tt: e.tensor_tensor(out=WHx[:, tt, 1, :], in0=hd[j][:, 0, :], in1=hd[j][:, 1, :], op=ALU.add),
                      r=[d_hd[j]], w=[d_WHx])
                k.pool(lambda e, j=j, tt=tt: e.tensor_tensor(out=WHx[:, tt, 2, :], in0=hd[j][:, 0, :], in1=hd[j][:, 1, :],
                                                             op=ALU.subtract), r=[d_hd[j]], w=[d_WHx])
        if pss == 0:
            k.dve(lambda e: e.tensor_copy(out=rn, in_=nrm_ps[:, 0:256]), r=[d_nrm], w=[d_rn])
            k.dve(lambda e: e.tensor_tensor(out=rn, in0=rn, in1=nrm_ps[:, 256:512], op=ALU.add), r=[d_nrm, d_rn], w=[d_rn])
            k.dve(lambda e: e.reciprocal(out=rn, in_=rn), r=[d_rn], w=[d_rn])
            k.dve(lambda e: e.tensor_scalar(out=rn, in0=rn, scalar1=2.0 / (2 * Ls), scalar2=None, op0=ALU.mult), r=[d_rn], w=[d_rn])


def _hy_dft(self, l, Ls, tag, toff, WHx, d_WHx, x0T, wTm, d_x0w, Yh, ring):
    k = self.k
    nt = Ls // 128
    Wd = min(512, Ls)
    ntb = Ls // Wd
    names = ["hy_Cf_", "hy_Sf_", "hy_Ci_", "hy_Si_"]
    tabs = []
    for nm in names:
        key = nm + tag
        if key not in self._hyin:
            shp = [nt, 128, nt * 128] if nm[4] == "f" else [ntb, 128, nt * Wd]
            self._hyin[key] = self.din(key, shp, BF16)
        tabs.append(self._hyin[key])
    Cf, Sf, Ci, Si = tabs
    d_Yh = Dep()
    Kr = self.aa([128, 4, 256])
    d_K = [Dep(), Dep()]
    tm = [self.aa([128, 256]) for _ in range(4)]
    d_tm = [Dep() for _ in range(4)]
    hybias = _ppv(self, l, "hybias")
    for j in range(nt):
        slot, dsl, chs = ring[self.hyring_i % len(ring)]
        self.hyring_i += 1
        cst = slot[:, 0:nt * 128].rearrange("p (a b) -> p a b", b=128)
        sst = slot[:, 2048:2048 + nt * 128].rearrange("p (a b) -> p a b", b=128)
        k.dma("sp", slot[:, 0:nt * 128], Cf[j], chs, w=[dsl])
        k.dma("sp", slot[:, 2048:2048 + nt * 128], Sf[j], chs, w=[dsl])
        ps_r = self.ps[0][:, 0:512]
        ps_i = self.ps[0][:, 512:1024]
        for tt in range(nt):
            k.pe(lambda e, tt=tt, cst=cst: e.matmul(ps_r, cst[:, tt, :], WHx[:, tt, 0:2, :], start=(tt == 0), stop=(tt == nt - 1)),
                 r=[dsl, d_WHx], w=[self.d_ps[0][0]])
        for tt in range(nt):
            k.pe(lambda e, tt=tt, sst=sst: e.matmul(ps_i, sst[:, tt, :], WHx[:, tt, 0:3:2, :], start=(tt == 0), stop=(tt == nt - 1)),
                 r=[dsl, d_WHx], w=[self.d_ps[0][1]])
        k.act(lambda e: e.activation(out=Kr[:, 0:2, :], in_=ps_r.rearrange("p (a b) -> p a b", a=2), func=AF.Copy),
              r=[self.d_ps[0][0]], w=[d_K[0]])
        k.act(lambda e: e.activation(out=Kr[:, 2:4, :], in_=ps_i.rearrange("p (a b) -> p a b", a=2), func=AF.Copy),
              r=[self.d_ps[0][1]], w=[d_K[1]])
        Ur, Kre, Ui, Kie = Kr[:, 0, :], Kr[:, 1, :], Kr[:, 2, :], Kr[:, 3, :]
        k.dve(lambda e: e.tensor_tensor(out=tm[0], in0=Ur, in1=Kre, op=ALU.mult), r=[d_K[0]], w=[d_tm[0]])
        k.pool(lambda e: e.tensor_tensor(out=tm[1], in0=Ui, in1=Kie, op=ALU.mult), r=[d_K[1]], w=[d_tm[1]])
        k.dve(lambda e, j=j: e.tensor_tensor(out=Yh[:, j, 0, :], in0=tm[0], in1=tm[1], op=ALU.subtract),
              r=[d_tm[0], d_tm[1]], w=[d_Yh])
        k.pool(lambda e: e.tensor_tensor(out=tm[2], in0=Ur, in1=Kie, op=ALU.mult), r=d_K, w=[d_tm[2]])
        k.dve(lambda e: e.tensor_tensor(out=tm[3], in0=Ui, in1=Kre, op=ALU.mult), r=d_K, w=[d_tm[3]])
        k.pool(lambda e, j=j: e.tensor_tensor(out=Yh[:, j, 1, :], in0=tm[2], in1=tm[3], op=ALU.add),
               r=[d_tm[2], d_tm[3]], w=[d_Yh])
    G = 4 if nt >= 4 else nt
    yt = [self.aa([128, 512]) for _ in range(2)]
    d_yt = [Dep(), Dep()]
    for tb in range(ntb):
        for g in range(nt // G):
            slot, dsl, chs = ring[self.hyring_i % len(ring)]
            self.hyring_i += 1
            k.dma("sp", slot[:, 0:G * Wd], Ci[tb, :, g * G * Wd:(g + 1) * G * Wd], chs, w=[dsl])
            k.dma("sp", slot[:, 2048:2048 + G * Wd], Si[tb, :, g * G * Wd:(g + 1) * G * Wd], chs, w=[dsl])
            cst = slot[:, 0:G * Wd].rearrange("p (a b) -> p a b", b=Wd)
            sst = slot[:, 2048:2048 + G * Wd].rearrange("p (a b) -> p a b", b=Wd)
            for fi in range(G):
                ft = g * G + fi
                for ct in range(2):
                    py = self.ps[1][:, ct * 512: ct * 512 + Wd]
                    k.pe(lambda e, fi=fi, ft=ft, ct=ct, py=py, cst=cst: e.matmul(py, Yh[:, ft, 0, ct * 128:(ct + 1) * 128], cst[:, fi, :],
                                                                               start=(ft == 0), stop=False),
                         r=[dsl, d_Yh], w=[self.d_ps[1][ct]])
                    k.pe(lambda e, fi=fi, ft=ft, ct=ct, py=py, sst=sst: e.matmul(py, Yh[:, ft, 1, ct * 128:(ct + 1) * 128], sst[:, fi, :],
                                                                               start=False, stop=(ft == nt - 1)),
                         r=[dsl, d_Yh], w=[self.d_ps[1][ct]])
        a = toff + tb * Wd
        tts = [self.d_C[t] for t in range(a // 128, (a + Wd) // 128)]
        for ct in range(2):
            py = self.ps[1][:, ct * 512: ct * 512 + Wd]
            k.dve(lambda e, ct=ct, py=py, a=a: e.scalar_tensor_tensor(out=yt[ct][:, 0:Wd], in0=wTm[:, ct, a:a + Wd], scalar=hybias[:, ct:ct + 1],
                                                                      in1=py, op0=ALU.mult, op1=ALU.add),
                  r=[self.d_ps[1][ct], d_x0w, self.d_pp], w=[d_yt[ct]])
            k.pool(lambda e, ct=ct, a=a: e.tensor_tensor(out=self.C[:, 4 + ct, a:a + Wd], in0=yt[ct][:, 0:Wd], in1=x0T[:, ct, a:a + Wd],
                                                         op=ALU.mult), r=[d_yt[ct], d_x0w], w=tts)


def _mix_hy(self, l, with_ctx):
    k = self.k
    self.arena_reset()
    if not hasattr(self, "_hyin"):
        self._hyin = {}
        self.ch_hy = k.chan()
        self.ch_hyring = [k.chan() for _ in range(3)]
    ntok = T if with_ctx else L
    WH = self.aa([128, 16, 3, 256], BF16)
    d_WH = Dep()
    if with_ctx:
        WHc = self.aa([128, 2, 3, 256], BF16)
        d_WHc = Dep()
    x0T = self.aa([128, 2, T], BF16)
    wTm = self.aa([128, 2, T], BF16)
    d_x0w = Dep()
    markA = self.ar_off
    _hy_filters(self, l, L, "lat", WH, d_WH)
    if with_ctx:
        _arena_rewind(self, markA)
        _hy_filters(self, l, LC, "ctx", WHc, d_WHc)
    _arena_rewind(self, markA)
    _wring_init(self)
    ranges = [(0, L)] + ([(L, T)] if with_ctx else [])
    blocks = _blocks(self, with_ctx)
    pin = [self.aa([128, T]) for _ in range(1)]
    d_pin = [Dep()]
    x1v = self.aa([128, 4, T], BF16)
    d_x1v = [Dep() for _ in range(4)]
    cacc = [self.aa([128, T]) for _ in range(1)]
    d_cacc = Dep()
    hyw = _ppv(self, l, "hyw")

    def evac(c, m, t0, n, pst, dps):
        ct = (c - OFF_HY) // 128
        j = 0
        k.act(lambda e: e.activation(out=pin[j][:, t0:t0 + n], in_=pst, func=AF.Copy), r=[dps], w=[d_pin[j]])
        if (t0, n) == blocks[-1]:
            _dwconv(self, "dve", cacc[0], pin[j], hyw[:, ct * 3:(ct + 1) * 3], ranges, [d_pin[j], self.d_pp], [d_cacc])
            if ct < 2:
                k.pool(lambda e: e.tensor_copy(out=x0T[:, ct, 0:ntok], in_=cacc[0][:, 0:ntok]), r=[d_cacc], w=[d_x0w])
            else:
                k.pool(lambda e: e.tensor_copy(out=x1v[:, ct - 2, 0:ntok], in_=cacc[0][:, 0:ntok]), r=[d_cacc], w=[d_x1v[ct - 2]])
    _proj(self, l, [(OFF_HY, 512), (OFF_HY + 512, 256)], blocks, evac, PROJ_BANKS)
    for ct in range(2):
        k.pool(lambda e, ct=ct: e.tensor_tensor(out=wTm[:, ct, 0:ntok], in0=x1v[:, ct, 0:ntok], in1=x1v[:, 2 + ct, 0:ntok], op=ALU.mult),
               r=[d_x1v[ct], d_x1v[2 + ct]], w=[d_x0w])
    psT = self.ps[2][:, 0:512].bitcast(BF16)
    d_psT = self.d_ps[2][0]
    for tt in range(ntok // 128):
        for ct in range(2):
            k.pe(lambda e, tt=tt, ct=ct: e.transpose(out=psT[:, ct * 128:(ct + 1) * 128], in_=wTm[:, ct, tt * 128:(tt + 1) * 128],
                                                     identity=self.ident_b[:]), r=[d_x0w, self.d_const], w=[d_psT])
        if tt < 16:
            k.act(lambda e, tt=tt: e.activation(out=WH[:, tt, 0, :], in_=psT[:, 0:256], func=AF.Copy), r=[d_psT], w=[d_WH])
        else:
            k.act(lambda e, tt=tt: e.activation(out=WHc[:, tt - 16, 0, :], in_=psT[:, 0:256], func=AF.Copy), r=[d_psT], w=[d_WHc])
    _arena_rewind(self, markA)
    Yh = self.aa([128, 16, 2, 256], BF16)
    ring = [(self.aa([128, 4096], BF16), Dep(), self.ch_hyring[i]) for i in range(3)]
    self.hyring_i = 0
    markD = self.ar_off
    _hy_dft(self, l, L, "lat", 0, WH, d_WH, x0T, wTm, d_x0w, Yh, ring)
    if with_ctx:
        _arena_rewind(self, markD)
        _hy_dft(self, l, LC, "ctx", L, WHc, d_WHc, x0T, wTm, d_x0w, Yh, ring)


def gdn_consts():
    out = {}
    m = np.arange(128)[:, None]
    i = np.arange(128)[None, :]
    LE = (m <= i).astype(np.float32)
    GE = (m >= i).astype(np.float32)
    GT = (m > i).astype(np.float32)
    LT = (m < i).astype(np.float32)
    NEGF = np.tile(np.where(i > m, -1e9, 0.0).astype(np.float32), (1, 4))
    NEGB = np.tile(np.where(i < m, -1e9, 0.0).astype(np.float32), (1, 4))
    out["gdn_gm"] = np.ascontiguousarray(np.concatenate([LE, GE, GT, LT, NEGF, NEGB], axis=1))
    pm = np.zeros((128, 128), np.float32)
    for mm in range(128):
        if (mm % 64) < 32:
            pm[mm + 32, mm] = -1.0
        else:
            pm[mm - 32, mm] = 1.0
    out["gdn_pm"] = pm
    ii = np.arange(128)[:, None]
    jj = np.arange(128)[None, :]
    lms, ums = [], []
    for s_ in range(7):
        b_ = 1 << s_
        same2 = (ii // (2 * b_)) == (jj // (2 * b_))
        diff1 = (ii // b_) != (jj // b_)
        lms.append(np.where(same2 & diff1 & (jj < ii), -1.0, 0.0))
        ums.append(np.where(same2 & diff1 & (jj > ii), -1.0, 0.0))
    out["gdn_lm"] = np.ascontiguousarray(np.concatenate(lms + ums, axis=1)).astype(ml_dtypes.bfloat16)
    pos = np.arange(L)
    row = (pos // 64).astype(np.float32)
    col = (pos % 64).astype(np.float32)
    inv = (10000.0 ** (-np.arange(16, dtype=np.float32) / 16)).astype(np.float32)
    ang = np.concatenate([row[:, None] * inv, col[:, None] * inv], axis=-1)
    idx = (np.arange(128) % 64) % 32
    cs = np.stack([np.cos(ang).T[idx], np.sin(ang).T[idx]], axis=1)
    out["gdn_rope"] = np.ascontiguousarray(cs.reshape(128, 2 * L)).astype(ml_dtypes.bfloat16)
    return out


def _mix_gdn(self, l, with_ctx):
    k = self.k
    self.arena_reset()
    if not hasattr(self, "gdn_gm_in"):
        self.gdn_gm_in = self.din("gdn_gm", [128, 1536])
        self.gdn_pm_in = self.din("gdn_pm", [128, 128])
        self.gdn_rope_in = self.din("gdn_rope", [128, 2 * L], BF16)
        self.ch_gdn = k.chan()
    otok = self.aa([128, NT, 256], BF16)
    markO = self.ar_off
    qT = self.aa([128, 2, T], BF16)
    kT = self.aa([128, 2, T], BF16)
    d_qk = Dep()
    vtok = self.aa([128, NT, 256], BF16)
    ktok = self.aa([128, NT, 256], BF16)
    d_vtok, d_ktok = Dep(), Dep()
    la = self.aa([128, NT, 8])
    beta = self.aa([128, NT, 8])
    d_la, d_beta = Dep(), Dep()
    GM = self.aa([128, 1536])
    d_GM = Dep()
    d_LMK = d_GM
    d_c1 = d_GM
    k.dma("sp", GM, self.gdn_gm_in[:, :], self.ch_gdn, w=[d_GM])
    LE, GE, GT, LT = (GM[:, i * 128:(i + 1) * 128] for i in range(4))
    NEGF, NEGB = GM[:, 512:1024], GM[:, 1024:1536]
    LMK = self.aa([128, 14 * 128], BF16)
    if not hasattr(self, "gdn_lm_in"):
        self.gdn_lm_in = self.din("gdn_lm", [128, 14 * 128], BF16)
    k.dma("sp", LMK, self.gdn_lm_in[:, :], self.ch_gdn, w=[d_LMK])
    markP = self.ar_off
    _wring_init(self, 1)
    pm = self.aa([128, 128])
    rope = self.aa([128, 2, L], BF16)
    blk64 = self.aa([128, 128])
    d_blk = Dep()
    k.dma("sp", pm, self.gdn_pm_in[:, :], self.ch_gdn, w=[d_c1])
    k.dma("sp", rope.rearrange("p a b -> p (a b)"), self.gdn_rope_in[:, :], self.ch_gdn, w=[d_c1])
    k.pool(lambda e: e.memset(blk64, 0.0), w=[d_blk])
    k.pool(lambda e: e.memset(blk64[0:64, 0:64], 1.0), w=[d_blk])
    k.pool(lambda e: e.memset(blk64[64:128, 64:128], 1.0), w=[d_blk])
    wab, dwab, chab = self.wring[0]
    k.dma("pool", wab[:, :, 0:16], self.W("w_in")[l, :, 1024:1040].rearrange("(kt p) c -> p kt c", p=128), chab, w=[dwab])
    ab_ps = self.ps[3][:, 0:NT * 16]
    d_ab = self.d_ps[3][0]
    for tt in range(NT):
        for dt in range(8):
            k.pe(lambda e, tt=tt, dt=dt: e.matmul(ab_ps[:, tt * 16:(tt + 1) * 16], self.H[:, dt, tt * 128:(tt + 1) * 128],
                                                  wab[:, dt, 0:16], start=(dt == 0), stop=(dt == 7)),
                 r=[dwab, self.d_H[tt]], w=[d_ab])
    ab3 = ab_ps.rearrange("p (t c) -> p t c", c=16)
    xa = self.aa([128, NT, 8])
    ea = self.aa([128, 8])
    d_xa, d_ea = Dep(), Dep()
    k.dve(lambda e: e.tensor_tensor(out=xa, in0=ab3[:, :, 0:8], in1=_ppv(self, l, "dtb").unsqueeze(1).to_broadcast([128, NT, 8]),
                                    op=ALU.add), r=[d_ab, self.d_pp], w=[d_xa])
    k.act(lambda e: e.activation(out=xa, in_=xa, func=AF.Exp), r=[d_xa], w=[d_xa])
    k.act(lambda e: e.activation(out=xa, in_=xa, func=AF.Ln, bias=1.0, scale=1.0), r=[d_xa], w=[d_xa])
    k.act(lambda e: e.activation(out=ea, in_=_ppv(self, l, "alog"), func=AF.Exp), r=[self.d_pp], w=[d_ea])
    k.dve(lambda e: e.scalar_tensor_tensor(out=la, in0=xa, scalar=-1.0, in1=ea.unsqueeze(1).to_broadcast([128, NT, 8]),
                                           op0=ALU.mult, op1=ALU.mult), r=[d_xa, d_ea], w=[d_la])
    k.act(lambda e: e.activation(out=beta, in_=ab3[:, :, 8:16], func=AF.Sigmoid), r=[d_ab], w=[d_beta])
    blocks = _blocks(self, True)
    ranges = [(0, L), (L, T)]
    pin = self.aa([128, T])
    cacc = self.aa([128, T])
    d_pin, d_cacc = Dep(), Dep()
    vTt = self.aa([128, T], BF16)
    d_vTt = Dep()
    rv = self.aa([128, 512])
    t1 = self.aa([128, 512])
    t2 = self.aa([128, 512])
    d_rv, d_t1, d_t2 = Dep(), Dep(), Dep()
    gdw = _ppv(self, l, "gdw")
    psT = self.ps[2][:, 0:512].bitcast(BF16)
    d_psT = self.d_ps[2][0]

    def finish_tile(ct):
        _dwconv(self, "dve", cacc, pin, gdw[:, ct * 3:(ct + 1) * 3], ranges, [d_pin, self.d_pp], [d_cacc])
        if ct >= 4:
            k.act(lambda e: e.activation(out=vTt, in_=cacc, func=AF.Silu), r=[d_cacc], w=[d_vTt])
            for tt in range(NT):
                k.pe(lambda e, tt=tt: e.transpose(out=psT[:, 0:128], in_=vTt[:, tt * 128:(tt + 1) * 128], identity=self.ident_b[:]),
                     r=[d_vTt, self.d_const], w=[d_psT])
                k.dve(lambda e, tt=tt: e.tensor_copy(out=vtok[:, tt, (ct - 4) * 128:(ct - 3) * 128], in_=psT[:, 0:128]),
                      r=[d_psT], w=[d_vtok])
            return
        isq = ct < 2
        dst = qT if isq else kT
        cc = ct % 2
        k.act(lambda e: e.activation(out=pin, in_=cacc, func=AF.Silu), r=[d_cacc, d_pin], w=[d_pin])
        k.act(lambda e: e.activation(out=cacc, in_=pin, func=AF.Square), r=[d_pin, d_cacc], w=[d_cacc])
        for (t0, n) in blocks:
            ss = self.ps[3][:, 512:512 + n]
            dss = self.d_ps[3][1]
            k.pe(lambda e, t0=t0, n=n, ss=ss: e.matmul(ss, blk64, cacc[:, t0:t0 + n], start=True, stop=True), r=[d_cacc, d_blk], w=[dss])
            sc_, bi_ = (64.0, 64.0 * EPS) if isq else (1.0, EPS)
            k.act(lambda e, n=n, ss=ss: e.activation(out=rv[:, 0:n], in_=ss, func=AF.Sqrt, bias=bi_, scale=sc_), r=[dss], w=[d_rv])
            k.dve(lambda e, n=n: e.reciprocal(out=rv[:, 0:n], in_=rv[:, 0:n]), r=[d_rv], w=[d_rv])
            k.dve(lambda e, t0=t0, n=n: e.tensor_tensor(out=pin[:, t0:t0 + n], in0=pin[:, t0:t0 + n], in1=rv[:, 0:n], op=ALU.mult),
                  r=[d_rv, d_pin], w=[d_pin])
            if t0 < L:
                pv = self.ps[2][:, 512:512 + n]
                dpv = self.d_ps[2][1]
                k.pe(lambda e, t0=t0, n=n, pv=pv: e.matmul(pv, pm, pin[:, t0:t0 + n], start=True, stop=True), r=[d_pin, d_c1], w=[dpv])
                k.dve(lambda e, t0=t0, n=n: e.tensor_tensor(out=t1[:, 0:n], in0=pin[:, t0:t0 + n], in1=rope[:, 0, t0:t0 + n], op=ALU.mult),
                      r=[d_pin, d_c1], w=[d_t1])
                k.dve(lambda e, t0=t0, n=n, pv=pv: e.tensor_tensor(out=t2[:, 0:n], in0=pv, in1=rope[:, 1, t0:t0 + n], op=ALU.mult),
                      r=[dpv, d_c1], w=[d_t2])
                k.pool(lambda e, t0=t0, n=n: e.tensor_tensor(out=dst[:, cc, t0:t0 + n], in0=t1[:, 0:n], in1=t2[:, 0:n], op=ALU.add),
                       r=[d_t1, d_t2], w=[d_qk])
            else:
                k.pool(lambda e, t0=t0, n=n: e.tensor_copy(out=dst[:, cc, t0:t0 + n], in_=pin[:, t0:t0 + n]), r=[d_pin], w=[d_qk])
        if not isq:
            for tt in range(NT):
                k.pe(lambda e, tt=tt: e.transpose(out=psT[:, 0:128], in_=kT[:, cc, tt * 128:(tt + 1) * 128], identity=self.ident_b[:]),
                     r=[d_qk, self.d_const], w=[d_psT])
                k.dve(lambda e, tt=tt: e.tensor_copy(out=ktok[:, tt, cc * 128:(cc + 1) * 128], in_=psT[:, 0:128]),
                      r=[d_psT], w=[d_ktok])

    def evac(c, m, t0, n, pst, dps):
        ct = c // 128
        k.act(lambda e: e.activation(out=pin[:, t0:t0 + n], in_=pst, func=AF.Copy), r=[dps], w=[d_pin])
        if (t0, n) == blocks[-1]:
            finish_tile(ct)
    _proj(self, l, [(512, 256), (256, 256), (0, 256)], blocks, evac, PROJ_BANKS)
    import os
    _stop = os.environ.get("GDN_STOP", "")
    if "gdn_g1" in self.dbg:
        self.k.barrier()
        chd = k.chan()
        for nm, ap_, n_ in (("qT", qT.rearrange("p a b -> p (a b)"), 2 * T), ("kT", kT.rearrange("p a b -> p (a b)"), 2 * T),
                            ("vtok", vtok.rearrange("p a b -> p (a b)"), NT * 256), ("ktok", ktok.rearrange("p a b -> p (a b)"), NT * 256)):
            o_ = self.dout("dbg_" + nm, [128, n_], BF16)
            k.dma("sp", o_[:, :], ap_, chd)
        for nm, ap_ in (("la", la), ("beta", beta)):
            o_ = self.dout("dbg_" + nm, [128, NT * 8])
            k.dma("sp", o_[:, :], ap_.rearrange("p a b -> p (a b)"), chd)
        self.k.barrier()
    if _stop == "g1":
        return
    _arena_rewind(self, markP)
    d_otok = [Dep() for _ in range(NT)]
    hm = self.aa([128, 4])
    d_hm = Dep()
    k.pool(lambda e: e.memset(hm, 0.0), w=[d_hm])
    k.pool(lambda e: e.memset(hm[0:64, 0:4:2], 1.0), w=[d_hm])
    k.pool(lambda e: e.memset(hm[64:128, 1:4:2], 1.0), w=[d_hm])
    first_visit = [True] * NT
    orderF = [16, 17] + list(range(16))
    orderB = [17, 16] + list(range(15, -1, -1))
    W_ = {}
    WP = {}
    for dd in range(2):
        w = {}
        w["Z"] = self.aa([128, 4, 128]); w["D"] = self.aa([128, 4, 128], BF16); w["Ds"] = w["Z"]
        w["E"] = [self.aa([128, 4, 128], BF16) for _ in range(2)]
        w["ET"] = [self.aa([128, 4, 128], BF16) for _ in range(2)]
        w["d_E"] = [Dep(), Dep()]; w["d_ET"] = [Dep(), Dep()]
        w["A"] = [self.aa([128, 4, 128], BF16) for _ in range(2)]
        w["AT"] = [self.aa([128, 4, 128], BF16) for _ in range(2)]
        w["X"] = [self.aa([128, 4, 128], BF16) for _ in range(2)]
        w["QKm"] = self.aa([128, 4, 128], BF16)
        w["R"] = self.aa([128, 4, 128], BF16)
        w["d_R"] = Dep()
        WP.setdefault(dd, [])
        for par in range(2):
            wp = {"QKT": self.aa([128, 4, 128], BF16), "wT": self.aa([64, 4, 128], BF16), "kd": self.aa([128, 4, 128], BF16),
                  "ev": self.aa([128, 12]), "u": self.aa([128, 4, 64], BF16)}
            for nm in ("QKT", "wT", "kd", "ev", "u"):
                wp["d_" + nm] = Dep()
            WP[dd].append(wp)
        w["kc"] = self.aa([128, 2, 2, 128], BF16)
        w["d_kc"] = Dep()
        k.pool(lambda e, w=w: e.memset(w["kc"], 0.0), w=[w["d_kc"]])
        w["SbQ"] = self.aa([128, 4, 64], BF16)
        w["d_SbQ"] = Dep()
        k.pool(lambda e, w=w: e.memset(w["SbQ"], 0.0), w=[w["d_SbQ"]])
        w["beg"] = self.aa([128, 4])
        w["vn"] = self.aa([128, 4, 64], BF16)
        w["o2"] = self.aa([128, 4, 64], BF16)
        w["ot"] = self.aa([128, 4, 64], BF16)
        w["S"] = self.aa([128, 4, 64])
        w["Sb"] = self.aa([128, 4, 64], BF16)
        for nm in ("Z", "D", "Ds", "QKm", "beg", "vn", "o2", "ot", "S", "Sb"):
            w["d_" + nm] = Dep()
        w["d_Ds"] = w["d_Z"]
        w["d_A"] = [Dep(), Dep()]; w["d_AT"] = [Dep(), Dep()]; w["d_X"] = [Dep(), Dep()]
        k.dve(lambda e, w=w: e.memset(w["S"], 0.0), w=[w["d_S"]])
        k.pool(lambda e, w=w: e.memset(w["Sb"], 0.0), w=[w["d_Sb"]])
        W_[dd] = w

    def bank(dd, i, half=None):
        t = self.ps[2 * dd + i // 2]
        hb = i % 2
        return t[:, hb * 512:(hb + 1) * 512], self.d_ps[2 * dd + i // 2][hb]

    def unit_pre(dd, n, par):
        w = dict(W_[dd])
        w.update(WP[dd][par])
        c0 = n * 128
        lacol = la[:, n, dd * 4:dd * 4 + 4]
        becol = beta[:, n, dd * 4:dd * 4 + 4]
        Mz = GT if dd == 0 else LT
        Um = LE if dd == 0 else GE
        Ugt = GT if dd == 0 else LT
        NEG = NEGF if dd == 0 else NEGB
        strict = GT if dd == 0 else LT
        B0, dB0 = bank(dd, 0)
        B1, dB1 = bank(dd, 1)
        B2, dB2 = bank(dd, 2)
        B3, dB3 = bank(dd, 3)
        k.dve(lambda e: e.tensor_tensor(out=w["Z"], in0=Mz.unsqueeze(1).to_broadcast([128, 4, 128]),
                                        in1=lacol.unsqueeze(2).to_broadcast([128, 4, 128]), op=ALU.mult),
              r=[d_GM, d_la], w=[w["d_Z"]])
        k.pe(lambda e: e.matmul(B0, Um, w["Z"].rearrange("p a b -> p (a b)"), start=True, stop=False), r=[d_GM, w["d_Z"]], w=[dB0])
        k.pe(lambda e: e.matmul(B0, self.ident_f[:], NEG, start=False, stop=True), r=[d_GM, self.d_const], w=[dB0])
        k.act(lambda e: e.activation(out=w["D"].rearrange("p a b -> p (a b)"), in_=B0, func=AF.Exp), r=[dB0], w=[w["d_D"]])
        yield
        k.pe(lambda e: e.matmul(B3[:, 0:4], Um, lacol, start=True, stop=True), r=[d_GM, d_la], w=[dB3])
        k.pe(lambda e: e.matmul(B3[:, 4:8], Ugt, lacol, start=True, stop=True), r=[d_GM, d_la], w=[dB3])
        k.pe(lambda e: e.matmul(B3[:, 8:12], self.ones_f[:], lacol, start=True, stop=True), r=[self.d_const, d_la], w=[dB3])
        yield
        k.act(lambda e: e.activation(out=w["ev"], in_=B3[:, 0:12], func=AF.Exp), r=[dB3], w=[w["d_ev"]])
        yield
        k.dve(lambda e: e.tensor_tensor(out=w["Ds"], in0=w["D"], in1=strict.unsqueeze(1).to_broadcast([128, 4, 128]), op=ALU.mult),
              r=[w["d_D"], d_GM], w=[w["d_Ds"]])
        yield
        for h in range(4):
            hp = slice(64 * (h % 2), 64 * (h % 2) + 64)
            if h == 0:
                k.act(lambda e: e.activation(out=w["kc"][0:64, :, 0, :], in_=kT[0:64, :, c0:c0 + 128], func=AF.Copy), r=[d_qk], w=[w["d_kc"]])
                k.act(lambda e: e.activation(out=w["kc"][64:128, :, 1, :], in_=kT[64:128, :, c0:c0 + 128], func=AF.Copy), r=[d_qk], w=[w["d_kc"]])
            k.pe(lambda e, h=h, hp=hp: e.matmul(B1[:, h * 128:(h + 1) * 128], kT[:, h // 2, c0:c0 + 128], w["kc"][:, h // 2, h % 2, :],
                                                start=True, stop=True), r=[d_qk, w["d_kc"]], w=[dB1])
        yield
        k.dve(lambda e: e.tensor_tensor(out=w["Ds"].rearrange("p a b -> p (a b)"), in0=B1, in1=w["Ds"].rearrange("p a b -> p (a b)"),
                                        op=ALU.mult), r=[dB1, w["d_Ds"]], w=[w["d_Ds"]])
        k.dve(lambda e: e.tensor_tensor(out=w["A"][0], in0=w["Ds"], in1=becol.unsqueeze(2).to_broadcast([128, 4, 128]), op=ALU.mult),
              r=[w["d_Ds"], d_beta], w=[w["d_A"][0]])
        yield
        for h in range(4):
            hp = slice(64 * (h % 2), 64 * (h % 2) + 64)
            k.pe(lambda e, h=h, hp=hp: e.matmul(B2[:, h * 128:(h + 1) * 128], qT[:, h // 2, c0:c0 + 128], w["kc"][:, h // 2, h % 2, :],
                                                start=True, stop=True), r=[d_qk, w["d_kc"]], w=[dB2])
        yield
        k.dve(lambda e: e.tensor_tensor(out=w["QKm"].rearrange("p a b -> p (a b)"), in0=B2, in1=w["D"].rearrange("p a b -> p (a b)"),
                                        op=ALU.mult), r=[dB2, w["d_D"]], w=[w["d_QKm"]])
        yield
        B1b = B1.bitcast(BF16)
        B2b = B2.bitcast(BF16)
        for h in range(4):
            k.pe(lambda e, h=h: e.transpose(out=B1b[:, h * 128:(h + 1) * 128], in_=w["A"][0][:, h, :], identity=self.ident_b[:]),
                 r=[w["d_A"][0], self.d_const], w=[dB1])
        yield
        k.act(lambda e: e.activation(out=w["AT"][0].rearrange("p a b -> p (a b)"), in_=B1b[:, 0:512], func=AF.Copy),
              r=[dB1], w=[w["d_AT"][0]])
        for h in range(4):
            k.pe(lambda e, h=h: e.transpose(out=B2b[:, h * 128:(h + 1) * 128], in_=w["QKm"][:, h, :], identity=self.ident_b[:]),
                 r=[w["d_QKm"], self.d_const], w=[dB2])
        yield
        k.act(lambda e: e.activation(out=w["QKT"].rearrange("p a b -> p (a b)"), in_=B2b[:, 0:512], func=AF.Copy),
              r=[dB2], w=[w["d_QKT"]])
        yield
        k.dve(lambda e: e.tensor_tensor(out=w["beg"], in0=becol, in1=w["ev"][:, 0:4], op=ALU.mult), r=[d_beta, w["d_ev"]], w=[w["d_beg"]])
        X0 = w["R"].rearrange("p h (two d) -> p h two d", two=2)
        k.dve(lambda e: e.tensor_tensor(out=X0[:, :, 0, :], in0=vtok[:, n, :].rearrange("p (h d) -> p h d", h=4),
                                        in1=becol.unsqueeze(2).to_broadcast([128, 4, 64]), op=ALU.mult),
              r=[d_vtok, d_beta], w=[w["d_R"]])
        k.dve(lambda e: e.tensor_tensor(out=X0[:, :, 1, :], in0=ktok[:, n, :].rearrange("p (h d) -> p h d", h=4),
                                        in1=w["beg"].unsqueeze(2).to_broadcast([128, 4, 64]), op=ALU.mult),
              r=[d_ktok, w["d_beg"]], w=[w["d_R"]])
        for half in range(2):
            k.pool(lambda e, half=half: e.tensor_tensor(out=w["kd"][:, :, half * 64:(half + 1) * 64],
                                                        in0=ktok[:, n, :].rearrange("p (h d) -> p h d", h=4),
                                                        in1=w["ev"][:, 4:8].unsqueeze(2).to_broadcast([128, 4, 64]), op=ALU.mult),
                   r=[d_ktok, w["d_ev"]], w=[w["d_kd"]])
        yield
        A, AT = w["A"][0], w["AT"][0]
        dA, dAT = w["d_A"][0], w["d_AT"][0]
        Tm, TT = w["A"][1], w["AT"][1]
        dT, dTT = w["d_A"][1], w["d_AT"][1]
        M1, M1t = w["X"][0], w["X"][1]
        dM1, dM1t = w["d_X"][0], w["d_X"][1]
        mo = 0 if dd == 0 else 7
        mt = 7 if dd == 0 else 0
        I4 = self.ident_b[:].unsqueeze(1).to_broadcast([128, 4, 128])
        msk = lambda s_: LMK[:, (mo + s_) * 128:(mo + s_ + 1) * 128].unsqueeze(1).to_broadcast([128, 4, 128])
        mskT = lambda s_: LMK[:, (mt + s_) * 128:(mt + s_ + 1) * 128].unsqueeze(1).to_broadcast([128, 4, 128])
        k.pool(lambda e: e.tensor_tensor(out=Tm, in0=A, in1=msk(0), op=ALU.mult), r=[dA, d_LMK], w=[dT])
        k.dve(lambda e: e.tensor_tensor(out=Tm, in0=Tm, in1=I4, op=ALU.add), r=[dT, self.d_const], w=[dT])
        k.pool(lambda e: e.tensor_tensor(out=TT, in0=AT, in1=mskT(0), op=ALU.mult), r=[dAT, d_LMK], w=[dTT])
        k.dve(lambda e: e.tensor_tensor(out=TT, in0=TT, in1=I4, op=ALU.add), r=[dTT, self.d_const], w=[dTT])
        yield
        fl = lambda x: x.rearrange("p a b -> p (a b)")
        def mk_E(lev):
            j = lev % 2
            k.pool(lambda e, lev=lev, j=j: e.tensor_tensor(out=w["E"][j], in0=A, in1=msk(lev), op=ALU.mult), r=[dA, d_LMK], w=[w["d_E"][j]])
            k.op("dve" if lev % 2 == 0 else "pool", lambda e, lev=lev, j=j: e.tensor_tensor(out=w["ET"][j], in0=AT, in1=mskT(lev), op=ALU.mult),
                 r=[dAT, d_LMK], w=[w["d_ET"][j]])
        mk_E(1)
        for lev in range(1, 7):
            j = lev % 2
            E, ET, dE, dET = w["E"][j], w["ET"][j], w["d_E"][j], w["d_ET"][j]
            for h in range(4):
                k.pe(lambda e, h=h, ET=ET: e.matmul(B0[:, h * 128:(h + 1) * 128], ET[:, h, :], Tm[:, h, :], start=True, stop=True),
                     r=[dET, dT], w=[dB0])
            for h in range(4):
                k.pe(lambda e, h=h, E=E: e.matmul(B1[:, h * 128:(h + 1) * 128], E[:, h, :], TT[:, h, :], start=True, stop=True),
                     r=[dE, dTT], w=[dB1])
            yield
            if lev < 6:
                mk_E(lev + 1)
            k.act(lambda e: e.activation(out=fl(M1), in_=B0, func=AF.Copy), r=[dB0], w=[dM1])
            k.act(lambda e: e.activation(out=fl(M1t), in_=B1, func=AF.Copy), r=[dB1], w=[dM1t])
            yield
            for h in range(4):
                k.pe(lambda e, h=h: e.matmul(B2[:, h * 128:(h + 1) * 128], TT[:, h, :], M1[:, h, :], start=True, stop=True),
                     r=[dTT, dM1], w=[dB2])
            for h in range(4):
                k.pe(lambda e, h=h: e.matmul(B3[:, h * 128:(h + 1) * 128], Tm[:, h, :], M1t[:, h, :], start=True, stop=True),
                     r=[dT, dM1t], w=[dB3])
            yield
            k.dve(lambda e: e.tensor_tensor(out=fl(Tm), in0=fl(Tm), in1=B2, op=ALU.add), r=[dT, dB2], w=[dT])
            k.dve(lambda e: e.tensor_tensor(out=fl(TT), in0=fl(TT), in1=B3, op=ALU.add), r=[dTT, dB3], w=[dTT])
        yield
        Rm = w["R"].rearrange("p h (two d) -> p h two d", two=2)
        for h in range(4):
            k.pe(lambda e, h=h: e.matmul(B1[:, h * 64:(h + 1) * 64], TT[:, h, :], Rm[:, h, 0, :], start=True, stop=True),
                 r=[dTT, w["d_R"]], w=[dB1])
        k.act(lambda e: e.activation(out=w["u"].rearrange("p a b -> p (a b)"), in_=B1[:, 0:256], func=AF.Copy), r=[dB1], w=[w["d_u"]])
        yield
        for h in range(4):
            k.pe(lambda e, h=h: e.matmul(B2[0:64, h * 128:(h + 1) * 128], Rm[:, h, 1, :], TT[:, h, :], start=True, stop=True),
                 r=[dTT, w["d_R"]], w=[dB2])
        k.act(lambda e: e.activation(out=w["wT"].rearrange("p a b -> p (a b)"), in_=B2[0:64, :], func=AF.Copy),
              r=[dB2], w=[w["d_wT"]])
        yield

    def unit_scan(dd, n, par, need_out):
        w = dict(W_[dd])
        w.update(WP[dd][par])
        Xf, dXf = w["u"], w["d_u"]
        c0 = n * 128
        B0, dB0 = bank(dd, 0)
        B1, dB1 = bank(dd, 1)
        for h in range(4):
            k.pe(lambda e, h=h: e.matmul(B0[:, h * 64:(h + 1) * 64], w["wT"][:, h, :], w["Sb"][0:64, h, :], start=True, stop=True),
                 r=[w["d_wT"], w["d_Sb"]], w=[dB0])
        if need_out:
            for h in range(4):
                hp = slice(64 * (h % 2), 64 * (h % 2) + 64)
                k.pe(lambda e, h=h, hp=hp: e.matmul(B0[:, 256 + h * 64:256 + (h + 1) * 64], qT[:, h // 2, c0:c0 + 128],
                                                    w["SbQ"][:, h, :], start=True, stop=True),
                     r=[d_qk, w["d_SbQ"]], w=[dB0])
        yield
        k.dve(lambda e: e.tensor_tensor(out=w["vn"], in0=Xf, in1=B0[:, 0:256].rearrange("p (h d) -> p h d", h=4), op=ALU.subtract),
              r=[dXf, dB0], w=[w["d_vn"]])
        if need_out:
            for h in range(4):
                k.pe(lambda e, h=h: e.matmul(B1[:, h * 64:(h + 1) * 64], w["QKT"][:, h, :], w["vn"][:, h, :], start=True, stop=True),
                     r=[w["d_QKT"], w["d_vn"]], w=[dB1])
            k.act(lambda e: e.activation(out=w["o2"].rearrange("p a b -> p (a b)"), in_=B1[:, 0:256], func=AF.Copy), r=[dB1], w=[w["d_o2"]])
            k.dve(lambda e: e.tensor_tensor(out=w["ot"], in0=B0[:, 256:512].rearrange("p (h d) -> p h d", h=4),
                                            in1=w["ev"][:, 0:4].unsqueeze(2).to_broadcast([128, 4, 64]), op=ALU.mult),
                  r=[dB0, w["d_ev"]], w=[w["d_ot"]])
            k.pool(lambda e: e.tensor_tensor(out=w["ot"], in0=w["ot"], in1=w["o2"], op=ALU.add), r=[w["d_ot"], w["d_o2"]], w=[w["d_ot"]])
            if first_visit[n]:
                first_visit[n] = False
                k.pool(lambda e: e.tensor_copy(out=otok[:, n, :].rearrange("p (h d) -> p h d", h=4), in_=w["ot"]), r=[w["d_ot"]], w=[d_otok[n]])
            else:
                k.pool(lambda e: e.tensor_tensor(out=otok[:, n, :].rearrange("p (h d) -> p h d", h=4),
                                                 in0=otok[:, n, :].rearrange("p (h d) -> p h d", h=4), in1=w["ot"], op=ALU.add),
                       r=[w["d_ot"], d_otok[n]], w=[d_otok[n]])
        yield
        for h in range(4):
            k.pe(lambda e, h=h: e.matmul(B1[:, 256 + h * 64:256 + (h + 1) * 64], w["kd"][:, h, :], w["vn"][:, h, :], start=True, stop=True),
                 r=[w["d_kd"], w["d_vn"]], w=[dB1])
        yield
        k.dve(lambda e: e.tensor_tensor(out=w["S"], in0=w["S"], in1=w["ev"][:, 8:12].unsqueeze(2).to_broadcast([128, 4, 64]), op=ALU.mult),
              r=[w["d_S"], w["d_ev"]], w=[w["d_S"]])
        k.dve(lambda e: e.tensor_tensor(out=w["S"], in0=w["S"], in1=B1[:, 256:512].rearrange("p (h d) -> p h d", h=4), op=ALU.add),
              r=[w["d_S"], dB1], w=[w["d_S"]])
        k.act(lambda e: e.activation(out=w["Sb"], in_=w["S"], func=AF.Copy), r=[w["d_S"]], w=[w["d_Sb"]])
        k.pool(lambda e: e.tensor_tensor(out=w["SbQ"], in0=w["S"], in1=hm.unsqueeze(2).to_broadcast([128, 4, 64]), op=ALU.mult),
               r=[w["d_S"], d_hm], w=[w["d_SbQ"]])

        yield

    def round_robin(gens):
        alive = list(gens)
        while alive:
            for g in list(alive):
                try:
                    next(g)
                except StopIteration:
                    alive.remove(g)

    _mode = os.environ.get("GDN_RR", "pipe")
    prev = []
    for step in range(18):
        par = step % 2
        cur = []
        pres = []
        for dd in range(2):
            n = (orderF if dd == 0 else orderB)[step]
            need_out = (n < 16) or with_ctx
            pres.append(unit_pre(dd, n, par))
            cur.append((dd, n, par, need_out))
        if _mode == "pipe":
            round_robin(pres + prev)
            prev = [unit_scan(*c_) for c_ in cur]
        else:
            round_robin(pres)
            round_robin([unit_scan(*c_) for c_ in cur])
    round_robin(prev)
    if "gdn_otok" in self.dbg:
        self.k.barrier()
        chd = k.chan()
        o_ = self.dout("dbg_otok", [128, NT * 256])
        k.dma("sp", o_[:, :], otok.rearrange("p a b -> p (a b)"), chd)
        self.k.barrier()
    if _stop:
        return
    _arena_rewind(self, markO)
    _gdn_out(self, l, with_ctx, otok, d_otok)


def _gdn_out(self, l, with_ctx, otok, d_otok):
    k = self.k
    ntt = NT if with_ctx else 16
    _wring_init(self, 1)
    gT = self.aa([128, 2, T], BF16)
    d_g = Dep()
    blocks = _blocks(self, with_ctx)

    def evac(c, m, t0, n, pst, dps):
        ct = (c - 768) // 128
        k.act(lambda e: e.activation(out=gT[:, ct, t0:t0 + n], in_=pst, func=AF.Silu), r=[dps], w=[d_g])
    _proj(self, l, [(768, 256)], blocks, evac, PROJ_BANKS)
    gn = _ppv(self, l, "gnorm")
    sq = self.aa([128, 4, 64])
    ss = self.aa([128, 4])
    on = [self.aa([128, 256], BF16) for _ in range(2)]
    d_sq, d_ss = Dep(), Dep()
    d_on = [Dep(), Dep()]
    psT = self.ps[2][:, 0:512].bitcast(BF16)
    d_psT = self.d_ps[2][0]
    for tt in range(ntt):
        o3 = otok[:, tt, :].rearrange("p (h d) -> p h d", h=4)
        j = tt % 2
        k.dve(lambda e, o3=o3: e.tensor_tensor(out=sq, in0=o3, in1=o3, op=ALU.mult), r=[d_otok[tt]], w=[d_sq])
        k.dve(lambda e: e.reduce_sum(out=ss, in_=sq, axis=mybir.AxisListType.X), r=[d_sq], w=[d_ss])
        k.act(lambda e: e.activation(out=ss, in_=ss, func=AF.Sqrt, bias=EPS, scale=1.0 / 64), r=[d_ss], w=[d_ss])
        k.dve(lambda e: e.reciprocal(out=ss, in_=ss), r=[d_ss], w=[d_ss])
        k.dve(lambda e, o3=o3, j=j: e.tensor_tensor(out=on[j].rearrange("p (h d) -> p h d", h=4), in0=o3,
                                                    in1=ss.unsqueeze(2).to_broadcast([128, 4, 64]), op=ALU.mult),
              r=[d_otok[tt], d_ss], w=[d_on[j]])
        for ct in range(2):
            k.pe(lambda e, j=j, ct=ct: e.transpose(out=psT[:, ct * 128:(ct + 1) * 128], in_=on[j][:, ct * 128:(ct + 1) * 128],
                                                   identity=self.ident_b[:]), r=[d_on[j], self.d_const], w=[d_psT])
        for ct in range(2):
            k.dve(lambda e, ct=ct, tt=tt: e.scalar_tensor_tensor(out=self.C[:, ct, tt * 128:(tt + 1) * 128], in0=psT[:, ct * 128:(ct + 1) * 128],
                                                                 scalar=gn[:, 0:1], in1=gT[:, ct, tt * 128:(tt + 1) * 128],
                                                                 op0=ALU.mult, op1=ALU.mult),
                  r=[d_psT, d_g, self.d_pp], w=[self.d_C[tt]])


from concourse.bass_utils import run_bass_kernel_spmd


def kernel(**inputs):
    inputs = {k_: np.asarray(v) for k_, v in inputs.items()}
    mk = MK()
    nc = mk.build()
    consts = const_inputs(inputs)
    pp = pp_host(inputs)

    def extra(b):
        e = {"pp": pp}
        e.update(consts)
        return e
    maps = host_inputs(mk, inputs, extra=extra)
    n = len(maps)
    res = run_bass_kernel_spmd(nc, maps, core_ids=list(range(n)))
    out = np.stack([np.asarray(res.results[b]["out"]) for b in range(n)], axis=0)
    return out.astype(np.float32)
```
